# Optimizing a Trainium2 kernel written in Bass

```python
import jax
import jax.numpy as jnp
from jax import lax
import numpy as np

D_MODEL = 1024
BATCH = 32
SEQ = 256
DEPTH = 2
DEC_BATCH = 2
DEC_SEQ = 2048
PAST_LEN = 512

GRID_W = 64
N_BRANCH = 4
BRANCH_W = 256
MLA_HEADS = 4
MLA_Q_LORA = 256
MLA_KV_LORA = 128
MLA_NOPE = 64
MLA_ROPE = 32
MLA_V = 64
MLA_SCALE = (MLA_NOPE + MLA_ROPE) ** -0.5
FNET_GROUPS = 4
FNET_CH = BRANCH_W // FNET_GROUPS
GLA_HEADS = 4
GLA_DK = 32
GLA_DV = 64
GLA_GATE_RANK = 16
GLA_TAU = 16.0
GLA_CHUNK = 64
SWA_HEADS = 4
SWA_KV_HEADS = 2
SWA_GROUP = SWA_HEADS // SWA_KV_HEADS
SWA_HEAD_DIM = 64
SWA_WINDOW = 128
SWA_SCALE = SWA_HEAD_DIM ** -0.5
ATTN_BLOCK = 128
PEER_HEADS = 8
PEER_N_KEYS = 128
PEER_N_EXPERTS = PEER_N_KEYS * PEER_N_KEYS
PEER_KEY_DIM = 256
PEER_HALF = PEER_KEY_DIM // 2
PEER_TOPK = 16
PEER_TOKEN_BLOCK = 128

ROPE_THETA = 10000.0
NORM_EPS = 1e-6
DEEPNORM_ALPHA = (2.0 * DEPTH) ** 0.25
DEEPNORM_BETA = (8.0 * DEPTH) ** -0.25

IN_SPLITS = (
    ('mla_q', MLA_Q_LORA),
    ('mla_kv', MLA_KV_LORA + MLA_ROPE),
    ('fnet', BRANCH_W),
    ('gla_q', GLA_HEADS * GLA_DK),
    ('gla_k', GLA_HEADS * GLA_DK),
    ('gla_v', GLA_HEADS * GLA_DV),
    ('gla_g', BRANCH_W),
    ('gla_af', GLA_GATE_RANK),
    ('gla_ab', GLA_GATE_RANK),
    ('swa_q', SWA_HEADS * SWA_HEAD_DIM),
    ('swa_k', SWA_KV_HEADS * SWA_HEAD_DIM),
    ('swa_v', SWA_KV_HEADS * SWA_HEAD_DIM),
    ('gates', N_BRANCH * D_MODEL),
)
IN_NAMES = tuple(n for n, _ in IN_SPLITS)
IN_OFFSETS = tuple(int(o) for o in np.cumsum([w for _, w in IN_SPLITS])[:-1])
IN_WIDTH = int(sum(w for _, w in IN_SPLITS))

kernel_name = 'hybrid_diffusion_mla_fnet_gla_swa_peer_step'


def layer_norm(x, g=None, b=None):
    xf = x.astype(jnp.float32)
    mu = jnp.mean(xf, -1, keepdims=True)
    var = jnp.mean(jnp.square(xf - mu), -1, keepdims=True)
    y = (xf - mu) * lax.rsqrt(var + NORM_EPS)
    if g is not None:
        y = y * g.astype(jnp.float32) + b.astype(jnp.float32)
    return y.astype(x.dtype)


def rms_norm(x, g):
    xf = x.astype(jnp.float32)
    y = xf * lax.rsqrt(jnp.mean(xf * xf, -1, keepdims=True) + NORM_EPS) * g.astype(jnp.float32)
    return y.astype(x.dtype)


def axial_rope(x):
    n = x.shape[-2]
    half = x.shape[-1] // 2
    t = jnp.arange(n)
    rows = (t // GRID_W).astype(jnp.float32)
    cols = (t % GRID_W).astype(jnp.float32)
    freqs = ROPE_THETA ** (-jnp.arange(0, half, 2, dtype=jnp.float32) / half)

    def rot(xa, pos):
        ang = pos[:, None] * freqs[None, :]
        cos, sin = jnp.cos(ang), jnp.sin(ang)
        x1, x2 = xa[..., :half // 2], xa[..., half // 2:]
        return jnp.concatenate([x1 * cos - x2 * sin, x2 * cos + x1 * sin], -1)

    xf = x.astype(jnp.float32)
    return jnp.concatenate([rot(xf[..., :half], rows), rot(xf[..., half:], cols)], -1).astype(x.dtype)


def softmax_with_sink(s, sink):
    if sink is None:
        return jax.nn.softmax(s, -1)
    s_all = jnp.concatenate([s, jnp.broadcast_to(sink, s[..., :1].shape)], -1)
    return jax.nn.softmax(s_all, -1)[..., :-1]


def dense_attention(q, k, v, scale, sink=None):
    b, hk, g, sq, dk = q.shape
    nb = sq // ATTN_BLOCK
    qb = jnp.moveaxis(q.reshape(b, hk, g, nb, ATTN_BLOCK, dk), 3, 0)
    sink_b = None if sink is None else sink.astype(jnp.float32)[None, :, :, None, None]

    def one_block(qi):
        s = jnp.einsum('bhgqd,bhkd->bhgqk', qi, k, preferred_element_type=jnp.float32) * scale
        p = softmax_with_sink(s, sink_b)
        return jnp.einsum('bhgqk,bhkd->bhgqd', p.astype(v.dtype), v)

    o = lax.map(one_block, qb)
    return jnp.moveaxis(o, 0, 3).reshape(b, hk, g, sq, v.shape[-1])


def windowed_attention(q, k, v, k_ctx, v_ctx, sink):
    b, hk, g, s, dh = q.shape
    blk = SWA_WINDOW
    nb = s // blk
    qb = q.reshape(b, hk, g, nb, blk, dh)

    def band(x):
        xp = jnp.pad(x, ((0, 0), (0, 0), (blk, blk), (0, 0))).reshape(b, hk, nb + 2, blk, x.shape[-1])
        return jnp.concatenate([xp[:, :, :-2], xp[:, :, 1:-1], xp[:, :, 2:]], axis=3)

    kb, vb = band(k), band(v)
    qpos = jnp.arange(nb)[:, None] * blk + jnp.arange(blk)[None, :]
    kpos = (jnp.arange(nb)[:, None] - 1) * blk + jnp.arange(3 * blk)[None, :]
    rel = kpos[:, None, :] - qpos[:, :, None]
    valid = (jnp.abs(rel) <= SWA_WINDOW) & (kpos[:, None, :] >= 0) & (kpos[:, None, :] < s)
    s_band = jnp.einsum('bhgnqd,bhnkd->bhgnqk', qb, kb, preferred_element_type=jnp.float32) * SWA_SCALE
    s_band = jnp.where(valid, s_band, -jnp.inf)
    s_ctx = jnp.einsum('bhgnqd,bhld->bhgnql', qb, k_ctx, preferred_element_type=jnp.float32) * SWA_SCALE
    p = softmax_with_sink(jnp.concatenate([s_band, s_ctx], -1),
                          sink.astype(jnp.float32)[None, :, :, None, None, None])
    p_band, p_ctx = p[..., :3 * blk], p[..., 3 * blk:]
    o = (jnp.einsum('bhgnqk,bhnkd->bhgnqd', p_band.astype(v.dtype), vb)
         + jnp.einsum('bhgnql,bhld->bhgnqd', p_ctx.astype(v.dtype), v_ctx))
    return o.reshape(b, hk, g, s, dh)


def gla_scan(q, k, v, log_a, s0):
    b, s, h, dk = q.shape
    dv = v.shape[-1]
    n = s // GLA_CHUNK
    f32 = jnp.float32
    qc, kc, vc, ac = [t.astype(f32).reshape(b, n, GLA_CHUNK, h, t.shape[-1]) for t in (q, k, v, log_a)]
    cum = jnp.cumsum(ac, axis=2)
    causal = jnp.tril(jnp.ones((GLA_CHUNK, GLA_CHUNK), dtype=bool))
    decay = jnp.exp(jnp.minimum(cum[:, :, :, None] - cum[:, :, None, :], 0.0))
    att = jnp.sum(qc[:, :, :, None] * kc[:, :, None, :] * decay, -1)
    att = jnp.where(causal[:, :, None], att, 0.0)
    o_intra = jnp.einsum('bntsh,bnshv->bnthv', att, vc)
    last = cum[:, :, -1:]
    u = jnp.einsum('bnshk,bnshv->bnhkv', kc * jnp.exp(last - cum), vc)
    g = jnp.exp(last[:, :, 0])

    def step(state, xs):
        g_n, u_n = xs
        return g_n[..., None] * state + u_n, state

    s_final, s_in = lax.scan(step, s0.astype(f32), (jnp.moveaxis(g, 1, 0), jnp.moveaxis(u, 1, 0)))
    s_in = jnp.moveaxis(s_in, 0, 1)
    o_inter = jnp.einsum('bnthk,bnhkv->bnthv', qc * jnp.exp(cum), s_in)
    return (o_intra + o_inter).reshape(b, s, h, dv).astype(v.dtype), s_final.astype(v.dtype)


def gla_bidirectional(parts, lp, s0_fwd, s0_bwd):
    b, s, _ = parts['gla_q'].shape
    q = parts['gla_q'].reshape(b, s, GLA_HEADS, GLA_DK) * (GLA_DK ** -0.5)
    k = parts['gla_k'].reshape(b, s, GLA_HEADS, GLA_DK)
    v = parts['gla_v'].reshape(b, s, GLA_HEADS, GLA_DV)

    def log_decay(a_low, w2, b2):
        z = (a_low @ w2 + b2).astype(jnp.float32)
        return (jax.nn.log_sigmoid(z) / GLA_TAU).reshape(b, s, GLA_HEADS, GLA_DK)

    la_f = log_decay(parts['gla_af'], lp['w_gla_a_fwd'], lp['b_gla_a_fwd'])
    la_b = log_decay(parts['gla_ab'], lp['w_gla_a_bwd'], lp['b_gla_a_bwd'])
    o_f, s_f = gla_scan(q, k, v, la_f, s0_fwd)
    flip = lambda t: jnp.flip(t, axis=1)
    o_b, s_b = gla_scan(flip(q), flip(k), flip(v), flip(la_b), s0_bwd)
    o = rms_norm(o_f + flip(o_b), lp['gla_norm']).reshape(b, s, BRANCH_W)
    return o * jax.nn.silu(parts['gla_g']), s_f, s_b


def fourier_mix(f):
    b, s, _ = f.shape
    fg = f.astype(jnp.float32).reshape(b, s, FNET_GROUPS, FNET_CH)
    return jnp.fft.fft2(fg, axes=(1, 3), norm='ortho').real.reshape(b, s, BRANCH_W).astype(f.dtype)


def mla_project(parts, lp, positional):
    b, s, _ = parts['mla_q'].shape
    q = (rms_norm(parts['mla_q'], lp['mla_q_norm']) @ lp['w_uq'])
    q = q.reshape(b, s, MLA_HEADS, MLA_NOPE + MLA_ROPE).transpose(0, 2, 1, 3)
    q_nope, q_rope = q[..., :MLA_NOPE], q[..., MLA_NOPE:]
    ckv = rms_norm(parts['mla_kv'][..., :MLA_KV_LORA], lp['mla_kv_norm'])
    k_rope = parts['mla_kv'][..., MLA_KV_LORA:]
    if positional:
        q_rope = axial_rope(q_rope)
        k_rope = axial_rope(k_rope)
    return jnp.concatenate([q_nope, q_rope], -1), ckv, k_rope


def mla_expand(ckv, k_rope, w_ukv):
    b, s, _ = ckv.shape
    kv = (ckv @ w_ukv).reshape(b, s, MLA_HEADS, MLA_NOPE + MLA_V).transpose(0, 2, 1, 3)
    k_nope, v = kv[..., :MLA_NOPE], kv[..., MLA_NOPE:]
    k = jnp.concatenate([k_nope, jnp.broadcast_to(k_rope[:, None], (b, MLA_HEADS, s, MLA_ROPE))], -1)
    return k, v


def mla_attend(q, k, v):
    o = dense_attention(q[:, :, None], k, v, MLA_SCALE)[:, :, 0]
    b, h, s, dv = o.shape
    return o.transpose(0, 2, 1, 3).reshape(b, s, h * dv)


def swa_project(parts, positional):
    b, s, _ = parts['swa_q'].shape
    q = parts['swa_q'].reshape(b, s, SWA_KV_HEADS, SWA_GROUP, SWA_HEAD_DIM).transpose(0, 2, 3, 1, 4)
    k = parts['swa_k'].reshape(b, s, SWA_KV_HEADS, SWA_HEAD_DIM).transpose(0, 2, 1, 3)
    v = parts['swa_v'].reshape(b, s, SWA_KV_HEADS, SWA_HEAD_DIM).transpose(0, 2, 1, 3)
    if positional:
        q = axial_rope(q)
        k = axial_rope(k)
    return q, k, v


def swa_merge_heads(o):
    b, hk, g, s, dh = o.shape
    return o.transpose(0, 3, 1, 2, 4).reshape(b, s, hk * g * dh)


def merge_branches(branches, gates, lp):
    b, s = branches.shape[:2]
    proj = jnp.einsum('bsnc,ncd->bsnd', branches, lp['w_branch'])
    g = jax.nn.sigmoid(gates.reshape(b, s, N_BRANCH, D_MODEL))
    return jnp.sum(g * proj, axis=2) @ lp['w_out']


def context_mixer(u, lp):
    b, s, _ = u.shape
    parts = dict(zip(IN_NAMES, jnp.split(u @ lp['w_in'], IN_OFFSETS, axis=-1)))
    q_a, ckv, k_rope = mla_project(parts, lp, positional=False)
    k_a, v_a = mla_expand(ckv, k_rope, lp['w_ukv'])
    o_a = mla_attend(q_a, k_a, v_a)
    o_b = fourier_mix(parts['fnet'])
    zero = jnp.zeros((b, GLA_HEADS, GLA_DK, GLA_DV), jnp.float32)
    o_c, s_f, s_b = gla_bidirectional(parts, lp, zero, zero)
    q_d, k_d, v_d = swa_project(parts, positional=False)
    o_d = swa_merge_heads(dense_attention(q_d, k_d, v_d, SWA_SCALE,
                                          lp['swa_sink'].reshape(SWA_KV_HEADS, SWA_GROUP)))
    mix = merge_branches(jnp.stack([o_a, o_b, o_c, o_d], axis=2), parts['gates'], lp)
    return mix, (ckv, k_rope, k_d, v_d, jnp.stack([s_f, s_b], axis=1))


def latent_mixer(u, lp, cache):
    ckv_c, krope_c, k_ctx, v_ctx, gla_state = cache
    parts = dict(zip(IN_NAMES, jnp.split(u @ lp['w_in'], IN_OFFSETS, axis=-1)))
    q_a, ckv, k_rope = mla_project(parts, lp, positional=True)
    k_lat, v_lat = mla_expand(ckv, k_rope, lp['w_ukv'])
    k_c, v_c = mla_expand(ckv_c, krope_c, lp['w_ukv'])
    o_a = mla_attend(q_a, jnp.concatenate([k_c, k_lat], axis=2), jnp.concatenate([v_c, v_lat], axis=2))
    o_b = fourier_mix(parts['fnet'])
    o_c, _, _ = gla_bidirectional(parts, lp, gla_state[:, 0], gla_state[:, 1])
    q_d, k_d, v_d = swa_project(parts, positional=True)
    o_d = swa_merge_heads(windowed_attention(q_d, k_d, v_d, k_ctx, v_ctx,
                                             lp['swa_sink'].reshape(SWA_KV_HEADS, SWA_GROUP)))
    return merge_branches(jnp.stack([o_a, o_b, o_c, o_d], axis=2), parts['gates'], lp)


def peer(u, lp):
    b, s, d = u.shape
    t = b * s
    xt = u.reshape(t, d)
    q = (xt @ lp['w_peer_q']).reshape(t, PEER_HEADS, 2, PEER_HALF)
    sc = jnp.einsum('thpc,hpkc->thpk', q, lp['peer_keys'], preferred_element_type=jnp.float32)
    v1, i1 = lax.top_k(sc[:, :, 0], PEER_TOPK)
    v2, i2 = lax.top_k(sc[:, :, 1], PEER_TOPK)
    cand = (v1[..., :, None] + v2[..., None, :]).reshape(t, PEER_HEADS, PEER_TOPK * PEER_TOPK)
    cidx = (i1[..., :, None] * PEER_N_KEYS + i2[..., None, :]).reshape(t, PEER_HEADS, PEER_TOPK * PEER_TOPK)
    top, pos = lax.top_k(cand, PEER_TOPK)
    idx = jnp.take_along_axis(cidx, pos, axis=-1)
    w = jax.nn.softmax(top, axis=-1)
    nb = t // PEER_TOKEN_BLOCK
    hk = PEER_HEADS * PEER_TOPK
    tab_u, tab_v = lp['peer_u'], lp['peer_v']

    def block(args):
        xb, ib, wb = args
        act = jax.nn.gelu(jnp.einsum('td,tkd->tk', xb, tab_u[ib]), approximate=False)
        return jnp.einsum('tk,tkd->td', (wb * act).astype(xb.dtype), tab_v[ib])

    out = lax.map(block, (xt.reshape(nb, PEER_TOKEN_BLOCK, d),
                          idx.reshape(nb, PEER_TOKEN_BLOCK, hk),
                          w.reshape(nb, PEER_TOKEN_BLOCK, hk)))
    return out.reshape(b, s, d).astype(u.dtype)


def trunk_layer(x, cond, lp, cache):
    m = jax.nn.silu(cond) @ lp['w_ada'] + lp['b_ada']
    sh1, sc1, g1, sh2, sc2, g2 = jnp.split(m[:, None, :], 6, axis=-1)
    u = layer_norm(x) * (1 + sc1) + sh1
    if cache is None:
        mix, new_cache = context_mixer(u, lp)
    else:
        mix, new_cache = latent_mixer(u, lp, cache), None
    x = layer_norm(DEEPNORM_ALPHA * x + g1 * mix, lp['ln1_g'], lp['ln1_b'])
    u = layer_norm(x) * (1 + sc2) + sh2
    x = layer_norm(DEEPNORM_ALPHA * x + g2 * peer(u, lp), lp['ln2_g'], lp['ln2_b'])
    return x, new_cache


def setup_inputs(seed: int = 0) -> dict:
    key = jax.random.key(seed)
    ks = iter(jax.random.split(key, 48))
    nrm = lambda shape, scale: jax.random.normal(next(ks), shape, jnp.float32) * scale
    L, D = DEPTH, D_MODEL
    return {
        'x_prompt': nrm((BATCH, SEQ, D), 1.0),
        'x_sample': nrm((DEC_BATCH, DEC_SEQ, D), 1.0),
        'c': nrm((DEC_BATCH, D), 1.0),
        'cache_mla_ckv': nrm((DEC_BATCH, L, PAST_LEN, MLA_KV_LORA), 1.0),
        'cache_mla_krope': nrm((DEC_BATCH, L, PAST_LEN, MLA_ROPE), 1.0),
        'cache_swa_k': nrm((DEC_BATCH, L, SWA_KV_HEADS, PAST_LEN, SWA_HEAD_DIM), 1.0),
        'cache_swa_v': nrm((DEC_BATCH, L, SWA_KV_HEADS, PAST_LEN, SWA_HEAD_DIM), 1.0),
        'state_gla': nrm((DEC_BATCH, L, 2, GLA_HEADS, GLA_DK, GLA_DV), 1.0),
        'c_ctx': nrm((D,), 1.0),
        'w_ada': nrm((L, D, 6 * D), 0.5 * D ** -0.5),
        'b_ada': nrm((L, 6 * D), 0.01),
        'w_in': nrm((L, D, IN_WIDTH), D ** -0.5),
        'mla_q_norm': 1.0 + nrm((L, MLA_Q_LORA), 0.02),
        'w_uq': nrm((L, MLA_Q_LORA, MLA_HEADS * (MLA_NOPE + MLA_ROPE)), MLA_Q_LORA ** -0.5),
        'mla_kv_norm': 1.0 + nrm((L, MLA_KV_LORA), 0.02),
        'w_ukv': nrm((L, MLA_KV_LORA, MLA_HEADS * (MLA_NOPE + MLA_V)), MLA_KV_LORA ** -0.5),
        'w_gla_a_fwd': nrm((L, GLA_GATE_RANK, GLA_HEADS * GLA_DK), GLA_GATE_RANK ** -0.5),
        'b_gla_a_fwd': nrm((L, GLA_HEADS * GLA_DK), 0.01),
        'w_gla_a_bwd': nrm((L, GLA_GATE_RANK, GLA_HEADS * GLA_DK), GLA_GATE_RANK ** -0.5),
        'b_gla_a_bwd': nrm((L, GLA_HEADS * GLA_DK), 0.01),
        'gla_norm': 1.0 + nrm((L, GLA_DV), 0.02),
        'swa_sink': nrm((L, SWA_HEADS), 0.1),
        'w_branch': nrm((L, N_BRANCH, BRANCH_W, D), BRANCH_W ** -0.5),
        'w_out': nrm((L, D, D), DEEPNORM_BETA * D ** -0.5),
        'ln1_g': 1.0 + nrm((L, D), 0.02),
        'ln1_b': nrm((L, D), 0.02),
        'ln2_g': 1.0 + nrm((L, D), 0.02),
        'ln2_b': nrm((L, D), 0.02),
        'w_peer_q': nrm((L, D, PEER_HEADS * PEER_KEY_DIM), D ** -0.5),
        'peer_keys': nrm((L, PEER_HEADS, 2, PEER_N_KEYS, PEER_HALF), PEER_HALF ** -0.5),
        'peer_u': nrm((L, PEER_N_EXPERTS, D), D ** -0.5),
        'peer_v': nrm((L, PEER_N_EXPERTS, D), DEEPNORM_BETA * PEER_HEADS ** -0.5),
    }


def reference(x_prompt, x_sample, c, cache_mla_ckv, cache_mla_krope, cache_swa_k, cache_swa_v, state_gla,
              c_ctx, w_ada, b_ada, w_in, mla_q_norm, w_uq, mla_kv_norm, w_ukv,
              w_gla_a_fwd, b_gla_a_fwd, w_gla_a_bwd, b_gla_a_bwd, gla_norm, swa_sink,
              w_branch, w_out, ln1_g, ln1_b, ln2_g, ln2_b, w_peer_q, peer_keys, peer_u, peer_v):
    def layer_params(l):
        return {
            'w_ada': w_ada[l], 'b_ada': b_ada[l], 'w_in': w_in[l],
            'mla_q_norm': mla_q_norm[l], 'w_uq': w_uq[l], 'mla_kv_norm': mla_kv_norm[l], 'w_ukv': w_ukv[l],
            'w_gla_a_fwd': w_gla_a_fwd[l], 'b_gla_a_fwd': b_gla_a_fwd[l],
            'w_gla_a_bwd': w_gla_a_bwd[l], 'b_gla_a_bwd': b_gla_a_bwd[l], 'gla_norm': gla_norm[l],
            'swa_sink': swa_sink[l], 'w_branch': w_branch[l], 'w_out': w_out[l],
            'ln1_g': ln1_g[l], 'ln1_b': ln1_b[l], 'ln2_g': ln2_g[l], 'ln2_b': ln2_b[l],
            'w_peer_q': w_peer_q[l], 'peer_keys': peer_keys[l], 'peer_u': peer_u[l], 'peer_v': peer_v[l],
        }

    h = x_prompt
    ctx_states = []
    for l in range(DEPTH):
        h, st = trunk_layer(h, c_ctx[None, :], layer_params(l), None)
        ctx_states.append(st)
    y_prompt = h
    new_mla_ckv = jnp.stack([st[0] for st in ctx_states], axis=1)
    new_mla_krope = jnp.stack([st[1] for st in ctx_states], axis=1)
    new_swa_k = jnp.stack([st[2] for st in ctx_states], axis=1)
    new_swa_v = jnp.stack([st[3] for st in ctx_states], axis=1)
    new_gla_state = jnp.stack([st[4] for st in ctx_states], axis=1)

    z = x_sample
    for l in range(DEPTH):
        cache = (cache_mla_ckv[:, l], cache_mla_krope[:, l], cache_swa_k[:, l], cache_swa_v[:, l], state_gla[:, l])
        z, _ = trunk_layer(z, c, layer_params(l), cache)
    y_sample = z
    return (y_prompt, y_sample, new_mla_ckv, new_mla_krope, new_swa_k, new_swa_v, new_gla_state)
```

```python
import numpy as np
import ml_dtypes
from contextlib import ExitStack
import concourse.bass as bass
import concourse.mybir as mybir
from concourse.bass_utils import run_bass_kernel_spmd

F32 = mybir.dt.float32
BF16 = mybir.dt.bfloat16
U32 = mybir.dt.uint32
AF = mybir.ActivationFunctionType
ALU = mybir.AluOpType
AX = mybir.AxisListType
s_ = np.s_

ENGS = ("pe", "act", "dve", "pool", "sp")
SAME_ENGINE_SYNC = True

T = 2048
NB = 16
D = 1024
L = 2
ALPHA = (2.0 * L) ** 0.25
EPS = 1e-6
NEG = -30000.0
MLA_SCALE = 96.0 ** -0.5
SWA_SCALE = 64.0 ** -0.5
WIN_EXT = 6496


class V:
    def __init__(self, ap, toks):
        self.ap = ap
        self.toks = toks

    def m(self, fn):
        return V(fn(self.ap), self.toks)


class Buf:
    def __init__(self, name, t, nsub=1):
        self.name = name
        self.t = t
        self.nsub = nsub

    def tok(self, subs=None):
        if subs is None:
            return [(self.name, s) for s in range(self.nsub)]
        if isinstance(subs, int):
            subs = [subs]
        return [(self.name, s) for s in subs]

    def v(self, key=None, subs=None):
        ap = self.t[:] if key is None else self.t[key]
        return V(ap, self.tok(subs))


class Sched:
    def __init__(self, nc, es, nd=24):
        self.nc = nc
        self.ops = {e: [] for e in ENGS}
        self.cnt = {e: 0 for e in ENGS}
        self.known = {e: {} for e in ENGS}
        self.nd = nd
        self.dma_tot = [0] * nd
        self.dma_rr = 0
        self.last_w = {}
        self.readers = {}
        self.sem = {e: es.enter_context(nc.semaphore("sem_" + e)) for e in ENGS if e != "sp"}
        self.dsem = [es.enter_context(nc.semaphore("dsem%d" % i)) for i in range(nd)]
        self.milestones = {e: set() for e in ENGS}

    def _need(self, eng, dep, waits):
        kind, key, val = dep
        if kind == "eng" and key == eng:
            if eng in ("pe", "sp") or not SAME_ENGINE_SYNC:
                return
        k = (kind, key)
        if self.known[eng].get(k, 0) >= val:
            return
        self.known[eng][k] = val
        waits.append((kind, key, val))
        if kind == "eng":
            self.milestones[key].add(val)

    def _deps(self, eng, reads, writes):
        waits = []
        for t in reads:
            lw = self.last_w.get(t)
            if lw is not None:
                self._need(eng, lw, waits)
        for t in writes:
            lw = self.last_w.get(t)
            if lw is not None:
                self._need(eng, lw, waits)
            for r in self.readers.get(t, ()):
                self._need(eng, r, waits)
        return waits

    def _commit(self, me, reads, writes):
        for t in reads:
            self.readers.setdefault(t, []).append(me)
        for t in writes:
            self.last_w[t] = me
            self.readers[t] = []

    def op(self, eng, fn, reads=(), writes=()):
        reads = list(reads); writes = list(writes)
        waits = self._deps(eng, reads, writes)
        self.cnt[eng] += 1
        me = ("eng", eng, self.cnt[eng])
        self.ops[eng].append((waits, fn, ("eng", self.cnt[eng])))
        self._commit(me, reads, writes)

    def dma(self, eng, fn, reads=(), writes=()):
        reads = list(reads); writes = list(writes)
        i = self.dma_rr
        self.dma_rr = (i + 1) % self.nd
        waits = []
        if self.dma_tot[i] > 0:
            self._need(eng, ("dma", i, self.dma_tot[i]), waits)
        waits += self._deps(eng, reads, writes)
        self.dma_tot[i] += 16
        me = ("dma", i, self.dma_tot[i])
        self.cnt[eng] += 1
        self.ops[eng].append((waits, fn, ("dma", i)))
        self._commit(me, reads, writes)

    def _last_seq(self, e):
        for w, fn, inc in reversed(self.ops[e]):
            if inc is not None and inc[0] == "eng":
                return inc[1]
        return 0

    def barrier(self):
        lasts = {e: self._last_seq(e) for e in ENGS}
        for e in ENGS:
            waits = []
            for e2 in ENGS:
                if e2 != e and e2 != "sp" and lasts[e2] > 0:
                    self._need(e, ("eng", e2, lasts[e2]), waits)
            for i in range(self.nd):
                if self.dma_tot[i] > 0:
                    self._need(e, ("dma", i, self.dma_tot[i]), waits)
            if waits:
                self.ops[e].append((waits, None, None))
        self.last_w = {}
        self.readers = {}

    def finish(self):
        self.barrier()

    def emit(self, blk):
        rank = {}
        for e in ENGS:
            ms = sorted(self.milestones[e])
            rank[e] = {s: i + 1 for i, s in enumerate(ms)}

        def run(e, eng):
            for waits, fn, inc in self.ops[e]:
                for kind, key, val in waits:
                    if kind == "eng":
                        eng.wait_ge(self.sem[key], rank[key][val])
                    else:
                        eng.wait_ge(self.dsem[key], val)
                if fn is None:
                    continue
                ins = fn(eng)
                if inc[0] == "dma":
                    ins.then_inc(self.dsem[inc[1]], 16)
                elif inc[1] in rank[e]:
                    ins.then_inc(self.sem[e], 1)

        blk.sync(lambda eng: run("sp", eng))
        blk.scalar(lambda eng: run("act", eng))
        blk.vector(lambda eng: run("dve", eng))
        blk.gpsimd(lambda eng: run("pool", eng))
        blk.tensor(lambda eng: run("pe", eng))


class Prog:
    def __init__(self, debug=None, stop_after=None):
        self.debug = debug or []
        self.stop_after = stop_after
        self.nc = bass.Bass("TRN2", target_bir_lowering=False)
        self.din = {}
        self.dout = {}

    def inp(self, name, shape, dt=F32):
        self.din[name] = self.nc.dram_tensor(name, list(shape), dt, kind="ExternalInput").ap()
        return self.din[name]

    def outp(self, name, shape, dt=F32):
        self.dout[name] = self.nc.dram_tensor(name, list(shape), dt, kind="ExternalOutput").ap()
        return self.dout[name]

    def sb(self, es, name, shape, dt=F32, nsub=1):
        self.uid = getattr(self, "uid", 0) + 1
        name = "%s_u%d" % (name, self.uid)
        return Buf(name, es.enter_context(self.nc.sbuf_tensor(name, list(shape), dt)), nsub)

    def I(self, eng, meth, out, *args, **kw):
        def conv(a):
            return a.ap if isinstance(a, V) else a
        reads = []
        writes = list(out.toks)
        for a in list(args) + list(kw.values()):
            if isinstance(a, V):
                reads += a.toks
        if "accum_out" in kw:
            writes += kw["accum_out"].toks
        a2 = [conv(a) for a in args]
        k2 = {k: conv(v) for k, v in kw.items()}
        o = out.ap
        self.S.op(eng, lambda e: getattr(e, meth)(o, *a2, **k2), reads, writes)

    def dma(self, q, out, in_):
        reads = in_.toks if isinstance(in_, V) else []
        writes = out.toks if isinstance(out, V) else []
        o = out.ap if isinstance(out, V) else out
        i = in_.ap if isinstance(in_, V) else in_
        self.S.dma(q, lambda e: e.dma_start(out=o, in_=i), reads, writes)

    def mm(self, out, lhsT, rhs, start, stop):
        self.I("pe", "matmul", out, lhsT=lhsT, rhs=rhs, start=start, stop=stop)

    def build(self):
        nc = self.nc
        P = self
        inp = self.inp
        x_d = inp("x", [T, D])
        condT_d = inp("condT", [128, 8])
        w_ada_d = inp("w_ada", [L, D, 6 * D])
        b_adaT_d = inp("b_adaT", [L, 128, 48])
        w_in_d = inp("w_in", [L, D, WIN_EXT])
        w_branch_d = inp("w_branch", [L, 4, 256, D])
        w_out_d = inp("w_out", [L, D, D])
        ln_d = inp("ln", [L, 4, D])
        dft_c_d = inp("dft_c", [T, T], BF16)
        dft_s_d = inp("dft_s", [T, T], BF16)
        cdft_d = inp("cdft", [2, 128, 128], BF16)
        ident_d = inp("ident", [128, 128])
        inp("w_peer_q", [L, D, 2048])
        inp("w_uq", [L, 256, 512]); inp("w_ukv", [L, 128, 512]); inp("mla_q_norm", [L, 128, 2]); inp("mla_kv_norm", [L, 128])
        inp("gla_mats", [5, 128, 128]); inp("gla_mask", [2, 128, 128], BF16); inp("gla_keep", [128, 2, 32]); inp("gla_init", [L, 2, 128, 64])
        inp("w_gla_a", [L, 2, 16, 128]); inp("b_gla_a", [L, 2, 128]); inp("gla_norm", [L, 64])
        inp("rope_m", [2, 32, T]); inp("rope_s", [2, 64, T]); inp("bias_m", [128, 160]); inp("bias_s", [128, 4])
        inp("mask_s", [128, NB, 2, 128], BF16); inp("swa_sink", [L, 128, 4])
        inp("cache_ckv", [L, 512, 128]); inp("cache_krope", [L, 512, 32]); inp("cache_swak", [L, 2, 512, 64]); inp("cache_swav", [L, 2, 512, 64])
        self.outp("ckv_o", [L, T, 128]); self.outp("krope_o", [L, T, 32]); self.outp("swakv_o", [L, T, 256]); self.outp("gla_o", [L, 8, 2, 128, 64])
        inp("peer_keysT", [L, 8, 2, 128, 128])
        inp("peer_uT", [L, D, 16384])
        inp("peer_v", [L, 16384, D])
        y_d = self.outp("y", [T, D])
        dbg = {}
        for name, shape in self.debug:
            dbg[name] = self.outp(name, shape, F32)

        with ExitStack() as es:
            self.S = S = Sched(nc, es)
            sb = lambda *a, **k: P.sb(es, *a, **k)
            x = sb("xres", [128, NB, D], F32, nsub=NB)
            uT = sb("uT", [128, 8, T], BF16, nsub=NB)
            ident = sb("identS", [128, 128], F32)
            ones = sb("onesS", [128, 128], F32)
            mcol = sb("mcol", [128, L, 48], F32, nsub=L)
            condT = sb("condTS", [128, 8], F32)
            ps = [Buf("ps%d" % i, es.enter_context(nc.psum_tensor("ps%d" % i, [128, 512], F32))) for i in range(8)]
            self.x, self.uT, self.ps, self.ident, self.ones, self.mcol = x, uT, ps, ident, ones, mcol
            iota = sb("iotaS", [128, 128], F32)
            P.I("pool", "iota", iota.v(), pattern=[[1, 128]], base=0, channel_multiplier=0, allow_small_or_imprecise_dtypes=True)
            bm = sb("bmS", [128, 8], F32)
            P.dma("sp", bm.v(), inp("bm", [128, 8])[:, :])
            self.iota, self.bm = iota, bm

            for tb in range(NB):
                P.dma("sp", x.v(s_[:, tb, :], tb), x_d[tb * 128:(tb + 1) * 128, :])
            P.dma("sp", ident.v(), ident_d[:, :])
            P.I("pool", "memset", ones.v(), 1.0)
            P.dma("sp", condT.v(), condT_d[:, :])

            with ExitStack() as es0:
                scond = P.sb(es0, "scond", [128, 8], F32)
                wa = [P.sb(es0, "wa%d" % i, [128, 8, 768], F32) for i in range(1)]
                badaT = P.sb(es0, "badaT", [128, L, 48], F32)
                P.I("act", "activation", scond.v(), condT.v(), AF.Silu)
                P.dma("sp", badaT.v(), b_adaT_d.rearrange("l p j -> p l j"))
                n = 0
                for l in range(L):
                    for cg in range(8):
                        pt = ps[cg % 2]
                        wt = wa[0]
                        P.dma("sp", wt.v(), w_ada_d[l, :, cg * 768:(cg + 1) * 768].rearrange("(k p) c -> p k c", p=128))
                        for j in range(6):
                            for k in range(8):
                                P.mm(pt.v(s_[:, j:j + 1]), wt.v(s_[:, k, j * 128:(j + 1) * 128]), scond.v(s_[:, k:k + 1]),
                                     start=(k == 0), stop=(k == 7))
                        P.I("dve", "tensor_tensor", mcol.v(s_[:, l, cg * 6:(cg + 1) * 6], l), pt.v(s_[:, 0:6]),
                            badaT.v(s_[:, l, cg * 6:(cg + 1) * 6]), op=ALU.add)
                    for a in (8, 32):
                        P.I("dve", "tensor_scalar_add", mcol.v(s_[:, l, a:a + 8], l), mcol.v(s_[:, l, a:a + 8], l), 1.0)
                S.barrier()
            if "mcol" in dbg:
                P.dma("pool", dbg["mcol"], mcol.v())

            for l in range(L):
                self.layer(l, dbg)
                if self.stop_after == ("layer", l):
                    break

            for tb in range(NB):
                P.dma("sp", y_d[tb * 128:(tb + 1) * 128, :], x.v(s_[:, tb, :], tb))
            S.finish()
            blk = es.enter_context(nc.Block())
            S.emit(blk)
        return nc

    def ln_block(self, tmp, src, dst):
        P = self
        st, mv, rstd, nmr = tmp
        P.I("dve", "bn_stats", st.v(s_[:, 0, :]), src.m(lambda a: a[:, 0:512]))
        P.I("dve", "bn_stats", st.v(s_[:, 1, :]), src.m(lambda a: a[:, 512:1024]))
        P.I("dve", "bn_aggr", mv.v(), st.v())
        P.I("act", "activation", rstd.v(), mv.v(s_[:, 1:2]), AF.Sqrt, bias=EPS, scale=1.0)
        P.I("dve", "reciprocal", rstd.v(), rstd.v())
        P.I("dve", "scalar_tensor_tensor", nmr.v(), mv.v(s_[:, 0:1]), -1.0, rstd.v(), op0=ALU.mult, op1=ALU.mult)
        P.I("act", "activation", dst, src, AF.Identity, bias=nmr.v(), scale=rstd.v())

    def ln_tmp(self, es, tag):
        return (self.sb(es, "st" + tag, [128, 2, 6]), self.sb(es, "mv" + tag, [128, 2]),
                self.sb(es, "rstd" + tag, [128, 1]), self.sb(es, "nmr" + tag, [128, 1]))

    def mod_to_uT(self, l, which):
        P = self; S = self.S
        x, uT, ps, mcol, ident = self.x, self.uT, self.ps, self.mcol, self.ident
        sh0 = 0 if which == 0 else 24
        sc0 = 8 if which == 0 else 32
        with ExitStack() as es:
            tmps = [self.ln_tmp(es, "m%d" % i) for i in range(2)]
            xn = [P.sb(es, "xn%d" % i, [128, D]) for i in range(2)]
            for tb in range(NB):
                xnb = xn[tb % 2]
                self.ln_block(tmps[tb % 2], x.v(s_[:, tb, :], tb), xnb.v())
                for half in range(2):
                    pt = ps[(tb * 2 + half) % 4]
                    for j in range(4):
                        k = half * 4 + j
                        P.I("pe", "transpose", pt.v(s_[:, j * 128:(j + 1) * 128]), xnb.v(s_[:, k * 128:(k + 1) * 128]), ident.v())
                    for j in range(4):
                        k = half * 4 + j
                        eng = "dve" if j % 2 == 0 else "pool"
                        eng = "dve"
                        P.I(eng, "tensor_scalar", uT.v(s_[:, k, tb * 128:(tb + 1) * 128], tb), pt.v(s_[:, j * 128:(j + 1) * 128]),
                            mcol.v(s_[:, l, sc0 + k:sc0 + k + 1], l), mcol.v(s_[:, l, sh0 + k:sh0 + k + 1], l), op0=ALU.mult, op1=ALU.add)
            S.barrier()

    def load_w(self, wb, l, c0, n):
        w_in_d = self.din["w_in"]
        self.dma("pool", wb.v(s_[:, :, 0:n]), w_in_d[l, :, c0:c0 + n].rearrange("(k p) c -> p k c", p=128))

    def proj_fm(self, wb, wc0, m, dst_fn, pbase=0):
        P = self
        for tg in range(4):
            pt = self.ps[self.psi % 8]; self.psi += 1
            for k in range(8):
                P.mm(pt.v(s_[0:m, :]), wb.v(s_[:, k, wc0:wc0 + m]), self.uT.v(s_[:, k, tg * 512:(tg + 1) * 512], range(tg * 4, tg * 4 + 4)),
                     start=(k == 0), stop=(k == 7))
            dst_fn(tg, pt.v(s_[0:m, :]))

    def proj_tm(self, wb, wc0, n, dst_fn):
        P = self
        for tb in range(NB):
            pt = self.ps[self.psi % 8]; self.psi += 1
            for k in range(8):
                P.mm(pt.v(s_[:, 0:n]), self.uT.v(s_[:, k, tb * 128:(tb + 1) * 128], tb), wb.v(s_[:, k, wc0:wc0 + n]),
                     start=(k == 0), stop=(k == 7))
            dst_fn(tb, pt.v(s_[:, 0:n]))


    def rope_evac(self, dst, pa, pb, cosv, sinv, tmpa, tmpb):
        P = self
        P.I("dve", "tensor_tensor", tmpa, pa, cosv, op=ALU.mult)
        P.I("dve", "tensor_tensor", tmpb, pb, sinv, op=ALU.mult)
        P.I("pool", "tensor_tensor", dst, tmpa, tmpb, op=ALU.add)

    def fm_group(self, wb, specs, tg):
        P = self
        outs = []
        for (wc0, m) in specs:
            pt = self.ps[self.psi % 6]; self.psi += 1
            for k in range(8):
                P.mm(pt.v(s_[0:m, :]), wb.v(s_[:, k, wc0:wc0 + m]), self.uT.v(s_[:, k, tg * 512:(tg + 1) * 512], range(tg * 4, tg * 4 + 4)),
                     start=(k == 0), stop=(k == 7))
            outs.append(pt.v(s_[0:m, :]))
        return outs

    def mla(self, l, dbg):
        P = self; S = self.S
        d = self.din
        ps, ident, ones = self.ps, self.ident, self.ones
        NK = 20
        with ExitStack() as es:
            wb = P.sb(es, "wbM", [128, 8, 448], BF16)
            wuq = P.sb(es, "wuq", [128, 2, 512], BF16)
            wukv = P.sb(es, "wukv", [128, 2, 256], BF16)
            gq = P.sb(es, "gq", [128, 2], F32)
            gkv = P.sb(es, "gkv", [128, 128], F32)
            cosT = P.sb(es, "cosM", [32, 512], F32)
            sinT = P.sb(es, "sinM", [32, 512], F32)
            biasM = P.sb(es, "biasM", [128, 8 * NK], F32)
            qnT = P.sb(es, "qnT", [128, 2, T], BF16, nsub=4)
            rs = P.sb(es, "qrs", [128, 512], F32)
            qno = P.sb(es, "qno", [64, T], BF16, nsub=4)
            qro = P.sb(es, "qro", [32, T], BF16, nsub=4)
            kno = P.sb(es, "kno", [64, 2560], BF16)
            kro = P.sb(es, "kro", [32, 2560], BF16)
            ckvT = P.sb(es, "ckvT", [128, 2560], BF16, nsub=NK)
            Vt = P.sb(es, "VtM", [128, NK, 4, 65], BF16, nsub=NK)
            ta = P.sb(es, "rta", [128, 512], F32)
            tb_ = P.sb(es, "rtb", [128, 512], F32)
            kvt = P.sb(es, "kvt", [128, 160], F32)
            ckt = [P.sb(es, "ckt%d" % i, [128, 128], F32) for i in range(2)]
            ss = P.sb(es, "kss", [128, 1], F32)
            junk = P.sb(es, "kjunk", [128, 128], F32)
            PT = [P.sb(es, "PT%d" % i, [128, 256], BF16) for i in range(2)]
            oacc = P.sb(es, "oacc", [128, NB, 128], BF16, nsub=NB)
            identb = P.sb(es, "identb", [128, 128], BF16)
            P.I("dve", "tensor_copy", identb.v(), ident.v())
            rden = P.sb(es, "rden", [128, 1], F32)
            cch = P.sb(es, "cch", [128, 4, 160], F32)
            self.load_w(wb, l, 0, 416)
            P.dma("pool", wb.v(s_[:, :, 416:448]), d["w_in"][l, :, 6080:6112].rearrange("(k p) c -> p k c", p=128))
            P.dma("pool", wuq.v(), d["w_uq"][l].rearrange("(k p) c -> p k c", p=128))
            for two in range(2):
                P.dma("pool", wukv.v(s_[:, two, :]).m(lambda a: a.rearrange("p (h c) -> p h c", h=4)), d["w_ukv"][l].rearrange("p (h two c) -> p two h c", h=4, two=2)[:, two, :, :])
            P.dma("sp", gq.v(), d["mla_q_norm"][l])
            P.dma("sp", gkv.v(), d["mla_kv_norm"][l].partition_broadcast(128))
            P.dma("sp", biasM.v(), d["bias_m"][:, :])
            P.I("pool", "memset", Vt.v(), 1.0)
            for tg in range(4):
                tsl = slice(tg * 512, (tg + 1) * 512)
                P.dma("sp", cosT.v(), d["rope_m"][0, :, tsl])
                P.dma("sp", sinT.v(), d["rope_m"][1, :, tsl])
                pq = self.fm_group(wb, [(0, 128), (128, 128)], tg)
                P.I("act", "activation", ta.v(), pq[0], AF.Square)
                P.I("act", "activation", tb_.v(), pq[1], AF.Square)
                pt = ps[self.psi % 6]; self.psi += 1
                P.mm(pt.v(), ones.v(), ta.v(), start=True, stop=False)
                P.mm(pt.v(), ones.v(), tb_.v(), start=False, stop=True)
                P.I("act", "activation", rs.v(), pt.v(), AF.Sqrt, bias=EPS, scale=1.0 / 256)
                P.I("dve", "reciprocal", rs.v(), rs.v())
                for c in range(2):
                    P.I("dve", "scalar_tensor_tensor", qnT.v(s_[:, c, tsl], tg), pq[c], gq.v(s_[:, c:c + 1]), rs.v(), op0=ALU.mult, op1=ALU.mult)
                pk = self.fm_group(wb, [(384, 32), (416, 32)], tg)
                self.rope_evac(kro.v(s_[:, 512 + tg * 512:512 + (tg + 1) * 512]), pk[0], pk[1], cosT.v(), sinT.v(),
                               ta.v(s_[0:32, :]), tb_.v(s_[0:32, :]))
            for tb in range(NB):
                pt = ps[self.psi % 6]; self.psi += 1
                for k in range(8):
                    P.mm(pt.v(s_[:, 0:160]), self.uT.v(s_[:, k, tb * 128:(tb + 1) * 128], tb), wb.v(s_[:, k, 256:416]), start=(k == 0), stop=(k == 7))
                P.I("act", "activation", kvt.v(), pt.v(s_[:, 0:160]), AF.Copy)
                ck = ckt[tb % 2]
                P.I("act", "activation", junk.v(), kvt.v(s_[:, 0:128]), AF.Square, accum_out=ss.v())
                P.I("act", "activation", ss.v(), ss.v(), AF.Sqrt, bias=EPS, scale=1.0 / 128)
                P.I("dve", "reciprocal", ss.v(), ss.v())
                P.I("dve", "scalar_tensor_tensor", ck.v(), kvt.v(s_[:, 0:128]), ss.v(), gkv.v(), op0=ALU.mult, op1=ALU.mult)
                P.dma("sp", self.dout["ckv_o"][l, tb * 128:(tb + 1) * 128, :], ck.v())
                P.dma("sp", self.dout["krope_o"][l, tb * 128:(tb + 1) * 128, :], kvt.v(s_[:, 128:160]))
                p2 = ps[self.psi % 6]; self.psi += 1
                P.I("pe", "transpose", p2.v(s_[:, 0:128]), ck.v(), ident.v())
                P.I("act", "activation", ckvT.v(s_[:, 512 + tb * 128:512 + (tb + 1) * 128], 4 + tb), p2.v(s_[:, 0:128]), AF.Copy)
            P.dma("sp", cch.v(s_[:, :, 0:128]), d["cache_ckv"][l].rearrange("(j p) c -> p j c", p=128))
            P.dma("sp", cch.v(s_[:, :, 128:160]), d["cache_krope"][l].rearrange("(j p) c -> p j c", p=128))
            for j in range(4):
                p2 = ps[self.psi % 6]; self.psi += 1
                P.I("pe", "transpose", p2.v(s_[:, 0:128]), cch.v(s_[:, j, 0:128]), ident.v())
                P.I("act", "activation", ckvT.v(s_[:, j * 128:(j + 1) * 128], j), p2.v(s_[:, 0:128]), AF.Copy)
                p3 = ps[self.psi % 6]; self.psi += 1
                P.I("pe", "transpose", p3.v(s_[0:32, 0:128]), cch.v(s_[:, j, 128:160]), ident.v())
                P.I("act", "activation", kro.v(s_[:, j * 128:(j + 1) * 128]), p3.v(s_[0:32, 0:128]), AF.Copy)
            for kc in range(NK):
                pt = ps[self.psi % 6]; self.psi += 1
                P.mm(pt.v(s_[:, 0:256]), ckvT.v(s_[:, kc * 128:(kc + 1) * 128], kc), wukv.v(s_[:, 1, :]), start=True, stop=True)
                P.I("dve", "tensor_copy", Vt.v(s_[:, kc, :, 0:64], kc), pt.v(s_[:, 0:256]).m(lambda a: a.rearrange("p (h c) -> p h c", h=4)))
            n = 0
            for h in range(4):
                for tg in range(4):
                    tsl = slice(tg * 512, (tg + 1) * 512)
                    P.dma("sp", cosT.v(), d["rope_m"][0, :, tsl])
                    P.dma("sp", sinT.v(), d["rope_m"][1, :, tsl])
                    pn = ps[self.psi % 6]; pa = ps[(self.psi + 1) % 6]; pb = ps[(self.psi + 2) % 6]; self.psi += 3
                    for c in range(2):
                        P.mm(pn.v(s_[0:64, :]), wuq.v(s_[:, c, h * 96:h * 96 + 64]), qnT.v(s_[:, c, tsl], tg), start=(c == 0), stop=(c == 1))
                    for c in range(2):
                        P.mm(pa.v(s_[0:32, :]), wuq.v(s_[:, c, h * 96 + 64:h * 96 + 96]), qnT.v(s_[:, c, tsl], tg), start=(c == 0), stop=(c == 1))
                    for c in range(2):
                        P.mm(pb.v(s_[0:32, :]), wuq.v(s_[:, c, 384 + h * 32:384 + h * 32 + 32]), qnT.v(s_[:, c, tsl], tg), start=(c == 0), stop=(c == 1))
                    P.I("act", "activation", qno.v(s_[:, tsl], tg), pn.v(s_[0:64, :]), AF.Copy)
                    self.rope_evac(qro.v(s_[:, tsl], tg), pa.v(s_[0:32, :]), pb.v(s_[0:32, :]), cosT.v(), sinT.v(),
                                   ta.v(s_[0:32, :]), tb_.v(s_[0:32, :]))
                for g5 in range(5):
                    gsl = slice(g5 * 512, (g5 + 1) * 512)
                    pt = ps[self.psi % 6]; self.psi += 1
                    P.mm(pt.v(s_[0:64, :]), wukv.v(s_[:, 0, h * 64:(h + 1) * 64]), ckvT.v(s_[:, gsl], range(g5 * 4, g5 * 4 + 4)), start=True, stop=True)
                    P.I("act", "activation", kno.v(s_[:, gsl]), pt.v(s_[0:64, :]), AF.Copy)
                for qu in range(8):
                    qsl = slice(qu * 256, (qu + 1) * 256)
                    po = [ps[6], ps[7]]
                    for kc in range(NK):
                        ksl = slice(kc * 128, (kc + 1) * 128)
                        pt = ps[self.psi % 6]; self.psi += 1
                        P.mm(pt.v(s_[:, 0:256]), kno.v(s_[:, ksl]), qno.v(s_[:, qsl], qu // 2), start=True, stop=False)
                        P.mm(pt.v(s_[:, 0:256]), kro.v(s_[:, ksl]), qro.v(s_[:, qsl], qu // 2), start=False, stop=True)
                        pT = PT[n % 2]; n += 1
                        P.I("act", "activation", pT.v(), pt.v(s_[:, 0:256]), AF.Exp, bias=biasM.v(s_[:, qu * NK + kc:qu * NK + kc + 1]), scale=MLA_SCALE)
                        for qb in range(2):
                            P.mm(po[qb].v(s_[:, 0:65]), pT.v(s_[:, qb * 128:(qb + 1) * 128]), Vt.v(s_[:, kc, h, :], kc), start=(kc == 0), stop=(kc == NK - 1))
                    for qb in range(2):
                        tb = qu * 2 + qb
                        P.I("dve", "reciprocal", rden.v(), po[qb].v(s_[:, 64:65]))
                        P.I("dve", "tensor_scalar_mul", oacc.v(s_[:, tb, (h % 2) * 64:(h % 2) * 64 + 64], tb), po[qb].v(s_[:, 0:64]), rden.v())
                if h % 2 == 1:
                    for tb in range(NB):
                        p2 = ps[self.psi % 6]; self.psi += 1
                        P.mm(p2.v(s_[:, 0:128]), oacc.v(s_[:, tb, :], tb), identb.v(), start=True, stop=True)
                        P.I("act", "activation", self.brT[0].v(s_[:, h // 2, tb * 128:(tb + 1) * 128], tb), p2.v(s_[:, 0:128]), AF.Copy)
            S.barrier()

    def gla_block(self, l, tb, wb, afT, qT, kT, w2, b2, onesb, ktm, e1t, spt, eq, ek, el, mats, vtm, kl, gcol, qeT, keT, LNQ):
        P = self; ps = self.ps
        lsl = slice((tb % 4) * 128, (tb % 4) * 128 + 128)
        bsl = slice(tb * 128, (tb + 1) * 128)
        pt = ps[self.psi % 6]; self.psi += 1
        for k in range(8):
            P.mm(pt.v(s_[:, 0:384]), self.uT.v(s_[:, k, bsl], tb), wb.v(s_[:, k, 128:512]), start=(k == 0), stop=(k == 7))
        P.I("dve", "tensor_copy", ktm.v(), pt.v(s_[:, 0:128]))
        P.I("dve", "tensor_copy", vtm.v(s_[:, tb, :], tb), pt.v(s_[:, 128:384]))
        for dr in range(2):
            pz = ps[self.psi % 6]; self.psi += 1
            P.mm(pz.v(s_[:, 0:128]), afT.v(s_[:, dr, lsl]), w2.v(s_[:, dr, :]), start=True, stop=False)
            P.mm(pz.v(s_[:, 0:128]), onesb.v(), b2.v(s_[:, dr, :]), start=False, stop=True)
            P.I("act", "activation", e1t.v(), pz.v(s_[:, 0:128]), AF.Exp, scale=-1.0)
            P.I("act", "activation", spt.v(), e1t.v(), AF.Ln, bias=1.0, scale=1.0)
            mi = 0 if dr == 0 else 2
            if self.G1 <= 3:
                continue
            pl = ps[self.psi % 6]; self.psi += 1
            P.mm(pl.v(s_[:, 0:128]), mats.v(s_[:, mi + 1, :]), spt.v(), start=True, stop=True)
            P.I("act", "activation", el.v(), pl.v(s_[:, 0:128]), AF.Exp)
            P.I("dve", "tensor_tensor", kl[dr].v(s_[:, tb, :], tb), ktm.v(), el.v(), op=ALU.mult)
            for hp in range(2):
                pc = ps[self.psi % 6]; pg = ps[(self.psi + 1) % 6]; self.psi += 2
                P.mm(pc.v(s_[0:64, 0:128]), spt.v(s_[:, hp * 64:(hp + 1) * 64]), mats.v(s_[:, mi, :]), start=True, stop=True)
                P.mm(pg.v(s_[0:64, 0:2]), spt.v(s_[:, hp * 64:(hp + 1) * 64]), mats.v(s_[:, 4, 0:2]), start=True, stop=True)
                P.I("act", "activation", eq.v(s_[:, hp, :]), pc.v(s_[0:64, 0:128]), AF.Exp, bias=LNQ, scale=1.0)
                P.I("act", "activation", ek.v(s_[:, hp, :]), pc.v(s_[0:64, 0:128]), AF.Exp, scale=-1.0)
                P.I("act", "activation", gcol.v(s_[:, dr, hp, 2 * tb:2 * tb + 2], dr), pg.v(s_[0:64, 0:2]), AF.Exp)
            P.I("dve", "tensor_tensor", qeT[dr].v(s_[:, :, bsl], tb), qT.v(s_[:, :, lsl]), eq.v(), op=ALU.mult)
            P.I("dve", "tensor_tensor", keT[dr].v(s_[:, :, bsl], tb), kT.v(s_[:, :, lsl]), ek.v(), op=ALU.mult)

    def gla(self, l, dbg):
        P = self; S = self.S
        d = self.din
        ps, ident, ones = self.ps, self.ident, self.ones
        LNQ = float(np.log(32.0 ** -0.5))
        with ExitStack() as es:
            qeT = [P.sb(es, "qeT%d" % i, [64, 2, T], BF16, nsub=NB) for i in range(2)]
            keT = [P.sb(es, "keT%d" % i, [64, 2, T], BF16, nsub=NB) for i in range(2)]
            kl = [P.sb(es, "kl%d" % i, [128, NB, 128], BF16, nsub=NB) for i in range(2)]
            gcol = P.sb(es, "gcol", [64, 2, 2, 32], F32, nsub=2)
            vtm = P.sb(es, "vtm", [128, NB, 256], BF16, nsub=NB)
            mats = P.sb(es, "gmats", [128, 5, 128], F32)
            gmask = P.sb(es, "gmask", [128, 2, 128], BF16)
            keep = P.sb(es, "gkeep", [128, 2, 32], F32)
            identb = P.sb(es, "identbG", [128, 128], BF16)
            P.dma("sp", mats.v(), d["gla_mats"].rearrange("a p c -> p a c"))
            P.dma("sp", gmask.v(), d["gla_mask"].rearrange("a p c -> p a c"))
            P.dma("sp", keep.v(), d["gla_keep"][:, :, :])
            P.I("dve", "tensor_copy", identb.v(), ident.v())
            with ExitStack() as e1:
                wb = P.sb(e1, "wbG", [128, 8, 800], BF16)
                afT = P.sb(e1, "afT", [16, 2, 512], BF16)
                qT = P.sb(e1, "gqT", [64, 2, 512], BF16)
                kT = P.sb(e1, "gkT", [64, 2, 512], BF16)
                w2 = P.sb(e1, "gw2", [16, 2, 128], BF16)
                b2 = P.sb(e1, "gb2", [1, 2, 128], BF16)
                onesb = P.sb(e1, "onesb", [1, 128], BF16)
                ktm = P.sb(e1, "ktm", [128, 128], F32)
                e1t = P.sb(e1, "ge1", [128, 128], F32)
                spt = P.sb(e1, "gsp", [128, 128], F32)
                eq = P.sb(e1, "geq", [64, 2, 128], F32)
                ek = P.sb(e1, "gek", [64, 2, 128], F32)
                el = P.sb(e1, "gel", [128, 128], F32)
                self.load_w(wb, l, 672, 800)
                P.dma("pool", w2.v(), d["w_gla_a"][l].rearrange("a r c -> r a c"))
                P.dma("pool", b2.v(), d["b_gla_a"][l:l + 1, :, :])
                P.I("pool", "memset", onesb.v(), 1.0)
                import os
                G1 = int(os.environ.get("KGLA_G1", "99"))
                for tg in range(4 if G1 >= 2 else 0):
                    tsl = slice(tg * 512, (tg + 1) * 512)
                    pq = self.fm_group(wb, [(0, 64), (64, 64), (128, 64), (192, 64)], tg)
                    P.I("act", "activation", qT.v(s_[:, 0, :]), pq[0], AF.Copy)
                    P.I("dve", "tensor_copy", qT.v(s_[:, 1, :]), pq[1])
                    P.I("act", "activation", kT.v(s_[:, 0, :]), pq[2], AF.Copy)
                    P.I("dve", "tensor_copy", kT.v(s_[:, 1, :]), pq[3])
                    pq = self.fm_group(wb, [(768, 16), (784, 16)], tg)
                    P.I("act", "activation", afT.v(s_[:, 0, :]), pq[0], AF.Copy)
                    P.I("dve", "tensor_copy", afT.v(s_[:, 1, :]), pq[1])
                    for tb in range(tg * 4, tg * 4 + 4) if G1 >= 3 else []:
                        self.G1 = G1
                        self.gla_block(l, tb, wb, afT, qT, kT, w2, b2, onesb, ktm, e1t, spt, eq, ek, el, mats, vtm, kl, gcol, qeT, keT, LNQ)
                S.barrier()
            import os
            GSTOP = int(os.environ.get("KGLA_STOP", "99"))
            if GSTOP <= 1:
                return
            with ExitStack() as e2:
                of = P.sb(e2, "gof", [128, NB, 256], F32, nsub=NB)
                St = P.sb(e2, "gS", [64, 2, 64], F32)
                tmpS = P.sb(e2, "gtmp", [64, 2, 64], F32)
                Sb = [P.sb(e2, "gSb%d" % i, [64, 2, 64], BF16) for i in range(2)]
                att = [P.sb(e2, "gatt%d" % i, [128, 128], BF16) for i in range(2)]
                na = 0
                for dr in range(2):
                    P.dma("sp", St.v(), d["gla_init"][l, dr].rearrange("(a p) c -> p a c", p=64))
                    blks = range(NB) if dr == 0 else range(NB - 1, -1, -1)
                    for tb in blks:
                        bsl = slice(tb * 128, (tb + 1) * 128)
                        halves = (0, 1) if dr == 0 else (1, 0)
                        for hf in halves:
                            n = 2 * tb + hf
                            r0 = hf * 64
                            P.I("act", "activation", Sb[hf].v(), St.v(), AF.Copy)
                            pu = ps[self.psi % 6]; self.psi += 1
                            for h in range(4):
                                P.mm(pu.v(s_[(h % 2) * 32:(h % 2) * 32 + 32, (h // 2) * 64:(h // 2) * 64 + 64]), kl[dr].v(s_[r0:r0 + 64, tb, h * 32:(h + 1) * 32], tb),
                                     vtm.v(s_[r0:r0 + 64, tb, h * 64:(h + 1) * 64], tb), start=True, stop=True)
                            for hp in range(2):
                                P.I("dve", "scalar_tensor_tensor", tmpS.v(s_[:, hp, :]), St.v(s_[:, hp, :]), gcol.v(s_[:, dr, hp, n:n + 1], dr), pu.v(s_[0:64, hp * 64:(hp + 1) * 64]), op0=ALU.mult, op1=ALU.add)
                            if (dr == 0 and n % 4 == 3) or (dr == 1 and n % 4 == 0):
                                P.dma("sp", self.dout["gla_o"][l, n // 4, dr].rearrange("(a p) c -> p a c", p=64), tmpS.v())
                            nn = n + 1 if dr == 0 else n - 1
                            if 0 <= nn < 32:
                                P.I("dve", "tensor_scalar_mul", St.v(), tmpS.v(), keep.v(s_[0:64, dr, nn:nn + 1]))
                        po = ps[6 + (tb % 2)]
                        for h in range(4):
                            hs = slice((h % 2) * 32, (h % 2) * 32 + 32); hp = h // 2
                            pa = ps[self.psi % 6]; self.psi += 1
                            P.mm(pa.v(s_[:, 0:128]), keT[dr].v(s_[hs, hp, bsl], tb), qeT[dr].v(s_[hs, hp, bsl], tb), start=True, stop=True)
                            at = att[na % 2]; na += 1
                            P.I("dve", "tensor_tensor", at.v(), pa.v(s_[:, 0:128]), gmask.v(s_[:, dr, :]), op=ALU.mult)
                            oc = slice(h * 64, (h + 1) * 64)
                            P.mm(po.v(s_[:, oc]), at.v(), vtm.v(s_[:, tb, oc], tb), start=True, stop=False)
                            P.mm(po.v(s_[0:64, oc]), qeT[dr].v(s_[hs, hp, tb * 128:tb * 128 + 64], tb), Sb[0].v(s_[hs, hp, :]), start=False, stop=False)
                            P.mm(po.v(s_[64:128, oc]), qeT[dr].v(s_[hs, hp, tb * 128 + 64:tb * 128 + 128], tb), Sb[1].v(s_[hs, hp, :]), start=False, stop=True)
                        if dr == 0:
                            P.I("act", "activation", of.v(s_[:, tb, :], tb), po.v(s_[:, 0:256]), AF.Copy)
                        else:
                            P.I("dve", "tensor_tensor", of.v(s_[:, tb, :], tb), of.v(s_[:, tb, :], tb), po.v(s_[:, 0:256]), op=ALU.add)
                if GSTOP <= 2:
                    S.barrier(); return
                wg = P.sb(e2, "wgG", [128, 8, 256], BF16)
                gn = P.sb(e2, "ggn", [128, 64], F32)
                ss4 = P.sb(e2, "gss", [128, 4], F32)
                junk = P.sb(e2, "gjunk", [128, 64], F32)
                sg = P.sb(e2, "gsil", [128, 256], F32)
                ob = P.sb(e2, "gob", [128, 256], BF16)
                self.load_w(wg, l, 1184, 256)
                P.dma("sp", gn.v(), d["gla_norm"][l].partition_broadcast(128))
                for tb in range(NB):
                    bsl = slice(tb * 128, (tb + 1) * 128)
                    for h in range(4):
                        P.I("act", "activation", junk.v(), of.v(s_[:, tb, h * 64:(h + 1) * 64], tb), AF.Square, accum_out=ss4.v(s_[:, h:h + 1]))
                    P.I("act", "activation", ss4.v(), ss4.v(), AF.Sqrt, bias=EPS, scale=1.0 / 64)
                    P.I("dve", "reciprocal", ss4.v(), ss4.v())
                    ov = of.v(s_[:, tb, :], tb).m(lambda a: a.rearrange("p (h c) -> p h c", h=4))
                    P.I("dve", "tensor_tensor", ov, ov, ss4.v().m(lambda a: a.unsqueeze(2).to_broadcast([128, 4, 64])), op=ALU.mult)
                    P.I("dve", "tensor_tensor", ov, ov, gn.v().m(lambda a: a.unsqueeze(1).to_broadcast([128, 4, 64])), op=ALU.mult)
                    pg = ps[self.psi % 6]; self.psi += 1
                    for k in range(8):
                        P.mm(pg.v(s_[:, 0:256]), self.uT.v(s_[:, k, bsl], tb), wg.v(s_[:, k, :]), start=(k == 0), stop=(k == 7))
                    P.I("act", "activation", sg.v(), pg.v(s_[:, 0:256]), AF.Silu)
                    P.I("dve", "tensor_tensor", ob.v(), of.v(s_[:, tb, :], tb), sg.v(), op=ALU.mult)
                    for c in range(2):
                        p2 = ps[self.psi % 6]; self.psi += 1
                        P.mm(p2.v(s_[:, 0:128]), ob.v(s_[:, c * 128:(c + 1) * 128]), identb.v(), start=True, stop=True)
                        P.I("act", "activation", self.brT[2].v(s_[:, c, bsl], tb), p2.v(s_[:, 0:128]), AF.Copy)
                S.barrier()

    def swa_kv_only(self, l):
        P = self; S = self.S
        ps = self.ps
        with ExitStack() as es:
            wb = P.sb(es, "wbS2", [128, 8, 256], BF16)
            kvo = [P.sb(es, "kvo2_%d" % i, [128, 256], F32) for i in range(2)]
            self.load_w(wb, l, 1728, 256)
            for tb in range(NB):
                pt = ps[self.psi % 6]; self.psi += 1
                for k in range(8):
                    P.mm(pt.v(s_[:, 0:256]), self.uT.v(s_[:, k, tb * 128:(tb + 1) * 128], tb), wb.v(s_[:, k, 0:256]), start=(k == 0), stop=(k == 7))
                ko = kvo[tb % 2]
                P.I("act", "activation", ko.v(), pt.v(s_[:, 0:256]), AF.Copy)
                P.dma("sp", self.dout["swakv_o"][l, tb * 128:(tb + 1) * 128, :], ko.v())
            S.barrier()

    def swa(self, l, dbg):
        P = self; S = self.S
        d = self.din
        ps, ident, ones = self.ps, self.ident, self.ones
        NK = 20
        with ExitStack() as es:
            wb = P.sb(es, "wbS", [128, 8, 896], BF16)
            cosT = P.sb(es, "cosS", [64, 512], F32)
            sinT = P.sb(es, "sinS", [64, 512], F32)
            biasS = P.sb(es, "biasS", [128, 4], F32)
            esink = P.sb(es, "esink", [128, 4], F32)
            msk = P.sb(es, "mskS", [128, NB, 2, 128], BF16)
            qT = P.sb(es, "qTS", [64, 4, T], BF16, nsub=4)
            kT = P.sb(es, "kTS", [64, 2, 2560], BF16)
            Vt = P.sb(es, "VtS", [128, NK, 2, 65], BF16, nsub=NK)
            ta = P.sb(es, "sta", [64, 512], F32)
            tb_ = P.sb(es, "stb", [64, 512], F32)
            kvo = [P.sb(es, "kvo%d" % i, [128, 256], F32) for i in range(2)]
            cch = P.sb(es, "cchS", [128, 4, 2, 64], F32)
            PT = [P.sb(es, "PTS%d" % i, [128, 256], BF16) for i in range(2)]
            oa = P.sb(es, "oaS", [128, 256], F32)
            rden = P.sb(es, "rdenS", [128, 1], F32)
            self.load_w(wb, l, 1472, 512)
            P.dma("pool", wb.v(s_[:, :, 512:896]), d["w_in"][l, :, 6112:6496].rearrange("(k p) c -> p k c", p=128))
            P.dma("sp", biasS.v(), d["bias_s"][:, :])
            P.dma("sp", esink.v(), d["swa_sink"][l])
            P.I("act", "activation", esink.v(), esink.v(), AF.Exp)
            P.dma("sp", msk.v(), d["mask_s"][:, :, :, :])
            P.I("pool", "memset", Vt.v(), 1.0)
            import os
            STOP = int(os.environ.get("KSWA_STOP", "99"))
            if STOP <= 1:
                S.barrier(); return
            for tg in range(4):
                tsl = slice(tg * 512, (tg + 1) * 512)
                P.dma("sp", cosT.v(), d["rope_s"][0, :, tsl])
                P.dma("sp", sinT.v(), d["rope_s"][1, :, tsl])
                for h in range(4):
                    pq = self.fm_group(wb, [(h * 64, 64), (512 + h * 64, 64)], tg)
                    self.rope_evac(qT.v(s_[:, h, tsl], h), pq[0], pq[1], cosT.v(), sinT.v(), ta.v(), tb_.v())
                for j in range(2):
                    pk = self.fm_group(wb, [(256 + j * 64, 64), (768 + j * 64, 64)], tg)
                    self.rope_evac(kT.v(s_[:, j, 512 + tg * 512:512 + (tg + 1) * 512]), pk[0], pk[1], cosT.v(), sinT.v(), ta.v(), tb_.v())
            if STOP <= 2:
                S.barrier(); return
            for tb in range(NB):
                pt = ps[self.psi % 6]; self.psi += 1
                for k in range(8):
                    P.mm(pt.v(s_[:, 0:256]), self.uT.v(s_[:, k, tb * 128:(tb + 1) * 128], tb), wb.v(s_[:, k, 256:512]), start=(k == 0), stop=(k == 7))
                ko = kvo[tb % 2]
                P.I("act", "activation", ko.v(), pt.v(s_[:, 0:256]), AF.Copy)
                P.dma("sp", self.dout["swakv_o"][l, tb * 128:(tb + 1) * 128, :], ko.v())
                P.I("dve", "tensor_copy", Vt.v(s_[:, 4 + tb, :, 0:64], 4 + tb), ko.v(s_[:, 128:256]).m(lambda a: a.rearrange("p (j c) -> p j c", j=2)))
            if STOP <= 3:
                S.barrier(); return
            for j in range(2):
                P.dma("sp", cch.v(s_[:, :, j, :]), d["cache_swak"][l, j].rearrange("(c p) e -> p c e", p=128))
            for c in range(4):
                for j in range(2):
                    p2 = ps[self.psi % 6]; self.psi += 1
                    P.I("pe", "transpose", p2.v(s_[0:64, 0:128]), cch.v(s_[:, c, j, :]), ident.v())
                    P.I("act", "activation", kT.v(s_[:, j, c * 128:(c + 1) * 128]), p2.v(s_[0:64, 0:128]), AF.Copy)
            cchv = P.sb(es, "cchV", [128, 4, 2, 64], F32)
            for j in range(2):
                P.dma("sp", cchv.v(s_[:, :, j, :]), d["cache_swav"][l, j].rearrange("(c p) e -> p c e", p=128))
            for c in range(4):
                P.I("dve", "tensor_copy", Vt.v(s_[:, c, :, 0:64], c), cchv.v(s_[:, c, :, :]))
            n = 0
            import os
            for tb in range(NB if "noatt" not in os.environ.get("KSKIP", "") else 0):
                qsl = slice(tb * 128, (tb + 1) * 128)
                for j in range(2):
                    po = [ps[6], ps[7]]
                    kcs = [(4 + tb + dd, dd) for dd in (-1, 0, 1) if 0 <= tb + dd < NB] + [(c, 2) for c in range(4)]
                    for i, (kc, kind) in enumerate(kcs):
                        ksl = slice(kc * 128, (kc + 1) * 128)
                        pt = ps[self.psi % 6]; self.psi += 1
                        for g in range(2):
                            P.mm(pt.v(s_[:, g * 128:(g + 1) * 128]), kT.v(s_[:, j, ksl]), qT.v(s_[:, 2 * j + g, qsl], 2 * j + g), start=True, stop=True)
                        pT = PT[n % 2]; n += 1
                        if kind == 2:
                            P.I("act", "activation", pT.v(), pt.v(s_[:, 0:256]), AF.Exp, bias=biasS.v(s_[:, 0:1]), scale=SWA_SCALE)
                        else:
                            P.I("act", "activation", pT.v(), pt.v(s_[:, 0:256]), AF.Exp, scale=SWA_SCALE)
                            if kind != 0:
                                mi = 0 if kind == -1 else 1
                                for g in range(2):
                                    P.I("dve", "tensor_tensor", pT.v(s_[:, g * 128:(g + 1) * 128]), pT.v(s_[:, g * 128:(g + 1) * 128]), msk.v(s_[:, tb, mi, :]), op=ALU.mult)
                        for g in range(2):
                            P.mm(po[g].v(s_[:, 0:65]), pT.v(s_[:, g * 128:(g + 1) * 128]), Vt.v(s_[:, kc, j, :], kc), start=(i == 0), stop=(i == len(kcs) - 1))
                    for g in range(2):
                        hh = 2 * j + g
                        P.I("dve", "tensor_tensor", rden.v(), po[g].v(s_[:, 64:65]), esink.v(s_[:, hh:hh + 1]), op=ALU.add)
                        P.I("dve", "reciprocal", rden.v(), rden.v())
                        P.I("dve", "tensor_scalar_mul", oa.v(s_[:, hh * 64:(hh + 1) * 64]), po[g].v(s_[:, 0:64]), rden.v())
                for c in range(2):
                    p2 = ps[self.psi % 6]; self.psi += 1
                    P.I("pe", "transpose", p2.v(s_[:, 0:128]), oa.v(s_[:, c * 128:(c + 1) * 128]), ident.v())
                    P.I("act", "activation", self.brT[3].v(s_[:, c, tb * 128:(tb + 1) * 128], tb), p2.v(s_[:, 0:128]), AF.Copy)
            S.barrier()

    def fnet(self, l):
        P = self; S = self.S
        d = self.din
        with ExitStack() as es:
            wb = P.sb(es, "wbF", [128, 8, 256], BF16)
            fT = P.sb(es, "fT", [128, 2, T], BF16, nsub=4)
            cd = P.sb(es, "cdft", [128, 2, 128], BF16)
            A = P.sb(es, "fA", [128, NB, 256], BF16, nsub=NB)
            B = P.sb(es, "fB", [128, NB, 256], BF16, nsub=NB)
            tc_ = [P.sb(es, "dc%d" % i, [128, 512], BF16) for i in range(4)]
            ts_ = [P.sb(es, "ds%d" % i, [128, 512], BF16) for i in range(4)]
            self.load_w(wb, l, 416, 256)
            P.dma("sp", cd.v(), d["cdft"].rearrange("a p c -> p a c"))
            for c in range(2):
                def ev(tg, pv, c=c):
                    P.I("act", "activation", fT.v(s_[:, c, tg * 512:(tg + 1) * 512], tg), pv, AF.Copy)
                self.proj_fm(wb, c * 128, 128, ev)
            for tb in range(NB):
                pa = self.ps[self.psi % 8]; self.psi += 1
                for c in range(2):
                    P.mm(pa.v(s_[:, c * 128:(c + 1) * 128]), fT.v(s_[:, c, tb * 128:(tb + 1) * 128], tb // 4), cd.v(s_[:, 0, :]), start=True, stop=True)
                    P.mm(pa.v(s_[:, 256 + c * 128:256 + (c + 1) * 128]), fT.v(s_[:, c, tb * 128:(tb + 1) * 128], tb // 4), cd.v(s_[:, 1, :]), start=True, stop=True)
                P.I("act", "activation", A.v(s_[:, tb, :], tb), pa.v(s_[:, 0:256]), AF.Copy)
                P.I("act", "activation", B.v(s_[:, tb, :], tb), pa.v(s_[:, 256:512]), AF.Copy)
            n = 0
            for tg in range(4):
                p0 = self.ps[self.psi % 8]; p1 = self.ps[(self.psi + 1) % 8]; self.psi += 2
                for tb in range(NB):
                    ct = tc_[n % 4]; st = ts_[n % 4]; n += 1
                    P.dma("sp", ct.v(), d["dft_c"][tb * 128:(tb + 1) * 128, tg * 512:(tg + 1) * 512])
                    P.dma("act", st.v(), d["dft_s"][tb * 128:(tb + 1) * 128, tg * 512:(tg + 1) * 512])
                    for c, pp in ((0, p0), (1, p1)):
                        P.mm(pp.v(), A.v(s_[:, tb, c * 128:(c + 1) * 128], tb), ct.v(), start=(tb == 0), stop=False)
                        P.mm(pp.v(), B.v(s_[:, tb, c * 128:(c + 1) * 128], tb), st.v(), start=False, stop=(tb == NB - 1))
                for c, pp in ((0, p0), (1, p1)):
                    P.I("act" if c == 0 else "dve", "activation" if c == 0 else "tensor_copy", self.brT[1].v(s_[:, c, tg * 512:(tg + 1) * 512], range(tg * 4, tg * 4 + 4)),
                        pp.v(), *((AF.Copy,) if c == 0 else ()))
            S.barrier()

    def merge(self, l, dbg):
        P = self; S = self.S
        d = self.din
        x, uT, ps, mcol, ident, ones = self.x, self.uT, self.ps, self.mcol, self.ident, self.ones
        with ExitStack() as es:
            wbr = P.sb(es, "wbr", [128, 8, D], BF16)
            wo = P.sb(es, "wo", [128, 8, D], BF16)
            wg = [P.sb(es, "wg%d" % i, [128, 8, 512], BF16) for i in range(1)]
            G = P.sb(es, "Gacc", [128, 512], F32)
            GT = P.sb(es, "GT", [128, 8, 512], BF16, nsub=8)
            sg = [P.sb(es, "sg%d" % i, [128, 512], F32) for i in range(2)]
            gbc = P.sb(es, "g1bc", [128, D], F32)
            lng = P.sb(es, "ln1g", [128, D], F32)
            lnb = P.sb(es, "ln1b", [128, D], F32)
            dg = P.sb(es, "dgm", [128, 128], F32)
            xt = [P.sb(es, "xt%d" % i, [128, D], F32) for i in range(1)]
            tmps = [self.ln_tmp(es, "g%d" % i) for i in range(2)]
            P.dma("pool", wbr.v(), d["w_branch"][l].rearrange("b (k p) d -> p (b k) d", p=128))
            P.dma("pool", wo.v(), d["w_out"][l].rearrange("(k p) d -> p k d", p=128))
            P.dma("sp", lng.v(), d["ln"][l, 0, :].partition_broadcast(128))
            P.dma("sp", lnb.v(), d["ln"][l, 1, :].partition_broadcast(128))
            for k in range(8):
                P.I("dve", "tensor_scalar_mul", dg.v(), ident.v(), mcol.v(s_[:, l, 16 + k:17 + k], l))
                pt = ps[self.psi % 8]; self.psi += 1
                P.mm(pt.v(s_[:, 0:128]), ones.v(), dg.v(), start=True, stop=True)
                P.I("act", "activation", gbc.v(s_[:, k * 128:(k + 1) * 128]), pt.v(s_[:, 0:128]), AF.Copy)
            n = 0
            for tg in range(4):
                tsub = range(tg * 4, tg * 4 + 4)
                tsl = slice(tg * 512, (tg + 1) * 512)
                for dc in range(8):
                    w = wg[0]; n += 1
                    for b in range(4):
                        c0 = 1984 + b * D + dc * 128
                        P.dma("pool", w.v(s_[:, :, b * 128:(b + 1) * 128]),
                              d["w_in"][l, :, c0:c0 + 128].rearrange("(k p) c -> p k c", p=128))
                    for b in range(4):
                        pg = ps[self.psi % 8]; pp = ps[(self.psi + 1) % 8]; self.psi += 2
                        for k in range(8):
                            P.mm(pg.v(), w.v(s_[:, k, b * 128:(b + 1) * 128]), uT.v(s_[:, k, tsl], tsub), start=(k == 0), stop=(k == 7))
                        for kc in range(2):
                            P.mm(pp.v(), wbr.v(s_[:, b * 2 + kc, dc * 128:(dc + 1) * 128]), self.brT[b].v(s_[:, kc, tsl], tsub),
                                 start=(kc == 0), stop=(kc == 1))
                        sgt = sg[b % 2]
                        P.I("act", "activation", sgt.v(), pg.v(), AF.Sigmoid)
                        if b == 0:
                            P.I("dve", "tensor_tensor", G.v(), sgt.v(), pp.v(), op=ALU.mult)
                        else:
                            P.I("dve", "tensor_tensor", sgt.v(), sgt.v(), pp.v(), op=ALU.mult)
                            if b < 3:
                                P.I("pool", "tensor_tensor", G.v(), G.v(), sgt.v(), op=ALU.add)
                            else:
                                P.I("pool", "tensor_tensor", GT.v(s_[:, dc, :], dc), G.v(), sgt.v(), op=ALU.add)
                for j in range(4):
                    tb = tg * 4 + j
                    xtb = xt[0]
                    for hf in range(2):
                        pm = ps[self.psi % 8]; self.psi += 1
                        for k in range(8):
                            P.mm(pm.v(), GT.v(s_[:, k, j * 128:(j + 1) * 128], k), wo.v(s_[:, k, hf * 512:(hf + 1) * 512]), start=(k == 0), stop=(k == 7))
                        hs = slice(hf * 512, (hf + 1) * 512)
                        P.I("dve", "tensor_tensor", xtb.v(s_[:, hs]), pm.v(), gbc.v(s_[:, hs]), op=ALU.mult)
                        P.I("dve", "scalar_tensor_tensor", xtb.v(s_[:, hs]), x.v(s_[:, tb, hs], tb), ALPHA, xtb.v(s_[:, hs]), op0=ALU.mult, op1=ALU.add)
                    self.ln_block(tmps[tb % 2], xtb.v(), x.v(s_[:, tb, :], tb))
                    P.I("pool", "tensor_tensor", x.v(s_[:, tb, :], tb), x.v(s_[:, tb, :], tb), lng.v(), op=ALU.mult)
                    P.I("pool", "tensor_tensor", x.v(s_[:, tb, :], tb), x.v(s_[:, tb, :], tb), lnb.v(), op=ALU.add)
            S.barrier()

    def post_ffn(self, l, yacc_is_x=True):
        P = self; S = self.S
        d = self.din
        x = self.x
        with ExitStack() as es:
            lng = P.sb(es, "ln2g", [128, D], F32)
            lnb = P.sb(es, "ln2b", [128, D], F32)
            xt = [P.sb(es, "xq%d" % i, [128, D], F32) for i in range(2)]
            tmps = [self.ln_tmp(es, "q%d" % i) for i in range(2)]
            P.dma("sp", lng.v(), d["ln"][l, 2, :].partition_broadcast(128))
            P.dma("sp", lnb.v(), d["ln"][l, 3, :].partition_broadcast(128))
            for tb in range(NB):
                xtb = xt[tb % 2]
                P.I("act", "activation", xtb.v(), x.v(s_[:, tb, :], tb), AF.Copy)
                self.ln_block(tmps[tb % 2], xtb.v(), x.v(s_[:, tb, :], tb))
                P.I("pool", "tensor_tensor", x.v(s_[:, tb, :], tb), x.v(s_[:, tb, :], tb), lng.v(), op=ALU.mult)
                P.I("pool", "tensor_tensor", x.v(s_[:, tb, :], tb), x.v(s_[:, tb, :], tb), lnb.v(), op=ALU.add)
            S.barrier()

    def layer(self, l, dbg):
        P = self; S = self.S
        self.psi = 0
        self.mod_to_uT(l, 0)
        if "uT" in dbg and l == 0:
            P.dma("pool", dbg["uT"], self.uT.v())
        with ExitStack() as esl:
            self.brT = [P.sb(esl, "brT%d" % b, [128, 2, T], BF16, nsub=NB) for b in range(4)]
            import os
            skip = os.environ.get("KSKIP", "")
            for b, nm in ((0, "mla"), (3, "swa"), (1, "fnet"), (2, "gla")):
                if nm in skip:
                    for tb4 in range(4):
                        P.I("pool", "memset", self.brT[b].v(s_[:, :, tb4 * 512:(tb4 + 1) * 512], range(tb4 * 4, tb4 * 4 + 4)), 0.0)
            if "mla" not in skip:
                self.mla(l, dbg)
            if "swa" not in skip:
                self.swa(l, dbg)
            else:
                self.swa_kv_only(l)
            if "fnet" not in skip:
                self.fnet(l)
            if "gla" not in skip:
                self.gla(l, dbg)
            if "brT1" in dbg and l == 0:
                P.dma("pool", dbg["brT1"], self.brT[1].v())
            self.merge(l, dbg)
            S.barrier()
        if "x1" in dbg and l == 0:
            P.dma("pool", dbg["x1"], self.x.v())
        import os
        if "peer" in os.environ.get("KSKIP", ""):
            for tb in range(NB):
                P.I("act", "activation", self.x.v(s_[:, tb, :], tb), self.x.v(s_[:, tb, :], tb), AF.Copy, scale=ALPHA)
        else:
            self.mod_to_uT(l, 1)
            self.peer(l, dbg)
        self.post_ffn(l)

    def peer(self, l, dbg):
        P = self; S = self.S
        d = self.din
        x, uT, ps, mcol, ident, ones, iota, bm = self.x, self.uT, self.ps, self.mcol, self.ident, self.ones, self.iota, self.bm
        NCH = 128
        with ExitStack() as es:
            wq = P.sb(es, "wq", [128, 8, 512], BF16)
            kT = P.sb(es, "keysT", [128, 16, 128], BF16)
            gbc = P.sb(es, "g2bc", [128, D], F32)
            dg = P.sb(es, "dg2", [128, 128], F32)
            qpT = P.sb(es, "qpT", [128, 16, 128], BF16, nsub=16)
            sc = P.sb(es, "psc", [128, 16, 128], F32, nsub=16)
            wk = P.sb(es, "pwk", [128, 256], F32)
            vtop = P.sb(es, "vtop", [128, 16, 16], F32, nsub=16)
            itop = P.sb(es, "itop", [128, 16, 16], U32, nsub=16)
            idx1f = P.sb(es, "idx1f", [128, 128], F32)
            idx2f = P.sb(es, "idx2f", [128, 128], F32)
            idxT = P.sb(es, "idxT", [128, 2, 128], F32)
            cand = P.sb(es, "cand", [128, 8, 256], F32, nsub=8)
            t8a = P.sb(es, "t8a", [128, 8, 8], F32, nsub=8)
            t8b = P.sb(es, "t8b", [128, 8, 8], F32, nsub=8)
            nmx = P.sb(es, "nmx", [128, 8], F32)
            zz = P.sb(es, "pz", [128, 8], F32)
            wCT = P.sb(es, "wCT", [128, 128, 16], BF16)
            O1 = P.sb(es, "O1", [128, 16, 128], BF16)
            O2 = P.sb(es, "O2", [128, 16, 128], BF16)
            Cbd = P.sb(es, "Cbd", [128, 16, 128], BF16)
            tmpS = [P.sb(es, "ptmp%d" % i, [128, 4, 128], BF16) for i in range(2)]
            WtT = P.sb(es, "WtT", [128, 128, 128], BF16, nsub=32)
            Ut = [P.sb(es, "Ut%d" % i, [128, 8, 256], BF16) for i in range(2)]
            Vt = [P.sb(es, "Vt%d" % i, [128, 2, D], BF16) for i in range(2)]
            actS = [P.sb(es, "pact%d" % i, [128, 128], BF16) for i in range(2)]
            GS = [P.sb(es, "pG%d" % i, [128, 128], BF16) for i in range(2)]
            P.dma("pool", kT.v(), d["peer_keysT"][l].rearrange("h q c k -> c (h q) k"))
            for k in range(8):
                P.I("dve", "tensor_scalar_mul", dg.v(), ident.v(), mcol.v(s_[:, l, 40 + k:41 + k], l))
                pt = ps[self.psi % 4]; self.psi += 1
                P.mm(pt.v(s_[:, 0:128]), ones.v(), dg.v(), start=True, stop=True)
                P.I("act", "activation", gbc.v(s_[:, k * 128:(k + 1) * 128]), pt.v(s_[:, 0:128]), AF.Copy)
            py = [ps[6], ps[7]]
            nld = 0
            for tb in range(NB):
                tsl = slice(tb * 128, (tb + 1) * 128)
                for c4 in range(4):
                    pt = ps[self.psi % 4]; self.psi += 1
                    P.dma("pool", wq.v(), d["w_peer_q"][l, :, c4 * 512:(c4 + 1) * 512].rearrange("(k p) c -> p k c", p=128))
                    for j in range(4):
                        c = c4 * 4 + j
                        for k in range(8):
                            P.mm(pt.v(s_[:, j * 128:(j + 1) * 128]), wq.v(s_[:, k, j * 128:(j + 1) * 128]), uT.v(s_[:, k, tsl], tb), start=(k == 0), stop=(k == 7))
                    P.I("act", "activation", qpT.v(s_[:, c4 * 4:(c4 + 1) * 4, :], range(c4 * 4, c4 * 4 + 4)), pt.v().m(lambda a: a.rearrange("p (j t) -> p j t", j=4)), AF.Copy)
                for c4 in range(4):
                    pt = ps[self.psi % 4]; self.psi += 1
                    for j in range(4):
                        c = c4 * 4 + j
                        P.mm(pt.v(s_[:, j * 128:(j + 1) * 128]), qpT.v(s_[:, c, :], c), kT.v(s_[:, c, :]), start=True, stop=True)
                    P.I("act", "activation", sc.v(s_[:, c4 * 4:(c4 + 1) * 4, :], range(c4 * 4, c4 * 4 + 4)), pt.v().m(lambda a: a.rearrange("p (j t) -> p j t", j=4)), AF.Copy)
                for c in range(16):
                    P.I("dve", "max", vtop.v(s_[:, c, 0:8], c), sc.v(s_[:, c, :], c))
                    P.I("dve", "max_index", itop.v(s_[:, c, 0:8], c), vtop.v(s_[:, c, 0:8], c), sc.v(s_[:, c, :], c))
                    P.I("dve", "match_replace", wk.v(s_[:, 0:128]), vtop.v(s_[:, c, 0:8], c), sc.v(s_[:, c, :], c), -1e30)
                    P.I("dve", "max", vtop.v(s_[:, c, 8:16], c), wk.v(s_[:, 0:128]))
                    P.I("dve", "max_index", itop.v(s_[:, c, 8:16], c), vtop.v(s_[:, c, 8:16], c), wk.v(s_[:, 0:128]))
                v4 = lambda a: a.rearrange("p (h q) r -> p h q r", q=2)
                P.I("dve", "tensor_copy", idx1f.v().m(lambda a: a.rearrange("p (h r) -> p h r", h=8)), itop.v().m(lambda a: v4(a)[:, :, 0, :]))
                P.I("dve", "tensor_copy", idx2f.v().m(lambda a: a.rearrange("p (h r) -> p h r", h=8)), itop.v().m(lambda a: v4(a)[:, :, 1, :]))
                P.I("dve", "tensor_tensor", cand.v().m(lambda a: a.rearrange("p h (a b) -> p h a b", a=16)),
                    vtop.v().m(lambda a: v4(a)[:, :, 0, :].unsqueeze(3).to_broadcast([128, 8, 16, 16])),
                    vtop.v().m(lambda a: v4(a)[:, :, 1, :].unsqueeze(2).to_broadcast([128, 8, 16, 16])), op=ALU.add)
                for h in range(8):
                    P.I("dve", "max", t8a.v(s_[:, h, :], h), cand.v(s_[:, h, :], h))
                    P.I("dve", "match_replace", wk.v(), t8a.v(s_[:, h, :], h), cand.v(s_[:, h, :], h), -1e30)
                    P.I("dve", "max", t8b.v(s_[:, h, :], h), wk.v())
                P.I("dve", "tensor_scalar_mul", nmx.v(), t8a.v(s_[:, :, 0]), -1.0)
                for h in range(8):
                    P.I("act", "activation", sc.v(s_[:, 2 * h:2 * h + 2, :], [2 * h, 2 * h + 1]).m(lambda a: a.rearrange("p a b -> p (a b)")), cand.v(s_[:, h, :], h), AF.Exp, bias=nmx.v(s_[:, h:h + 1]), scale=1.0)
                    P.I("dve", "scalar_tensor_tensor", sc.v(s_[:, 2 * h:2 * h + 2, :], [2 * h, 2 * h + 1]).m(lambda a: a.rearrange("p a b -> p (a b)")), cand.v(s_[:, h, :], h), t8b.v(s_[:, h, 7:8], h), sc.v(s_[:, 2 * h:2 * h + 2, :], [2 * h, 2 * h + 1]).m(lambda a: a.rearrange("p a b -> p (a b)")), op0=ALU.is_ge, op1=ALU.mult)
                P.I("dve", "tensor_reduce", zz.v(), sc.v().m(lambda a: a.rearrange("p (h a) b -> p h (a b)", a=2)), axis=AX.X, op=ALU.add)
                P.I("dve", "reciprocal", zz.v(), zz.v())
                P.I("dve", "tensor_tensor", sc.v().m(lambda a: a.rearrange("p (h a) b -> p h (a b)", a=2)), sc.v().m(lambda a: a.rearrange("p (h a) b -> p h (a b)", a=2)), zz.v().m(lambda a: a.unsqueeze(2).to_broadcast([128, 8, 256])), op=ALU.mult)
                pt = ps[self.psi % 4]; self.psi += 1
                P.I("pe", "transpose", pt.v(s_[:, 0:128]), idx1f.v(), ident.v())
                P.I("pe", "transpose", pt.v(s_[:, 128:256]), idx2f.v(), ident.v())
                P.I("act", "activation", idxT.v(), pt.v(s_[:, 0:256]).m(lambda a: a.rearrange("p (a t) -> p a t", a=2)), AF.Copy)
                for r4 in range(4):
                    pt = ps[self.psi % 4]; self.psi += 1
                    for j in range(4):
                        r2 = r4 * 4 + j
                        P.I("pe", "transpose", pt.v(s_[:, j * 128:(j + 1) * 128]),
                            sc.v().m(lambda a, r2=r2: a.rearrange("p c (a b) -> p (c a) b", b=16)[:, :, r2]), ident.v())
                    P.I("act", "activation", wCT.v(s_[:, :, r4 * 4:(r4 + 1) * 4]).m(lambda a: a.rearrange("p t j -> p j t")),
                        pt.v().m(lambda a: a.rearrange("p (j t) -> p j t", j=4)), AF.Copy)
                for sbk in range(8):
                    t0 = sbk * 16
                    P.I("dve", "tensor_tensor", O1.v(), iota.v().m(lambda a: a.unsqueeze(1).to_broadcast([128, 16, 128])),
                        idxT.v(s_[:, 0, t0:t0 + 16]).m(lambda a: a.unsqueeze(2).to_broadcast([128, 16, 128])), op=ALU.is_equal)
                    P.I("dve", "tensor_tensor", O2.v(), iota.v().m(lambda a: a.unsqueeze(1).to_broadcast([128, 16, 128])),
                        idxT.v(s_[:, 1, t0:t0 + 16]).m(lambda a: a.unsqueeze(2).to_broadcast([128, 16, 128])), op=ALU.is_equal)
                    P.I("pool", "tensor_tensor", Cbd.v().m(lambda a: a.rearrange("p t (h r) -> p t h r", h=8)),
                        wCT.v(s_[:, t0:t0 + 16, :]).m(lambda a: a.unsqueeze(2).to_broadcast([128, 16, 8, 16])),
                        bm.v().m(lambda a: a.unsqueeze(1).unsqueeze(3).to_broadcast([128, 16, 8, 16])), op=ALU.mult)
                    for g4 in range(4):
                        pt = ps[self.psi % 4]; self.psi += 1
                        tS = tmpS[g4 % 2]
                        for j in range(4):
                            tt = g4 * 4 + j
                            P.mm(pt.v(s_[:, j * 128:(j + 1) * 128]), Cbd.v(s_[:, tt, :]), O1.v(s_[:, tt, :]), start=True, stop=True)
                        P.I("act", "activation", tS.v(), pt.v().m(lambda a: a.rearrange("p (j i) -> p j i", j=4)), AF.Copy)
                        pt2 = ps[self.psi % 4]; self.psi += 1
                        for j in range(4):
                            tt = g4 * 4 + j
                            P.mm(pt2.v(s_[:, j * 128:(j + 1) * 128]), O2.v(s_[:, tt, :]), tS.v(s_[:, j, :]), start=True, stop=True)
                        ta = t0 + g4 * 4
                        P.I("dve" if g4 % 2 == 0 else "act", "tensor_copy" if g4 % 2 == 0 else "activation", WtT.v(s_[:, ta:ta + 4, :], ta // 4),
                            pt2.v().m(lambda a: a.rearrange("p (j i) -> p j i", j=4)), *(() if g4 % 2 == 0 else (AF.Copy,)))
                for c2 in range(NCH // 2):
                    ut = Ut[nld % 2]; vt = Vt[nld % 2]; nld += 1
                    P.dma("pool", ut.v(), d["peer_uT"][l, :, c2 * 256:(c2 + 1) * 256].rearrange("(k p) e -> p k e", p=128))
                    P.dma("pool", vt.v(), d["peer_v"][l, c2 * 256:(c2 + 1) * 256, :].rearrange("(j p) d -> p j d", p=128))
                    for j in range(2):
                        c = c2 * 2 + j
                        pa = ps[4 + (c % 2)]
                        for k in range(8):
                            P.mm(pa.v(s_[:, 0:128]), ut.v(s_[:, k, j * 128:(j + 1) * 128]), uT.v(s_[:, k, tsl], tb), start=(k == 0), stop=(k == 7))
                        aS = actS[c % 2]; gS = GS[c % 2]
                        P.I("act", "activation", aS.v(), pa.v(s_[:, 0:128]), AF.Gelu)
                        P.I("dve", "tensor_tensor", gS.v(), aS.v(), WtT.v(s_[:, :, c]), op=ALU.mult)
                        for hf in range(2):
                            P.mm(py[hf].v(), gS.v(), vt.v(s_[:, j, hf * 512:(hf + 1) * 512]), start=(c == 0), stop=(c == NCH - 1))
                for hf in range(2):
                    hs = slice(hf * 512, (hf + 1) * 512)
                    yv = cand.v(s_[:, 0:2, :], [0, 1]).m(lambda a: a.rearrange("p a b -> p (a b)"))
                    P.I("dve", "tensor_tensor", yv, py[hf].v(), gbc.v(s_[:, hs]), op=ALU.mult)
                    P.I("dve", "scalar_tensor_tensor", x.v(s_[:, tb, hs], tb), x.v(s_[:, tb, hs], tb), ALPHA, yv, op0=ALU.mult, op1=ALU.add)
            S.barrier()

def _bf(a):
    return np.ascontiguousarray(a).astype(ml_dtypes.bfloat16)


def host_consts(kind):
    c = {}
    c["ident"] = np.eye(128, dtype=np.float32)
    c["bm"] = np.ascontiguousarray((np.arange(128)[:, None] // 16 == np.arange(8)[None, :]).astype(np.float32))
    seqlen = T if kind == "sample" else 256
    n = np.arange(seqlen)
    ang = 2.0 * np.pi * np.outer(n, n) / seqlen
    sc = 1.0 / np.sqrt(seqlen * 64.0)
    cb = np.cos(ang) * sc
    sbm = -np.sin(ang) * sc
    Cf = np.zeros((T, T), np.float64)
    Sf = np.zeros((T, T), np.float64)
    for i in range(T // seqlen):
        sl = slice(i * seqlen, (i + 1) * seqlen)
        Cf[sl, sl] = cb
        Sf[sl, sl] = sbm
    c["dft_c"] = _bf(Cf.astype(np.float32))
    c["dft_s"] = _bf(Sf.astype(np.float32))
    m = np.arange(64)
    a2 = 2.0 * np.pi * np.outer(m, m) / 64.0
    cc = np.zeros((2, 128, 128), np.float64)
    for g in range(2):
        cc[0, g * 64:(g + 1) * 64, g * 64:(g + 1) * 64] = np.cos(a2)
        cc[1, g * 64:(g + 1) * 64, g * 64:(g + 1) * 64] = np.sin(a2)
    c["cdft"] = _bf(cc.astype(np.float32))
    t = np.arange(T)
    rows = (t // 64).astype(np.float64); cols = (t % 64).astype(np.float64)
    for nm, R in (("rope_m", 32), ("rope_s", 64)):
        half = R // 2; q = R // 4
        tab = np.zeros((2, R, T), np.float64)
        for dd in range(R):
            pos = rows if dd < half else cols
            fi = dd % q
            freq = 10000.0 ** (-(2.0 * fi) / half)
            ang = pos * freq
            if kind == "sample":
                tab[0, dd] = np.cos(ang)
                tab[1, dd] = np.sin(ang) * (-1.0 if (dd // q) % 2 == 0 else 1.0)
            else:
                tab[0, dd] = 1.0
        c[nm] = np.ascontiguousarray(tab.astype(np.float32))
    bm_ = np.zeros((160,), np.float32)
    if kind == "prompt":
        for qu in range(8):
            for kc in range(20):
                ok = kc >= 4 and (kc - 4) // 2 == qu
                bm_[qu * 20 + kc] = 0.0 if ok else NEG
    c["bias_m"] = np.ascontiguousarray(np.broadcast_to(bm_[None, :], (128, 160)))
    bs_ = np.zeros((128, 4), np.float32)
    if kind == "prompt":
        bs_[:, 0] = NEG
    c["bias_s"] = bs_
    mk = np.zeros((128, NB, 2, 128), np.float32)
    kk = np.arange(128)[:, None]; qq = np.arange(128)[None, :]
    for tb in range(NB):
        if kind == "sample":
            mk[:, tb, 0, :] = (kk >= qq)
            mk[:, tb, 1, :] = (kk <= qq)
        else:
            mk[:, tb, 0, :] = 1.0 if tb % 2 == 1 else 0.0
            mk[:, tb, 1, :] = 1.0 if tb % 2 == 0 else 0.0
    c["mask_s"] = _bf(mk)
    tt = np.arange(128)[:, None]; tp = np.arange(128)[None, :]
    same = (tt // 64) == (tp // 64)
    cc_ = -1.0 / 16.0
    gm = np.zeros((5, 128, 128), np.float32)
    gm[0] = cc_ * (same & (tt <= tp))
    gm[1] = cc_ * (same & (tt > tp))
    gm[2] = cc_ * (same & (tt >= tp))
    gm[3] = cc_ * (same & (tt < tp))
    gm[4, :, 0] = cc_ * (np.arange(128) < 64)
    gm[4, :, 1] = cc_ * (np.arange(128) >= 64)
    c["gla_mats"] = gm
    c["gla_mask"] = _bf(np.stack([(same & (tt <= tp)), (same & (tt >= tp))]).astype(np.float32))
    kp = np.ones((128, 2, 32), np.float32)
    if kind == "prompt":
        for n_ in range(32):
            if n_ % 4 == 0:
                kp[:, 0, n_] = 0.0
            if n_ % 4 == 3:
                kp[:, 1, n_] = 0.0
    c["gla_keep"] = kp
    return c


def perm_swap(R):
    q = R // 4
    return np.array([d + q if (d // q) % 2 == 0 else d - q for d in range(R)])


def host_weights(inp):
    w = {}
    w["w_ada"] = np.ascontiguousarray(inp["w_ada"], dtype=np.float32)
    w["b_adaT"] = np.ascontiguousarray(inp["b_ada"].reshape(L, 48, 128).transpose(0, 2, 1), dtype=np.float32)
    w_in = np.asarray(inp["w_in"], dtype=np.float32)
    p32 = perm_swap(32); p64 = perm_swap(64)
    kr = w_in[:, :, 384:416][:, :, p32]
    sq = w_in[:, :, 1472:1728].reshape(L, D, 4, 64)[:, :, :, p64].reshape(L, D, 256)
    sk = w_in[:, :, 1728:1856].reshape(L, D, 2, 64)[:, :, :, p64].reshape(L, D, 128)
    w["w_in"] = np.ascontiguousarray(np.concatenate([w_in, kr, sq, sk], axis=2))
    w["w_branch"] = np.ascontiguousarray(inp["w_branch"], dtype=np.float32)
    w["w_out"] = np.ascontiguousarray(inp["w_out"], dtype=np.float32)
    w_uq = np.asarray(inp["w_uq"], dtype=np.float32)
    uq_sw = w_uq.reshape(L, 256, 4, 96)[:, :, :, 64:96][:, :, :, p32].reshape(L, 256, 128)
    w["w_uq"] = np.ascontiguousarray(np.concatenate([w_uq, uq_sw], axis=2))
    w["w_ukv"] = np.ascontiguousarray(inp["w_ukv"], dtype=np.float32)
    w["mla_q_norm"] = np.ascontiguousarray(np.asarray(inp["mla_q_norm"], dtype=np.float32).reshape(L, 2, 128).transpose(0, 2, 1))
    w["mla_kv_norm"] = np.ascontiguousarray(inp["mla_kv_norm"], dtype=np.float32)
    w["w_gla_a"] = np.ascontiguousarray(np.stack([inp["w_gla_a_fwd"], inp["w_gla_a_bwd"]], axis=1), dtype=np.float32)
    w["b_gla_a"] = np.ascontiguousarray(np.stack([inp["b_gla_a_fwd"], inp["b_gla_a_bwd"]], axis=1), dtype=np.float32)
    w["gla_norm"] = np.ascontiguousarray(inp["gla_norm"], dtype=np.float32)
    w["swa_sink"] = np.ascontiguousarray(np.broadcast_to(np.asarray(inp["swa_sink"], dtype=np.float32)[:, None, :], (L, 128, 4)))
    w["w_peer_q"] = np.ascontiguousarray(inp["w_peer_q"], dtype=np.float32)
    w["peer_keysT"] = np.ascontiguousarray(np.asarray(inp["peer_keys"], dtype=np.float32).transpose(0, 1, 2, 4, 3))
    w["peer_uT"] = np.ascontiguousarray(np.asarray(inp["peer_u"], dtype=np.float32).transpose(0, 2, 1))
    w["peer_v"] = np.ascontiguousarray(inp["peer_v"], dtype=np.float32)
    w["ln"] = np.ascontiguousarray(np.stack([inp["ln1_g"], inp["ln1_b"], inp["ln2_g"], inp["ln2_b"]], axis=1), dtype=np.float32)
    return w


def core_inputs(inp, core, W, CS, CP):
    m = dict(W)
    if core < 2:
        m.update(CS)
        m["x"] = np.ascontiguousarray(inp["x_sample"][core], dtype=np.float32)
        cond = np.asarray(inp["c"][core], dtype=np.float32)
        m["cache_ckv"] = np.ascontiguousarray(inp["cache_mla_ckv"][core], dtype=np.float32)
        m["cache_krope"] = np.ascontiguousarray(inp["cache_mla_krope"][core], dtype=np.float32)
        m["cache_swak"] = np.ascontiguousarray(inp["cache_swa_k"][core], dtype=np.float32)
        m["cache_swav"] = np.ascontiguousarray(inp["cache_swa_v"][core], dtype=np.float32)
        m["gla_init"] = np.ascontiguousarray(np.asarray(inp["state_gla"][core], dtype=np.float32).reshape(L, 2, 128, 64))
    else:
        m.update(CP)
        j = core - 2 if core < 6 else 0
        m["x"] = np.ascontiguousarray(np.asarray(inp["x_prompt"][8 * j:8 * j + 8], dtype=np.float32).reshape(T, D))
        cond = np.asarray(inp["c_ctx"], dtype=np.float32)
        m["cache_ckv"] = np.zeros((L, 512, 128), np.float32)
        m["cache_krope"] = np.zeros((L, 512, 32), np.float32)
        m["cache_swak"] = np.zeros((L, 2, 512, 64), np.float32)
        m["cache_swav"] = np.zeros((L, 2, 512, 64), np.float32)
        m["gla_init"] = np.zeros((L, 2, 128, 64), np.float32)
    m["condT"] = np.ascontiguousarray(cond.reshape(8, 128).T)
    return m


_CACHE = {}


def kernel(**inputs):
    cores = inputs.pop("_cores", list(range(8)))
    debug = inputs.pop("_debug", None)
    stop_after = inputs.pop("_stop_after", None)
    prog = Prog(debug=debug, stop_after=stop_after)
    nc = prog.build()
    W = host_weights(inputs)
    CS = host_consts("sample")
    CP = host_consts("prompt")
    in_maps = [core_inputs(inputs, c, W, CS, CP) for c in cores]
    res = run_bass_kernel_spmd(nc, in_maps, core_ids=list(range(len(cores))))
    R = res.results
    if debug is not None:
        return R
    y_sample = np.stack([R[0]["y"], R[1]["y"]], axis=0)
    y_prompt = np.concatenate([R[2 + j]["y"].reshape(8, 256, D) for j in range(4)], axis=0)
    ckv = np.concatenate([R[2 + j]["ckv_o"].reshape(L, 8, 256, 128).transpose(1, 0, 2, 3) for j in range(4)], axis=0)
    kr = np.concatenate([R[2 + j]["krope_o"].reshape(L, 8, 256, 32).transpose(1, 0, 2, 3) for j in range(4)], axis=0)
    kvs = [R[2 + j]["swakv_o"].reshape(L, 8, 256, 2, 2, 64) for j in range(4)]
    sk = np.concatenate([a[:, :, :, 0].transpose(1, 0, 3, 2, 4) for a in kvs], axis=0)
    sv = np.concatenate([a[:, :, :, 1].transpose(1, 0, 3, 2, 4) for a in kvs], axis=0)
    gl = np.concatenate([R[2 + j]["gla_o"].reshape(L, 8, 2, 4, 32, 64).transpose(1, 0, 2, 3, 4, 5) for j in range(4)], axis=0)
    f = lambda a: np.ascontiguousarray(a, dtype=np.float32)
    return (f(y_prompt), f(y_sample), f(ckv), f(kr), f(sk), f(sv), f(gl))
```

```python
import numpy as np
import ml_dtypes
from contextlib import ExitStack
import concourse.bass as bass
import concourse.mybir as mybir
from concourse.bass_utils import run_bass_kernel_spmd

F32 = mybir.dt.float32
BF16 = mybir.dt.bfloat16
U32 = mybir.dt.uint32
AF = mybir.ActivationFunctionType
ALU = mybir.AluOpType
AX = mybir.AxisListType
s_ = np.s_

ENGS = ("pe", "act", "dve", "pool", "sp")
SAME_ENGINE_SYNC = True
import os as _os0
SES_ALL = not bool(_os0.environ.get("KNOSES"))

T = 2048
NB = 16
D = 1024
L = 2
ALPHA = (2.0 * L) ** 0.25
EPS = 1e-6
NEG = -30000.0
MLA_SCALE = 96.0 ** -0.5
SWA_SCALE = 64.0 ** -0.5
WIN_EXT = 6496


class V:
    def __init__(self, ap, toks):
        self.ap = ap
        self.toks = toks

    def m(self, fn):
        return V(fn(self.ap), self.toks)


class Buf:
    def __init__(self, name, t, nsub=1):
        self.name = name
        self.t = t
        self.nsub = nsub

    def tok(self, subs=None):
        if subs is None:
            return [(self.name, s) for s in range(self.nsub)]
        if isinstance(subs, int):
            subs = [subs]
        return [(self.name, s) for s in subs]

    def v(self, key=None, subs=None):
        ap = self.t[:] if key is None else self.t[key]
        return V(ap, self.tok(subs))


class Sched:
    def __init__(self, nc, es, nd=24):
        self.nc = nc
        self.ops = {e: [] for e in ENGS}
        self.cnt = {e: 0 for e in ENGS}
        self.known = {e: {} for e in ENGS}
        self.nd = nd
        self.dma_tot = [0] * nd
        self.dma_rr = 0
        self.last_w = {}
        self.readers = {}
        self.sem = {e: es.enter_context(nc.semaphore("sem_" + e)) for e in ENGS if e != "sp"}
        self.dsem = [es.enter_context(nc.semaphore("dsem%d" % i)) for i in range(nd)]
        self.milestones = {e: set() for e in ENGS}

    def _need(self, eng, dep, waits):
        kind, key, val = dep
        if kind == "eng" and key == eng:
            if eng in ("pe", "sp") or (eng in ("act", "dve") and not SES_ALL) or not SAME_ENGINE_SYNC:
                return
        k = (kind, key)
        if self.known[eng].get(k, 0) >= val:
            return
        self.known[eng][k] = val
        waits.append((kind, key, val))
        if kind == "eng":
            self.milestones[key].add(val)

    def _deps(self, eng, reads, writes):
        waits = []
        for t in reads:
            lw = self.last_w.get(t)
            if lw is not None:
                self._need(eng, lw, waits)
        for t in writes:
            lw = self.last_w.get(t)
            if lw is not None:
                self._need(eng, lw, waits)
            for r in self.readers.get(t, ()):
                self._need(eng, r, waits)
        return waits

    def _commit(self, me, reads, writes):
        for t in reads:
            self.readers.setdefault(t, []).append(me)
        for t in writes:
            self.last_w[t] = me
            self.readers[t] = []

    def op(self, eng, fn, reads=(), writes=()):
        reads = list(reads); writes = list(writes)
        waits = self._deps(eng, reads, writes)
        self.cnt[eng] += 1
        me = ("eng", eng, self.cnt[eng])
        self.ops[eng].append((waits, fn, ("eng", self.cnt[eng])))
        self._commit(me, reads, writes)

    def dma(self, eng, fn, reads=(), writes=()):
        reads = list(reads); writes = list(writes)
        i = self.dma_rr
        self.dma_rr = (i + 1) % self.nd
        waits = []
        if self.dma_tot[i] > 0:
            self._need(eng, ("dma", i, self.dma_tot[i]), waits)
        waits += self._deps(eng, reads, writes)
        self.dma_tot[i] += 16
        me = ("dma", i, self.dma_tot[i])
        self.cnt[eng] += 1
        self.ops[eng].append((waits, fn, ("dma", i)))
        self._commit(me, reads, writes)

    def _last_seq(self, e):
        for w, fn, inc in reversed(self.ops[e]):
            if inc is not None and inc[0] == "eng":
                return inc[1]
        return 0

    def barrier(self):
        lasts = {e: self._last_seq(e) for e in ENGS}
        for e in ENGS:
            waits = []
            for e2 in ENGS:
                if e2 != e and e2 != "sp" and lasts[e2] > 0:
                    self._need(e, ("eng", e2, lasts[e2]), waits)
            for i in range(self.nd):
                if self.dma_tot[i] > 0:
                    self._need(e, ("dma", i, self.dma_tot[i]), waits)
            if waits:
                self.ops[e].append((waits, None, None))
        self.last_w = {}
        self.readers = {}

    def finish(self):
        self.barrier()

    def emit(self, blk):
        rank = {}
        for e in ENGS:
            ms = sorted(self.milestones[e])
            rank[e] = {s: i + 1 for i, s in enumerate(ms)}

        def run(e, eng):
            for waits, fn, inc in self.ops[e]:
                for kind, key, val in waits:
                    if kind == "eng":
                        eng.wait_ge(self.sem[key], rank[key][val])
                    else:
                        eng.wait_ge(self.dsem[key], val)
                if fn is None:
                    continue
                ins = fn(eng)
                if inc[0] == "dma":
                    ins.then_inc(self.dsem[inc[1]], 16)
                elif inc[1] in rank[e]:
                    ins.then_inc(self.sem[e], 1)

        blk.sync(lambda eng: run("sp", eng))
        blk.scalar(lambda eng: run("act", eng))
        blk.vector(lambda eng: run("dve", eng))
        blk.gpsimd(lambda eng: run("pool", eng))
        blk.tensor(lambda eng: run("pe", eng))


class Prog:
    def __init__(self, debug=None, stop_after=None):
        self.debug = debug or []
        self.stop_after = stop_after
        self.nc = bass.Bass("TRN2", target_bir_lowering=False)
        self.din = {}
        self.dout = {}

    def inp(self, name, shape, dt=F32):
        self.din[name] = self.nc.dram_tensor(name, list(shape), dt, kind="ExternalInput").ap()
        return self.din[name]

    def outp(self, name, shape, dt=F32):
        self.dout[name] = self.nc.dram_tensor(name, list(shape), dt, kind="ExternalOutput").ap()
        return self.dout[name]

    def sb(self, es, name, shape, dt=F32, nsub=1):
        self.uid = getattr(self, "uid", 0) + 1
        name = "%s_u%d" % (name, self.uid)
        return Buf(name, es.enter_context(self.nc.sbuf_tensor(name, list(shape), dt)), nsub)

    def I(self, eng, meth, out, *args, **kw):
        def conv(a):
            return a.ap if isinstance(a, V) else a
        reads = []
        writes = list(out.toks)
        for a in list(args) + list(kw.values()):
            if isinstance(a, V):
                reads += a.toks
        if "accum_out" in kw:
            writes += kw["accum_out"].toks
        a2 = [conv(a) for a in args]
        k2 = {k: conv(v) for k, v in kw.items()}
        o = out.ap
        self.S.op(eng, lambda e: getattr(e, meth)(o, *a2, **k2), reads, writes)

    def dma(self, q, out, in_):
        reads = in_.toks if isinstance(in_, V) else []
        writes = out.toks if isinstance(out, V) else []
        o = out.ap if isinstance(out, V) else out
        i = in_.ap if isinstance(in_, V) else in_
        self.S.dma(q, lambda e: e.dma_start(out=o, in_=i), reads, writes)

    def mm(self, out, lhsT, rhs, start, stop):
        self.I("pe", "matmul", out, lhsT=lhsT, rhs=rhs, start=start, stop=stop)

    def build(self):
        nc = self.nc
        P = self
        inp = self.inp
        x_d = inp("x", [T, D])
        condT_d = inp("condT", [128, 8])
        w_ada_d = inp("w_ada", [L, D, 6 * D])
        b_adaT_d = inp("b_adaT", [L, 128, 48])
        w_in_d = inp("w_in", [L, D, WIN_EXT])
        w_branch_d = inp("w_branch", [L, 4, 256, D])
        w_out_d = inp("w_out", [L, D, D])
        ln_d = inp("ln", [L, 4, D])
        dft_c_d = inp("dft_c", [T, T], BF16)
        dft_s_d = inp("dft_s", [T, T], BF16)
        cdft_d = inp("cdft", [2, 128, 128], BF16)
        ident_d = inp("ident", [128, 128])
        inp("w_peer_q", [L, 4, 128, 8, 512])
        inp("w_uq", [L, 256, 512]); inp("w_ukv", [L, 128, 512]); inp("mla_q_norm", [L, 128, 2]); inp("mla_kv_norm", [L, 128])
        inp("gla_mats", [5, 128, 128]); inp("gla_mask", [2, 128, 128], BF16); inp("gla_keep", [128, 2, 32]); inp("gla_init", [L, 2, 128, 64])
        inp("w_gla_a", [L, 2, 16, 128]); inp("b_gla_a", [L, 2, 128]); inp("gla_norm", [L, 64])
        inp("rope_m", [2, 32, T]); inp("rope_s", [2, 64, T]); inp("bias_m", [128, 160]); inp("bias_s", [128, 4])
        inp("mask_s", [128, NB, 2, 128], BF16); inp("swa_sink", [L, 128, 4])
        inp("cache_ckv", [L, 512, 128]); inp("cache_krope", [L, 512, 32]); inp("cache_swak", [L, 2, 512, 64]); inp("cache_swav", [L, 2, 512, 64])
        self.outp("ckv_o", [L, T, 128]); self.outp("krope_o", [L, T, 32]); self.outp("swakv_o", [L, T, 256]); self.outp("gla_o", [L, 8, 2, 128, 64])
        inp("peer_keysT", [L, 8, 2, 128, 128])
        inp("peer_uT", [L, 64, 128, 8, 256])
        inp("peer_v", [L, 64, 128, 2, D])
        y_d = self.outp("y", [T, D])
        self.ubf = Buf("ubf", nc.dram_tensor("peer_u_bf", [L, 64, 128, 8 * 256], BF16, kind="Internal").ap(), nsub=L * 16)
        self.vbf = Buf("vbf", nc.dram_tensor("peer_v_bf", [L, 64, 128, 2 * D], BF16, kind="Internal").ap(), nsub=L * 16)
        dbg = {}
        for name, shape in self.debug:
            dbg[name] = self.outp(name, shape, F32)

        with ExitStack() as es:
            self.S = S = Sched(nc, es)
            sb = lambda *a, **k: P.sb(es, *a, **k)
            x = sb("xres", [128, NB, D], F32, nsub=NB)
            uT = sb("uT", [128, 8, T], BF16, nsub=NB)
            ident = sb("identS", [128, 128], F32)
            ones = sb("onesS", [128, 128], F32)
            mcol = sb("mcol", [128, L, 48], F32, nsub=L)
            condT = sb("condTS", [128, 8], F32)
            ps = [Buf("ps%d" % i, es.enter_context(nc.psum_tensor("ps%d" % i, [128, 512], F32))) for i in range(8)]
            self.x, self.uT, self.ps, self.ident, self.ones, self.mcol = x, uT, ps, ident, ones, mcol
            iota = sb("iotaS", [128, 128], F32)
            P.I("pool", "iota", iota.v(), pattern=[[1, 128]], base=0, channel_multiplier=0, allow_small_or_imprecise_dtypes=True)
            bm = sb("bmS", [128, 8], F32)
            P.dma("sp", bm.v(), inp("bm", [128, 8])[:, :])
            self.iota, self.bm = iota, bm

            for tb in range(NB):
                P.dma("sp", x.v(s_[:, tb, :], tb), x_d[tb * 128:(tb + 1) * 128, :])
            P.dma("sp", ident.v(), ident_d[:, :])
            P.I("pool", "memset", ones.v(), 1.0)
            P.dma("sp", condT.v(), condT_d[:, :])

            with ExitStack() as es0:
                scond = P.sb(es0, "scond", [128, 8], F32)
                wa = [P.sb(es0, "wa%d" % i, [128, 8, 768], F32) for i in range(1)]
                badaT = P.sb(es0, "badaT", [128, L, 48], F32)
                P.I("act", "activation", scond.v(), condT.v(), AF.Silu)
                P.dma("sp", badaT.v(), b_adaT_d.rearrange("l p j -> p l j"))
                n = 0
                for l in range(L):
                    for cg in range(8):
                        pt = ps[cg % 2]
                        wt = wa[0]
                        P.dma("sp", wt.v(), w_ada_d[l, :, cg * 768:(cg + 1) * 768].rearrange("(k p) c -> p k c", p=128))
                        for j in range(6):
                            for k in range(8):
                                P.mm(pt.v(s_[:, j:j + 1]), wt.v(s_[:, k, j * 128:(j + 1) * 128]), scond.v(s_[:, k:k + 1]),
                                     start=(k == 0), stop=(k == 7))
                        P.I("dve", "tensor_tensor", mcol.v(s_[:, l, cg * 6:(cg + 1) * 6], l), pt.v(s_[:, 0:6]),
                            badaT.v(s_[:, l, cg * 6:(cg + 1) * 6]), op=ALU.add)
                    for a in (8, 32):
                        P.I("dve", "tensor_scalar_add", mcol.v(s_[:, l, a:a + 8], l), mcol.v(s_[:, l, a:a + 8], l), 1.0)
                S.barrier()
            if "mcol" in dbg:
                P.dma("pool", dbg["mcol"], mcol.v())

            for l in range(L):
                self.layer(l, dbg)
                if self.stop_after == ("layer", l):
                    break

            for tb in range(NB):
                P.dma("sp", y_d[tb * 128:(tb + 1) * 128, :], x.v(s_[:, tb, :], tb))
            S.finish()
            blk = es.enter_context(nc.Block())
            S.emit(blk)
        return nc

    def ln_block(self, tmp, src, dst):
        P = self
        st, mv, rstd, nmr = tmp
        P.I("dve", "bn_stats", st.v(s_[:, 0, :]), src.m(lambda a: a[:, 0:512]))
        P.I("dve", "bn_stats", st.v(s_[:, 1, :]), src.m(lambda a: a[:, 512:1024]))
        P.I("dve", "bn_aggr", mv.v(), st.v())
        P.I("act", "activation", rstd.v(), mv.v(s_[:, 1:2]), AF.Sqrt, bias=EPS, scale=1.0)
        P.I("dve", "reciprocal", rstd.v(), rstd.v())
        P.I("dve", "scalar_tensor_tensor", nmr.v(), mv.v(s_[:, 0:1]), -1.0, rstd.v(), op0=ALU.mult, op1=ALU.mult)
        P.I("act", "activation", dst, src, AF.Identity, bias=nmr.v(), scale=rstd.v())

    def ln_tmp(self, es, tag):
        return (self.sb(es, "st" + tag, [128, 2, 6]), self.sb(es, "mv" + tag, [128, 2]),
                self.sb(es, "rstd" + tag, [128, 1]), self.sb(es, "nmr" + tag, [128, 1]))

    def mod_to_uT(self, l, which):
        P = self; S = self.S
        x, uT, ps, mcol, ident = self.x, self.uT, self.ps, self.mcol, self.ident
        sh0 = 0 if which == 0 else 24
        sc0 = 8 if which == 0 else 32
        with ExitStack() as es:
            tmps = [self.ln_tmp(es, "m%d" % i) for i in range(2)]
            xn = [P.sb(es, "xn%d" % i, [128, D]) for i in range(2)]
            for tb in range(NB):
                xnb = xn[tb % 2]
                self.ln_block(tmps[tb % 2], x.v(s_[:, tb, :], tb), xnb.v())
                for half in range(2):
                    pt = ps[(tb * 2 + half) % 4]
                    for j in range(4):
                        k = half * 4 + j
                        P.I("pe", "transpose", pt.v(s_[:, j * 128:(j + 1) * 128]), xnb.v(s_[:, k * 128:(k + 1) * 128]), ident.v())
                    for j in range(4):
                        k = half * 4 + j
                        eng = "dve" if j % 2 == 0 else "pool"
                        eng = "dve"
                        P.I(eng, "tensor_scalar", uT.v(s_[:, k, tb * 128:(tb + 1) * 128], tb), pt.v(s_[:, j * 128:(j + 1) * 128]),
                            mcol.v(s_[:, l, sc0 + k:sc0 + k + 1], l), mcol.v(s_[:, l, sh0 + k:sh0 + k + 1], l), op0=ALU.mult, op1=ALU.add)
            S.barrier()

    def load_w(self, wb, l, c0, n):
        w_in_d = self.din["w_in"]
        self.dma("pool", wb.v(s_[:, :, 0:n]), w_in_d[l, :, c0:c0 + n].rearrange("(k p) c -> p k c", p=128))

    def proj_fm(self, wb, wc0, m, dst_fn, pbase=0):
        P = self
        for tg in range(4):
            pt = self.ps[self.psi % 8]; self.psi += 1
            for k in range(8):
                P.mm(pt.v(s_[0:m, :]), wb.v(s_[:, k, wc0:wc0 + m]), self.uT.v(s_[:, k, tg * 512:(tg + 1) * 512], range(tg * 4, tg * 4 + 4)),
                     start=(k == 0), stop=(k == 7))
            dst_fn(tg, pt.v(s_[0:m, :]))

    def proj_tm(self, wb, wc0, n, dst_fn):
        P = self
        for tb in range(NB):
            pt = self.ps[self.psi % 8]; self.psi += 1
            for k in range(8):
                P.mm(pt.v(s_[:, 0:n]), self.uT.v(s_[:, k, tb * 128:(tb + 1) * 128], tb), wb.v(s_[:, k, wc0:wc0 + n]),
                     start=(k == 0), stop=(k == 7))
            dst_fn(tb, pt.v(s_[:, 0:n]))


    def rope_evac(self, dst, pa, pb, cosv, sinv, tmpa, tmpb):
        P = self
        P.I("dve", "tensor_tensor", tmpa, pa, cosv, op=ALU.mult)
        P.I("dve", "tensor_tensor", tmpb, pb, sinv, op=ALU.mult)
        P.I("pool", "tensor_tensor", dst, tmpa, tmpb, op=ALU.add)

    def fm_group(self, wb, specs, tg):
        P = self
        outs = []
        for (wc0, m) in specs:
            pt = self.ps[self.psi % 6]; self.psi += 1
            for k in range(8):
                P.mm(pt.v(s_[0:m, :]), wb.v(s_[:, k, wc0:wc0 + m]), self.uT.v(s_[:, k, tg * 512:(tg + 1) * 512], range(tg * 4, tg * 4 + 4)),
                     start=(k == 0), stop=(k == 7))
            outs.append(pt.v(s_[0:m, :]))
        return outs

    def mla(self, l, dbg):
        P = self; S = self.S
        d = self.din
        ps, ident, ones = self.ps, self.ident, self.ones
        NK = 20
        with ExitStack() as es:
            wb = P.sb(es, "wbM", [128, 8, 448], BF16)
            wuq = P.sb(es, "wuq", [128, 2, 512], BF16)
            wukv = P.sb(es, "wukv", [128, 2, 256], BF16)
            gq = P.sb(es, "gq", [128, 2], F32)
            gkv = P.sb(es, "gkv", [128, 128], F32)
            cosT = P.sb(es, "cosM", [32, 512], F32)
            sinT = P.sb(es, "sinM", [32, 512], F32)
            biasM = P.sb(es, "biasM", [128, 8 * NK], F32)
            qnT = P.sb(es, "qnT", [128, 2, T], BF16, nsub=4)
            rs = P.sb(es, "qrs", [128, 512], F32)
            qno = P.sb(es, "qno", [64, T], BF16, nsub=4)
            qro = P.sb(es, "qro", [32, T], BF16, nsub=4)
            kno = P.sb(es, "kno", [64, 2560], BF16)
            kro = P.sb(es, "kro", [32, 2560], BF16)
            ckvT = P.sb(es, "ckvT", [128, 2560], BF16, nsub=NK)
            Vt = P.sb(es, "VtM", [128, NK, 4, 65], BF16, nsub=NK)
            ta = P.sb(es, "rta", [128, 512], F32)
            tb_ = P.sb(es, "rtb", [128, 512], F32)
            kvt = P.sb(es, "kvt", [128, 160], F32)
            ckt = [P.sb(es, "ckt%d" % i, [128, 128], F32) for i in range(2)]
            ss = P.sb(es, "kss", [128, 1], F32)
            junk = P.sb(es, "kjunk", [128, 128], F32)
            PT = [P.sb(es, "PT%d" % i, [128, 256], BF16) for i in range(2)]
            oacc = P.sb(es, "oacc", [128, NB, 128], BF16, nsub=NB)
            identb = P.sb(es, "identb", [128, 128], BF16)
            P.I("dve", "tensor_copy", identb.v(), ident.v())
            rden = P.sb(es, "rden", [128, 1], F32)
            cch = P.sb(es, "cch", [128, 4, 160], F32)
            self.load_w(wb, l, 0, 416)
            P.dma("pool", wb.v(s_[:, :, 416:448]), d["w_in"][l, :, 6080:6112].rearrange("(k p) c -> p k c", p=128))
            P.dma("pool", wuq.v(), d["w_uq"][l].rearrange("(k p) c -> p k c", p=128))
            for two in range(2):
                P.dma("pool", wukv.v(s_[:, two, :]).m(lambda a: a.rearrange("p (h c) -> p h c", h=4)), d["w_ukv"][l].rearrange("p (h two c) -> p two h c", h=4, two=2)[:, two, :, :])
            P.dma("sp", gq.v(), d["mla_q_norm"][l])
            P.dma("sp", gkv.v(), d["mla_kv_norm"][l].partition_broadcast(128))
            P.dma("sp", biasM.v(), d["bias_m"][:, :])
            P.I("pool", "memset", Vt.v(), 1.0)
            for tg in range(4):
                tsl = slice(tg * 512, (tg + 1) * 512)
                P.dma("sp", cosT.v(), d["rope_m"][0, :, tsl])
                P.dma("sp", sinT.v(), d["rope_m"][1, :, tsl])
                pq = self.fm_group(wb, [(0, 128), (128, 128)], tg)
                P.I("act", "activation", ta.v(), pq[0], AF.Square)
                P.I("act", "activation", tb_.v(), pq[1], AF.Square)
                pt = ps[self.psi % 6]; self.psi += 1
                P.mm(pt.v(), ones.v(), ta.v(), start=True, stop=False)
                P.mm(pt.v(), ones.v(), tb_.v(), start=False, stop=True)
                P.I("act", "activation", rs.v(), pt.v(), AF.Sqrt, bias=EPS, scale=1.0 / 256)
                P.I("dve", "reciprocal", rs.v(), rs.v())
                for c in range(2):
                    P.I("dve", "scalar_tensor_tensor", qnT.v(s_[:, c, tsl], tg), pq[c], gq.v(s_[:, c:c + 1]), rs.v(), op0=ALU.mult, op1=ALU.mult)
                pk = self.fm_group(wb, [(384, 32), (416, 32)], tg)
                self.rope_evac(kro.v(s_[:, 512 + tg * 512:512 + (tg + 1) * 512]), pk[0], pk[1], cosT.v(), sinT.v(),
                               ta.v(s_[0:32, :]), tb_.v(s_[0:32, :]))
            for tb in range(NB):
                pt = ps[self.psi % 6]; self.psi += 1
                for k in range(8):
                    P.mm(pt.v(s_[:, 0:160]), self.uT.v(s_[:, k, tb * 128:(tb + 1) * 128], tb), wb.v(s_[:, k, 256:416]), start=(k == 0), stop=(k == 7))
                P.I("act", "activation", kvt.v(), pt.v(s_[:, 0:160]), AF.Copy)
                ck = ckt[tb % 2]
                P.I("act", "activation", junk.v(), kvt.v(s_[:, 0:128]), AF.Square, accum_out=ss.v())
                P.I("act", "activation", ss.v(), ss.v(), AF.Sqrt, bias=EPS, scale=1.0 / 128)
                P.I("dve", "reciprocal", ss.v(), ss.v())
                P.I("dve", "scalar_tensor_tensor", ck.v(), kvt.v(s_[:, 0:128]), ss.v(), gkv.v(), op0=ALU.mult, op1=ALU.mult)
                P.dma("sp", self.dout["ckv_o"][l, tb * 128:(tb + 1) * 128, :], ck.v())
                P.dma("sp", self.dout["krope_o"][l, tb * 128:(tb + 1) * 128, :], kvt.v(s_[:, 128:160]))
                p2 = ps[self.psi % 6]; self.psi += 1
                P.I("pe", "transpose", p2.v(s_[:, 0:128]), ck.v(), ident.v())
                P.I("act", "activation", ckvT.v(s_[:, 512 + tb * 128:512 + (tb + 1) * 128], 4 + tb), p2.v(s_[:, 0:128]), AF.Copy)
            P.dma("sp", cch.v(s_[:, :, 0:128]), d["cache_ckv"][l].rearrange("(j p) c -> p j c", p=128))
            P.dma("sp", cch.v(s_[:, :, 128:160]), d["cache_krope"][l].rearrange("(j p) c -> p j c", p=128))
            for j in range(4):
                p2 = ps[self.psi % 6]; self.psi += 1
                P.I("pe", "transpose", p2.v(s_[:, 0:128]), cch.v(s_[:, j, 0:128]), ident.v())
                P.I("act", "activation", ckvT.v(s_[:, j * 128:(j + 1) * 128], j), p2.v(s_[:, 0:128]), AF.Copy)
                p3 = ps[self.psi % 6]; self.psi += 1
                P.I("pe", "transpose", p3.v(s_[0:32, 0:128]), cch.v(s_[:, j, 128:160]), ident.v())
                P.I("act", "activation", kro.v(s_[:, j * 128:(j + 1) * 128]), p3.v(s_[0:32, 0:128]), AF.Copy)
            for kc in range(NK):
                pt = ps[self.psi % 6]; self.psi += 1
                P.mm(pt.v(s_[:, 0:256]), ckvT.v(s_[:, kc * 128:(kc + 1) * 128], kc), wukv.v(s_[:, 1, :]), start=True, stop=True)
                P.I("dve", "tensor_copy", Vt.v(s_[:, kc, :, 0:64], kc), pt.v(s_[:, 0:256]).m(lambda a: a.rearrange("p (h c) -> p h c", h=4)))
            n = 0
            for h in range(4):
                for tg in range(4):
                    tsl = slice(tg * 512, (tg + 1) * 512)
                    P.dma("sp", cosT.v(), d["rope_m"][0, :, tsl])
                    P.dma("sp", sinT.v(), d["rope_m"][1, :, tsl])
                    pn = ps[self.psi % 6]; pa = ps[(self.psi + 1) % 6]; pb = ps[(self.psi + 2) % 6]; self.psi += 3
                    for c in range(2):
                        P.mm(pn.v(s_[0:64, :]), wuq.v(s_[:, c, h * 96:h * 96 + 64]), qnT.v(s_[:, c, tsl], tg), start=(c == 0), stop=(c == 1))
                    for c in range(2):
                        P.mm(pa.v(s_[0:32, :]), wuq.v(s_[:, c, h * 96 + 64:h * 96 + 96]), qnT.v(s_[:, c, tsl], tg), start=(c == 0), stop=(c == 1))
                    for c in range(2):
                        P.mm(pb.v(s_[0:32, :]), wuq.v(s_[:, c, 384 + h * 32:384 + h * 32 + 32]), qnT.v(s_[:, c, tsl], tg), start=(c == 0), stop=(c == 1))
                    P.I("act", "activation", qno.v(s_[:, tsl], tg), pn.v(s_[0:64, :]), AF.Copy)
                    self.rope_evac(qro.v(s_[:, tsl], tg), pa.v(s_[0:32, :]), pb.v(s_[0:32, :]), cosT.v(), sinT.v(),
                                   ta.v(s_[0:32, :]), tb_.v(s_[0:32, :]))
                for g5 in range(5):
                    gsl = slice(g5 * 512, (g5 + 1) * 512)
                    pt = ps[self.psi % 6]; self.psi += 1
                    P.mm(pt.v(s_[0:64, :]), wukv.v(s_[:, 0, h * 64:(h + 1) * 64]), ckvT.v(s_[:, gsl], range(g5 * 4, g5 * 4 + 4)), start=True, stop=True)
                    P.I("act", "activation", kno.v(s_[:, gsl]), pt.v(s_[0:64, :]), AF.Copy)
                for qu in range(8):
                    qsl = slice(qu * 256, (qu + 1) * 256)
                    po = [ps[6], ps[7]]
                    for kc in range(NK):
                        ksl = slice(kc * 128, (kc + 1) * 128)
                        pt = ps[self.psi % 6]; self.psi += 1
                        P.mm(pt.v(s_[:, 0:256]), kno.v(s_[:, ksl]), qno.v(s_[:, qsl], qu // 2), start=True, stop=False)
                        P.mm(pt.v(s_[:, 0:256]), kro.v(s_[:, ksl]), qro.v(s_[:, qsl], qu // 2), start=False, stop=True)
                        pT = PT[n % 2]; n += 1
                        P.I("act", "activation", pT.v(), pt.v(s_[:, 0:256]), AF.Exp, bias=biasM.v(s_[:, qu * NK + kc:qu * NK + kc + 1]), scale=MLA_SCALE)
                        for qb in range(2):
                            P.mm(po[qb].v(s_[:, 0:65]), pT.v(s_[:, qb * 128:(qb + 1) * 128]), Vt.v(s_[:, kc, h, :], kc), start=(kc == 0), stop=(kc == NK - 1))
                    for qb in range(2):
                        tb = qu * 2 + qb
                        P.I("dve", "reciprocal", rden.v(), po[qb].v(s_[:, 64:65]))
                        P.I("dve", "tensor_scalar_mul", oacc.v(s_[:, tb, (h % 2) * 64:(h % 2) * 64 + 64], tb), po[qb].v(s_[:, 0:64]), rden.v())
                if h % 2 == 1:
                    for tb in range(NB):
                        p2 = ps[self.psi % 6]; self.psi += 1
                        P.mm(p2.v(s_[:, 0:128]), oacc.v(s_[:, tb, :], tb), identb.v(), start=True, stop=True)
                        P.I("act", "activation", self.brT[0].v(s_[:, h // 2, tb * 128:(tb + 1) * 128], tb), p2.v(s_[:, 0:128]), AF.Copy)
            S.barrier()

    def gla_block(self, l, tb, wb, afT, qT, kT, w2, b2, onesb, ktm, e1t, spt, eq, ek, el, mats, vtm, kl, gcol, qeT, keT, LNQ):
        P = self; ps = self.ps
        lsl = slice((tb % 4) * 128, (tb % 4) * 128 + 128)
        bsl = slice(tb * 128, (tb + 1) * 128)
        pt = ps[self.psi % 6]; self.psi += 1
        for k in range(8):
            P.mm(pt.v(s_[:, 0:384]), self.uT.v(s_[:, k, bsl], tb), wb.v(s_[:, k, 128:512]), start=(k == 0), stop=(k == 7))
        P.I("dve", "tensor_copy", ktm.v(), pt.v(s_[:, 0:128]))
        P.I("dve", "tensor_copy", vtm.v(s_[:, tb, :], tb), pt.v(s_[:, 128:384]))
        for dr in range(2):
            pz = ps[self.psi % 6]; self.psi += 1
            P.mm(pz.v(s_[:, 0:128]), afT.v(s_[:, dr, lsl]), w2.v(s_[:, dr, :]), start=True, stop=False)
            P.mm(pz.v(s_[:, 0:128]), onesb.v(), b2.v(s_[:, dr, :]), start=False, stop=True)
            P.I("act", "activation", e1t.v(), pz.v(s_[:, 0:128]), AF.Exp, scale=-1.0)
            P.I("act", "activation", spt.v(), e1t.v(), AF.Ln, bias=1.0, scale=1.0)
            mi = 0 if dr == 0 else 2
            if self.G1 <= 3:
                continue
            pl = ps[self.psi % 6]; self.psi += 1
            P.mm(pl.v(s_[:, 0:128]), mats.v(s_[:, mi + 1, :]), spt.v(), start=True, stop=True)
            P.I("act", "activation", el.v(), pl.v(s_[:, 0:128]), AF.Exp)
            P.I("dve", "tensor_tensor", kl[dr].v(s_[:, tb, :], tb), ktm.v(), el.v(), op=ALU.mult)
            for hp in range(2):
                pc = ps[self.psi % 6]; pg = ps[(self.psi + 1) % 6]; self.psi += 2
                P.mm(pc.v(s_[0:64, 0:128]), spt.v(s_[:, hp * 64:(hp + 1) * 64]), mats.v(s_[:, mi, :]), start=True, stop=True)
                P.mm(pg.v(s_[0:64, 0:2]), spt.v(s_[:, hp * 64:(hp + 1) * 64]), mats.v(s_[:, 4, 0:2]), start=True, stop=True)
                P.I("act", "activation", eq.v(s_[:, hp, :]), pc.v(s_[0:64, 0:128]), AF.Exp, bias=LNQ, scale=1.0)
                P.I("act", "activation", ek.v(s_[:, hp, :]), pc.v(s_[0:64, 0:128]), AF.Exp, scale=-1.0)
                P.I("act", "activation", gcol.v(s_[:, dr, hp, 2 * tb:2 * tb + 2], dr), pg.v(s_[0:64, 0:2]), AF.Exp)
            P.I("dve", "tensor_tensor", qeT[dr].v(s_[:, :, bsl], tb), qT.v(s_[:, :, lsl]), eq.v(), op=ALU.mult)
            P.I("dve", "tensor_tensor", keT[dr].v(s_[:, :, bsl], tb), kT.v(s_[:, :, lsl]), ek.v(), op=ALU.mult)

    def gla(self, l, dbg):
        P = self; S = self.S
        d = self.din
        ps, ident, ones = self.ps, self.ident, self.ones
        LNQ = float(np.log(32.0 ** -0.5))
        with ExitStack() as es:
            qeT = [P.sb(es, "qeT%d" % i, [64, 2, T], BF16, nsub=NB) for i in range(2)]
            keT = [P.sb(es, "keT%d" % i, [64, 2, T], BF16, nsub=NB) for i in range(2)]
            kl = [P.sb(es, "kl%d" % i, [128, NB, 128], BF16, nsub=NB) for i in range(2)]
            gcol = P.sb(es, "gcol", [64, 2, 2, 32], F32, nsub=2)
            vtm = P.sb(es, "vtm", [128, NB, 256], BF16, nsub=NB)
            mats = P.sb(es, "gmats", [128, 5, 128], F32)
            gmask = P.sb(es, "gmask", [128, 2, 128], BF16)
            keep = P.sb(es, "gkeep", [128, 2, 32], F32)
            identb = P.sb(es, "identbG", [128, 128], BF16)
            P.dma("sp", mats.v(), d["gla_mats"].rearrange("a p c -> p a c"))
            P.dma("sp", gmask.v(), d["gla_mask"].rearrange("a p c -> p a c"))
            P.dma("sp", keep.v(), d["gla_keep"][:, :, :])
            P.I("dve", "tensor_copy", identb.v(), ident.v())
            with ExitStack() as e1:
                wb = P.sb(e1, "wbG", [128, 8, 800], BF16)
                afT = P.sb(e1, "afT", [16, 2, 512], BF16)
                qT = P.sb(e1, "gqT", [64, 2, 512], BF16)
                kT = P.sb(e1, "gkT", [64, 2, 512], BF16)
                w2 = P.sb(e1, "gw2", [16, 2, 128], BF16)
                b2 = P.sb(e1, "gb2", [1, 2, 128], BF16)
                onesb = P.sb(e1, "onesb", [1, 128], BF16)
                ktm = P.sb(e1, "ktm", [128, 128], F32)
                e1t = P.sb(e1, "ge1", [128, 128], F32)
                spt = P.sb(e1, "gsp", [128, 128], F32)
                eq = P.sb(e1, "geq", [64, 2, 128], F32)
                ek = P.sb(e1, "gek", [64, 2, 128], F32)
                el = P.sb(e1, "gel", [128, 128], F32)
                self.load_w(wb, l, 672, 800)
                P.dma("pool", w2.v(), d["w_gla_a"][l].rearrange("a r c -> r a c"))
                P.dma("pool", b2.v(), d["b_gla_a"][l:l + 1, :, :])
                P.I("pool", "memset", onesb.v(), 1.0)
                import os
                G1 = int(os.environ.get("KGLA_G1", "99"))
                for tg in range(4 if G1 >= 2 else 0):
                    tsl = slice(tg * 512, (tg + 1) * 512)
                    pq = self.fm_group(wb, [(0, 64), (64, 64), (128, 64), (192, 64)], tg)
                    P.I("act", "activation", qT.v(s_[:, 0, :]), pq[0], AF.Copy)
                    P.I("dve", "tensor_copy", qT.v(s_[:, 1, :]), pq[1])
                    P.I("act", "activation", kT.v(s_[:, 0, :]), pq[2], AF.Copy)
                    P.I("dve", "tensor_copy", kT.v(s_[:, 1, :]), pq[3])
                    pq = self.fm_group(wb, [(768, 16), (784, 16)], tg)
                    P.I("act", "activation", afT.v(s_[:, 0, :]), pq[0], AF.Copy)
                    P.I("dve", "tensor_copy", afT.v(s_[:, 1, :]), pq[1])
                    for tb in range(tg * 4, tg * 4 + 4) if G1 >= 3 else []:
                        self.G1 = G1
                        self.gla_block(l, tb, wb, afT, qT, kT, w2, b2, onesb, ktm, e1t, spt, eq, ek, el, mats, vtm, kl, gcol, qeT, keT, LNQ)
                S.barrier()
            import os
            GSTOP = int(os.environ.get("KGLA_STOP", "99"))
            if GSTOP <= 1:
                return
            with ExitStack() as e2:
                of = P.sb(e2, "gof", [128, NB, 256], F32, nsub=NB)
                St = P.sb(e2, "gS", [64, 2, 64], F32)
                tmpS = P.sb(e2, "gtmp", [64, 2, 64], F32)
                Sb = [P.sb(e2, "gSb%d" % i, [64, 2, 64], BF16) for i in range(2)]
                att = [P.sb(e2, "gatt%d" % i, [128, 128], BF16) for i in range(2)]
                na = 0
                for dr in range(2):
                    P.dma("sp", St.v(), d["gla_init"][l, dr].rearrange("(a p) c -> p a c", p=64))
                    blks = range(NB) if dr == 0 else range(NB - 1, -1, -1)
                    for tb in blks:
                        bsl = slice(tb * 128, (tb + 1) * 128)
                        halves = (0, 1) if dr == 0 else (1, 0)
                        for hf in halves:
                            n = 2 * tb + hf
                            r0 = hf * 64
                            P.I("act", "activation", Sb[hf].v(), St.v(), AF.Copy)
                            pu = ps[self.psi % 6]; self.psi += 1
                            for h in range(4):
                                P.mm(pu.v(s_[(h % 2) * 32:(h % 2) * 32 + 32, (h // 2) * 64:(h // 2) * 64 + 64]), kl[dr].v(s_[r0:r0 + 64, tb, h * 32:(h + 1) * 32], tb),
                                     vtm.v(s_[r0:r0 + 64, tb, h * 64:(h + 1) * 64], tb), start=True, stop=True)
                            for hp in range(2):
                                P.I("dve", "scalar_tensor_tensor", tmpS.v(s_[:, hp, :]), St.v(s_[:, hp, :]), gcol.v(s_[:, dr, hp, n:n + 1], dr), pu.v(s_[0:64, hp * 64:(hp + 1) * 64]), op0=ALU.mult, op1=ALU.add)
                            if (dr == 0 and n % 4 == 3) or (dr == 1 and n % 4 == 0):
                                P.dma("sp", self.dout["gla_o"][l, n // 4, dr].rearrange("(a p) c -> p a c", p=64), tmpS.v())
                            nn = n + 1 if dr == 0 else n - 1
                            if 0 <= nn < 32:
                                P.I("dve", "tensor_scalar_mul", St.v(), tmpS.v(), keep.v(s_[0:64, dr, nn:nn + 1]))
                        po = ps[6 + (tb % 2)]
                        for h in range(4):
                            hs = slice((h % 2) * 32, (h % 2) * 32 + 32); hp = h // 2
                            pa = ps[self.psi % 6]; self.psi += 1
                            P.mm(pa.v(s_[:, 0:128]), keT[dr].v(s_[hs, hp, bsl], tb), qeT[dr].v(s_[hs, hp, bsl], tb), start=True, stop=True)
                            at = att[na % 2]; na += 1
                            P.I("dve", "tensor_tensor", at.v(), pa.v(s_[:, 0:128]), gmask.v(s_[:, dr, :]), op=ALU.mult)
                            oc = slice(h * 64, (h + 1) * 64)
                            P.mm(po.v(s_[:, oc]), at.v(), vtm.v(s_[:, tb, oc], tb), start=True, stop=False)
                            P.mm(po.v(s_[0:64, oc]), qeT[dr].v(s_[hs, hp, tb * 128:tb * 128 + 64], tb), Sb[0].v(s_[hs, hp, :]), start=False, stop=False)
                            P.mm(po.v(s_[64:128, oc]), qeT[dr].v(s_[hs, hp, tb * 128 + 64:tb * 128 + 128], tb), Sb[1].v(s_[hs, hp, :]), start=False, stop=True)
                        if dr == 0:
                            P.I("act", "activation", of.v(s_[:, tb, :], tb), po.v(s_[:, 0:256]), AF.Copy)
                        else:
                            P.I("dve", "tensor_tensor", of.v(s_[:, tb, :], tb), of.v(s_[:, tb, :], tb), po.v(s_[:, 0:256]), op=ALU.add)
                if GSTOP <= 2:
                    S.barrier(); return
                wg = P.sb(e2, "wgG", [128, 8, 256], BF16)
                gn = P.sb(e2, "ggn", [128, 64], F32)
                ss4 = P.sb(e2, "gss", [128, 4], F32)
                junk = P.sb(e2, "gjunk", [128, 64], F32)
                sg = P.sb(e2, "gsil", [128, 256], F32)
                ob = P.sb(e2, "gob", [128, 256], BF16)
                self.load_w(wg, l, 1184, 256)
                P.dma("sp", gn.v(), d["gla_norm"][l].partition_broadcast(128))
                for tb in range(NB):
                    bsl = slice(tb * 128, (tb + 1) * 128)
                    for h in range(4):
                        P.I("act", "activation", junk.v(), of.v(s_[:, tb, h * 64:(h + 1) * 64], tb), AF.Square, accum_out=ss4.v(s_[:, h:h + 1]))
                    P.I("act", "activation", ss4.v(), ss4.v(), AF.Sqrt, bias=EPS, scale=1.0 / 64)
                    P.I("dve", "reciprocal", ss4.v(), ss4.v())
                    ov = of.v(s_[:, tb, :], tb).m(lambda a: a.rearrange("p (h c) -> p h c", h=4))
                    P.I("dve", "tensor_tensor", ov, ov, ss4.v().m(lambda a: a.unsqueeze(2).to_broadcast([128, 4, 64])), op=ALU.mult)
                    P.I("dve", "tensor_tensor", ov, ov, gn.v().m(lambda a: a.unsqueeze(1).to_broadcast([128, 4, 64])), op=ALU.mult)
                    pg = ps[self.psi % 6]; self.psi += 1
                    for k in range(8):
                        P.mm(pg.v(s_[:, 0:256]), self.uT.v(s_[:, k, bsl], tb), wg.v(s_[:, k, :]), start=(k == 0), stop=(k == 7))
                    P.I("act", "activation", sg.v(), pg.v(s_[:, 0:256]), AF.Silu)
                    P.I("dve", "tensor_tensor", ob.v(), of.v(s_[:, tb, :], tb), sg.v(), op=ALU.mult)
                    for c in range(2):
                        p2 = ps[self.psi % 6]; self.psi += 1
                        P.mm(p2.v(s_[:, 0:128]), ob.v(s_[:, c * 128:(c + 1) * 128]), identb.v(), start=True, stop=True)
                        P.I("act", "activation", self.brT[2].v(s_[:, c, bsl], tb), p2.v(s_[:, 0:128]), AF.Copy)
                S.barrier()

    def swa_kv_only(self, l):
        P = self; S = self.S
        ps = self.ps
        with ExitStack() as es:
            wb = P.sb(es, "wbS2", [128, 8, 256], BF16)
            kvo = [P.sb(es, "kvo2_%d" % i, [128, 256], F32) for i in range(2)]
            self.load_w(wb, l, 1728, 256)
            for tb in range(NB):
                pt = ps[self.psi % 6]; self.psi += 1
                for k in range(8):
                    P.mm(pt.v(s_[:, 0:256]), self.uT.v(s_[:, k, tb * 128:(tb + 1) * 128], tb), wb.v(s_[:, k, 0:256]), start=(k == 0), stop=(k == 7))
                ko = kvo[tb % 2]
                P.I("act", "activation", ko.v(), pt.v(s_[:, 0:256]), AF.Copy)
                P.dma("sp", self.dout["swakv_o"][l, tb * 128:(tb + 1) * 128, :], ko.v())
            S.barrier()

    def swa(self, l, dbg):
        P = self; S = self.S
        d = self.din
        ps, ident, ones = self.ps, self.ident, self.ones
        NK = 20
        with ExitStack() as es:
            wb = P.sb(es, "wbS", [128, 8, 896], BF16)
            cosT = P.sb(es, "cosS", [64, 512], F32)
            sinT = P.sb(es, "sinS", [64, 512], F32)
            biasS = P.sb(es, "biasS", [128, 4], F32)
            esink = P.sb(es, "esink", [128, 4], F32)
            msk = P.sb(es, "mskS", [128, NB, 2, 128], BF16)
            qT = P.sb(es, "qTS", [64, 4, T], BF16, nsub=4)
            kT = P.sb(es, "kTS", [64, 2, 2560], BF16)
            Vt = P.sb(es, "VtS", [128, NK, 2, 65], BF16, nsub=NK)
            ta = P.sb(es, "sta", [64, 512], F32)
            tb_ = P.sb(es, "stb", [64, 512], F32)
            kvo = [P.sb(es, "kvo%d" % i, [128, 256], F32) for i in range(2)]
            cch = P.sb(es, "cchS", [128, 4, 2, 64], F32)
            PT = [P.sb(es, "PTS%d" % i, [128, 256], BF16) for i in range(2)]
            oa = P.sb(es, "oaS", [128, 256], F32)
            rden = P.sb(es, "rdenS", [128, 1], F32)
            self.load_w(wb, l, 1472, 512)
            P.dma("pool", wb.v(s_[:, :, 512:896]), d["w_in"][l, :, 6112:6496].rearrange("(k p) c -> p k c", p=128))
            P.dma("sp", biasS.v(), d["bias_s"][:, :])
            P.dma("sp", esink.v(), d["swa_sink"][l])
            P.I("act", "activation", esink.v(), esink.v(), AF.Exp)
            P.dma("sp", msk.v(), d["mask_s"][:, :, :, :])
            P.I("pool", "memset", Vt.v(), 1.0)
            import os
            STOP = int(os.environ.get("KSWA_STOP", "99"))
            if STOP <= 1:
                S.barrier(); return
            for tg in range(4):
                tsl = slice(tg * 512, (tg + 1) * 512)
                P.dma("sp", cosT.v(), d["rope_s"][0, :, tsl])
                P.dma("sp", sinT.v(), d["rope_s"][1, :, tsl])
                for h in range(4):
                    pq = self.fm_group(wb, [(h * 64, 64), (512 + h * 64, 64)], tg)
                    self.rope_evac(qT.v(s_[:, h, tsl], h), pq[0], pq[1], cosT.v(), sinT.v(), ta.v(), tb_.v())
                for j in range(2):
                    pk = self.fm_group(wb, [(256 + j * 64, 64), (768 + j * 64, 64)], tg)
                    self.rope_evac(kT.v(s_[:, j, 512 + tg * 512:512 + (tg + 1) * 512]), pk[0], pk[1], cosT.v(), sinT.v(), ta.v(), tb_.v())
            if STOP <= 2:
                S.barrier(); return
            for tb in range(NB):
                pt = ps[self.psi % 6]; self.psi += 1
                for k in range(8):
                    P.mm(pt.v(s_[:, 0:256]), self.uT.v(s_[:, k, tb * 128:(tb + 1) * 128], tb), wb.v(s_[:, k, 256:512]), start=(k == 0), stop=(k == 7))
                ko = kvo[tb % 2]
                P.I("act", "activation", ko.v(), pt.v(s_[:, 0:256]), AF.Copy)
                P.dma("sp", self.dout["swakv_o"][l, tb * 128:(tb + 1) * 128, :], ko.v())
                P.I("dve", "tensor_copy", Vt.v(s_[:, 4 + tb, :, 0:64], 4 + tb), ko.v(s_[:, 128:256]).m(lambda a: a.rearrange("p (j c) -> p j c", j=2)))
            if STOP <= 3:
                S.barrier(); return
            for j in range(2):
                P.dma("sp", cch.v(s_[:, :, j, :]), d["cache_swak"][l, j].rearrange("(c p) e -> p c e", p=128))
            for c in range(4):
                for j in range(2):
                    p2 = ps[self.psi % 6]; self.psi += 1
                    P.I("pe", "transpose", p2.v(s_[0:64, 0:128]), cch.v(s_[:, c, j, :]), ident.v())
                    P.I("act", "activation", kT.v(s_[:, j, c * 128:(c + 1) * 128]), p2.v(s_[0:64, 0:128]), AF.Copy)
            cchv = P.sb(es, "cchV", [128, 4, 2, 64], F32)
            for j in range(2):
                P.dma("sp", cchv.v(s_[:, :, j, :]), d["cache_swav"][l, j].rearrange("(c p) e -> p c e", p=128))
            for c in range(4):
                P.I("dve", "tensor_copy", Vt.v(s_[:, c, :, 0:64], c), cchv.v(s_[:, c, :, :]))
            n = 0
            import os
            for tb in range(NB if "noatt" not in os.environ.get("KSKIP", "") else 0):
                qsl = slice(tb * 128, (tb + 1) * 128)
                for j in range(2):
                    po = [ps[6], ps[7]]
                    kcs = [(4 + tb + dd, dd) for dd in (-1, 0, 1) if 0 <= tb + dd < NB] + [(c, 2) for c in range(4)]
                    for i, (kc, kind) in enumerate(kcs):
                        ksl = slice(kc * 128, (kc + 1) * 128)
                        pt = ps[self.psi % 6]; self.psi += 1
                        for g in range(2):
                            P.mm(pt.v(s_[:, g * 128:(g + 1) * 128]), kT.v(s_[:, j, ksl]), qT.v(s_[:, 2 * j + g, qsl], 2 * j + g), start=True, stop=True)
                        pT = PT[n % 2]; n += 1
                        if kind == 2:
                            P.I("act", "activation", pT.v(), pt.v(s_[:, 0:256]), AF.Exp, bias=biasS.v(s_[:, 0:1]), scale=SWA_SCALE)
                        else:
                            P.I("act", "activation", pT.v(), pt.v(s_[:, 0:256]), AF.Exp, scale=SWA_SCALE)
                            if kind != 0:
                                mi = 0 if kind == -1 else 1
                                for g in range(2):
                                    P.I("dve", "tensor_tensor", pT.v(s_[:, g * 128:(g + 1) * 128]), pT.v(s_[:, g * 128:(g + 1) * 128]), msk.v(s_[:, tb, mi, :]), op=ALU.mult)
                        for g in range(2):
                            P.mm(po[g].v(s_[:, 0:65]), pT.v(s_[:, g * 128:(g + 1) * 128]), Vt.v(s_[:, kc, j, :], kc), start=(i == 0), stop=(i == len(kcs) - 1))
                    for g in range(2):
                        hh = 2 * j + g
                        P.I("dve", "tensor_tensor", rden.v(), po[g].v(s_[:, 64:65]), esink.v(s_[:, hh:hh + 1]), op=ALU.add)
                        P.I("dve", "reciprocal", rden.v(), rden.v())
                        P.I("dve", "tensor_scalar_mul", oa.v(s_[:, hh * 64:(hh + 1) * 64]), po[g].v(s_[:, 0:64]), rden.v())
                for c in range(2):
                    p2 = ps[self.psi % 6]; self.psi += 1
                    P.I("pe", "transpose", p2.v(s_[:, 0:128]), oa.v(s_[:, c * 128:(c + 1) * 128]), ident.v())
                    P.I("act", "activation", self.brT[3].v(s_[:, c, tb * 128:(tb + 1) * 128], tb), p2.v(s_[:, 0:128]), AF.Copy)
            S.barrier()

    def fnet(self, l):
        P = self; S = self.S
        d = self.din
        with ExitStack() as es:
            wb = P.sb(es, "wbF", [128, 8, 256], BF16)
            fT = P.sb(es, "fT", [128, 2, T], BF16, nsub=4)
            cd = P.sb(es, "cdft", [128, 2, 128], BF16)
            A = P.sb(es, "fA", [128, NB, 256], BF16, nsub=NB)
            B = P.sb(es, "fB", [128, NB, 256], BF16, nsub=NB)
            tc_ = [P.sb(es, "dc%d" % i, [128, 512], BF16) for i in range(4)]
            ts_ = [P.sb(es, "ds%d" % i, [128, 512], BF16) for i in range(4)]
            self.load_w(wb, l, 416, 256)
            P.dma("sp", cd.v(), d["cdft"].rearrange("a p c -> p a c"))
            for c in range(2):
                def ev(tg, pv, c=c):
                    P.I("act", "activation", fT.v(s_[:, c, tg * 512:(tg + 1) * 512], tg), pv, AF.Copy)
                self.proj_fm(wb, c * 128, 128, ev)
            for tb in range(NB):
                pa = self.ps[self.psi % 8]; self.psi += 1
                for c in range(2):
                    P.mm(pa.v(s_[:, c * 128:(c + 1) * 128]), fT.v(s_[:, c, tb * 128:(tb + 1) * 128], tb // 4), cd.v(s_[:, 0, :]), start=True, stop=True)
                    P.mm(pa.v(s_[:, 256 + c * 128:256 + (c + 1) * 128]), fT.v(s_[:, c, tb * 128:(tb + 1) * 128], tb // 4), cd.v(s_[:, 1, :]), start=True, stop=True)
                P.I("act", "activation", A.v(s_[:, tb, :], tb), pa.v(s_[:, 0:256]), AF.Copy)
                P.I("act", "activation", B.v(s_[:, tb, :], tb), pa.v(s_[:, 256:512]), AF.Copy)
            n = 0
            for tg in range(4):
                p0 = self.ps[self.psi % 8]; p1 = self.ps[(self.psi + 1) % 8]; self.psi += 2
                for tb in range(NB):
                    ct = tc_[n % 4]; st = ts_[n % 4]; n += 1
                    P.dma("sp", ct.v(), d["dft_c"][tb * 128:(tb + 1) * 128, tg * 512:(tg + 1) * 512])
                    P.dma("act", st.v(), d["dft_s"][tb * 128:(tb + 1) * 128, tg * 512:(tg + 1) * 512])
                    for c, pp in ((0, p0), (1, p1)):
                        P.mm(pp.v(), A.v(s_[:, tb, c * 128:(c + 1) * 128], tb), ct.v(), start=(tb == 0), stop=False)
                        P.mm(pp.v(), B.v(s_[:, tb, c * 128:(c + 1) * 128], tb), st.v(), start=False, stop=(tb == NB - 1))
                for c, pp in ((0, p0), (1, p1)):
                    P.I("act" if c == 0 else "dve", "activation" if c == 0 else "tensor_copy", self.brT[1].v(s_[:, c, tg * 512:(tg + 1) * 512], range(tg * 4, tg * 4 + 4)),
                        pp.v(), *((AF.Copy,) if c == 0 else ()))
            S.barrier()

    def merge(self, l, dbg):
        P = self; S = self.S
        d = self.din
        x, uT, ps, mcol, ident, ones = self.x, self.uT, self.ps, self.mcol, self.ident, self.ones
        with ExitStack() as es:
            wbr = P.sb(es, "wbr", [128, 8, D], BF16)
            wo = P.sb(es, "wo", [128, 8, D], BF16)
            wg = [P.sb(es, "wg%d" % i, [128, 8, 512], BF16) for i in range(1)]
            G = P.sb(es, "Gacc", [128, 512], F32)
            GT = P.sb(es, "GT", [128, 8, 512], BF16, nsub=8)
            sg = [P.sb(es, "sg%d" % i, [128, 512], F32) for i in range(2)]
            gbc = P.sb(es, "g1bc", [128, D], F32)
            lng = P.sb(es, "ln1g", [128, D], F32)
            lnb = P.sb(es, "ln1b", [128, D], F32)
            dg = P.sb(es, "dgm", [128, 128], F32)
            xt = [P.sb(es, "xt%d" % i, [128, D], F32) for i in range(1)]
            tmps = [self.ln_tmp(es, "g%d" % i) for i in range(2)]
            P.dma("pool", wbr.v(), d["w_branch"][l].rearrange("b (k p) d -> p (b k) d", p=128))
            P.dma("pool", wo.v(), d["w_out"][l].rearrange("(k p) d -> p k d", p=128))
            P.dma("sp", lng.v(), d["ln"][l, 0, :].partition_broadcast(128))
            P.dma("sp", lnb.v(), d["ln"][l, 1, :].partition_broadcast(128))
            for k in range(8):
                P.I("dve", "tensor_scalar_mul", dg.v(), ident.v(), mcol.v(s_[:, l, 16 + k:17 + k], l))
                pt = ps[self.psi % 8]; self.psi += 1
                P.mm(pt.v(s_[:, 0:128]), ones.v(), dg.v(), start=True, stop=True)
                P.I("act", "activation", gbc.v(s_[:, k * 128:(k + 1) * 128]), pt.v(s_[:, 0:128]), AF.Copy)
            n = 0
            for tg in range(4):
                tsub = range(tg * 4, tg * 4 + 4)
                tsl = slice(tg * 512, (tg + 1) * 512)
                for dc in range(8):
                    w = wg[0]; n += 1
                    for b in range(4):
                        c0 = 1984 + b * D + dc * 128
                        P.dma("pool", w.v(s_[:, :, b * 128:(b + 1) * 128]),
                              d["w_in"][l, :, c0:c0 + 128].rearrange("(k p) c -> p k c", p=128))
                    for b in range(4):
                        pg = ps[self.psi % 8]; pp = ps[(self.psi + 1) % 8]; self.psi += 2
                        for k in range(8):
                            P.mm(pg.v(), w.v(s_[:, k, b * 128:(b + 1) * 128]), uT.v(s_[:, k, tsl], tsub), start=(k == 0), stop=(k == 7))
                        for kc in range(2):
                            P.mm(pp.v(), wbr.v(s_[:, b * 2 + kc, dc * 128:(dc + 1) * 128]), self.brT[b].v(s_[:, kc, tsl], tsub),
                                 start=(kc == 0), stop=(kc == 1))
                        sgt = sg[b % 2]
                        P.I("act", "activation", sgt.v(), pg.v(), AF.Sigmoid)
                        if b == 0:
                            P.I("dve", "tensor_tensor", G.v(), sgt.v(), pp.v(), op=ALU.mult)
                        else:
                            P.I("dve", "tensor_tensor", sgt.v(), sgt.v(), pp.v(), op=ALU.mult)
                            if b < 3:
                                P.I("pool", "tensor_tensor", G.v(), G.v(), sgt.v(), op=ALU.add)
                            else:
                                P.I("pool", "tensor_tensor", GT.v(s_[:, dc, :], dc), G.v(), sgt.v(), op=ALU.add)
                for j in range(4):
                    tb = tg * 4 + j
                    xtb = xt[0]
                    for hf in range(2):
                        pm = ps[self.psi % 8]; self.psi += 1
                        for k in range(8):
                            P.mm(pm.v(), GT.v(s_[:, k, j * 128:(j + 1) * 128], k), wo.v(s_[:, k, hf * 512:(hf + 1) * 512]), start=(k == 0), stop=(k == 7))
                        hs = slice(hf * 512, (hf + 1) * 512)
                        P.I("dve", "tensor_tensor", xtb.v(s_[:, hs]), pm.v(), gbc.v(s_[:, hs]), op=ALU.mult)
                        P.I("dve", "scalar_tensor_tensor", xtb.v(s_[:, hs]), x.v(s_[:, tb, hs], tb), ALPHA, xtb.v(s_[:, hs]), op0=ALU.mult, op1=ALU.add)
                    self.ln_block(tmps[tb % 2], xtb.v(), x.v(s_[:, tb, :], tb))
                    P.I("pool", "tensor_tensor", x.v(s_[:, tb, :], tb), x.v(s_[:, tb, :], tb), lng.v(), op=ALU.mult)
                    P.I("pool", "tensor_tensor", x.v(s_[:, tb, :], tb), x.v(s_[:, tb, :], tb), lnb.v(), op=ALU.add)
            S.barrier()

    def post_ffn(self, l, yacc_is_x=True):
        P = self; S = self.S
        d = self.din
        x = self.x
        with ExitStack() as es:
            lng = P.sb(es, "ln2g", [128, D], F32)
            lnb = P.sb(es, "ln2b", [128, D], F32)
            xt = [P.sb(es, "xq%d" % i, [128, D], F32) for i in range(2)]
            tmps = [self.ln_tmp(es, "q%d" % i) for i in range(2)]
            P.dma("sp", lng.v(), d["ln"][l, 2, :].partition_broadcast(128))
            P.dma("sp", lnb.v(), d["ln"][l, 3, :].partition_broadcast(128))
            for tb in range(NB):
                xtb = xt[tb % 2]
                P.I("act", "activation", xtb.v(), x.v(s_[:, tb, :], tb), AF.Copy)
                self.ln_block(tmps[tb % 2], xtb.v(), x.v(s_[:, tb, :], tb))
                P.I("pool", "tensor_tensor", x.v(s_[:, tb, :], tb), x.v(s_[:, tb, :], tb), lng.v(), op=ALU.mult)
                P.I("pool", "tensor_tensor", x.v(s_[:, tb, :], tb), x.v(s_[:, tb, :], tb), lnb.v(), op=ALU.add)
            S.barrier()

    def layer(self, l, dbg):
        P = self; S = self.S
        self.psi = 0
        for g4 in range(16):
            P.dma("pool", self.ubf.v(s_[l, g4 * 4:(g4 + 1) * 4], l * 16 + g4), self.din["peer_uT"][l, g4 * 4:(g4 + 1) * 4].rearrange("g p k e -> g p (k e)"))
            P.dma("pool", self.vbf.v(s_[l, g4 * 4:(g4 + 1) * 4], l * 16 + g4), self.din["peer_v"][l, g4 * 4:(g4 + 1) * 4].rearrange("g p j d -> g p (j d)"))
        self.mod_to_uT(l, 0)
        if "uT" in dbg and l == 0:
            P.dma("pool", dbg["uT"], self.uT.v())
        with ExitStack() as esl:
            self.brT = [P.sb(esl, "brT%d" % b, [128, 2, T], BF16, nsub=NB) for b in range(4)]
            import os
            skip = os.environ.get("KSKIP", "")
            for b, nm in ((0, "mla"), (3, "swa"), (1, "fnet"), (2, "gla")):
                if nm in skip:
                    for tb4 in range(4):
                        P.I("pool", "memset", self.brT[b].v(s_[:, :, tb4 * 512:(tb4 + 1) * 512], range(tb4 * 4, tb4 * 4 + 4)), 0.0)
            if "mla" not in skip:
                self.mla(l, dbg)
            if "swa" not in skip:
                self.swa(l, dbg)
            else:
                self.swa_kv_only(l)
            if "fnet" not in skip:
                self.fnet(l)
            if "gla" not in skip:
                self.gla(l, dbg)
            if "brT1" in dbg and l == 0:
                P.dma("pool", dbg["brT1"], self.brT[1].v())
            self.merge(l, dbg)
            S.barrier()
        if "x1" in dbg and l == 0:
            P.dma("pool", dbg["x1"], self.x.v())
        import os
        if "peer" in os.environ.get("KSKIP", ""):
            for tb in range(NB):
                P.I("act", "activation", self.x.v(s_[:, tb, :], tb), self.x.v(s_[:, tb, :], tb), AF.Copy, scale=ALPHA)
        else:
            self.mod_to_uT(l, 1)
            self.peer(l, dbg)
        self.post_ffn(l)

    def peer(self, l, dbg):
        P = self; S = self.S
        d = self.din
        x, uT, ps, mcol, ident, ones, iota, bm = self.x, self.uT, self.ps, self.mcol, self.ident, self.ones, self.iota, self.bm
        NCH = 128
        with ExitStack() as es:
            wq = P.sb(es, "wq", [128, 8, 512], BF16)
            kT = P.sb(es, "keysT", [128, 16, 128], BF16)
            gbc = P.sb(es, "g2bc", [128, D], F32)
            dg = P.sb(es, "dg2", [128, 128], F32)
            qpT = P.sb(es, "qpT", [128, 16, 128], BF16, nsub=16)
            sc = P.sb(es, "psc", [128, 16, 128], F32, nsub=16)
            wk = P.sb(es, "pwk", [128, 256], F32)
            vtop = P.sb(es, "vtop", [128, 16, 16], F32, nsub=16)
            itop = P.sb(es, "itop", [128, 16, 16], U32, nsub=16)
            idx1f = P.sb(es, "idx1f", [128, 128], F32)
            idx2f = P.sb(es, "idx2f", [128, 128], F32)
            idxT = P.sb(es, "idxT", [128, 2, 128], F32)
            cand = P.sb(es, "cand", [128, 8, 256], F32, nsub=8)
            t8a = P.sb(es, "t8a", [128, 8, 8], F32, nsub=8)
            t8b = P.sb(es, "t8b", [128, 8, 8], F32, nsub=8)
            nmx = P.sb(es, "nmx", [128, 8], F32)
            zz = P.sb(es, "pz", [128, 8], F32)
            wCT = P.sb(es, "wCT", [128, 128, 16], BF16)
            O1 = P.sb(es, "O1", [128, 16, 128], BF16)
            O2 = P.sb(es, "O2", [128, 16, 128], BF16)
            Cbd = P.sb(es, "Cbd", [128, 16, 128], BF16)
            tmpS = [P.sb(es, "ptmp%d" % i, [128, 4, 128], BF16) for i in range(2)]
            WtT = P.sb(es, "WtT", [128, 128, 128], BF16, nsub=32)
            Ut = [P.sb(es, "Ut%d" % i, [128, 8, 256], BF16) for i in range(2)]
            Vt = [P.sb(es, "Vt%d" % i, [128, 2, D], BF16) for i in range(2)]
            actS = [P.sb(es, "pact%d" % i, [128, 128], BF16) for i in range(2)]
            GS = [P.sb(es, "pG%d" % i, [128, 128], BF16) for i in range(2)]
            P.dma("pool", kT.v(), d["peer_keysT"][l].rearrange("h q c k -> c (h q) k"))
            for k in range(8):
                P.I("dve", "tensor_scalar_mul", dg.v(), ident.v(), mcol.v(s_[:, l, 40 + k:41 + k], l))
                pt = ps[self.psi % 4]; self.psi += 1
                P.mm(pt.v(s_[:, 0:128]), ones.v(), dg.v(), start=True, stop=True)
                P.I("act", "activation", gbc.v(s_[:, k * 128:(k + 1) * 128]), pt.v(s_[:, 0:128]), AF.Copy)
            py = [ps[6], ps[7]]
            nld = 0
            for tb in range(NB):
                tsl = slice(tb * 128, (tb + 1) * 128)
                for c4 in range(4):
                    pt = ps[self.psi % 4]; self.psi += 1
                    P.dma("pool", wq.v(), d["w_peer_q"][l, c4])
                    for j in range(4):
                        c = c4 * 4 + j
                        for k in range(8):
                            P.mm(pt.v(s_[:, j * 128:(j + 1) * 128]), wq.v(s_[:, k, j * 128:(j + 1) * 128]), uT.v(s_[:, k, tsl], tb), start=(k == 0), stop=(k == 7))
                    P.I("act", "activation", qpT.v(s_[:, c4 * 4:(c4 + 1) * 4, :], range(c4 * 4, c4 * 4 + 4)), pt.v().m(lambda a: a.rearrange("p (j t) -> p j t", j=4)), AF.Copy)
                for c4 in range(4):
                    pt = ps[self.psi % 4]; self.psi += 1
                    for j in range(4):
                        c = c4 * 4 + j
                        P.mm(pt.v(s_[:, j * 128:(j + 1) * 128]), qpT.v(s_[:, c, :], c), kT.v(s_[:, c, :]), start=True, stop=True)
                    P.I("act", "activation", sc.v(s_[:, c4 * 4:(c4 + 1) * 4, :], range(c4 * 4, c4 * 4 + 4)), pt.v().m(lambda a: a.rearrange("p (j t) -> p j t", j=4)), AF.Copy)
                for c in range(16):
                    P.I("dve", "max", vtop.v(s_[:, c, 0:8], c), sc.v(s_[:, c, :], c))
                    P.I("dve", "max_index", itop.v(s_[:, c, 0:8], c), vtop.v(s_[:, c, 0:8], c), sc.v(s_[:, c, :], c))
                    P.I("dve", "match_replace", wk.v(s_[:, 0:128]), vtop.v(s_[:, c, 0:8], c), sc.v(s_[:, c, :], c), -1e30)
                    P.I("dve", "max", vtop.v(s_[:, c, 8:16], c), wk.v(s_[:, 0:128]))
                    P.I("dve", "max_index", itop.v(s_[:, c, 8:16], c), vtop.v(s_[:, c, 8:16], c), wk.v(s_[:, 0:128]))
                v4 = lambda a: a.rearrange("p (h q) r -> p h q r", q=2)
                P.I("dve", "tensor_copy", idx1f.v().m(lambda a: a.rearrange("p (h r) -> p h r", h=8)), itop.v().m(lambda a: v4(a)[:, :, 0, :]))
                P.I("dve", "tensor_copy", idx2f.v().m(lambda a: a.rearrange("p (h r) -> p h r", h=8)), itop.v().m(lambda a: v4(a)[:, :, 1, :]))
                P.I("dve", "tensor_tensor", cand.v().m(lambda a: a.rearrange("p h (a b) -> p h a b", a=16)),
                    vtop.v().m(lambda a: v4(a)[:, :, 0, :].unsqueeze(3).to_broadcast([128, 8, 16, 16])),
                    vtop.v().m(lambda a: v4(a)[:, :, 1, :].unsqueeze(2).to_broadcast([128, 8, 16, 16])), op=ALU.add)
                for h in range(8):
                    P.I("dve", "max", t8a.v(s_[:, h, :], h), cand.v(s_[:, h, :], h))
                    P.I("dve", "match_replace", wk.v(), t8a.v(s_[:, h, :], h), cand.v(s_[:, h, :], h), -1e30)
                    P.I("dve", "max", t8b.v(s_[:, h, :], h), wk.v())
                P.I("dve", "tensor_scalar_mul", nmx.v(), t8a.v(s_[:, :, 0]), -1.0)
                for h in range(8):
                    P.I("act", "activation", sc.v(s_[:, 2 * h:2 * h + 2, :], [2 * h, 2 * h + 1]).m(lambda a: a.rearrange("p a b -> p (a b)")), cand.v(s_[:, h, :], h), AF.Exp, bias=nmx.v(s_[:, h:h + 1]), scale=1.0)
                    P.I("dve", "scalar_tensor_tensor", sc.v(s_[:, 2 * h:2 * h + 2, :], [2 * h, 2 * h + 1]).m(lambda a: a.rearrange("p a b -> p (a b)")), cand.v(s_[:, h, :], h), t8b.v(s_[:, h, 7:8], h), sc.v(s_[:, 2 * h:2 * h + 2, :], [2 * h, 2 * h + 1]).m(lambda a: a.rearrange("p a b -> p (a b)")), op0=ALU.is_ge, op1=ALU.mult)
                P.I("dve", "tensor_reduce", zz.v(), sc.v().m(lambda a: a.rearrange("p (h a) b -> p h (a b)", a=2)), axis=AX.X, op=ALU.add)
                P.I("dve", "reciprocal", zz.v(), zz.v())
                P.I("dve", "tensor_tensor", sc.v().m(lambda a: a.rearrange("p (h a) b -> p h (a b)", a=2)), sc.v().m(lambda a: a.rearrange("p (h a) b -> p h (a b)", a=2)), zz.v().m(lambda a: a.unsqueeze(2).to_broadcast([128, 8, 256])), op=ALU.mult)
                pt = ps[self.psi % 4]; self.psi += 1
                P.I("pe", "transpose", pt.v(s_[:, 0:128]), idx1f.v(), ident.v())
                P.I("pe", "transpose", pt.v(s_[:, 128:256]), idx2f.v(), ident.v())
                P.I("act", "activation", idxT.v(), pt.v(s_[:, 0:256]).m(lambda a: a.rearrange("p (a t) -> p a t", a=2)), AF.Copy)
                for r4 in range(4):
                    pt = ps[self.psi % 4]; self.psi += 1
                    for j in range(4):
                        r2 = r4 * 4 + j
                        P.I("pe", "transpose", pt.v(s_[:, j * 128:(j + 1) * 128]),
                            sc.v().m(lambda a, r2=r2: a.rearrange("p c (a b) -> p (c a) b", b=16)[:, :, r2]), ident.v())
                    P.I("act", "activation", wCT.v(s_[:, :, r4 * 4:(r4 + 1) * 4]).m(lambda a: a.rearrange("p t j -> p j t")),
                        pt.v().m(lambda a: a.rearrange("p (j t) -> p j t", j=4)), AF.Copy)
                for sbk in range(8):
                    t0 = sbk * 16
                    P.I("dve", "tensor_tensor", O1.v(), iota.v().m(lambda a: a.unsqueeze(1).to_broadcast([128, 16, 128])),
                        idxT.v(s_[:, 0, t0:t0 + 16]).m(lambda a: a.unsqueeze(2).to_broadcast([128, 16, 128])), op=ALU.is_equal)
                    P.I("dve", "tensor_tensor", O2.v(), iota.v().m(lambda a: a.unsqueeze(1).to_broadcast([128, 16, 128])),
                        idxT.v(s_[:, 1, t0:t0 + 16]).m(lambda a: a.unsqueeze(2).to_broadcast([128, 16, 128])), op=ALU.is_equal)
                    P.I("pool", "tensor_tensor", Cbd.v().m(lambda a: a.rearrange("p t (h r) -> p t h r", h=8)),
                        wCT.v(s_[:, t0:t0 + 16, :]).m(lambda a: a.unsqueeze(2).to_broadcast([128, 16, 8, 16])),
                        bm.v().m(lambda a: a.unsqueeze(1).unsqueeze(3).to_broadcast([128, 16, 8, 16])), op=ALU.mult)
                    for g4 in range(4):
                        pt = ps[self.psi % 4]; self.psi += 1
                        tS = tmpS[g4 % 2]
                        for j in range(4):
                            tt = g4 * 4 + j
                            P.mm(pt.v(s_[:, j * 128:(j + 1) * 128]), Cbd.v(s_[:, tt, :]), O1.v(s_[:, tt, :]), start=True, stop=True)
                        P.I("act", "activation", tS.v(), pt.v().m(lambda a: a.rearrange("p (j i) -> p j i", j=4)), AF.Copy)
                        pt2 = ps[self.psi % 4]; self.psi += 1
                        for j in range(4):
                            tt = g4 * 4 + j
                            P.mm(pt2.v(s_[:, j * 128:(j + 1) * 128]), O2.v(s_[:, tt, :]), tS.v(s_[:, j, :]), start=True, stop=True)
                        ta = t0 + g4 * 4
                        P.I("dve" if g4 % 2 == 0 else "act", "tensor_copy" if g4 % 2 == 0 else "activation", WtT.v(s_[:, ta:ta + 4, :], ta // 4),
                            pt2.v().m(lambda a: a.rearrange("p (j i) -> p j i", j=4)), *(() if g4 % 2 == 0 else (AF.Copy,)))
                cur = {}
                def emitU(c):
                    c2, j = c // 2, c % 2
                    if j == 0:
                        ut = Ut[c2 % 2]; vt = Vt[c2 % 2]
                        P.dma("sp", ut.v().m(lambda a: a.rearrange("p k e -> p (k e)")), self.ubf.v(s_[l, c2], l * 16 + c2 // 4))
                        P.dma("sp", vt.v().m(lambda a: a.rearrange("p j d -> p (j d)")), self.vbf.v(s_[l, c2], l * 16 + c2 // 4))
                    ut = Ut[c2 % 2]
                    pa = ps[4 + (c % 2)]
                    for k in range(8):
                        P.mm(pa.v(s_[:, 0:128]), ut.v(s_[:, k, j * 128:(j + 1) * 128]), uT.v(s_[:, k, tsl], tb), start=(k == 0), stop=(k == 7))
                def emitMV(c):
                    c2, j = c // 2, c % 2
                    vt = Vt[c2 % 2]
                    pa = ps[4 + (c % 2)]
                    aS = actS[c % 2]; gS = GS[c % 2]
                    P.I("act", "activation", aS.v(), pa.v(s_[:, 0:128]), AF.Gelu)
                    P.I("dve", "tensor_tensor", gS.v(), aS.v(), WtT.v(s_[:, :, c]), op=ALU.mult)
                    for hf in range(2):
                        P.mm(py[hf].v(), gS.v(), vt.v(s_[:, j, hf * 512:(hf + 1) * 512]), start=(c == 0), stop=(c == NCH - 1))
                emitU(0)
                for c in range(NCH):
                    if c + 1 < NCH:
                        emitU(c + 1)
                    emitMV(c)
                for hf in range(2):
                    hs = slice(hf * 512, (hf + 1) * 512)
                    yv = cand.v(s_[:, 0:2, :], [0, 1]).m(lambda a: a.rearrange("p a b -> p (a b)"))
                    P.I("dve", "tensor_tensor", yv, py[hf].v(), gbc.v(s_[:, hs]), op=ALU.mult)
                    P.I("dve", "scalar_tensor_tensor", x.v(s_[:, tb, hs], tb), x.v(s_[:, tb, hs], tb), ALPHA, yv, op0=ALU.mult, op1=ALU.add)
            S.barrier()

def _bf(a):
    return np.ascontiguousarray(a).astype(ml_dtypes.bfloat16)


def host_consts(kind):
    c = {}
    c["ident"] = np.eye(128, dtype=np.float32)
    c["bm"] = np.ascontiguousarray((np.arange(128)[:, None] // 16 == np.arange(8)[None, :]).astype(np.float32))
    seqlen = T if kind == "sample" else 256
    n = np.arange(seqlen)
    ang = 2.0 * np.pi * np.outer(n, n) / seqlen
    sc = 1.0 / np.sqrt(seqlen * 64.0)
    cb = np.cos(ang) * sc
    sbm = -np.sin(ang) * sc
    Cf = np.zeros((T, T), np.float64)
    Sf = np.zeros((T, T), np.float64)
    for i in range(T // seqlen):
        sl = slice(i * seqlen, (i + 1) * seqlen)
        Cf[sl, sl] = cb
        Sf[sl, sl] = sbm
    c["dft_c"] = _bf(Cf.astype(np.float32))
    c["dft_s"] = _bf(Sf.astype(np.float32))
    m = np.arange(64)
    a2 = 2.0 * np.pi * np.outer(m, m) / 64.0
    cc = np.zeros((2, 128, 128), np.float64)
    for g in range(2):
        cc[0, g * 64:(g + 1) * 64, g * 64:(g + 1) * 64] = np.cos(a2)
        cc[1, g * 64:(g + 1) * 64, g * 64:(g + 1) * 64] = np.sin(a2)
    c["cdft"] = _bf(cc.astype(np.float32))
    t = np.arange(T)
    rows = (t // 64).astype(np.float64); cols = (t % 64).astype(np.float64)
    for nm, R in (("rope_m", 32), ("rope_s", 64)):
        half = R // 2; q = R // 4
        tab = np.zeros((2, R, T), np.float64)
        for dd in range(R):
            pos = rows if dd < half else cols
            fi = dd % q
            freq = 10000.0 ** (-(2.0 * fi) / half)
            ang = pos * freq
            if kind == "sample":
                tab[0, dd] = np.cos(ang)
                tab[1, dd] = np.sin(ang) * (-1.0 if (dd // q) % 2 == 0 else 1.0)
            else:
                tab[0, dd] = 1.0
        c[nm] = np.ascontiguousarray(tab.astype(np.float32))
    bm_ = np.zeros((160,), np.float32)
    if kind == "prompt":
        for qu in range(8):
            for kc in range(20):
                ok = kc >= 4 and (kc - 4) // 2 == qu
                bm_[qu * 20 + kc] = 0.0 if ok else NEG
    c["bias_m"] = np.ascontiguousarray(np.broadcast_to(bm_[None, :], (128, 160)))
    bs_ = np.zeros((128, 4), np.float32)
    if kind == "prompt":
        bs_[:, 0] = NEG
    c["bias_s"] = bs_
    mk = np.zeros((128, NB, 2, 128), np.float32)
    kk = np.arange(128)[:, None]; qq = np.arange(128)[None, :]
    for tb in range(NB):
        if kind == "sample":
            mk[:, tb, 0, :] = (kk >= qq)
            mk[:, tb, 1, :] = (kk <= qq)
        else:
            mk[:, tb, 0, :] = 1.0 if tb % 2 == 1 else 0.0
            mk[:, tb, 1, :] = 1.0 if tb % 2 == 0 else 0.0
    c["mask_s"] = _bf(mk)
    tt = np.arange(128)[:, None]; tp = np.arange(128)[None, :]
    same = (tt // 64) == (tp // 64)
    cc_ = -1.0 / 16.0
    gm = np.zeros((5, 128, 128), np.float32)
    gm[0] = cc_ * (same & (tt <= tp))
    gm[1] = cc_ * (same & (tt > tp))
    gm[2] = cc_ * (same & (tt >= tp))
    gm[3] = cc_ * (same & (tt < tp))
    gm[4, :, 0] = cc_ * (np.arange(128) < 64)
    gm[4, :, 1] = cc_ * (np.arange(128) >= 64)
    c["gla_mats"] = gm
    c["gla_mask"] = _bf(np.stack([(same & (tt <= tp)), (same & (tt >= tp))]).astype(np.float32))
    kp = np.ones((128, 2, 32), np.float32)
    if kind == "prompt":
        for n_ in range(32):
            if n_ % 4 == 0:
                kp[:, 0, n_] = 0.0
            if n_ % 4 == 3:
                kp[:, 1, n_] = 0.0
    c["gla_keep"] = kp
    return c


def perm_swap(R):
    q = R // 4
    return np.array([d + q if (d // q) % 2 == 0 else d - q for d in range(R)])


def host_weights(inp):
    w = {}
    w["w_ada"] = np.ascontiguousarray(inp["w_ada"], dtype=np.float32)
    w["b_adaT"] = np.ascontiguousarray(inp["b_ada"].reshape(L, 48, 128).transpose(0, 2, 1), dtype=np.float32)
    w_in = np.asarray(inp["w_in"], dtype=np.float32)
    p32 = perm_swap(32); p64 = perm_swap(64)
    kr = w_in[:, :, 384:416][:, :, p32]
    sq = w_in[:, :, 1472:1728].reshape(L, D, 4, 64)[:, :, :, p64].reshape(L, D, 256)
    sk = w_in[:, :, 1728:1856].reshape(L, D, 2, 64)[:, :, :, p64].reshape(L, D, 128)
    w["w_in"] = np.ascontiguousarray(np.concatenate([w_in, kr, sq, sk], axis=2))
    w["w_branch"] = np.ascontiguousarray(inp["w_branch"], dtype=np.float32)
    w["w_out"] = np.ascontiguousarray(inp["w_out"], dtype=np.float32)
    w_uq = np.asarray(inp["w_uq"], dtype=np.float32)
    uq_sw = w_uq.reshape(L, 256, 4, 96)[:, :, :, 64:96][:, :, :, p32].reshape(L, 256, 128)
    w["w_uq"] = np.ascontiguousarray(np.concatenate([w_uq, uq_sw], axis=2))
    w["w_ukv"] = np.ascontiguousarray(inp["w_ukv"], dtype=np.float32)
    w["mla_q_norm"] = np.ascontiguousarray(np.asarray(inp["mla_q_norm"], dtype=np.float32).reshape(L, 2, 128).transpose(0, 2, 1))
    w["mla_kv_norm"] = np.ascontiguousarray(inp["mla_kv_norm"], dtype=np.float32)
    w["w_gla_a"] = np.ascontiguousarray(np.stack([inp["w_gla_a_fwd"], inp["w_gla_a_bwd"]], axis=1), dtype=np.float32)
    w["b_gla_a"] = np.ascontiguousarray(np.stack([inp["b_gla_a_fwd"], inp["b_gla_a_bwd"]], axis=1), dtype=np.float32)
    w["gla_norm"] = np.ascontiguousarray(inp["gla_norm"], dtype=np.float32)
    w["swa_sink"] = np.ascontiguousarray(np.broadcast_to(np.asarray(inp["swa_sink"], dtype=np.float32)[:, None, :], (L, 128, 4)))
    w["w_peer_q"] = np.ascontiguousarray(np.asarray(inp["w_peer_q"], dtype=np.float32).reshape(L, 8, 128, 4, 512).transpose(0, 3, 2, 1, 4))
    w["peer_keysT"] = np.ascontiguousarray(np.asarray(inp["peer_keys"], dtype=np.float32).transpose(0, 1, 2, 4, 3))
    w["peer_uT"] = np.ascontiguousarray(np.asarray(inp["peer_u"], dtype=np.float32).reshape(L, 64, 256, 8, 128).transpose(0, 1, 4, 3, 2))
    w["peer_v"] = np.ascontiguousarray(np.asarray(inp["peer_v"], dtype=np.float32).reshape(L, 64, 2, 128, D).transpose(0, 1, 3, 2, 4))
    w["ln"] = np.ascontiguousarray(np.stack([inp["ln1_g"], inp["ln1_b"], inp["ln2_g"], inp["ln2_b"]], axis=1), dtype=np.float32)
    return w


def core_inputs(inp, core, W, CS, CP):
    m = dict(W)
    if core < 2:
        m.update(CS)
        m["x"] = np.ascontiguousarray(inp["x_sample"][core], dtype=np.float32)
        cond = np.asarray(inp["c"][core], dtype=np.float32)
        m["cache_ckv"] = np.ascontiguousarray(inp["cache_mla_ckv"][core], dtype=np.float32)
        m["cache_krope"] = np.ascontiguousarray(inp["cache_mla_krope"][core], dtype=np.float32)
        m["cache_swak"] = np.ascontiguousarray(inp["cache_swa_k"][core], dtype=np.float32)
        m["cache_swav"] = np.ascontiguousarray(inp["cache_swa_v"][core], dtype=np.float32)
        m["gla_init"] = np.ascontiguousarray(np.asarray(inp["state_gla"][core], dtype=np.float32).reshape(L, 2, 128, 64))
    else:
        m.update(CP)
        j = core - 2 if core < 6 else 0
        m["x"] = np.ascontiguousarray(np.asarray(inp["x_prompt"][8 * j:8 * j + 8], dtype=np.float32).reshape(T, D))
        cond = np.asarray(inp["c_ctx"], dtype=np.float32)
        m["cache_ckv"] = np.zeros((L, 512, 128), np.float32)
        m["cache_krope"] = np.zeros((L, 512, 32), np.float32)
        m["cache_swak"] = np.zeros((L, 2, 512, 64), np.float32)
        m["cache_swav"] = np.zeros((L, 2, 512, 64), np.float32)
        m["gla_init"] = np.zeros((L, 2, 128, 64), np.float32)
    m["condT"] = np.ascontiguousarray(cond.reshape(8, 128).T)
    return m


_CACHE = {}


def kernel(**inputs):
    cores = inputs.pop("_cores", list(range(8)))
    debug = inputs.pop("_debug", None)
    stop_after = inputs.pop("_stop_after", None)
    prog = Prog(debug=debug, stop_after=stop_after)
    nc = prog.build()
    W = host_weights(inputs)
    CS = host_consts("sample")
    CP = host_consts("prompt")
    in_maps = [core_inputs(inputs, c, W, CS, CP) for c in cores]
    import os as _os
    if _os.environ.get("KTRACE"):
        res = run_bass_kernel_spmd(nc, in_maps, core_ids=list(range(len(cores))), trace=True)
        print("EXEC_TIME_NS", res.exec_time_ns)
        globals()["_LAST_RES"] = res
    else:
        res = run_bass_kernel_spmd(nc, in_maps, core_ids=list(range(len(cores))))
    R = res.results
    if debug is not None:
        return R
    y_sample = np.stack([R[0]["y"], R[1]["y"]], axis=0)
    y_prompt = np.concatenate([R[2 + j]["y"].reshape(8, 256, D) for j in range(4)], axis=0)
    ckv = np.concatenate([R[2 + j]["ckv_o"].reshape(L, 8, 256, 128).transpose(1, 0, 2, 3) for j in range(4)], axis=0)
    kr = np.concatenate([R[2 + j]["krope_o"].reshape(L, 8, 256, 32).transpose(1, 0, 2, 3) for j in range(4)], axis=0)
    kvs = [R[2 + j]["swakv_o"].reshape(L, 8, 256, 2, 2, 64) for j in range(4)]
    sk = np.concatenate([a[:, :, :, 0].transpose(1, 0, 3, 2, 4) for a in kvs], axis=0)
    sv = np.concatenate([a[:, :, :, 1].transpose(1, 0, 3, 2, 4) for a in kvs], axis=0)
    gl = np.concatenate([R[2 + j]["gla_o"].reshape(L, 8, 2, 4, 32, 64).transpose(1, 0, 2, 3, 4, 5) for j in range(4)], axis=0)
    f = lambda a: np.ascontiguousarray(a, dtype=np.float32)
    return (f(y_prompt), f(y_sample), f(ckv), f(kr), f(sk), f(sv), f(gl))
```

```python
import numpy as np
import ml_dtypes
from contextlib import ExitStack
import concourse.bass as bass
import concourse.mybir as mybir
from concourse.bass_utils import run_bass_kernel_spmd

F32 = mybir.dt.float32
BF16 = mybir.dt.bfloat16
U32 = mybir.dt.uint32
AF = mybir.ActivationFunctionType
ALU = mybir.AluOpType
AX = mybir.AxisListType
s_ = np.s_

ENGS = ("pe", "act", "dve", "pool", "sp")
SAME_ENGINE_SYNC = True
import os as _os0
SES_ALL = not bool(_os0.environ.get("KNOSES"))

T = 2048
NB = 16
D = 1024
L = 2
ALPHA = (2.0 * L) ** 0.25
EPS = 1e-6
NEG = -30000.0
MLA_SCALE = 96.0 ** -0.5
SWA_SCALE = 64.0 ** -0.5
WIN_EXT = 6496


class V:
    def __init__(self, ap, toks):
        self.ap = ap
        self.toks = toks

    def m(self, fn):
        return V(fn(self.ap), self.toks)


class Buf:
    def __init__(self, name, t, nsub=1):
        self.name = name
        self.t = t
        self.nsub = nsub

    def tok(self, subs=None):
        if subs is None:
            return [(self.name, s) for s in range(self.nsub)]
        if isinstance(subs, int):
            subs = [subs]
        return [(self.name, s) for s in subs]

    def v(self, key=None, subs=None):
        ap = self.t[:] if key is None else self.t[key]
        return V(ap, self.tok(subs))


class Sched:
    def __init__(self, nc, es, nd=24):
        self.nc = nc
        self.ops = {e: [] for e in ENGS}
        self.cnt = {e: 0 for e in ENGS}
        self.known = {e: {} for e in ENGS}
        self.nd = nd
        self.dma_tot = [0] * nd
        self.dma_rr = 0
        self.last_w = {}
        self.readers = {}
        self.sem = {e: es.enter_context(nc.semaphore("sem_" + e)) for e in ENGS if e != "sp"}
        self.dsem = [es.enter_context(nc.semaphore("dsem%d" % i)) for i in range(nd)]
        self.milestones = {e: set() for e in ENGS}

    def _need(self, eng, dep, waits):
        kind, key, val = dep
        if kind == "eng" and key == eng:
            if eng in ("pe", "sp") or (eng in ("act", "dve") and not SES_ALL) or not SAME_ENGINE_SYNC:
                return
        k = (kind, key)
        if self.known[eng].get(k, 0) >= val:
            return
        self.known[eng][k] = val
        waits.append((kind, key, val))
        if kind == "eng":
            self.milestones[key].add(val)

    def _deps(self, eng, reads, writes):
        waits = []
        for t in reads:
            lw = self.last_w.get(t)
            if lw is not None:
                self._need(eng, lw, waits)
        for t in writes:
            lw = self.last_w.get(t)
            if lw is not None:
                self._need(eng, lw, waits)
            for r in self.readers.get(t, ()):
                self._need(eng, r, waits)
        return waits

    def _commit(self, me, reads, writes):
        for t in reads:
            self.readers.setdefault(t, []).append(me)
        for t in writes:
            self.last_w[t] = me
            self.readers[t] = []

    def op(self, eng, fn, reads=(), writes=()):
        reads = list(reads); writes = list(writes)
        waits = self._deps(eng, reads, writes)
        self.cnt[eng] += 1
        me = ("eng", eng, self.cnt[eng])
        self.ops[eng].append((waits, fn, ("eng", self.cnt[eng])))
        self._commit(me, reads, writes)

    def dma(self, eng, fn, reads=(), writes=()):
        reads = list(reads); writes = list(writes)
        i = self.dma_rr
        self.dma_rr = (i + 1) % self.nd
        waits = []
        if self.dma_tot[i] > 0:
            self._need(eng, ("dma", i, self.dma_tot[i]), waits)
        waits += self._deps(eng, reads, writes)
        self.dma_tot[i] += 16
        me = ("dma", i, self.dma_tot[i])
        self.cnt[eng] += 1
        self.ops[eng].append((waits, fn, ("dma", i)))
        self._commit(me, reads, writes)

    def _last_seq(self, e):
        for w, fn, inc in reversed(self.ops[e]):
            if inc is not None and inc[0] == "eng":
                return inc[1]
        return 0

    def barrier(self):
        lasts = {e: self._last_seq(e) for e in ENGS}
        for e in ENGS:
            waits = []
            for e2 in ENGS:
                if e2 != e and e2 != "sp" and lasts[e2] > 0:
                    self._need(e, ("eng", e2, lasts[e2]), waits)
            for i in range(self.nd):
                if self.dma_tot[i] > 0:
                    self._need(e, ("dma", i, self.dma_tot[i]), waits)
            if waits:
                self.ops[e].append((waits, None, None))
        self.last_w = {}
        self.readers = {}

    def finish(self):
        self.barrier()

    def emit(self, blk):
        rank = {}
        for e in ENGS:
            ms = sorted(self.milestones[e])
            rank[e] = {s: i + 1 for i, s in enumerate(ms)}

        def run(e, eng):
            for waits, fn, inc in self.ops[e]:
                for kind, key, val in waits:
                    if kind == "eng":
                        eng.wait_ge(self.sem[key], rank[key][val])
                    else:
                        eng.wait_ge(self.dsem[key], val)
                if fn is None:
                    continue
                ins = fn(eng)
                if inc[0] == "dma":
                    ins.then_inc(self.dsem[inc[1]], 16)
                elif inc[1] in rank[e]:
                    ins.then_inc(self.sem[e], 1)

        blk.sync(lambda eng: run("sp", eng))
        blk.scalar(lambda eng: run("act", eng))
        blk.vector(lambda eng: run("dve", eng))
        blk.gpsimd(lambda eng: run("pool", eng))
        blk.tensor(lambda eng: run("pe", eng))


class Prog:
    def __init__(self, debug=None, stop_after=None):
        self.debug = debug or []
        self.stop_after = stop_after
        self.nc = bass.Bass("TRN2", target_bir_lowering=False)
        self.din = {}
        self.dout = {}

    def inp(self, name, shape, dt=F32):
        self.din[name] = self.nc.dram_tensor(name, list(shape), dt, kind="ExternalInput").ap()
        return self.din[name]

    def outp(self, name, shape, dt=F32):
        self.dout[name] = self.nc.dram_tensor(name, list(shape), dt, kind="ExternalOutput").ap()
        return self.dout[name]

    def sb(self, es, name, shape, dt=F32, nsub=1):
        self.uid = getattr(self, "uid", 0) + 1
        name = "%s_u%d" % (name, self.uid)
        return Buf(name, es.enter_context(self.nc.sbuf_tensor(name, list(shape), dt)), nsub)

    def I(self, eng, meth, out, *args, **kw):
        def conv(a):
            return a.ap if isinstance(a, V) else a
        reads = []
        writes = list(out.toks)
        for a in list(args) + list(kw.values()):
            if isinstance(a, V):
                reads += a.toks
        if "accum_out" in kw:
            writes += kw["accum_out"].toks
        a2 = [conv(a) for a in args]
        k2 = {k: conv(v) for k, v in kw.items()}
        o = out.ap
        self.S.op(eng, lambda e: getattr(e, meth)(o, *a2, **k2), reads, writes)

    def dma(self, q, out, in_):
        reads = in_.toks if isinstance(in_, V) else []
        writes = out.toks if isinstance(out, V) else []
        o = out.ap if isinstance(out, V) else out
        i = in_.ap if isinstance(in_, V) else in_
        self.S.dma(q, lambda e: e.dma_start(out=o, in_=i), reads, writes)

    def mm(self, out, lhsT, rhs, start, stop):
        self.I("pe", "matmul", out, lhsT=lhsT, rhs=rhs, start=start, stop=stop)

    def build(self):
        nc = self.nc
        P = self
        inp = self.inp
        x_d = inp("x", [T, D])
        condT_d = inp("condT", [128, 8])
        w_ada_d = inp("w_ada", [L, D, 6 * D])
        b_adaT_d = inp("b_adaT", [L, 128, 48])
        w_in_d = inp("w_in", [L, D, WIN_EXT])
        w_branch_d = inp("w_branch", [L, 4, 256, D])
        w_out_d = inp("w_out", [L, D, D])
        inp("w_gate", [L, 8, 128, 8, 512])
        ln_d = inp("ln", [L, 4, D])
        dft_c_d = inp("dft_c", [T, T], BF16)
        dft_s_d = inp("dft_s", [T, T], BF16)
        cdft_d = inp("cdft", [2, 128, 128], BF16)
        ident_d = inp("ident", [128, 128])
        inp("w_peer_q", [L, 8, 128, 8, 256])
        inp("w_uq", [L, 256, 512]); inp("w_ukv", [L, 128, 512]); inp("mla_q_norm", [L, 128, 2]); inp("mla_kv_norm", [L, 128])
        inp("gla_mats", [5, 128, 128]); inp("gla_mask", [2, 128, 128], BF16); inp("gla_keep", [128, 2, 32]); inp("gla_init", [L, 2, 128, 64])
        inp("w_gla_a", [L, 2, 16, 128]); inp("b_gla_a", [L, 2, 128]); inp("gla_norm", [L, 64])
        inp("rope_m", [2, 32, T]); inp("rope_s", [2, 64, T]); inp("bias_m", [128, 160]); inp("bias_s", [128, 4])
        inp("mask_s", [128, NB, 2, 128], BF16); inp("swa_sink", [L, 128, 4])
        inp("cache_ckv", [L, 512, 128]); inp("cache_krope", [L, 512, 32]); inp("cache_swak", [L, 2, 512, 64]); inp("cache_swav", [L, 2, 512, 64])
        self.outp("ckv_o", [L, T, 128]); self.outp("krope_o", [L, T, 32]); self.outp("swakv_o", [L, T, 256]); self.outp("gla_o", [L, 8, 2, 128, 64])
        inp("peer_keysT", [L, 8, 2, 128, 128])
        inp("peer_uT", [L, 64, 128, 8, 256])
        inp("peer_v", [L, 64, 128, 2, D])
        y_d = self.outp("y", [T, D])
        self.ubf = Buf("ubf", nc.dram_tensor("peer_u_bf", [L, 64, 128, 8 * 256], BF16, kind="Internal").ap(), nsub=L * 16)
        self.vbf = Buf("vbf", nc.dram_tensor("peer_v_bf", [L, 64, 128, 2 * D], BF16, kind="Internal").ap(), nsub=L * 16)
        dbg = {}
        for name, shape in self.debug:
            dbg[name] = self.outp(name, shape, F32)

        with ExitStack() as es:
            self.S = S = Sched(nc, es)
            sb = lambda *a, **k: P.sb(es, *a, **k)
            x = sb("xres", [128, NB, D], F32, nsub=NB)
            ident = sb("identS", [128, 128], F32)
            ones = sb("onesS", [128, 128], F32)
            mcol = sb("mcol", [128, L, 48], F32, nsub=L)
            condT = sb("condTS", [128, 8], F32)
            ps = [Buf("ps%d" % i, es.enter_context(nc.psum_tensor("ps%d" % i, [128, 512], F32))) for i in range(8)]
            self.x, self.ps, self.ident, self.ones, self.mcol = x, ps, ident, ones, mcol
            iota = sb("iotaS", [128, 128], F32)
            P.I("pool", "iota", iota.v(), pattern=[[1, 128]], base=0, channel_multiplier=0, allow_small_or_imprecise_dtypes=True)
            bm = sb("bmS", [128, 8], F32)
            P.dma("sp", bm.v(), inp("bm", [128, 8])[:, :])
            self.iota, self.bm = iota, bm

            for tb in range(NB):
                P.dma("sp", x.v(s_[:, tb, :], tb), x_d[tb * 128:(tb + 1) * 128, :])
            P.dma("sp", ident.v(), ident_d[:, :])
            P.I("pool", "memset", ones.v(), 1.0)
            P.dma("sp", condT.v(), condT_d[:, :])

            with ExitStack() as es0:
                scond = P.sb(es0, "scond", [128, 8], F32)
                wa = [P.sb(es0, "wa%d" % i, [128, 8, 768], F32) for i in range(1)]
                badaT = P.sb(es0, "badaT", [128, L, 48], F32)
                P.I("act", "activation", scond.v(), condT.v(), AF.Silu)
                P.dma("sp", badaT.v(), b_adaT_d.rearrange("l p j -> p l j"))
                n = 0
                for l in range(L):
                    for cg in range(8):
                        pt = ps[cg % 2]
                        wt = wa[0]
                        P.dma("sp", wt.v(), w_ada_d[l, :, cg * 768:(cg + 1) * 768].rearrange("(k p) c -> p k c", p=128))
                        for j in range(6):
                            for k in range(8):
                                P.mm(pt.v(s_[:, j:j + 1]), wt.v(s_[:, k, j * 128:(j + 1) * 128]), scond.v(s_[:, k:k + 1]),
                                     start=(k == 0), stop=(k == 7))
                        P.I("dve", "tensor_tensor", mcol.v(s_[:, l, cg * 6:(cg + 1) * 6], l), pt.v(s_[:, 0:6]),
                            badaT.v(s_[:, l, cg * 6:(cg + 1) * 6]), op=ALU.add)
                    for a in (8, 32):
                        P.I("dve", "tensor_scalar_add", mcol.v(s_[:, l, a:a + 8], l), mcol.v(s_[:, l, a:a + 8], l), 1.0)
                S.barrier()
            if "mcol" in dbg:
                P.dma("pool", dbg["mcol"], mcol.v())

            for l in range(L):
                self.layer(l, dbg)
                if self.stop_after == ("layer", l):
                    break

            for tb in range(NB):
                P.dma("sp", y_d[tb * 128:(tb + 1) * 128, :], x.v(s_[:, tb, :], tb))
            S.finish()
            blk = es.enter_context(nc.Block())
            S.emit(blk)
        return nc

    def ln_block(self, tmp, src, dst):
        P = self
        st, mv, rstd, nmr = tmp
        P.I("dve", "bn_stats", st.v(s_[:, 0, :]), src.m(lambda a: a[:, 0:512]))
        P.I("dve", "bn_stats", st.v(s_[:, 1, :]), src.m(lambda a: a[:, 512:1024]))
        P.I("dve", "bn_aggr", mv.v(), st.v())
        P.I("act", "activation", rstd.v(), mv.v(s_[:, 1:2]), AF.Sqrt, bias=EPS, scale=1.0)
        P.I("dve", "reciprocal", rstd.v(), rstd.v())
        P.I("dve", "scalar_tensor_tensor", nmr.v(), mv.v(s_[:, 0:1]), -1.0, rstd.v(), op0=ALU.mult, op1=ALU.mult)
        P.I("act", "activation", dst, src, AF.Identity, bias=nmr.v(), scale=rstd.v())

    def ln_tmp(self, es, tag):
        return (self.sb(es, "st" + tag, [128, 2, 6]), self.sb(es, "mv" + tag, [128, 2]),
                self.sb(es, "rstd" + tag, [128, 1]), self.sb(es, "nmr" + tag, [128, 1]))

    def mod_to_uT(self, l, which):
        P = self; S = self.S
        x, uT, ps, mcol, ident = self.x, self.uT, self.ps, self.mcol, self.ident
        sh0 = 0 if which == 0 else 24
        sc0 = 8 if which == 0 else 32
        with ExitStack() as es:
            tmps = [self.ln_tmp(es, "m%d" % i) for i in range(2)]
            xn = [P.sb(es, "xn%d" % i, [128, D]) for i in range(2)]
            for tb in range(NB):
                xnb = xn[tb % 2]
                self.ln_block(tmps[tb % 2], x.v(s_[:, tb, :], tb), xnb.v())
                for half in range(2):
                    pt = ps[(tb * 2 + half) % 4]
                    for j in range(4):
                        k = half * 4 + j
                        P.I("pe", "transpose", pt.v(s_[:, j * 128:(j + 1) * 128]), xnb.v(s_[:, k * 128:(k + 1) * 128]), ident.v())
                    for j in range(4):
                        k = half * 4 + j
                        eng = "dve" if j % 2 == 0 else "pool"
                        eng = "dve"
                        P.I(eng, "tensor_scalar", uT.v(s_[:, k, tb * 128:(tb + 1) * 128], tb), pt.v(s_[:, j * 128:(j + 1) * 128]),
                            mcol.v(s_[:, l, sc0 + k:sc0 + k + 1], l), mcol.v(s_[:, l, sh0 + k:sh0 + k + 1], l), op0=ALU.mult, op1=ALU.add)
            S.barrier()

    def load_w(self, wb, l, c0, n):
        w_in_d = self.din["w_in"]
        self.dma("pool", wb.v(s_[:, :, 0:n]), w_in_d[l, :, c0:c0 + n].rearrange("(k p) c -> p k c", p=128))

    def proj_fm(self, wb, wc0, m, dst_fn, pbase=0):
        P = self
        for tg in range(4):
            pt = self.ps[self.psi % 8]; self.psi += 1
            for k in range(8):
                P.mm(pt.v(s_[0:m, :]), wb.v(s_[:, k, wc0:wc0 + m]), self.uT.v(s_[:, k, tg * 512:(tg + 1) * 512], range(tg * 4, tg * 4 + 4)),
                     start=(k == 0), stop=(k == 7))
            dst_fn(tg, pt.v(s_[0:m, :]))

    def proj_tm(self, wb, wc0, n, dst_fn):
        P = self
        for tb in range(NB):
            pt = self.ps[self.psi % 8]; self.psi += 1
            for k in range(8):
                P.mm(pt.v(s_[:, 0:n]), self.uT.v(s_[:, k, tb * 128:(tb + 1) * 128], tb), wb.v(s_[:, k, wc0:wc0 + n]),
                     start=(k == 0), stop=(k == 7))
            dst_fn(tb, pt.v(s_[:, 0:n]))


    def rope_evac(self, dst, pa, pb, cosv, sinv, tmpa, tmpb):
        P = self
        P.I("dve", "tensor_tensor", tmpa, pa, cosv, op=ALU.mult)
        P.I("dve", "tensor_tensor", tmpb, pb, sinv, op=ALU.mult)
        P.I("pool", "tensor_tensor", dst, tmpa, tmpb, op=ALU.add)

    def fm_group(self, wb, specs, tg):
        P = self
        outs = []
        for (wc0, m) in specs:
            pt = self.ps[self.psi % 6]; self.psi += 1
            for k in range(8):
                P.mm(pt.v(s_[0:m, :]), wb.v(s_[:, k, wc0:wc0 + m]), self.uT.v(s_[:, k, tg * 512:(tg + 1) * 512], range(tg * 4, tg * 4 + 4)),
                     start=(k == 0), stop=(k == 7))
            outs.append(pt.v(s_[0:m, :]))
        return outs

    def mla(self, l, dbg):
        P = self; S = self.S
        d = self.din
        ps, ident, ones = self.ps, self.ident, self.ones
        NK = 20
        with ExitStack() as es:
            wb = P.sb(es, "wbM", [128, 8, 448], BF16)
            wuq = P.sb(es, "wuq", [128, 2, 512], BF16)
            wukv = P.sb(es, "wukv", [128, 2, 256], BF16)
            gq = P.sb(es, "gq", [128, 2], F32)
            gkv = P.sb(es, "gkv", [128, 128], F32)
            cosT = P.sb(es, "cosM", [32, 512], F32)
            sinT = P.sb(es, "sinM", [32, 512], F32)
            biasM = P.sb(es, "biasM", [128, 8 * NK], F32)
            qnT = P.sb(es, "qnT", [128, 2, T], BF16, nsub=4)
            rs = P.sb(es, "qrs", [128, 512], F32)
            qno = P.sb(es, "qno", [64, T], BF16, nsub=4)
            qro = P.sb(es, "qro", [32, T], BF16, nsub=4)
            kno = P.sb(es, "kno", [64, 2560], BF16)
            kro = P.sb(es, "kro", [32, 2560], BF16)
            ckvT = P.sb(es, "ckvT", [128, 2560], BF16, nsub=NK)
            Vt = P.sb(es, "VtM", [128, NK, 4, 65], BF16, nsub=NK)
            ta = P.sb(es, "rta", [128, 512], F32)
            tb_ = P.sb(es, "rtb", [128, 512], F32)
            kvt = P.sb(es, "kvt", [128, 160], F32)
            ckt = [P.sb(es, "ckt%d" % i, [128, 128], F32) for i in range(2)]
            ss = P.sb(es, "kss", [128, 1], F32)
            junk = P.sb(es, "kjunk", [128, 128], F32)
            PT = [P.sb(es, "PT%d" % i, [128, 256], BF16) for i in range(2)]
            oacc = P.sb(es, "oacc", [128, NB, 128], BF16, nsub=NB)
            identb = P.sb(es, "identb", [128, 128], BF16)
            P.I("dve", "tensor_copy", identb.v(), ident.v())
            rden = P.sb(es, "rden", [128, 1], F32)
            cch = P.sb(es, "cch", [128, 4, 160], F32)
            self.load_w(wb, l, 0, 416)
            P.dma("pool", wb.v(s_[:, :, 416:448]), d["w_in"][l, :, 6080:6112].rearrange("(k p) c -> p k c", p=128))
            P.dma("pool", wuq.v(), d["w_uq"][l].rearrange("(k p) c -> p k c", p=128))
            for two in range(2):
                P.dma("pool", wukv.v(s_[:, two, :]).m(lambda a: a.rearrange("p (h c) -> p h c", h=4)), d["w_ukv"][l].rearrange("p (h two c) -> p two h c", h=4, two=2)[:, two, :, :])
            P.dma("sp", gq.v(), d["mla_q_norm"][l])
            P.dma("sp", gkv.v(), d["mla_kv_norm"][l].partition_broadcast(128))
            P.dma("sp", biasM.v(), d["bias_m"][:, :])
            P.I("pool", "memset", Vt.v(), 1.0)
            for tg in range(4):
                tsl = slice(tg * 512, (tg + 1) * 512)
                P.dma("sp", cosT.v(), d["rope_m"][0, :, tsl])
                P.dma("sp", sinT.v(), d["rope_m"][1, :, tsl])
                pq = self.fm_group(wb, [(0, 128), (128, 128)], tg)
                P.I("act", "activation", ta.v(), pq[0], AF.Square)
                P.I("act", "activation", tb_.v(), pq[1], AF.Square)
                pt = ps[self.psi % 6]; self.psi += 1
                P.mm(pt.v(), ones.v(), ta.v(), start=True, stop=False)
                P.mm(pt.v(), ones.v(), tb_.v(), start=False, stop=True)
                P.I("act", "activation", rs.v(), pt.v(), AF.Sqrt, bias=EPS, scale=1.0 / 256)
                P.I("dve", "reciprocal", rs.v(), rs.v())
                for c in range(2):
                    P.I("dve", "scalar_tensor_tensor", qnT.v(s_[:, c, tsl], tg), pq[c], gq.v(s_[:, c:c + 1]), rs.v(), op0=ALU.mult, op1=ALU.mult)
                pk = self.fm_group(wb, [(384, 32), (416, 32)], tg)
                self.rope_evac(kro.v(s_[:, 512 + tg * 512:512 + (tg + 1) * 512]), pk[0], pk[1], cosT.v(), sinT.v(),
                               ta.v(s_[0:32, :]), tb_.v(s_[0:32, :]))
            for tb in range(NB):
                pt = ps[self.psi % 6]; self.psi += 1
                for k in range(8):
                    P.mm(pt.v(s_[:, 0:160]), self.uT.v(s_[:, k, tb * 128:(tb + 1) * 128], tb), wb.v(s_[:, k, 256:416]), start=(k == 0), stop=(k == 7))
                P.I("act", "activation", kvt.v(), pt.v(s_[:, 0:160]), AF.Copy)
                ck = ckt[tb % 2]
                P.I("act", "activation", junk.v(), kvt.v(s_[:, 0:128]), AF.Square, accum_out=ss.v())
                P.I("act", "activation", ss.v(), ss.v(), AF.Sqrt, bias=EPS, scale=1.0 / 128)
                P.I("dve", "reciprocal", ss.v(), ss.v())
                P.I("dve", "scalar_tensor_tensor", ck.v(), kvt.v(s_[:, 0:128]), ss.v(), gkv.v(), op0=ALU.mult, op1=ALU.mult)
                P.dma("sp", self.dout["ckv_o"][l, tb * 128:(tb + 1) * 128, :], ck.v())
                P.dma("sp", self.dout["krope_o"][l, tb * 128:(tb + 1) * 128, :], kvt.v(s_[:, 128:160]))
                p2 = ps[self.psi % 6]; self.psi += 1
                P.I("pe", "transpose", p2.v(s_[:, 0:128]), ck.v(), ident.v())
                P.I("act", "activation", ckvT.v(s_[:, 512 + tb * 128:512 + (tb + 1) * 128], 4 + tb), p2.v(s_[:, 0:128]), AF.Copy)
            P.dma("sp", cch.v(s_[:, :, 0:128]), d["cache_ckv"][l].rearrange("(j p) c -> p j c", p=128))
            P.dma("sp", cch.v(s_[:, :, 128:160]), d["cache_krope"][l].rearrange("(j p) c -> p j c", p=128))
            for j in range(4):
                p2 = ps[self.psi % 6]; self.psi += 1
                P.I("pe", "transpose", p2.v(s_[:, 0:128]), cch.v(s_[:, j, 0:128]), ident.v())
                P.I("act", "activation", ckvT.v(s_[:, j * 128:(j + 1) * 128], j), p2.v(s_[:, 0:128]), AF.Copy)
                p3 = ps[self.psi % 6]; self.psi += 1
                P.I("pe", "transpose", p3.v(s_[0:32, 0:128]), cch.v(s_[:, j, 128:160]), ident.v())
                P.I("act", "activation", kro.v(s_[:, j * 128:(j + 1) * 128]), p3.v(s_[0:32, 0:128]), AF.Copy)
            for kc in range(NK):
                pt = ps[self.psi % 6]; self.psi += 1
                P.mm(pt.v(s_[:, 0:256]), ckvT.v(s_[:, kc * 128:(kc + 1) * 128], kc), wukv.v(s_[:, 1, :]), start=True, stop=True)
                P.I("dve", "tensor_copy", Vt.v(s_[:, kc, :, 0:64], kc), pt.v(s_[:, 0:256]).m(lambda a: a.rearrange("p (h c) -> p h c", h=4)))
            n = 0
            for h in range(4):
                for tg in range(4):
                    tsl = slice(tg * 512, (tg + 1) * 512)
                    P.dma("sp", cosT.v(), d["rope_m"][0, :, tsl])
                    P.dma("sp", sinT.v(), d["rope_m"][1, :, tsl])
                    pn = ps[self.psi % 6]; pa = ps[(self.psi + 1) % 6]; pb = ps[(self.psi + 2) % 6]; self.psi += 3
                    for c in range(2):
                        P.mm(pn.v(s_[0:64, :]), wuq.v(s_[:, c, h * 96:h * 96 + 64]), qnT.v(s_[:, c, tsl], tg), start=(c == 0), stop=(c == 1))
                    for c in range(2):
                        P.mm(pa.v(s_[0:32, :]), wuq.v(s_[:, c, h * 96 + 64:h * 96 + 96]), qnT.v(s_[:, c, tsl], tg), start=(c == 0), stop=(c == 1))
                    for c in range(2):
                        P.mm(pb.v(s_[0:32, :]), wuq.v(s_[:, c, 384 + h * 32:384 + h * 32 + 32]), qnT.v(s_[:, c, tsl], tg), start=(c == 0), stop=(c == 1))
                    P.I("act", "activation", qno.v(s_[:, tsl], tg), pn.v(s_[0:64, :]), AF.Copy)
                    self.rope_evac(qro.v(s_[:, tsl], tg), pa.v(s_[0:32, :]), pb.v(s_[0:32, :]), cosT.v(), sinT.v(),
                                   ta.v(s_[0:32, :]), tb_.v(s_[0:32, :]))
                for g5 in range(5):
                    gsl = slice(g5 * 512, (g5 + 1) * 512)
                    pt = ps[self.psi % 6]; self.psi += 1
                    P.mm(pt.v(s_[0:64, :]), wukv.v(s_[:, 0, h * 64:(h + 1) * 64]), ckvT.v(s_[:, gsl], range(g5 * 4, g5 * 4 + 4)), start=True, stop=True)
                    P.I("act", "activation", kno.v(s_[:, gsl]), pt.v(s_[0:64, :]), AF.Copy)
                for qu in range(8):
                    qsl = slice(qu * 256, (qu + 1) * 256)
                    po = [ps[6], ps[7]]
                    for kc in range(NK):
                        ksl = slice(kc * 128, (kc + 1) * 128)
                        pt = ps[self.psi % 6]; self.psi += 1
                        P.mm(pt.v(s_[:, 0:256]), kno.v(s_[:, ksl]), qno.v(s_[:, qsl], qu // 2), start=True, stop=False)
                        P.mm(pt.v(s_[:, 0:256]), kro.v(s_[:, ksl]), qro.v(s_[:, qsl], qu // 2), start=False, stop=True)
                        pT = PT[n % 2]; n += 1
                        P.I("act", "activation", pT.v(), pt.v(s_[:, 0:256]), AF.Exp, bias=biasM.v(s_[:, qu * NK + kc:qu * NK + kc + 1]), scale=MLA_SCALE)
                        for qb in range(2):
                            P.mm(po[qb].v(s_[:, 0:65]), pT.v(s_[:, qb * 128:(qb + 1) * 128]), Vt.v(s_[:, kc, h, :], kc), start=(kc == 0), stop=(kc == NK - 1))
                    for qb in range(2):
                        tb = qu * 2 + qb
                        P.I("dve", "reciprocal", rden.v(), po[qb].v(s_[:, 64:65]))
                        P.I("dve", "tensor_scalar_mul", oacc.v(s_[:, tb, (h % 2) * 64:(h % 2) * 64 + 64], tb), po[qb].v(s_[:, 0:64]), rden.v())
                if h % 2 == 1:
                    for tb in range(NB):
                        p2 = ps[self.psi % 6]; self.psi += 1
                        P.mm(p2.v(s_[:, 0:128]), oacc.v(s_[:, tb, :], tb), identb.v(), start=True, stop=True)
                        P.I("act", "activation", self.brT[0].v(s_[:, h // 2, tb * 128:(tb + 1) * 128], tb), p2.v(s_[:, 0:128]), AF.Copy)
            S.barrier()

    def gla_block(self, l, tb, wb, afT, qT, kT, w2, b2, onesb, ktm, e1t, spt, eq, ek, el, mats, vtm, kl, gcol, qeT, keT, LNQ):
        P = self; ps = self.ps
        lsl = slice((tb % 4) * 128, (tb % 4) * 128 + 128)
        bsl = slice(tb * 128, (tb + 1) * 128)
        pt = ps[self.psi % 6]; self.psi += 1
        for k in range(8):
            P.mm(pt.v(s_[:, 0:384]), self.uT.v(s_[:, k, bsl], tb), wb.v(s_[:, k, 128:512]), start=(k == 0), stop=(k == 7))
        P.I("dve", "tensor_copy", ktm.v(), pt.v(s_[:, 0:128]))
        P.I("dve", "tensor_copy", vtm.v(s_[:, tb, :], tb), pt.v(s_[:, 128:384]))
        for dr in range(2):
            pz = ps[self.psi % 6]; self.psi += 1
            P.mm(pz.v(s_[:, 0:128]), afT.v(s_[:, dr, lsl]), w2.v(s_[:, dr, :]), start=True, stop=False)
            P.mm(pz.v(s_[:, 0:128]), onesb.v(), b2.v(s_[:, dr, :]), start=False, stop=True)
            P.I("act", "activation", e1t.v(), pz.v(s_[:, 0:128]), AF.Exp, scale=-1.0)
            P.I("act", "activation", spt.v(), e1t.v(), AF.Ln, bias=1.0, scale=1.0)
            mi = 0 if dr == 0 else 2
            if self.G1 <= 3:
                continue
            pl = ps[self.psi % 6]; self.psi += 1
            P.mm(pl.v(s_[:, 0:128]), mats.v(s_[:, mi + 1, :]), spt.v(), start=True, stop=True)
            P.I("act", "activation", el.v(), pl.v(s_[:, 0:128]), AF.Exp)
            P.I("dve", "tensor_tensor", kl[dr].v(s_[:, tb, :], tb), ktm.v(), el.v(), op=ALU.mult)
            for hp in range(2):
                pc = ps[self.psi % 6]; pg = ps[(self.psi + 1) % 6]; self.psi += 2
                P.mm(pc.v(s_[0:64, 0:128]), spt.v(s_[:, hp * 64:(hp + 1) * 64]), mats.v(s_[:, mi, :]), start=True, stop=True)
                P.mm(pg.v(s_[0:64, 0:2]), spt.v(s_[:, hp * 64:(hp + 1) * 64]), mats.v(s_[:, 4, 0:2]), start=True, stop=True)
                P.I("act", "activation", eq.v(s_[:, hp, :]), pc.v(s_[0:64, 0:128]), AF.Exp, bias=LNQ, scale=1.0)
                P.I("act", "activation", ek.v(s_[:, hp, :]), pc.v(s_[0:64, 0:128]), AF.Exp, scale=-1.0)
                P.I("act", "activation", gcol.v(s_[:, dr, hp, 2 * tb:2 * tb + 2], dr), pg.v(s_[0:64, 0:2]), AF.Exp)
            P.I("dve", "tensor_tensor", qeT[dr].v(s_[:, :, bsl], tb), qT.v(s_[:, :, lsl]), eq.v(), op=ALU.mult)
            P.I("dve", "tensor_tensor", keT[dr].v(s_[:, :, bsl], tb), kT.v(s_[:, :, lsl]), ek.v(), op=ALU.mult)

    def gla(self, l, dbg):
        P = self; S = self.S
        d = self.din
        ps, ident, ones = self.ps, self.ident, self.ones
        LNQ = float(np.log(32.0 ** -0.5))
        with ExitStack() as es:
            qeT = [P.sb(es, "qeT%d" % i, [64, 2, T], BF16, nsub=NB) for i in range(2)]
            keT = [P.sb(es, "keT%d" % i, [64, 2, T], BF16, nsub=NB) for i in range(2)]
            kl = [P.sb(es, "kl%d" % i, [128, NB, 128], BF16, nsub=NB) for i in range(2)]
            gcol = P.sb(es, "gcol", [64, 2, 2, 32], F32, nsub=2)
            vtm = P.sb(es, "vtm", [128, NB, 256], BF16, nsub=NB)
            mats = P.sb(es, "gmats", [128, 5, 128], F32)
            gmask = P.sb(es, "gmask", [128, 2, 128], BF16)
            keep = P.sb(es, "gkeep", [128, 2, 32], F32)
            identb = P.sb(es, "identbG", [128, 128], BF16)
            P.dma("sp", mats.v(), d["gla_mats"].rearrange("a p c -> p a c"))
            P.dma("sp", gmask.v(), d["gla_mask"].rearrange("a p c -> p a c"))
            P.dma("sp", keep.v(), d["gla_keep"][:, :, :])
            P.I("dve", "tensor_copy", identb.v(), ident.v())
            with ExitStack() as e1:
                wb = P.sb(e1, "wbG", [128, 8, 800], BF16)
                afT = P.sb(e1, "afT", [16, 2, 512], BF16)
                qT = P.sb(e1, "gqT", [64, 2, 512], BF16)
                kT = P.sb(e1, "gkT", [64, 2, 512], BF16)
                w2 = P.sb(e1, "gw2", [16, 2, 128], BF16)
                b2 = P.sb(e1, "gb2", [1, 2, 128], BF16)
                onesb = P.sb(e1, "onesb", [1, 128], BF16)
                ktm = P.sb(e1, "ktm", [128, 128], F32)
                e1t = P.sb(e1, "ge1", [128, 128], F32)
                spt = P.sb(e1, "gsp", [128, 128], F32)
                eq = P.sb(e1, "geq", [64, 2, 128], F32)
                ek = P.sb(e1, "gek", [64, 2, 128], F32)
                el = P.sb(e1, "gel", [128, 128], F32)
                self.load_w(wb, l, 672, 800)
                P.dma("pool", w2.v(), d["w_gla_a"][l].rearrange("a r c -> r a c"))
                P.dma("pool", b2.v(), d["b_gla_a"][l:l + 1, :, :])
                P.I("pool", "memset", onesb.v(), 1.0)
                import os
                G1 = int(os.environ.get("KGLA_G1", "99"))
                for tg in range(4 if G1 >= 2 else 0):
                    tsl = slice(tg * 512, (tg + 1) * 512)
                    pq = self.fm_group(wb, [(0, 64), (64, 64), (128, 64), (192, 64)], tg)
                    P.I("act", "activation", qT.v(s_[:, 0, :]), pq[0], AF.Copy)
                    P.I("dve", "tensor_copy", qT.v(s_[:, 1, :]), pq[1])
                    P.I("act", "activation", kT.v(s_[:, 0, :]), pq[2], AF.Copy)
                    P.I("dve", "tensor_copy", kT.v(s_[:, 1, :]), pq[3])
                    pq = self.fm_group(wb, [(768, 16), (784, 16)], tg)
                    P.I("act", "activation", afT.v(s_[:, 0, :]), pq[0], AF.Copy)
                    P.I("dve", "tensor_copy", afT.v(s_[:, 1, :]), pq[1])
                    for tb in range(tg * 4, tg * 4 + 4) if G1 >= 3 else []:
                        self.G1 = G1
                        self.gla_block(l, tb, wb, afT, qT, kT, w2, b2, onesb, ktm, e1t, spt, eq, ek, el, mats, vtm, kl, gcol, qeT, keT, LNQ)
                S.barrier()
            import os
            GSTOP = int(os.environ.get("KGLA_STOP", "99"))
            if GSTOP <= 1:
                return
            with ExitStack() as e2:
                of = P.sb(e2, "gof", [128, NB, 256], F32, nsub=NB)
                St = P.sb(e2, "gS", [64, 2, 64], F32)
                tmpS = P.sb(e2, "gtmp", [64, 2, 64], F32)
                Sb = [P.sb(e2, "gSb%d" % i, [64, 2, 64], BF16) for i in range(2)]
                att = [P.sb(e2, "gatt%d" % i, [128, 128], BF16) for i in range(2)]
                na = 0
                for dr in range(2):
                    P.dma("sp", St.v(), d["gla_init"][l, dr].rearrange("(a p) c -> p a c", p=64))
                    blks = range(NB) if dr == 0 else range(NB - 1, -1, -1)
                    for tb in blks:
                        bsl = slice(tb * 128, (tb + 1) * 128)
                        halves = (0, 1) if dr == 0 else (1, 0)
                        for hf in halves:
                            n = 2 * tb + hf
                            r0 = hf * 64
                            P.I("act", "activation", Sb[hf].v(), St.v(), AF.Copy)
                            pu = ps[self.psi % 6]; self.psi += 1
                            for h in range(4):
                                P.mm(pu.v(s_[(h % 2) * 32:(h % 2) * 32 + 32, (h // 2) * 64:(h // 2) * 64 + 64]), kl[dr].v(s_[r0:r0 + 64, tb, h * 32:(h + 1) * 32], tb),
                                     vtm.v(s_[r0:r0 + 64, tb, h * 64:(h + 1) * 64], tb), start=True, stop=True)
                            for hp in range(2):
                                P.I("dve", "scalar_tensor_tensor", tmpS.v(s_[:, hp, :]), St.v(s_[:, hp, :]), gcol.v(s_[:, dr, hp, n:n + 1], dr), pu.v(s_[0:64, hp * 64:(hp + 1) * 64]), op0=ALU.mult, op1=ALU.add)
                            if (dr == 0 and n % 4 == 3) or (dr == 1 and n % 4 == 0):
                                P.dma("sp", self.dout["gla_o"][l, n // 4, dr].rearrange("(a p) c -> p a c", p=64), tmpS.v())
                            nn = n + 1 if dr == 0 else n - 1
                            if 0 <= nn < 32:
                                P.I("dve", "tensor_scalar_mul", St.v(), tmpS.v(), keep.v(s_[0:64, dr, nn:nn + 1]))
                        po = ps[6 + (tb % 2)]
                        for h in range(4):
                            hs = slice((h % 2) * 32, (h % 2) * 32 + 32); hp = h // 2
                            pa = ps[self.psi % 6]; self.psi += 1
                            P.mm(pa.v(s_[:, 0:128]), keT[dr].v(s_[hs, hp, bsl], tb), qeT[dr].v(s_[hs, hp, bsl], tb), start=True, stop=True)
                            at = att[na % 2]; na += 1
                            P.I("dve", "tensor_tensor", at.v(), pa.v(s_[:, 0:128]), gmask.v(s_[:, dr, :]), op=ALU.mult)
                            oc = slice(h * 64, (h + 1) * 64)
                            P.mm(po.v(s_[:, oc]), at.v(), vtm.v(s_[:, tb, oc], tb), start=True, stop=False)
                            P.mm(po.v(s_[0:64, oc]), qeT[dr].v(s_[hs, hp, tb * 128:tb * 128 + 64], tb), Sb[0].v(s_[hs, hp, :]), start=False, stop=False)
                            P.mm(po.v(s_[64:128, oc]), qeT[dr].v(s_[hs, hp, tb * 128 + 64:tb * 128 + 128], tb), Sb[1].v(s_[hs, hp, :]), start=False, stop=True)
                        if dr == 0:
                            P.I("act", "activation", of.v(s_[:, tb, :], tb), po.v(s_[:, 0:256]), AF.Copy)
                        else:
                            P.I("dve", "tensor_tensor", of.v(s_[:, tb, :], tb), of.v(s_[:, tb, :], tb), po.v(s_[:, 0:256]), op=ALU.add)
                if GSTOP <= 2:
                    S.barrier(); return
                wg = P.sb(e2, "wgG", [128, 8, 256], BF16)
                gn = P.sb(e2, "ggn", [128, 64], F32)
                ss4 = P.sb(e2, "gss", [128, 4], F32)
                junk = P.sb(e2, "gjunk", [128, 64], F32)
                sg = P.sb(e2, "gsil", [128, 256], F32)
                ob = P.sb(e2, "gob", [128, 256], BF16)
                self.load_w(wg, l, 1184, 256)
                P.dma("sp", gn.v(), d["gla_norm"][l].partition_broadcast(128))
                for tb in range(NB):
                    bsl = slice(tb * 128, (tb + 1) * 128)
                    for h in range(4):
                        P.I("act", "activation", junk.v(), of.v(s_[:, tb, h * 64:(h + 1) * 64], tb), AF.Square, accum_out=ss4.v(s_[:, h:h + 1]))
                    P.I("act", "activation", ss4.v(), ss4.v(), AF.Sqrt, bias=EPS, scale=1.0 / 64)
                    P.I("dve", "reciprocal", ss4.v(), ss4.v())
                    ov = of.v(s_[:, tb, :], tb).m(lambda a: a.rearrange("p (h c) -> p h c", h=4))
                    P.I("dve", "tensor_tensor", ov, ov, ss4.v().m(lambda a: a.unsqueeze(2).to_broadcast([128, 4, 64])), op=ALU.mult)
                    P.I("dve", "tensor_tensor", ov, ov, gn.v().m(lambda a: a.unsqueeze(1).to_broadcast([128, 4, 64])), op=ALU.mult)
                    pg = ps[self.psi % 6]; self.psi += 1
                    for k in range(8):
                        P.mm(pg.v(s_[:, 0:256]), self.uT.v(s_[:, k, bsl], tb), wg.v(s_[:, k, :]), start=(k == 0), stop=(k == 7))
                    P.I("act", "activation", sg.v(), pg.v(s_[:, 0:256]), AF.Silu)
                    P.I("dve", "tensor_tensor", ob.v(), of.v(s_[:, tb, :], tb), sg.v(), op=ALU.mult)
                    for c in range(2):
                        p2 = ps[self.psi % 6]; self.psi += 1
                        P.mm(p2.v(s_[:, 0:128]), ob.v(s_[:, c * 128:(c + 1) * 128]), identb.v(), start=True, stop=True)
                        P.I("act", "activation", self.brT[2].v(s_[:, c, bsl], tb), p2.v(s_[:, 0:128]), AF.Copy)
                S.barrier()

    def swa_kv_only(self, l):
        P = self; S = self.S
        ps = self.ps
        with ExitStack() as es:
            wb = P.sb(es, "wbS2", [128, 8, 256], BF16)
            kvo = [P.sb(es, "kvo2_%d" % i, [128, 256], F32) for i in range(2)]
            self.load_w(wb, l, 1728, 256)
            for tb in range(NB):
                pt = ps[self.psi % 6]; self.psi += 1
                for k in range(8):
                    P.mm(pt.v(s_[:, 0:256]), self.uT.v(s_[:, k, tb * 128:(tb + 1) * 128], tb), wb.v(s_[:, k, 0:256]), start=(k == 0), stop=(k == 7))
                ko = kvo[tb % 2]
                P.I("act", "activation", ko.v(), pt.v(s_[:, 0:256]), AF.Copy)
                P.dma("sp", self.dout["swakv_o"][l, tb * 128:(tb + 1) * 128, :], ko.v())
            S.barrier()

    def swa(self, l, dbg):
        P = self; S = self.S
        d = self.din
        ps, ident, ones = self.ps, self.ident, self.ones
        NK = 20
        with ExitStack() as es:
            wb = P.sb(es, "wbS", [128, 8, 896], BF16)
            cosT = P.sb(es, "cosS", [64, 512], F32)
            sinT = P.sb(es, "sinS", [64, 512], F32)
            biasS = P.sb(es, "biasS", [128, 4], F32)
            esink = P.sb(es, "esink", [128, 4], F32)
            msk = P.sb(es, "mskS", [128, NB, 2, 128], BF16)
            qT = P.sb(es, "qTS", [64, 4, T], BF16, nsub=4)
            kT = P.sb(es, "kTS", [64, 2, 2560], BF16)
            Vt = P.sb(es, "VtS", [128, NK, 2, 65], BF16, nsub=NK)
            ta = P.sb(es, "sta", [64, 512], F32)
            tb_ = P.sb(es, "stb", [64, 512], F32)
            kvo = [P.sb(es, "kvo%d" % i, [128, 256], F32) for i in range(2)]
            cch = P.sb(es, "cchS", [128, 4, 2, 64], F32)
            PT = [P.sb(es, "PTS%d" % i, [128, 256], BF16) for i in range(2)]
            oa = P.sb(es, "oaS", [128, 256], F32)
            rden = P.sb(es, "rdenS", [128, 1], F32)
            self.load_w(wb, l, 1472, 512)
            P.dma("pool", wb.v(s_[:, :, 512:896]), d["w_in"][l, :, 6112:6496].rearrange("(k p) c -> p k c", p=128))
            P.dma("sp", biasS.v(), d["bias_s"][:, :])
            P.dma("sp", esink.v(), d["swa_sink"][l])
            P.I("act", "activation", esink.v(), esink.v(), AF.Exp)
            P.dma("sp", msk.v(), d["mask_s"][:, :, :, :])
            P.I("pool", "memset", Vt.v(), 1.0)
            import os
            STOP = int(os.environ.get("KSWA_STOP", "99"))
            if STOP <= 1:
                S.barrier(); return
            for tg in range(4):
                tsl = slice(tg * 512, (tg + 1) * 512)
                P.dma("sp", cosT.v(), d["rope_s"][0, :, tsl])
                P.dma("sp", sinT.v(), d["rope_s"][1, :, tsl])
                for h in range(4):
                    pq = self.fm_group(wb, [(h * 64, 64), (512 + h * 64, 64)], tg)
                    self.rope_evac(qT.v(s_[:, h, tsl], h), pq[0], pq[1], cosT.v(), sinT.v(), ta.v(), tb_.v())
                for j in range(2):
                    pk = self.fm_group(wb, [(256 + j * 64, 64), (768 + j * 64, 64)], tg)
                    self.rope_evac(kT.v(s_[:, j, 512 + tg * 512:512 + (tg + 1) * 512]), pk[0], pk[1], cosT.v(), sinT.v(), ta.v(), tb_.v())
            if STOP <= 2:
                S.barrier(); return
            for tb in range(NB):
                pt = ps[self.psi % 6]; self.psi += 1
                for k in range(8):
                    P.mm(pt.v(s_[:, 0:256]), self.uT.v(s_[:, k, tb * 128:(tb + 1) * 128], tb), wb.v(s_[:, k, 256:512]), start=(k == 0), stop=(k == 7))
                ko = kvo[tb % 2]
                P.I("act", "activation", ko.v(), pt.v(s_[:, 0:256]), AF.Copy)
                P.dma("sp", self.dout["swakv_o"][l, tb * 128:(tb + 1) * 128, :], ko.v())
                P.I("dve", "tensor_copy", Vt.v(s_[:, 4 + tb, :, 0:64], 4 + tb), ko.v(s_[:, 128:256]).m(lambda a: a.rearrange("p (j c) -> p j c", j=2)))
            if STOP <= 3:
                S.barrier(); return
            for j in range(2):
                P.dma("sp", cch.v(s_[:, :, j, :]), d["cache_swak"][l, j].rearrange("(c p) e -> p c e", p=128))
            for c in range(4):
                for j in range(2):
                    p2 = ps[self.psi % 6]; self.psi += 1
                    P.I("pe", "transpose", p2.v(s_[0:64, 0:128]), cch.v(s_[:, c, j, :]), ident.v())
                    P.I("act", "activation", kT.v(s_[:, j, c * 128:(c + 1) * 128]), p2.v(s_[0:64, 0:128]), AF.Copy)
            cchv = P.sb(es, "cchV", [128, 4, 2, 64], F32)
            for j in range(2):
                P.dma("sp", cchv.v(s_[:, :, j, :]), d["cache_swav"][l, j].rearrange("(c p) e -> p c e", p=128))
            for c in range(4):
                P.I("dve", "tensor_copy", Vt.v(s_[:, c, :, 0:64], c), cchv.v(s_[:, c, :, :]))
            n = 0
            import os
            for tb in range(NB if "noatt" not in os.environ.get("KSKIP", "") else 0):
                qsl = slice(tb * 128, (tb + 1) * 128)
                for j in range(2):
                    po = [ps[6], ps[7]]
                    kcs = [(4 + tb + dd, dd) for dd in (-1, 0, 1) if 0 <= tb + dd < NB] + [(c, 2) for c in range(4)]
                    for i, (kc, kind) in enumerate(kcs):
                        ksl = slice(kc * 128, (kc + 1) * 128)
                        pt = ps[self.psi % 6]; self.psi += 1
                        for g in range(2):
                            P.mm(pt.v(s_[:, g * 128:(g + 1) * 128]), kT.v(s_[:, j, ksl]), qT.v(s_[:, 2 * j + g, qsl], 2 * j + g), start=True, stop=True)
                        pT = PT[n % 2]; n += 1
                        if kind == 2:
                            P.I("act", "activation", pT.v(), pt.v(s_[:, 0:256]), AF.Exp, bias=biasS.v(s_[:, 0:1]), scale=SWA_SCALE)
                        else:
                            P.I("act", "activation", pT.v(), pt.v(s_[:, 0:256]), AF.Exp, scale=SWA_SCALE)
                            if kind != 0:
                                mi = 0 if kind == -1 else 1
                                for g in range(2):
                                    P.I("dve", "tensor_tensor", pT.v(s_[:, g * 128:(g + 1) * 128]), pT.v(s_[:, g * 128:(g + 1) * 128]), msk.v(s_[:, tb, mi, :]), op=ALU.mult)
                        for g in range(2):
                            P.mm(po[g].v(s_[:, 0:65]), pT.v(s_[:, g * 128:(g + 1) * 128]), Vt.v(s_[:, kc, j, :], kc), start=(i == 0), stop=(i == len(kcs) - 1))
                    for g in range(2):
                        hh = 2 * j + g
                        P.I("dve", "tensor_tensor", rden.v(), po[g].v(s_[:, 64:65]), esink.v(s_[:, hh:hh + 1]), op=ALU.add)
                        P.I("dve", "reciprocal", rden.v(), rden.v())
                        P.I("dve", "tensor_scalar_mul", oa.v(s_[:, hh * 64:(hh + 1) * 64]), po[g].v(s_[:, 0:64]), rden.v())
                for c in range(2):
                    p2 = ps[self.psi % 6]; self.psi += 1
                    P.I("pe", "transpose", p2.v(s_[:, 0:128]), oa.v(s_[:, c * 128:(c + 1) * 128]), ident.v())
                    P.I("act", "activation", self.brT[3].v(s_[:, c, tb * 128:(tb + 1) * 128], tb), p2.v(s_[:, 0:128]), AF.Copy)
            S.barrier()

    def fnet(self, l):
        P = self; S = self.S
        d = self.din
        with ExitStack() as es:
            wb = P.sb(es, "wbF", [128, 8, 256], BF16)
            fT = P.sb(es, "fT", [128, 2, T], BF16, nsub=4)
            cd = P.sb(es, "cdft", [128, 2, 128], BF16)
            A = P.sb(es, "fA", [128, NB, 256], BF16, nsub=NB)
            B = P.sb(es, "fB", [128, NB, 256], BF16, nsub=NB)
            tc_ = [P.sb(es, "dc%d" % i, [128, 512], BF16) for i in range(4)]
            ts_ = [P.sb(es, "ds%d" % i, [128, 512], BF16) for i in range(4)]
            self.load_w(wb, l, 416, 256)
            P.dma("sp", cd.v(), d["cdft"].rearrange("a p c -> p a c"))
            for c in range(2):
                def ev(tg, pv, c=c):
                    P.I("act", "activation", fT.v(s_[:, c, tg * 512:(tg + 1) * 512], tg), pv, AF.Copy)
                self.proj_fm(wb, c * 128, 128, ev)
            for tb in range(NB):
                pa = self.ps[self.psi % 8]; self.psi += 1
                for c in range(2):
                    P.mm(pa.v(s_[:, c * 128:(c + 1) * 128]), fT.v(s_[:, c, tb * 128:(tb + 1) * 128], tb // 4), cd.v(s_[:, 0, :]), start=True, stop=True)
                    P.mm(pa.v(s_[:, 256 + c * 128:256 + (c + 1) * 128]), fT.v(s_[:, c, tb * 128:(tb + 1) * 128], tb // 4), cd.v(s_[:, 1, :]), start=True, stop=True)
                P.I("act", "activation", A.v(s_[:, tb, :], tb), pa.v(s_[:, 0:256]), AF.Copy)
                P.I("act", "activation", B.v(s_[:, tb, :], tb), pa.v(s_[:, 256:512]), AF.Copy)
            n = 0
            for tg in range(4):
                p0 = self.ps[self.psi % 8]; p1 = self.ps[(self.psi + 1) % 8]; self.psi += 2
                for tb in range(NB):
                    ct = tc_[n % 4]; st = ts_[n % 4]; n += 1
                    P.dma("sp", ct.v(), d["dft_c"][tb * 128:(tb + 1) * 128, tg * 512:(tg + 1) * 512])
                    P.dma("act", st.v(), d["dft_s"][tb * 128:(tb + 1) * 128, tg * 512:(tg + 1) * 512])
                    for c, pp in ((0, p0), (1, p1)):
                        P.mm(pp.v(), A.v(s_[:, tb, c * 128:(c + 1) * 128], tb), ct.v(), start=(tb == 0), stop=False)
                        P.mm(pp.v(), B.v(s_[:, tb, c * 128:(c + 1) * 128], tb), st.v(), start=False, stop=(tb == NB - 1))
                for c, pp in ((0, p0), (1, p1)):
                    P.I("act" if c == 0 else "dve", "activation" if c == 0 else "tensor_copy", self.brT[1].v(s_[:, c, tg * 512:(tg + 1) * 512], range(tg * 4, tg * 4 + 4)),
                        pp.v(), *((AF.Copy,) if c == 0 else ()))
            S.barrier()

    def merge(self, l, dbg):
        P = self; S = self.S
        d = self.din
        x, uT, ps, mcol, ident, ones = self.x, self.uT, self.ps, self.mcol, self.ident, self.ones
        with ExitStack() as es:
            wbr = P.sb(es, "wbr", [128, 8, D], BF16)
            wo = P.sb(es, "wo", [128, 8, D], BF16)
            wg = [P.sb(es, "wg%d" % i, [128, 8, 512], BF16) for i in range(1)]
            G = P.sb(es, "Gacc", [128, 512], F32)
            GT = P.sb(es, "GT", [128, 8, 512], BF16, nsub=8)
            sg = [P.sb(es, "sg%d" % i, [128, 512], F32) for i in range(2)]
            gbc = P.sb(es, "g1bc", [128, D], F32)
            lng = P.sb(es, "ln1g", [128, D], F32)
            lnb = P.sb(es, "ln1b", [128, D], F32)
            dg = P.sb(es, "dgm", [128, 128], F32)
            xt = [P.sb(es, "xt%d" % i, [128, D], F32) for i in range(1)]
            tmps = [self.ln_tmp(es, "g%d" % i) for i in range(2)]
            P.dma("pool", wbr.v(), d["w_branch"][l].rearrange("b (k p) d -> p (b k) d", p=128))
            P.dma("pool", wo.v(), d["w_out"][l].rearrange("(k p) d -> p k d", p=128))
            P.dma("sp", lng.v(), d["ln"][l, 0, :].partition_broadcast(128))
            P.dma("sp", lnb.v(), d["ln"][l, 1, :].partition_broadcast(128))
            for k in range(8):
                P.I("dve", "tensor_scalar_mul", dg.v(), ident.v(), mcol.v(s_[:, l, 16 + k:17 + k], l))
                pt = ps[self.psi % 8]; self.psi += 1
                P.mm(pt.v(s_[:, 0:128]), ones.v(), dg.v(), start=True, stop=True)
                P.I("act", "activation", gbc.v(s_[:, k * 128:(k + 1) * 128]), pt.v(s_[:, 0:128]), AF.Copy)
            n = 0
            for tg in range(4):
                tsub = range(tg * 4, tg * 4 + 4)
                tsl = slice(tg * 512, (tg + 1) * 512)
                for dc in range(8):
                    w = wg[0]; n += 1
                    P.dma("pool", w.v(), d["w_gate"][l, dc])
                    for b in range(4):
                        pg = ps[self.psi % 8]; pp = ps[(self.psi + 1) % 8]; self.psi += 2
                        for k in range(8):
                            P.mm(pg.v(), w.v(s_[:, k, b * 128:(b + 1) * 128]), uT.v(s_[:, k, tsl], tsub), start=(k == 0), stop=(k == 7))
                        for kc in range(2):
                            P.mm(pp.v(), wbr.v(s_[:, b * 2 + kc, dc * 128:(dc + 1) * 128]), self.brT[b].v(s_[:, kc, tsl], tsub),
                                 start=(kc == 0), stop=(kc == 1))
                        sgt = sg[b % 2]
                        P.I("act", "activation", sgt.v(), pg.v(), AF.Sigmoid)
                        if b == 0:
                            P.I("dve", "tensor_tensor", G.v(), sgt.v(), pp.v(), op=ALU.mult)
                        else:
                            P.I("dve", "tensor_tensor", sgt.v(), sgt.v(), pp.v(), op=ALU.mult)
                            if b < 3:
                                P.I("pool", "tensor_tensor", G.v(), G.v(), sgt.v(), op=ALU.add)
                            else:
                                P.I("pool", "tensor_tensor", GT.v(s_[:, dc, :], dc), G.v(), sgt.v(), op=ALU.add)
                for j in range(4):
                    tb = tg * 4 + j
                    xtb = xt[0]
                    for hf in range(2):
                        pm = ps[self.psi % 8]; self.psi += 1
                        for k in range(8):
                            P.mm(pm.v(), GT.v(s_[:, k, j * 128:(j + 1) * 128], k), wo.v(s_[:, k, hf * 512:(hf + 1) * 512]), start=(k == 0), stop=(k == 7))
                        hs = slice(hf * 512, (hf + 1) * 512)
                        P.I("dve", "tensor_tensor", xtb.v(s_[:, hs]), pm.v(), gbc.v(s_[:, hs]), op=ALU.mult)
                        P.I("dve", "scalar_tensor_tensor", xtb.v(s_[:, hs]), x.v(s_[:, tb, hs], tb), ALPHA, xtb.v(s_[:, hs]), op0=ALU.mult, op1=ALU.add)
                    self.ln_block(tmps[tb % 2], xtb.v(), x.v(s_[:, tb, :], tb))
                    P.I("pool", "tensor_tensor", x.v(s_[:, tb, :], tb), x.v(s_[:, tb, :], tb), lng.v(), op=ALU.mult)
                    P.I("pool", "tensor_tensor", x.v(s_[:, tb, :], tb), x.v(s_[:, tb, :], tb), lnb.v(), op=ALU.add)
            S.barrier()

    def post_ffn(self, l, yacc_is_x=True):
        P = self; S = self.S
        d = self.din
        x = self.x
        with ExitStack() as es:
            lng = P.sb(es, "ln2g", [128, D], F32)
            lnb = P.sb(es, "ln2b", [128, D], F32)
            xt = [P.sb(es, "xq%d" % i, [128, D], F32) for i in range(2)]
            tmps = [self.ln_tmp(es, "q%d" % i) for i in range(2)]
            P.dma("sp", lng.v(), d["ln"][l, 2, :].partition_broadcast(128))
            P.dma("sp", lnb.v(), d["ln"][l, 3, :].partition_broadcast(128))
            for tb in range(NB):
                xtb = xt[tb % 2]
                P.I("act", "activation", xtb.v(), x.v(s_[:, tb, :], tb), AF.Copy)
                self.ln_block(tmps[tb % 2], xtb.v(), x.v(s_[:, tb, :], tb))
                P.I("pool", "tensor_tensor", x.v(s_[:, tb, :], tb), x.v(s_[:, tb, :], tb), lng.v(), op=ALU.mult)
                P.I("pool", "tensor_tensor", x.v(s_[:, tb, :], tb), x.v(s_[:, tb, :], tb), lnb.v(), op=ALU.add)
            S.barrier()

    def layer(self, l, dbg):
        P = self; S = self.S
        self.psi = 0
        for g4 in range(16):
            P.dma("pool", self.ubf.v(s_[l, g4 * 4:(g4 + 1) * 4], l * 16 + g4), self.din["peer_uT"][l, g4 * 4:(g4 + 1) * 4].rearrange("g p k e -> g p (k e)"))
            P.dma("pool", self.vbf.v(s_[l, g4 * 4:(g4 + 1) * 4], l * 16 + g4), self.din["peer_v"][l, g4 * 4:(g4 + 1) * 4].rearrange("g p j d -> g p (j d)"))
        with ExitStack() as esl:
            self.uT = P.sb(esl, "uT", [128, 8, T], BF16, nsub=NB)
            self.mod_to_uT(l, 0)
            self.brT = [P.sb(esl, "brT%d" % b, [128, 2, T], BF16, nsub=NB) for b in range(4)]
            import os
            skip = os.environ.get("KSKIP", "")
            for b, nm in ((0, "mla"), (3, "swa"), (1, "fnet"), (2, "gla")):
                if nm in skip:
                    for tb4 in range(4):
                        P.I("pool", "memset", self.brT[b].v(s_[:, :, tb4 * 512:(tb4 + 1) * 512], range(tb4 * 4, tb4 * 4 + 4)), 0.0)
            if "mla" not in skip:
                self.mla(l, dbg)
            if "swa" not in skip:
                self.swa(l, dbg)
            else:
                self.swa_kv_only(l)
            if "fnet" not in skip:
                self.fnet(l)
            if "gla" not in skip:
                self.gla(l, dbg)
            self.merge(l, dbg)
            S.barrier()
        import os
        if "peer" in os.environ.get("KSKIP", ""):
            for tb in range(NB):
                P.I("act", "activation", self.x.v(s_[:, tb, :], tb), self.x.v(s_[:, tb, :], tb), AF.Copy, scale=ALPHA)
        else:
            self.peer(l, dbg)
        self.post_ffn(l)

    def peer(self, l, dbg):
        P = self; S = self.S
        d = self.din
        x, ps, mcol, ident, ones, iota, bm = self.x, self.ps, self.mcol, self.ident, self.ones, self.iota, self.bm
        NCH = 128
        TBS = 256
        with ExitStack() as es:
            u2T = P.sb(es, "u2T", [128, 8, TBS], BF16, nsub=2)
            xn = P.sb(es, "xnP", [128, D], F32)
            lt = self.ln_tmp(es, "P")
            wq = P.sb(es, "wq", [128, 8, 256], BF16)
            kT = P.sb(es, "keysT", [128, 16, 128], BF16)
            gbc = P.sb(es, "g2bc", [128, D], F32)
            dg = P.sb(es, "dg2", [128, 128], F32)
            qpT = P.sb(es, "qpT", [128, 16, 128], BF16, nsub=16)
            sc = P.sb(es, "psc", [128, 16, 128], F32, nsub=16)
            wk = P.sb(es, "pwk", [128, 256], F32)
            vtop = P.sb(es, "vtop", [128, 16, 16], F32, nsub=16)
            itop = P.sb(es, "itop", [128, 16, 16], U32, nsub=16)
            idx1f = P.sb(es, "idx1f", [128, 128], F32)
            idx2f = P.sb(es, "idx2f", [128, 128], F32)
            idxT = P.sb(es, "idxT", [128, 2, 128], F32)
            cand = P.sb(es, "cand", [128, 8, 256], F32, nsub=8)
            t8a = P.sb(es, "t8a", [128, 8, 8], F32, nsub=8)
            t8b = P.sb(es, "t8b", [128, 8, 8], F32, nsub=8)
            nmx = P.sb(es, "nmx", [128, 8], F32)
            zz = P.sb(es, "pz", [128, 8], F32)
            wCT = P.sb(es, "wCT", [128, 128, 16], BF16)
            O1 = P.sb(es, "O1", [128, 8, 128], BF16)
            O2 = P.sb(es, "O2", [128, 8, 128], BF16)
            Cbd = P.sb(es, "Cbd", [128, 8, 128], BF16)
            tmpS = [P.sb(es, "ptmp%d" % i, [128, 4, 128], BF16) for i in range(2)]
            WtT = P.sb(es, "WtT", [128, TBS, 128], BF16, nsub=TBS // 4)
            Ut = [P.sb(es, "Ut%d" % i, [128, 8, 256], BF16) for i in range(2)]
            Vt = [P.sb(es, "Vt%d" % i, [128, 2, D], BF16) for i in range(2)]
            actS = [P.sb(es, "pact%d" % i, [128, TBS], BF16) for i in range(2)]
            GS = [P.sb(es, "pG%d" % i, [128, TBS], BF16) for i in range(2)]
            P.dma("pool", kT.v(), d["peer_keysT"][l].rearrange("h q c k -> c (h q) k"))
            for k in range(8):
                P.I("dve", "tensor_scalar_mul", dg.v(), ident.v(), mcol.v(s_[:, l, 40 + k:41 + k], l))
                pt = ps[self.psi % 4]; self.psi += 1
                P.mm(pt.v(s_[:, 0:128]), ones.v(), dg.v(), start=True, stop=True)
                P.I("act", "activation", gbc.v(s_[:, k * 128:(k + 1) * 128]), pt.v(s_[:, 0:128]), AF.Copy)
            py = [ps[4], ps[5], ps[6], ps[7]]
            esc = lambda a: a.rearrange("p (h a) b -> p h (a b)", a=2)
            for sb_ in range(T // TBS):
                for sub in range(2):
                    tb = sb_ * 2 + sub
                    usl = slice(sub * 128, (sub + 1) * 128)
                    self.ln_block(lt, x.v(s_[:, tb, :], tb), xn.v())
                    for half in range(2):
                        pt = ps[self.psi % 4]; self.psi += 1
                        for j in range(4):
                            k = half * 4 + j
                            P.I("pe", "transpose", pt.v(s_[:, j * 128:(j + 1) * 128]), xn.v(s_[:, k * 128:(k + 1) * 128]), ident.v())
                        for j in range(4):
                            k = half * 4 + j
                            P.I("dve", "tensor_scalar", u2T.v(s_[:, k, usl], sub), pt.v(s_[:, j * 128:(j + 1) * 128]),
                                mcol.v(s_[:, l, 32 + k:33 + k], l), mcol.v(s_[:, l, 24 + k:25 + k], l), op0=ALU.mult, op1=ALU.add)
                    for c4 in range(4):
                        pt = ps[self.psi % 4]; self.psi += 1
                        for j in range(4):
                            c = c4 * 4 + j
                            if j % 2 == 0:
                                P.dma("pool", wq.v(), d["w_peer_q"][l, c // 2])
                            for k in range(8):
                                P.mm(pt.v(s_[:, j * 128:(j + 1) * 128]), wq.v(s_[:, k, (j % 2) * 128:(j % 2) * 128 + 128]), u2T.v(s_[:, k, usl], sub), start=(k == 0), stop=(k == 7))
                        P.I("act", "activation", qpT.v(s_[:, c4 * 4:(c4 + 1) * 4, :], range(c4 * 4, c4 * 4 + 4)), pt.v().m(lambda a: a.rearrange("p (j t) -> p j t", j=4)), AF.Copy)
                    for c4 in range(4):
                        pt = ps[self.psi % 4]; self.psi += 1
                        for j in range(4):
                            c = c4 * 4 + j
                            P.mm(pt.v(s_[:, j * 128:(j + 1) * 128]), qpT.v(s_[:, c, :], c), kT.v(s_[:, c, :]), start=True, stop=True)
                        P.I("act", "activation", sc.v(s_[:, c4 * 4:(c4 + 1) * 4, :], range(c4 * 4, c4 * 4 + 4)), pt.v().m(lambda a: a.rearrange("p (j t) -> p j t", j=4)), AF.Copy)
                    wkc = lambda c: cand.v(s_[:, c // 2, (c % 2) * 128:(c % 2) * 128 + 128], c // 2)
                    for c in range(16):
                        P.I("dve", "max", vtop.v(s_[:, c, 0:8], c), sc.v(s_[:, c, :], c))
                    for c in range(16):
                        P.I("dve", "max_index", itop.v(s_[:, c, 0:8], c), vtop.v(s_[:, c, 0:8], c), sc.v(s_[:, c, :], c))
                    for c in range(16):
                        P.I("dve", "match_replace", wkc(c), vtop.v(s_[:, c, 0:8], c), sc.v(s_[:, c, :], c), -1e30)
                    for c in range(16):
                        P.I("dve", "max", vtop.v(s_[:, c, 8:16], c), wkc(c))
                    for c in range(16):
                        P.I("dve", "max_index", itop.v(s_[:, c, 8:16], c), vtop.v(s_[:, c, 8:16], c), wkc(c))
                    v4 = lambda a: a.rearrange("p (h q) r -> p h q r", q=2)
                    P.I("dve", "tensor_copy", idx1f.v().m(lambda a: a.rearrange("p (h r) -> p h r", h=8)), itop.v().m(lambda a: v4(a)[:, :, 0, :]))
                    P.I("dve", "tensor_copy", idx2f.v().m(lambda a: a.rearrange("p (h r) -> p h r", h=8)), itop.v().m(lambda a: v4(a)[:, :, 1, :]))
                    P.I("dve", "tensor_tensor", cand.v().m(lambda a: a.rearrange("p h (a b) -> p h a b", a=16)),
                        vtop.v().m(lambda a: v4(a)[:, :, 0, :].unsqueeze(3).to_broadcast([128, 8, 16, 16])),
                        vtop.v().m(lambda a: v4(a)[:, :, 1, :].unsqueeze(2).to_broadcast([128, 8, 16, 16])), op=ALU.add)
                    wkh = lambda h: sc.v(s_[:, 2 * h:2 * h + 2, :], [2 * h, 2 * h + 1]).m(lambda a: a.rearrange("p a b -> p (a b)"))
                    for h in range(8):
                        P.I("dve", "max", t8a.v(s_[:, h, :], h), cand.v(s_[:, h, :], h))
                    for h in range(8):
                        P.I("dve", "match_replace", wkh(h), t8a.v(s_[:, h, :], h), cand.v(s_[:, h, :], h), -1e30)
                    for h in range(8):
                        P.I("dve", "max", t8b.v(s_[:, h, :], h), wkh(h))
                    P.I("dve", "tensor_scalar_mul", nmx.v(), t8a.v(s_[:, :, 0]), -1.0)
                    for h in range(8):
                        ev = sc.v(s_[:, 2 * h:2 * h + 2, :], [2 * h, 2 * h + 1]).m(lambda a: a.rearrange("p a b -> p (a b)"))
                        P.I("act", "activation", ev, cand.v(s_[:, h, :], h), AF.Exp, bias=nmx.v(s_[:, h:h + 1]), scale=1.0)
                        P.I("dve", "scalar_tensor_tensor", ev, cand.v(s_[:, h, :], h), t8b.v(s_[:, h, 7:8], h), ev, op0=ALU.is_ge, op1=ALU.mult)
                    P.I("dve", "tensor_reduce", zz.v(), sc.v().m(esc), axis=AX.X, op=ALU.add)
                    P.I("dve", "reciprocal", zz.v(), zz.v())
                    P.I("dve", "tensor_tensor", sc.v().m(esc), sc.v().m(esc), zz.v().m(lambda a: a.unsqueeze(2).to_broadcast([128, 8, 256])), op=ALU.mult)
                    pt = ps[self.psi % 4]; self.psi += 1
                    P.I("pe", "transpose", pt.v(s_[:, 0:128]), idx1f.v(), ident.v())
                    P.I("pe", "transpose", pt.v(s_[:, 128:256]), idx2f.v(), ident.v())
                    P.I("act", "activation", idxT.v(), pt.v(s_[:, 0:256]).m(lambda a: a.rearrange("p (a t) -> p a t", a=2)), AF.Copy)
                    for r4 in range(4):
                        pt = ps[self.psi % 4]; self.psi += 1
                        for j in range(4):
                            r2 = r4 * 4 + j
                            P.I("pe", "transpose", pt.v(s_[:, j * 128:(j + 1) * 128]),
                                sc.v().m(lambda a, r2=r2: a.rearrange("p c (a b) -> p (c a) b", b=16)[:, :, r2]), ident.v())
                        P.I("act", "activation", wCT.v(s_[:, :, r4 * 4:(r4 + 1) * 4]).m(lambda a: a.rearrange("p t j -> p j t")),
                            pt.v().m(lambda a: a.rearrange("p (j t) -> p j t", j=4)), AF.Copy)
                    for sbk in range(16):
                        t0 = sbk * 8
                        P.I("dve", "tensor_tensor", O1.v(), iota.v().m(lambda a: a.unsqueeze(1).to_broadcast([128, 8, 128])),
                            idxT.v(s_[:, 0, t0:t0 + 8]).m(lambda a: a.unsqueeze(2).to_broadcast([128, 8, 128])), op=ALU.is_equal)
                        P.I("dve", "tensor_tensor", O2.v(), iota.v().m(lambda a: a.unsqueeze(1).to_broadcast([128, 8, 128])),
                            idxT.v(s_[:, 1, t0:t0 + 8]).m(lambda a: a.unsqueeze(2).to_broadcast([128, 8, 128])), op=ALU.is_equal)
                        P.I("pool", "tensor_tensor", Cbd.v().m(lambda a: a.rearrange("p t (h r) -> p t h r", h=8)),
                            wCT.v(s_[:, t0:t0 + 8, :]).m(lambda a: a.unsqueeze(2).to_broadcast([128, 8, 8, 16])),
                            bm.v().m(lambda a: a.unsqueeze(1).unsqueeze(3).to_broadcast([128, 8, 8, 16])), op=ALU.mult)
                        for g4 in range(2):
                            pt = ps[self.psi % 4]; self.psi += 1
                            tS = tmpS[g4 % 2]
                            for j in range(4):
                                tt = g4 * 4 + j
                                P.mm(pt.v(s_[:, j * 128:(j + 1) * 128]), Cbd.v(s_[:, tt, :]), O1.v(s_[:, tt, :]), start=True, stop=True)
                            P.I("act", "activation", tS.v(), pt.v().m(lambda a: a.rearrange("p (j i) -> p j i", j=4)), AF.Copy)
                            pt2 = ps[self.psi % 4]; self.psi += 1
                            for j in range(4):
                                tt = g4 * 4 + j
                                P.mm(pt2.v(s_[:, j * 128:(j + 1) * 128]), O2.v(s_[:, tt, :]), tS.v(s_[:, j, :]), start=True, stop=True)
                            ta = sub * 128 + t0 + g4 * 4
                            P.I("dve" if g4 % 2 == 0 else "act", "tensor_copy" if g4 % 2 == 0 else "activation", WtT.v(s_[:, ta:ta + 4, :], ta // 4),
                                pt2.v().m(lambda a: a.rearrange("p (j i) -> p j i", j=4)), *(() if g4 % 2 == 0 else (AF.Copy,)))
                def emitU(c):
                    c2, j = c // 2, c % 2
                    if j == 0:
                        ut = Ut[c2 % 2]; vt = Vt[c2 % 2]
                        P.dma("sp", ut.v().m(lambda a: a.rearrange("p k e -> p (k e)")), self.ubf.v(s_[l, c2], l * 16 + c2 // 4))
                        P.dma("sp", vt.v().m(lambda a: a.rearrange("p j d -> p (j d)")), self.vbf.v(s_[l, c2], l * 16 + c2 // 4))
                    ut = Ut[c2 % 2]
                    pa = ps[c % 2]
                    for k in range(8):
                        P.mm(pa.v(s_[:, 0:TBS]), ut.v(s_[:, k, j * 128:(j + 1) * 128]), u2T.v(s_[:, k, :]), start=(k == 0), stop=(k == 7))
                def emitMV(c):
                    c2, j = c // 2, c % 2
                    vt = Vt[c2 % 2]
                    pa = ps[c % 2]
                    aS = actS[c % 2]; gS = GS[c % 2]
                    P.I("act", "activation", aS.v(), pa.v(s_[:, 0:TBS]), AF.Gelu)
                    P.I("dve", "tensor_tensor", gS.v(), aS.v(), WtT.v(s_[:, :, c]), op=ALU.mult)
                    for sub in range(2):
                        for hf in range(2):
                            P.mm(py[sub * 2 + hf].v(), gS.v(s_[:, sub * 128:(sub + 1) * 128]), vt.v(s_[:, j, hf * 512:(hf + 1) * 512]), start=(c == 0), stop=(c == NCH - 1))
                emitU(0)
                for c in range(NCH):
                    if c + 1 < NCH:
                        emitU(c + 1)
                    emitMV(c)
                for sub in range(2):
                    tb = sb_ * 2 + sub
                    for hf in range(2):
                        hs = slice(hf * 512, (hf + 1) * 512)
                        yv = cand.v(s_[:, 0:2, :], [0, 1]).m(lambda a: a.rearrange("p a b -> p (a b)"))
                        P.I("dve", "tensor_tensor", yv, py[sub * 2 + hf].v(), gbc.v(s_[:, hs]), op=ALU.mult)
                        P.I("dve", "scalar_tensor_tensor", x.v(s_[:, tb, hs], tb), x.v(s_[:, tb, hs], tb), ALPHA, yv, op0=ALU.mult, op1=ALU.add)
            S.barrier()

def _bf(a):
    return np.ascontiguousarray(a).astype(ml_dtypes.bfloat16)


def host_consts(kind):
    c = {}
    c["ident"] = np.eye(128, dtype=np.float32)
    c["bm"] = np.ascontiguousarray((np.arange(128)[:, None] // 16 == np.arange(8)[None, :]).astype(np.float32))
    seqlen = T if kind == "sample" else 256
    n = np.arange(seqlen)
    ang = 2.0 * np.pi * np.outer(n, n) / seqlen
    sc = 1.0 / np.sqrt(seqlen * 64.0)
    cb = np.cos(ang) * sc
    sbm = -np.sin(ang) * sc
    Cf = np.zeros((T, T), np.float64)
    Sf = np.zeros((T, T), np.float64)
    for i in range(T // seqlen):
        sl = slice(i * seqlen, (i + 1) * seqlen)
        Cf[sl, sl] = cb
        Sf[sl, sl] = sbm
    c["dft_c"] = _bf(Cf.astype(np.float32))
    c["dft_s"] = _bf(Sf.astype(np.float32))
    m = np.arange(64)
    a2 = 2.0 * np.pi * np.outer(m, m) / 64.0
    cc = np.zeros((2, 128, 128), np.float64)
    for g in range(2):
        cc[0, g * 64:(g + 1) * 64, g * 64:(g + 1) * 64] = np.cos(a2)
        cc[1, g * 64:(g + 1) * 64, g * 64:(g + 1) * 64] = np.sin(a2)
    c["cdft"] = _bf(cc.astype(np.float32))
    t = np.arange(T)
    rows = (t // 64).astype(np.float64); cols = (t % 64).astype(np.float64)
    for nm, R in (("rope_m", 32), ("rope_s", 64)):
        half = R // 2; q = R // 4
        tab = np.zeros((2, R, T), np.float64)
        for dd in range(R):
            pos = rows if dd < half else cols
            fi = dd % q
            freq = 10000.0 ** (-(2.0 * fi) / half)
            ang = pos * freq
            if kind == "sample":
                tab[0, dd] = np.cos(ang)
                tab[1, dd] = np.sin(ang) * (-1.0 if (dd // q) % 2 == 0 else 1.0)
            else:
                tab[0, dd] = 1.0
        c[nm] = np.ascontiguousarray(tab.astype(np.float32))
    bm_ = np.zeros((160,), np.float32)
    if kind == "prompt":
        for qu in range(8):
            for kc in range(20):
                ok = kc >= 4 and (kc - 4) // 2 == qu
                bm_[qu * 20 + kc] = 0.0 if ok else NEG
    c["bias_m"] = np.ascontiguousarray(np.broadcast_to(bm_[None, :], (128, 160)))
    bs_ = np.zeros((128, 4), np.float32)
    if kind == "prompt":
        bs_[:, 0] = NEG
    c["bias_s"] = bs_
    mk = np.zeros((128, NB, 2, 128), np.float32)
    kk = np.arange(128)[:, None]; qq = np.arange(128)[None, :]
    for tb in range(NB):
        if kind == "sample":
            mk[:, tb, 0, :] = (kk >= qq)
            mk[:, tb, 1, :] = (kk <= qq)
        else:
            mk[:, tb, 0, :] = 1.0 if tb % 2 == 1 else 0.0
            mk[:, tb, 1, :] = 1.0 if tb % 2 == 0 else 0.0
    c["mask_s"] = _bf(mk)
    tt = np.arange(128)[:, None]; tp = np.arange(128)[None, :]
    same = (tt // 64) == (tp // 64)
    cc_ = -1.0 / 16.0
    gm = np.zeros((5, 128, 128), np.float32)
    gm[0] = cc_ * (same & (tt <= tp))
    gm[1] = cc_ * (same & (tt > tp))
    gm[2] = cc_ * (same & (tt >= tp))
    gm[3] = cc_ * (same & (tt < tp))
    gm[4, :, 0] = cc_ * (np.arange(128) < 64)
    gm[4, :, 1] = cc_ * (np.arange(128) >= 64)
    c["gla_mats"] = gm
    c["gla_mask"] = _bf(np.stack([(same & (tt <= tp)), (same & (tt >= tp))]).astype(np.float32))
    kp = np.ones((128, 2, 32), np.float32)
    if kind == "prompt":
        for n_ in range(32):
            if n_ % 4 == 0:
                kp[:, 0, n_] = 0.0
            if n_ % 4 == 3:
                kp[:, 1, n_] = 0.0
    c["gla_keep"] = kp
    return c


def perm_swap(R):
    q = R // 4
    return np.array([d + q if (d // q) % 2 == 0 else d - q for d in range(R)])


def host_weights(inp):
    w = {}
    w["w_ada"] = np.ascontiguousarray(inp["w_ada"], dtype=np.float32)
    w["b_adaT"] = np.ascontiguousarray(inp["b_ada"].reshape(L, 48, 128).transpose(0, 2, 1), dtype=np.float32)
    w_in = np.asarray(inp["w_in"], dtype=np.float32)
    p32 = perm_swap(32); p64 = perm_swap(64)
    kr = w_in[:, :, 384:416][:, :, p32]
    sq = w_in[:, :, 1472:1728].reshape(L, D, 4, 64)[:, :, :, p64].reshape(L, D, 256)
    sk = w_in[:, :, 1728:1856].reshape(L, D, 2, 64)[:, :, :, p64].reshape(L, D, 128)
    w["w_in"] = np.ascontiguousarray(np.concatenate([w_in, kr, sq, sk], axis=2))
    w["w_gate"] = np.ascontiguousarray(w_in[:, :, 1984:6080].reshape(L, 8, 128, 4, 8, 128).transpose(0, 4, 2, 1, 3, 5).reshape(L, 8, 128, 8, 512))
    w["w_branch"] = np.ascontiguousarray(inp["w_branch"], dtype=np.float32)
    w["w_out"] = np.ascontiguousarray(inp["w_out"], dtype=np.float32)
    w_uq = np.asarray(inp["w_uq"], dtype=np.float32)
    uq_sw = w_uq.reshape(L, 256, 4, 96)[:, :, :, 64:96][:, :, :, p32].reshape(L, 256, 128)
    w["w_uq"] = np.ascontiguousarray(np.concatenate([w_uq, uq_sw], axis=2))
    w["w_ukv"] = np.ascontiguousarray(inp["w_ukv"], dtype=np.float32)
    w["mla_q_norm"] = np.ascontiguousarray(np.asarray(inp["mla_q_norm"], dtype=np.float32).reshape(L, 2, 128).transpose(0, 2, 1))
    w["mla_kv_norm"] = np.ascontiguousarray(inp["mla_kv_norm"], dtype=np.float32)
    w["w_gla_a"] = np.ascontiguousarray(np.stack([inp["w_gla_a_fwd"], inp["w_gla_a_bwd"]], axis=1), dtype=np.float32)
    w["b_gla_a"] = np.ascontiguousarray(np.stack([inp["b_gla_a_fwd"], inp["b_gla_a_bwd"]], axis=1), dtype=np.float32)
    w["gla_norm"] = np.ascontiguousarray(inp["gla_norm"], dtype=np.float32)
    w["swa_sink"] = np.ascontiguousarray(np.broadcast_to(np.asarray(inp["swa_sink"], dtype=np.float32)[:, None, :], (L, 128, 4)))
    w["w_peer_q"] = np.ascontiguousarray(np.asarray(inp["w_peer_q"], dtype=np.float32).reshape(L, 8, 128, 8, 256).transpose(0, 3, 2, 1, 4))
    w["peer_keysT"] = np.ascontiguousarray(np.asarray(inp["peer_keys"], dtype=np.float32).transpose(0, 1, 2, 4, 3))
    w["peer_uT"] = np.ascontiguousarray(np.asarray(inp["peer_u"], dtype=np.float32).reshape(L, 64, 256, 8, 128).transpose(0, 1, 4, 3, 2))
    w["peer_v"] = np.ascontiguousarray(np.asarray(inp["peer_v"], dtype=np.float32).reshape(L, 64, 2, 128, D).transpose(0, 1, 3, 2, 4))
    w["ln"] = np.ascontiguousarray(np.stack([inp["ln1_g"], inp["ln1_b"], inp["ln2_g"], inp["ln2_b"]], axis=1), dtype=np.float32)
    return w


def core_inputs(inp, core, W, CS, CP):
    m = dict(W)
    if core < 2:
        m.update(CS)
        m["x"] = np.ascontiguousarray(inp["x_sample"][core], dtype=np.float32)
        cond = np.asarray(inp["c"][core], dtype=np.float32)
        m["cache_ckv"] = np.ascontiguousarray(inp["cache_mla_ckv"][core], dtype=np.float32)
        m["cache_krope"] = np.ascontiguousarray(inp["cache_mla_krope"][core], dtype=np.float32)
        m["cache_swak"] = np.ascontiguousarray(inp["cache_swa_k"][core], dtype=np.float32)
        m["cache_swav"] = np.ascontiguousarray(inp["cache_swa_v"][core], dtype=np.float32)
        m["gla_init"] = np.ascontiguousarray(np.asarray(inp["state_gla"][core], dtype=np.float32).reshape(L, 2, 128, 64))
    else:
        m.update(CP)
        j = core - 2 if core < 6 else 0
        m["x"] = np.ascontiguousarray(np.asarray(inp["x_prompt"][8 * j:8 * j + 8], dtype=np.float32).reshape(T, D))
        cond = np.asarray(inp["c_ctx"], dtype=np.float32)
        m["cache_ckv"] = np.zeros((L, 512, 128), np.float32)
        m["cache_krope"] = np.zeros((L, 512, 32), np.float32)
        m["cache_swak"] = np.zeros((L, 2, 512, 64), np.float32)
        m["cache_swav"] = np.zeros((L, 2, 512, 64), np.float32)
        m["gla_init"] = np.zeros((L, 2, 128, 64), np.float32)
    m["condT"] = np.ascontiguousarray(cond.reshape(8, 128).T)
    return m


_CACHE = {}


def kernel(**inputs):
    cores = inputs.pop("_cores", list(range(8)))
    debug = inputs.pop("_debug", None)
    stop_after = inputs.pop("_stop_after", None)
    prog = Prog(debug=debug, stop_after=stop_after)
    nc = prog.build()
    W = host_weights(inputs)
    CS = host_consts("sample")
    CP = host_consts("prompt")
    in_maps = [core_inputs(inputs, c, W, CS, CP) for c in cores]
    import os as _os
    if _os.environ.get("KTRACE"):
        res = run_bass_kernel_spmd(nc, in_maps, core_ids=list(range(len(cores))), trace=True)
        print("EXEC_TIME_NS", res.exec_time_ns)
        globals()["_LAST_RES"] = res
    else:
        res = run_bass_kernel_spmd(nc, in_maps, core_ids=list(range(len(cores))))
    R = res.results
    if debug is not None:
        return R
    y_sample = np.stack([R[0]["y"], R[1]["y"]], axis=0)
    y_prompt = np.concatenate([R[2 + j]["y"].reshape(8, 256, D) for j in range(4)], axis=0)
    ckv = np.concatenate([R[2 + j]["ckv_o"].reshape(L, 8, 256, 128).transpose(1, 0, 2, 3) for j in range(4)], axis=0)
    kr = np.concatenate([R[2 + j]["krope_o"].reshape(L, 8, 256, 32).transpose(1, 0, 2, 3) for j in range(4)], axis=0)
    kvs = [R[2 + j]["swakv_o"].reshape(L, 8, 256, 2, 2, 64) for j in range(4)]
    sk = np.concatenate([a[:, :, :, 0].transpose(1, 0, 3, 2, 4) for a in kvs], axis=0)
    sv = np.concatenate([a[:, :, :, 1].transpose(1, 0, 3, 2, 4) for a in kvs], axis=0)
    gl = np.concatenate([R[2 + j]["gla_o"].reshape(L, 8, 2, 4, 32, 64).transpose(1, 0, 2, 3, 4, 5) for j in range(4)], axis=0)
    f = lambda a: np.ascontiguousarray(a, dtype=np.float32)
    return (f(y_prompt), f(y_sample), f(ckv), f(kr), f(sk), f(sv), f(gl))
```

```python
import numpy as np
import ml_dtypes
from contextlib import ExitStack
import concourse.bass as bass
import concourse.mybir as mybir
from concourse.bass_utils import run_bass_kernel_spmd

F32 = mybir.dt.float32
BF16 = mybir.dt.bfloat16
U32 = mybir.dt.uint32
AF = mybir.ActivationFunctionType
ALU = mybir.AluOpType
AX = mybir.AxisListType
s_ = np.s_

ENGS = ("pe", "act", "dve", "pool", "sp")
SAME_ENGINE_SYNC = True
import os as _os0
SES_ALL = not bool(_os0.environ.get("KNOSES"))

T = 2048
NB = 16
D = 1024
L = 2
ALPHA = (2.0 * L) ** 0.25
EPS = 1e-6
NEG = -30000.0
MLA_SCALE = 96.0 ** -0.5
SWA_SCALE = 64.0 ** -0.5
WIN_EXT = 6496


class V:
    def __init__(self, ap, toks):
        self.ap = ap
        self.toks = toks

    def m(self, fn):
        return V(fn(self.ap), self.toks)


class Buf:
    def __init__(self, name, t, nsub=1):
        self.name = name
        self.t = t
        self.nsub = nsub

    def tok(self, subs=None):
        if subs is None:
            return [(self.name, s) for s in range(self.nsub)]
        if isinstance(subs, int):
            subs = [subs]
        return [(self.name, s) for s in subs]

    def v(self, key=None, subs=None):
        ap = self.t[:] if key is None else self.t[key]
        return V(ap, self.tok(subs))


class Sched:
    def __init__(self, nc, es, nd=24):
        self.nc = nc
        self.ops = {e: [] for e in ENGS}
        self.cnt = {e: 0 for e in ENGS}
        self.known = {e: {} for e in ENGS}
        self.nd = nd
        self.dma_tot = [0] * nd
        self.dma_rr = 0
        self.last_w = {}
        self.readers = {}
        self.sem = {e: es.enter_context(nc.semaphore("sem_" + e)) for e in ENGS if e != "sp"}
        self.dsem = [es.enter_context(nc.semaphore("dsem%d" % i)) for i in range(nd)]
        self.milestones = {e: set() for e in ENGS}

    def _need(self, eng, dep, waits):
        kind, key, val = dep
        if kind == "eng" and key == eng:
            if eng in ("pe", "sp") or (eng in ("act", "dve") and not SES_ALL) or not SAME_ENGINE_SYNC:
                return
        k = (kind, key)
        if self.known[eng].get(k, 0) >= val:
            return
        self.known[eng][k] = val
        waits.append((kind, key, val))
        if kind == "eng":
            self.milestones[key].add(val)

    def _deps(self, eng, reads, writes):
        waits = []
        for t in reads:
            lw = self.last_w.get(t)
            if lw is not None:
                self._need(eng, lw, waits)
        for t in writes:
            lw = self.last_w.get(t)
            if lw is not None:
                self._need(eng, lw, waits)
            for r in self.readers.get(t, ()):
                self._need(eng, r, waits)
        return waits

    def _commit(self, me, reads, writes):
        for t in reads:
            self.readers.setdefault(t, []).append(me)
        for t in writes:
            self.last_w[t] = me
            self.readers[t] = []

    def op(self, eng, fn, reads=(), writes=()):
        reads = list(reads); writes = list(writes)
        waits = self._deps(eng, reads, writes)
        self.cnt[eng] += 1
        me = ("eng", eng, self.cnt[eng])
        self.ops[eng].append((waits, fn, ("eng", self.cnt[eng])))
        self._commit(me, reads, writes)

    def dma(self, eng, fn, reads=(), writes=()):
        reads = list(reads); writes = list(writes)
        i = self.dma_rr
        self.dma_rr = (i + 1) % self.nd
        waits = []
        if self.dma_tot[i] > 0:
            self._need(eng, ("dma", i, self.dma_tot[i]), waits)
        waits += self._deps(eng, reads, writes)
        self.dma_tot[i] += 16
        me = ("dma", i, self.dma_tot[i])
        self.cnt[eng] += 1
        self.ops[eng].append((waits, fn, ("dma", i)))
        self._commit(me, reads, writes)

    def _last_seq(self, e):
        for w, fn, inc in reversed(self.ops[e]):
            if inc is not None and inc[0] == "eng":
                return inc[1]
        return 0

    def barrier(self):
        lasts = {e: self._last_seq(e) for e in ENGS}
        for e in ENGS:
            waits = []
            for e2 in ENGS:
                if e2 != e and e2 != "sp" and lasts[e2] > 0:
                    self._need(e, ("eng", e2, lasts[e2]), waits)
            for i in range(self.nd):
                if self.dma_tot[i] > 0:
                    self._need(e, ("dma", i, self.dma_tot[i]), waits)
            if waits:
                self.ops[e].append((waits, None, None))
        self.last_w = {}
        self.readers = {}

    def finish(self):
        self.barrier()

    def emit(self, blk):
        rank = {}
        for e in ENGS:
            ms = sorted(self.milestones[e])
            rank[e] = {s: i + 1 for i, s in enumerate(ms)}

        def run(e, eng):
            for waits, fn, inc in self.ops[e]:
                for kind, key, val in waits:
                    if kind == "eng":
                        eng.wait_ge(self.sem[key], rank[key][val])
                    else:
                        eng.wait_ge(self.dsem[key], val)
                if fn is None:
                    continue
                ins = fn(eng)
                if inc[0] == "dma":
                    ins.then_inc(self.dsem[inc[1]], 16)
                elif inc[1] in rank[e]:
                    ins.then_inc(self.sem[e], 1)

        blk.sync(lambda eng: run("sp", eng))
        blk.scalar(lambda eng: run("act", eng))
        blk.vector(lambda eng: run("dve", eng))
        blk.gpsimd(lambda eng: run("pool", eng))
        blk.tensor(lambda eng: run("pe", eng))


class Prog:
    def __init__(self, debug=None, stop_after=None):
        self.debug = debug or []
        self.stop_after = stop_after
        self.nc = bass.Bass("TRN2", target_bir_lowering=False)
        self.din = {}
        self.dout = {}

    def inp(self, name, shape, dt=F32):
        self.din[name] = self.nc.dram_tensor(name, list(shape), dt, kind="ExternalInput").ap()
        return self.din[name]

    def outp(self, name, shape, dt=F32):
        self.dout[name] = self.nc.dram_tensor(name, list(shape), dt, kind="ExternalOutput").ap()
        return self.dout[name]

    def sb(self, es, name, shape, dt=F32, nsub=1):
        self.uid = getattr(self, "uid", 0) + 1
        name = "%s_u%d" % (name, self.uid)
        return Buf(name, es.enter_context(self.nc.sbuf_tensor(name, list(shape), dt)), nsub)

    def I(self, eng, meth, out, *args, **kw):
        def conv(a):
            return a.ap if isinstance(a, V) else a
        reads = []
        writes = list(out.toks)
        for a in list(args) + list(kw.values()):
            if isinstance(a, V):
                reads += a.toks
        if "accum_out" in kw:
            writes += kw["accum_out"].toks
        a2 = [conv(a) for a in args]
        k2 = {k: conv(v) for k, v in kw.items()}
        o = out.ap
        self.S.op(eng, lambda e: getattr(e, meth)(o, *a2, **k2), reads, writes)

    def dma(self, q, out, in_):
        reads = in_.toks if isinstance(in_, V) else []
        writes = out.toks if isinstance(out, V) else []
        o = out.ap if isinstance(out, V) else out
        i = in_.ap if isinstance(in_, V) else in_
        self.S.dma(q, lambda e: e.dma_start(out=o, in_=i), reads, writes)

    def mm(self, out, lhsT, rhs, start, stop):
        self.I("pe", "matmul", out, lhsT=lhsT, rhs=rhs, start=start, stop=stop)

    def build(self):
        nc = self.nc
        P = self
        inp = self.inp
        x_d = inp("x", [T, D])
        condT_d = inp("condT", [128, 8])
        w_ada_d = inp("w_ada", [L, D, 6 * D])
        b_adaT_d = inp("b_adaT", [L, 128, 48])
        w_in_d = inp("w_in", [L, D, WIN_EXT])
        w_branch_d = inp("w_branch", [L, 4, 256, D])
        w_out_d = inp("w_out", [L, D, D])
        inp("w_gate", [L, 8, 128, 8, 512])
        ln_d = inp("ln", [L, 4, D])
        dft_c_d = inp("dft_c", [T, T], BF16)
        dft_s_d = inp("dft_s", [T, T], BF16)
        cdft_d = inp("cdft", [2, 128, 128], BF16)
        ident_d = inp("ident", [128, 128])
        inp("w_peer_q", [L, 8, 128, 8, 256])
        inp("w_uq", [L, 256, 512]); inp("w_ukv", [L, 128, 512]); inp("mla_q_norm", [L, 128, 2]); inp("mla_kv_norm", [L, 128])
        inp("gla_mats", [5, 128, 128]); inp("gla_mask", [2, 128, 128], BF16); inp("gla_keep", [128, 2, 32]); inp("gla_init", [L, 2, 128, 64])
        inp("w_gla_a", [L, 2, 16, 128]); inp("b_gla_a", [L, 2, 128]); inp("gla_norm", [L, 64])
        inp("rope_m", [2, 32, T]); inp("rope_s", [2, 64, T]); inp("bias_m", [128, 160]); inp("bias_s", [128, 4])
        inp("mask_s", [128, NB, 2, 128], BF16); inp("swa_sink", [L, 128, 4])
        inp("cache_ckv", [L, 512, 128]); inp("cache_krope", [L, 512, 32]); inp("cache_swak", [L, 2, 512, 64]); inp("cache_swav", [L, 2, 512, 64])
        self.outp("ckv_o", [L, T, 128]); self.outp("krope_o", [L, T, 32]); self.outp("swakv_o", [L, T, 256]); self.outp("gla_o", [L, 8, 2, 128, 64])
        inp("peer_keysT", [L, 8, 2, 128, 128])
        inp("peer_uT", [L, 64, 128, 8, 256])
        inp("peer_v", [L, 64, 128, 2, D])
        y_d = self.outp("y", [T, D])
        self.ubf = Buf("ubf", nc.dram_tensor("peer_u_bf", [L, 64, 128, 8 * 256], BF16, kind="Internal").ap(), nsub=L * 16)
        self.vbf = Buf("vbf", nc.dram_tensor("peer_v_bf", [L, 64, 128, 2 * D], BF16, kind="Internal").ap(), nsub=L * 16)
        dbg = {}
        for name, shape in self.debug:
            dbg[name] = self.outp(name, shape, F32)

        with ExitStack() as es:
            self.S = S = Sched(nc, es)
            sb = lambda *a, **k: P.sb(es, *a, **k)
            x = sb("xres", [128, NB, D], F32, nsub=NB)
            ident = sb("identS", [128, 128], F32)
            ones = sb("onesS", [128, 128], F32)
            mcol = sb("mcol", [128, L, 48], F32, nsub=L)
            condT = sb("condTS", [128, 8], F32)
            ps = [Buf("ps%d" % i, es.enter_context(nc.psum_tensor("ps%d" % i, [128, 512], F32))) for i in range(8)]
            self.x, self.ps, self.ident, self.ones, self.mcol = x, ps, ident, ones, mcol
            iota = sb("iotaS", [128, 128], F32)
            P.I("pool", "iota", iota.v(), pattern=[[1, 128]], base=0, channel_multiplier=0, allow_small_or_imprecise_dtypes=True)
            bm = sb("bmS", [128, 8], F32)
            P.dma("sp", bm.v(), inp("bm", [128, 8])[:, :])
            self.iota, self.bm = iota, bm

            for tb in range(NB):
                P.dma("sp", x.v(s_[:, tb, :], tb), x_d[tb * 128:(tb + 1) * 128, :])
            P.dma("sp", ident.v(), ident_d[:, :])
            P.I("pool", "memset", ones.v(), 1.0)
            P.dma("sp", condT.v(), condT_d[:, :])

            with ExitStack() as es0:
                scond = P.sb(es0, "scond", [128, 8], F32)
                wa = [P.sb(es0, "wa%d" % i, [128, 8, 768], F32) for i in range(1)]
                badaT = P.sb(es0, "badaT", [128, L, 48], F32)
                P.I("act", "activation", scond.v(), condT.v(), AF.Silu)
                P.dma("sp", badaT.v(), b_adaT_d.rearrange("l p j -> p l j"))
                n = 0
                for l in range(L):
                    for cg in range(8):
                        pt = ps[cg % 2]
                        wt = wa[0]
                        P.dma("sp", wt.v(), w_ada_d[l, :, cg * 768:(cg + 1) * 768].rearrange("(k p) c -> p k c", p=128))
                        for j in range(6):
                            for k in range(8):
                                P.mm(pt.v(s_[:, j:j + 1]), wt.v(s_[:, k, j * 128:(j + 1) * 128]), scond.v(s_[:, k:k + 1]),
                                     start=(k == 0), stop=(k == 7))
                        P.I("dve", "tensor_tensor", mcol.v(s_[:, l, cg * 6:(cg + 1) * 6], l), pt.v(s_[:, 0:6]),
                            badaT.v(s_[:, l, cg * 6:(cg + 1) * 6]), op=ALU.add)
                    for a in (8, 32):
                        P.I("dve", "tensor_scalar_add", mcol.v(s_[:, l, a:a + 8], l), mcol.v(s_[:, l, a:a + 8], l), 1.0)
                S.barrier()
            if "mcol" in dbg:
                P.dma("pool", dbg["mcol"], mcol.v())

            for l in range(L):
                self.layer(l, dbg)
                if self.stop_after == ("layer", l):
                    break

            for tb in range(NB):
                P.dma("sp", y_d[tb * 128:(tb + 1) * 128, :], x.v(s_[:, tb, :], tb))
            S.finish()
            blk = es.enter_context(nc.Block())
            S.emit(blk)
        return nc

    def ln_block(self, tmp, src, dst):
        P = self
        st, mv, rstd, nmr = tmp
        P.I("dve", "bn_stats", st.v(s_[:, 0, :]), src.m(lambda a: a[:, 0:512]))
        P.I("dve", "bn_stats", st.v(s_[:, 1, :]), src.m(lambda a: a[:, 512:1024]))
        P.I("dve", "bn_aggr", mv.v(), st.v())
        P.I("act", "activation", rstd.v(), mv.v(s_[:, 1:2]), AF.Sqrt, bias=EPS, scale=1.0)
        P.I("dve", "reciprocal", rstd.v(), rstd.v())
        P.I("dve", "scalar_tensor_tensor", nmr.v(), mv.v(s_[:, 0:1]), -1.0, rstd.v(), op0=ALU.mult, op1=ALU.mult)
        P.I("act", "activation", dst, src, AF.Identity, bias=nmr.v(), scale=rstd.v())

    def ln_tmp(self, es, tag):
        return (self.sb(es, "st" + tag, [128, 2, 6]), self.sb(es, "mv" + tag, [128, 2]),
                self.sb(es, "rstd" + tag, [128, 1]), self.sb(es, "nmr" + tag, [128, 1]))

    def mod_to_uT(self, l, which):
        P = self; S = self.S
        x, uT, ps, mcol, ident = self.x, self.uT, self.ps, self.mcol, self.ident
        sh0 = 0 if which == 0 else 24
        sc0 = 8 if which == 0 else 32
        with ExitStack() as es:
            tmps = [self.ln_tmp(es, "m%d" % i) for i in range(2)]
            xn = [P.sb(es, "xn%d" % i, [128, D]) for i in range(2)]
            for tb in range(NB):
                xnb = xn[tb % 2]
                self.ln_block(tmps[tb % 2], x.v(s_[:, tb, :], tb), xnb.v())
                for half in range(2):
                    pt = ps[(tb * 2 + half) % 4]
                    for j in range(4):
                        k = half * 4 + j
                        P.I("pe", "transpose", pt.v(s_[:, j * 128:(j + 1) * 128]), xnb.v(s_[:, k * 128:(k + 1) * 128]), ident.v())
                    for j in range(4):
                        k = half * 4 + j
                        eng = "dve" if j % 2 == 0 else "pool"
                        eng = "dve"
                        P.I(eng, "tensor_scalar", uT.v(s_[:, k, tb * 128:(tb + 1) * 128], tb), pt.v(s_[:, j * 128:(j + 1) * 128]),
                            mcol.v(s_[:, l, sc0 + k:sc0 + k + 1], l), mcol.v(s_[:, l, sh0 + k:sh0 + k + 1], l), op0=ALU.mult, op1=ALU.add)
            S.barrier()

    def load_w(self, wb, l, c0, n):
        w_in_d = self.din["w_in"]
        self.dma("pool", wb.v(s_[:, :, 0:n]), w_in_d[l, :, c0:c0 + n].rearrange("(k p) c -> p k c", p=128))

    def proj_fm(self, wb, wc0, m, dst_fn, pbase=0):
        P = self
        for tg in range(4):
            pt = self.ps[self.psi % 8]; self.psi += 1
            for k in range(8):
                P.mm(pt.v(s_[0:m, :]), wb.v(s_[:, k, wc0:wc0 + m]), self.uT.v(s_[:, k, tg * 512:(tg + 1) * 512], range(tg * 4, tg * 4 + 4)),
                     start=(k == 0), stop=(k == 7))
            dst_fn(tg, pt.v(s_[0:m, :]))

    def proj_tm(self, wb, wc0, n, dst_fn):
        P = self
        for tb in range(NB):
            pt = self.ps[self.psi % 8]; self.psi += 1
            for k in range(8):
                P.mm(pt.v(s_[:, 0:n]), self.uT.v(s_[:, k, tb * 128:(tb + 1) * 128], tb), wb.v(s_[:, k, wc0:wc0 + n]),
                     start=(k == 0), stop=(k == 7))
            dst_fn(tb, pt.v(s_[:, 0:n]))


    def rope_evac(self, dst, pa, pb, cosv, sinv, tmpa, tmpb):
        P = self
        P.I("dve", "tensor_tensor", tmpa, pa, cosv, op=ALU.mult)
        P.I("dve", "tensor_tensor", tmpb, pb, sinv, op=ALU.mult)
        P.I("pool", "tensor_tensor", dst, tmpa, tmpb, op=ALU.add)

    def fm_group(self, wb, specs, tg):
        P = self
        outs = []
        for (wc0, m) in specs:
            pt = self.ps[self.psi % 6]; self.psi += 1
            for k in range(8):
                P.mm(pt.v(s_[0:m, :]), wb.v(s_[:, k, wc0:wc0 + m]), self.uT.v(s_[:, k, tg * 512:(tg + 1) * 512], range(tg * 4, tg * 4 + 4)),
                     start=(k == 0), stop=(k == 7))
            outs.append(pt.v(s_[0:m, :]))
        return outs

    def mla(self, l, dbg):
        P = self; S = self.S
        d = self.din
        ps, ident, ones = self.ps, self.ident, self.ones
        NK = 20
        with ExitStack() as es:
            wb = P.sb(es, "wbM", [128, 8, 448], BF16)
            wuq = P.sb(es, "wuq", [128, 2, 512], BF16)
            wukv = P.sb(es, "wukv", [128, 2, 256], BF16)
            gq = P.sb(es, "gq", [128, 2], F32)
            gkv = P.sb(es, "gkv", [128, 128], F32)
            cosT = P.sb(es, "cosM", [32, 512], F32)
            sinT = P.sb(es, "sinM", [32, 512], F32)
            biasM = P.sb(es, "biasM", [128, 8 * NK], F32)
            qnT = P.sb(es, "qnT", [128, 2, T], BF16, nsub=4)
            rs = P.sb(es, "qrs", [128, 512], F32)
            qno = P.sb(es, "qno", [64, T], BF16, nsub=4)
            qro = P.sb(es, "qro", [32, T], BF16, nsub=4)
            kno = P.sb(es, "kno", [64, 2560], BF16)
            kro = P.sb(es, "kro", [32, 2560], BF16)
            ckvT = P.sb(es, "ckvT", [128, 2560], BF16, nsub=NK)
            Vt = P.sb(es, "VtM", [128, NK, 4, 65], BF16, nsub=NK)
            ta = P.sb(es, "rta", [128, 512], F32)
            tb_ = P.sb(es, "rtb", [128, 512], F32)
            kvt = P.sb(es, "kvt", [128, 160], F32)
            ckt = [P.sb(es, "ckt%d" % i, [128, 128], F32) for i in range(2)]
            ss = P.sb(es, "kss", [128, 1], F32)
            junk = P.sb(es, "kjunk", [128, 128], F32)
            PT = [P.sb(es, "PT%d" % i, [128, 256], BF16) for i in range(2)]
            oacc = P.sb(es, "oacc", [128, NB, 128], BF16, nsub=NB)
            identb = P.sb(es, "identb", [128, 128], BF16)
            P.I("dve", "tensor_copy", identb.v(), ident.v())
            rden = P.sb(es, "rden", [128, 1], F32)
            cch = P.sb(es, "cch", [128, 4, 160], F32)
            self.load_w(wb, l, 0, 416)
            P.dma("pool", wb.v(s_[:, :, 416:448]), d["w_in"][l, :, 6080:6112].rearrange("(k p) c -> p k c", p=128))
            P.dma("pool", wuq.v(), d["w_uq"][l].rearrange("(k p) c -> p k c", p=128))
            for two in range(2):
                P.dma("pool", wukv.v(s_[:, two, :]).m(lambda a: a.rearrange("p (h c) -> p h c", h=4)), d["w_ukv"][l].rearrange("p (h two c) -> p two h c", h=4, two=2)[:, two, :, :])
            P.dma("sp", gq.v(), d["mla_q_norm"][l])
            P.dma("sp", gkv.v(), d["mla_kv_norm"][l].partition_broadcast(128))
            P.dma("sp", biasM.v(), d["bias_m"][:, :])
            P.I("pool", "memset", Vt.v(), 1.0)
            for tg in range(4):
                tsl = slice(tg * 512, (tg + 1) * 512)
                P.dma("sp", cosT.v(), d["rope_m"][0, :, tsl])
                P.dma("sp", sinT.v(), d["rope_m"][1, :, tsl])
                pq = self.fm_group(wb, [(0, 128), (128, 128)], tg)
                P.I("act", "activation", ta.v(), pq[0], AF.Square)
                P.I("act", "activation", tb_.v(), pq[1], AF.Square)
                pt = ps[self.psi % 6]; self.psi += 1
                P.mm(pt.v(), ones.v(), ta.v(), start=True, stop=False)
                P.mm(pt.v(), ones.v(), tb_.v(), start=False, stop=True)
                P.I("act", "activation", rs.v(), pt.v(), AF.Sqrt, bias=EPS, scale=1.0 / 256)
                P.I("dve", "reciprocal", rs.v(), rs.v())
                for c in range(2):
                    P.I("dve", "scalar_tensor_tensor", qnT.v(s_[:, c, tsl], tg), pq[c], gq.v(s_[:, c:c + 1]), rs.v(), op0=ALU.mult, op1=ALU.mult)
                pk = self.fm_group(wb, [(384, 32), (416, 32)], tg)
                self.rope_evac(kro.v(s_[:, 512 + tg * 512:512 + (tg + 1) * 512]), pk[0], pk[1], cosT.v(), sinT.v(),
                               ta.v(s_[0:32, :]), tb_.v(s_[0:32, :]))
            for tb in range(NB):
                pt = ps[self.psi % 6]; self.psi += 1
                for k in range(8):
                    P.mm(pt.v(s_[:, 0:160]), self.uT.v(s_[:, k, tb * 128:(tb + 1) * 128], tb), wb.v(s_[:, k, 256:416]), start=(k == 0), stop=(k == 7))
                P.I("act", "activation", kvt.v(), pt.v(s_[:, 0:160]), AF.Copy)
                ck = ckt[tb % 2]
                P.I("act", "activation", junk.v(), kvt.v(s_[:, 0:128]), AF.Square, accum_out=ss.v())
                P.I("act", "activation", ss.v(), ss.v(), AF.Sqrt, bias=EPS, scale=1.0 / 128)
                P.I("dve", "reciprocal", ss.v(), ss.v())
                P.I("dve", "scalar_tensor_tensor", ck.v(), kvt.v(s_[:, 0:128]), ss.v(), gkv.v(), op0=ALU.mult, op1=ALU.mult)
                P.dma("sp", self.dout["ckv_o"][l, tb * 128:(tb + 1) * 128, :], ck.v())
                P.dma("sp", self.dout["krope_o"][l, tb * 128:(tb + 1) * 128, :], kvt.v(s_[:, 128:160]))
                p2 = ps[self.psi % 6]; self.psi += 1
                P.I("pe", "transpose", p2.v(s_[:, 0:128]), ck.v(), ident.v())
                P.I("act", "activation", ckvT.v(s_[:, 512 + tb * 128:512 + (tb + 1) * 128], 4 + tb), p2.v(s_[:, 0:128]), AF.Copy)
            P.dma("sp", cch.v(s_[:, :, 0:128]), d["cache_ckv"][l].rearrange("(j p) c -> p j c", p=128))
            P.dma("sp", cch.v(s_[:, :, 128:160]), d["cache_krope"][l].rearrange("(j p) c -> p j c", p=128))
            for j in range(4):
                p2 = ps[self.psi % 6]; self.psi += 1
                P.I("pe", "transpose", p2.v(s_[:, 0:128]), cch.v(s_[:, j, 0:128]), ident.v())
                P.I("act", "activation", ckvT.v(s_[:, j * 128:(j + 1) * 128], j), p2.v(s_[:, 0:128]), AF.Copy)
                p3 = ps[self.psi % 6]; self.psi += 1
                P.I("pe", "transpose", p3.v(s_[0:32, 0:128]), cch.v(s_[:, j, 128:160]), ident.v())
                P.I("act", "activation", kro.v(s_[:, j * 128:(j + 1) * 128]), p3.v(s_[0:32, 0:128]), AF.Copy)
            for kc in range(NK):
                pt = ps[self.psi % 6]; self.psi += 1
                P.mm(pt.v(s_[:, 0:256]), ckvT.v(s_[:, kc * 128:(kc + 1) * 128], kc), wukv.v(s_[:, 1, :]), start=True, stop=True)
                P.I("dve", "tensor_copy", Vt.v(s_[:, kc, :, 0:64], kc), pt.v(s_[:, 0:256]).m(lambda a: a.rearrange("p (h c) -> p h c", h=4)))
            n = 0
            for h in range(4):
                for tg in range(4):
                    tsl = slice(tg * 512, (tg + 1) * 512)
                    P.dma("sp", cosT.v(), d["rope_m"][0, :, tsl])
                    P.dma("sp", sinT.v(), d["rope_m"][1, :, tsl])
                    pn = ps[self.psi % 6]; pa = ps[(self.psi + 1) % 6]; pb = ps[(self.psi + 2) % 6]; self.psi += 3
                    for c in range(2):
                        P.mm(pn.v(s_[0:64, :]), wuq.v(s_[:, c, h * 96:h * 96 + 64]), qnT.v(s_[:, c, tsl], tg), start=(c == 0), stop=(c == 1))
                    for c in range(2):
                        P.mm(pa.v(s_[0:32, :]), wuq.v(s_[:, c, h * 96 + 64:h * 96 + 96]), qnT.v(s_[:, c, tsl], tg), start=(c == 0), stop=(c == 1))
                    for c in range(2):
                        P.mm(pb.v(s_[0:32, :]), wuq.v(s_[:, c, 384 + h * 32:384 + h * 32 + 32]), qnT.v(s_[:, c, tsl], tg), start=(c == 0), stop=(c == 1))
                    P.I("act", "activation", qno.v(s_[:, tsl], tg), pn.v(s_[0:64, :]), AF.Copy)
                    self.rope_evac(qro.v(s_[:, tsl], tg), pa.v(s_[0:32, :]), pb.v(s_[0:32, :]), cosT.v(), sinT.v(),
                                   ta.v(s_[0:32, :]), tb_.v(s_[0:32, :]))
                for g5 in range(5):
                    gsl = slice(g5 * 512, (g5 + 1) * 512)
                    pt = ps[self.psi % 6]; self.psi += 1
                    P.mm(pt.v(s_[0:64, :]), wukv.v(s_[:, 0, h * 64:(h + 1) * 64]), ckvT.v(s_[:, gsl], range(g5 * 4, g5 * 4 + 4)), start=True, stop=True)
                    P.I("act", "activation", kno.v(s_[:, gsl]), pt.v(s_[0:64, :]), AF.Copy)
                for qu in range(8):
                    qsl = slice(qu * 256, (qu + 1) * 256)
                    po = [ps[6], ps[7]]
                    for kc in range(NK):
                        ksl = slice(kc * 128, (kc + 1) * 128)
                        pt = ps[self.psi % 6]; self.psi += 1
                        P.mm(pt.v(s_[:, 0:256]), kno.v(s_[:, ksl]), qno.v(s_[:, qsl], qu // 2), start=True, stop=False)
                        P.mm(pt.v(s_[:, 0:256]), kro.v(s_[:, ksl]), qro.v(s_[:, qsl], qu // 2), start=False, stop=True)
                        pT = PT[n % 2]; n += 1
                        P.I("act", "activation", pT.v(), pt.v(s_[:, 0:256]), AF.Exp, bias=biasM.v(s_[:, qu * NK + kc:qu * NK + kc + 1]), scale=MLA_SCALE)
                        for qb in range(2):
                            P.mm(po[qb].v(s_[:, 0:65]), pT.v(s_[:, qb * 128:(qb + 1) * 128]), Vt.v(s_[:, kc, h, :], kc), start=(kc == 0), stop=(kc == NK - 1))
                    for qb in range(2):
                        tb = qu * 2 + qb
                        P.I("dve", "reciprocal", rden.v(), po[qb].v(s_[:, 64:65]))
                        P.I("dve", "tensor_scalar_mul", oacc.v(s_[:, tb, (h % 2) * 64:(h % 2) * 64 + 64], tb), po[qb].v(s_[:, 0:64]), rden.v())
                if h % 2 == 1:
                    for tb in range(NB):
                        p2 = ps[self.psi % 6]; self.psi += 1
                        P.mm(p2.v(s_[:, 0:128]), oacc.v(s_[:, tb, :], tb), identb.v(), start=True, stop=True)
                        P.I("act", "activation", self.brT[0].v(s_[:, h // 2, tb * 128:(tb + 1) * 128], tb), p2.v(s_[:, 0:128]), AF.Copy)
            S.barrier()

    def gla_block(self, l, tb, wb, afT, qT, kT, w2, b2, onesb, ktm, e1t, spt, eq, ek, el, mats, vtm, kl, gcol, qeT, keT, LNQ):
        P = self; ps = self.ps
        lsl = slice((tb % 4) * 128, (tb % 4) * 128 + 128)
        bsl = slice(tb * 128, (tb + 1) * 128)
        pt = ps[self.psi % 6]; self.psi += 1
        for k in range(8):
            P.mm(pt.v(s_[:, 0:384]), self.uT.v(s_[:, k, bsl], tb), wb.v(s_[:, k, 128:512]), start=(k == 0), stop=(k == 7))
        P.I("dve", "tensor_copy", ktm.v(), pt.v(s_[:, 0:128]))
        P.I("dve", "tensor_copy", vtm.v(s_[:, tb, :], tb), pt.v(s_[:, 128:384]))
        for dr in range(2):
            pz = ps[self.psi % 6]; self.psi += 1
            P.mm(pz.v(s_[:, 0:128]), afT.v(s_[:, dr, lsl]), w2.v(s_[:, dr, :]), start=True, stop=False)
            P.mm(pz.v(s_[:, 0:128]), onesb.v(), b2.v(s_[:, dr, :]), start=False, stop=True)
            P.I("act", "activation", e1t.v(), pz.v(s_[:, 0:128]), AF.Exp, scale=-1.0)
            P.I("act", "activation", spt.v(), e1t.v(), AF.Ln, bias=1.0, scale=1.0)
            mi = 0 if dr == 0 else 2
            if self.G1 <= 3:
                continue
            pl = ps[self.psi % 6]; self.psi += 1
            P.mm(pl.v(s_[:, 0:128]), mats.v(s_[:, mi + 1, :]), spt.v(), start=True, stop=True)
            P.I("act", "activation", el.v(), pl.v(s_[:, 0:128]), AF.Exp)
            P.I("dve", "tensor_tensor", kl[dr].v(s_[:, tb, :], tb), ktm.v(), el.v(), op=ALU.mult)
            for hp in range(2):
                pc = ps[self.psi % 6]; pg = ps[(self.psi + 1) % 6]; self.psi += 2
                P.mm(pc.v(s_[0:64, 0:128]), spt.v(s_[:, hp * 64:(hp + 1) * 64]), mats.v(s_[:, mi, :]), start=True, stop=True)
                P.mm(pg.v(s_[0:64, 0:2]), spt.v(s_[:, hp * 64:(hp + 1) * 64]), mats.v(s_[:, 4, 0:2]), start=True, stop=True)
                P.I("act", "activation", eq.v(s_[:, hp, :]), pc.v(s_[0:64, 0:128]), AF.Exp, bias=LNQ, scale=1.0)
                P.I("act", "activation", ek.v(s_[:, hp, :]), pc.v(s_[0:64, 0:128]), AF.Exp, scale=-1.0)
                P.I("act", "activation", gcol.v(s_[:, dr, hp, 2 * tb:2 * tb + 2], dr), pg.v(s_[0:64, 0:2]), AF.Exp)
            P.I("dve", "tensor_tensor", qeT[dr].v(s_[:, :, bsl], tb), qT.v(s_[:, :, lsl]), eq.v(), op=ALU.mult)
            P.I("dve", "tensor_tensor", keT[dr].v(s_[:, :, bsl], tb), kT.v(s_[:, :, lsl]), ek.v(), op=ALU.mult)

    def gla(self, l, dbg):
        P = self; S = self.S
        d = self.din
        ps, ident, ones = self.ps, self.ident, self.ones
        LNQ = float(np.log(32.0 ** -0.5))
        with ExitStack() as es:
            qeT = [P.sb(es, "qeT%d" % i, [64, 2, T], BF16, nsub=NB) for i in range(2)]
            keT = [P.sb(es, "keT%d" % i, [64, 2, T], BF16, nsub=NB) for i in range(2)]
            kl = [P.sb(es, "kl%d" % i, [128, NB, 128], BF16, nsub=NB) for i in range(2)]
            gcol = P.sb(es, "gcol", [64, 2, 2, 32], F32, nsub=2)
            vtm = P.sb(es, "vtm", [128, NB, 256], BF16, nsub=NB)
            mats = P.sb(es, "gmats", [128, 5, 128], F32)
            gmask = P.sb(es, "gmask", [128, 2, 128], BF16)
            keep = P.sb(es, "gkeep", [128, 2, 32], F32)
            identb = P.sb(es, "identbG", [128, 128], BF16)
            P.dma("sp", mats.v(), d["gla_mats"].rearrange("a p c -> p a c"))
            P.dma("sp", gmask.v(), d["gla_mask"].rearrange("a p c -> p a c"))
            P.dma("sp", keep.v(), d["gla_keep"][:, :, :])
            P.I("dve", "tensor_copy", identb.v(), ident.v())
            with ExitStack() as e1:
                wb = P.sb(e1, "wbG", [128, 8, 800], BF16)
                afT = P.sb(e1, "afT", [16, 2, 512], BF16)
                qT = P.sb(e1, "gqT", [64, 2, 512], BF16)
                kT = P.sb(e1, "gkT", [64, 2, 512], BF16)
                w2 = P.sb(e1, "gw2", [16, 2, 128], BF16)
                b2 = P.sb(e1, "gb2", [1, 2, 128], BF16)
                onesb = P.sb(e1, "onesb", [1, 128], BF16)
                ktm = P.sb(e1, "ktm", [128, 128], F32)
                e1t = P.sb(e1, "ge1", [128, 128], F32)
                spt = P.sb(e1, "gsp", [128, 128], F32)
                eq = P.sb(e1, "geq", [64, 2, 128], F32)
                ek = P.sb(e1, "gek", [64, 2, 128], F32)
                el = P.sb(e1, "gel", [128, 128], F32)
                self.load_w(wb, l, 672, 800)
                P.dma("pool", w2.v(), d["w_gla_a"][l].rearrange("a r c -> r a c"))
                P.dma("pool", b2.v(), d["b_gla_a"][l:l + 1, :, :])
                P.I("pool", "memset", onesb.v(), 1.0)
                import os
                G1 = int(os.environ.get("KGLA_G1", "99"))
                for tg in range(4 if G1 >= 2 else 0):
                    tsl = slice(tg * 512, (tg + 1) * 512)
                    pq = self.fm_group(wb, [(0, 64), (64, 64), (128, 64), (192, 64)], tg)
                    P.I("act", "activation", qT.v(s_[:, 0, :]), pq[0], AF.Copy)
                    P.I("dve", "tensor_copy", qT.v(s_[:, 1, :]), pq[1])
                    P.I("act", "activation", kT.v(s_[:, 0, :]), pq[2], AF.Copy)
                    P.I("dve", "tensor_copy", kT.v(s_[:, 1, :]), pq[3])
                    pq = self.fm_group(wb, [(768, 16), (784, 16)], tg)
                    P.I("act", "activation", afT.v(s_[:, 0, :]), pq[0], AF.Copy)
                    P.I("dve", "tensor_copy", afT.v(s_[:, 1, :]), pq[1])
                    for tb in range(tg * 4, tg * 4 + 4) if G1 >= 3 else []:
                        self.G1 = G1
                        self.gla_block(l, tb, wb, afT, qT, kT, w2, b2, onesb, ktm, e1t, spt, eq, ek, el, mats, vtm, kl, gcol, qeT, keT, LNQ)
                S.barrier()
            import os
            GSTOP = int(os.environ.get("KGLA_STOP", "99"))
            if GSTOP <= 1:
                return
            with ExitStack() as e2:
                of = P.sb(e2, "gof", [128, NB, 256], F32, nsub=NB)
                St = P.sb(e2, "gS", [64, 2, 64], F32)
                tmpS = P.sb(e2, "gtmp", [64, 2, 64], F32)
                Sb = [P.sb(e2, "gSb%d" % i, [64, 2, 64], BF16) for i in range(2)]
                att = [P.sb(e2, "gatt%d" % i, [128, 128], BF16) for i in range(2)]
                na = 0
                for dr in range(2):
                    P.dma("sp", St.v(), d["gla_init"][l, dr].rearrange("(a p) c -> p a c", p=64))
                    blks = range(NB) if dr == 0 else range(NB - 1, -1, -1)
                    for tb in blks:
                        bsl = slice(tb * 128, (tb + 1) * 128)
                        halves = (0, 1) if dr == 0 else (1, 0)
                        for hf in halves:
                            n = 2 * tb + hf
                            r0 = hf * 64
                            P.I("act", "activation", Sb[hf].v(), St.v(), AF.Copy)
                            pu = ps[self.psi % 6]; self.psi += 1
                            for h in range(4):
                                P.mm(pu.v(s_[(h % 2) * 32:(h % 2) * 32 + 32, (h // 2) * 64:(h // 2) * 64 + 64]), kl[dr].v(s_[r0:r0 + 64, tb, h * 32:(h + 1) * 32], tb),
                                     vtm.v(s_[r0:r0 + 64, tb, h * 64:(h + 1) * 64], tb), start=True, stop=True)
                            for hp in range(2):
                                P.I("dve", "scalar_tensor_tensor", tmpS.v(s_[:, hp, :]), St.v(s_[:, hp, :]), gcol.v(s_[:, dr, hp, n:n + 1], dr), pu.v(s_[0:64, hp * 64:(hp + 1) * 64]), op0=ALU.mult, op1=ALU.add)
                            if (dr == 0 and n % 4 == 3) or (dr == 1 and n % 4 == 0):
                                P.dma("sp", self.dout["gla_o"][l, n // 4, dr].rearrange("(a p) c -> p a c", p=64), tmpS.v())
                            nn = n + 1 if dr == 0 else n - 1
                            if 0 <= nn < 32:
                                P.I("dve", "tensor_scalar_mul", St.v(), tmpS.v(), keep.v(s_[0:64, dr, nn:nn + 1]))
                        po = ps[6 + (tb % 2)]
                        for h in range(4):
                            hs = slice((h % 2) * 32, (h % 2) * 32 + 32); hp = h // 2
                            pa = ps[self.psi % 6]; self.psi += 1
                            P.mm(pa.v(s_[:, 0:128]), keT[dr].v(s_[hs, hp, bsl], tb), qeT[dr].v(s_[hs, hp, bsl], tb), start=True, stop=True)
                            at = att[na % 2]; na += 1
                            P.I("dve", "tensor_tensor", at.v(), pa.v(s_[:, 0:128]), gmask.v(s_[:, dr, :]), op=ALU.mult)
                            oc = slice(h * 64, (h + 1) * 64)
                            P.mm(po.v(s_[:, oc]), at.v(), vtm.v(s_[:, tb, oc], tb), start=True, stop=False)
                            P.mm(po.v(s_[0:64, oc]), qeT[dr].v(s_[hs, hp, tb * 128:tb * 128 + 64], tb), Sb[0].v(s_[hs, hp, :]), start=False, stop=False)
                            P.mm(po.v(s_[64:128, oc]), qeT[dr].v(s_[hs, hp, tb * 128 + 64:tb * 128 + 128], tb), Sb[1].v(s_[hs, hp, :]), start=False, stop=True)
                        if dr == 0:
                            P.I("act", "activation", of.v(s_[:, tb, :], tb), po.v(s_[:, 0:256]), AF.Copy)
                        else:
                            P.I("dve", "tensor_tensor", of.v(s_[:, tb, :], tb), of.v(s_[:, tb, :], tb), po.v(s_[:, 0:256]), op=ALU.add)
                if GSTOP <= 2:
                    S.barrier(); return
                wg = P.sb(e2, "wgG", [128, 8, 256], BF16)
                gn = P.sb(e2, "ggn", [128, 64], F32)
                ss4 = P.sb(e2, "gss", [128, 4], F32)
                junk = P.sb(e2, "gjunk", [128, 64], F32)
                sg = P.sb(e2, "gsil", [128, 256], F32)
                ob = P.sb(e2, "gob", [128, 256], BF16)
                self.load_w(wg, l, 1184, 256)
                P.dma("sp", gn.v(), d["gla_norm"][l].partition_broadcast(128))
                for tb in range(NB):
                    bsl = slice(tb * 128, (tb + 1) * 128)
                    for h in range(4):
                        P.I("act", "activation", junk.v(), of.v(s_[:, tb, h * 64:(h + 1) * 64], tb), AF.Square, accum_out=ss4.v(s_[:, h:h + 1]))
                    P.I("act", "activation", ss4.v(), ss4.v(), AF.Sqrt, bias=EPS, scale=1.0 / 64)
                    P.I("dve", "reciprocal", ss4.v(), ss4.v())
                    ov = of.v(s_[:, tb, :], tb).m(lambda a: a.rearrange("p (h c) -> p h c", h=4))
                    P.I("dve", "tensor_tensor", ov, ov, ss4.v().m(lambda a: a.unsqueeze(2).to_broadcast([128, 4, 64])), op=ALU.mult)
                    P.I("dve", "tensor_tensor", ov, ov, gn.v().m(lambda a: a.unsqueeze(1).to_broadcast([128, 4, 64])), op=ALU.mult)
                    pg = ps[self.psi % 6]; self.psi += 1
                    for k in range(8):
                        P.mm(pg.v(s_[:, 0:256]), self.uT.v(s_[:, k, bsl], tb), wg.v(s_[:, k, :]), start=(k == 0), stop=(k == 7))
                    P.I("act", "activation", sg.v(), pg.v(s_[:, 0:256]), AF.Silu)
                    P.I("dve", "tensor_tensor", ob.v(), of.v(s_[:, tb, :], tb), sg.v(), op=ALU.mult)
                    for c in range(2):
                        p2 = ps[self.psi % 6]; self.psi += 1
                        P.mm(p2.v(s_[:, 0:128]), ob.v(s_[:, c * 128:(c + 1) * 128]), identb.v(), start=True, stop=True)
                        P.I("act", "activation", self.brT[2].v(s_[:, c, bsl], tb), p2.v(s_[:, 0:128]), AF.Copy)
                S.barrier()

    def swa_kv_only(self, l):
        P = self; S = self.S
        ps = self.ps
        with ExitStack() as es:
            wb = P.sb(es, "wbS2", [128, 8, 256], BF16)
            kvo = [P.sb(es, "kvo2_%d" % i, [128, 256], F32) for i in range(2)]
            self.load_w(wb, l, 1728, 256)
            for tb in range(NB):
                pt = ps[self.psi % 6]; self.psi += 1
                for k in range(8):
                    P.mm(pt.v(s_[:, 0:256]), self.uT.v(s_[:, k, tb * 128:(tb + 1) * 128], tb), wb.v(s_[:, k, 0:256]), start=(k == 0), stop=(k == 7))
                ko = kvo[tb % 2]
                P.I("act", "activation", ko.v(), pt.v(s_[:, 0:256]), AF.Copy)
                P.dma("sp", self.dout["swakv_o"][l, tb * 128:(tb + 1) * 128, :], ko.v())
            S.barrier()

    def swa(self, l, dbg):
        P = self; S = self.S
        d = self.din
        ps, ident, ones = self.ps, self.ident, self.ones
        NK = 20
        with ExitStack() as es:
            wb = P.sb(es, "wbS", [128, 8, 896], BF16)
            cosT = P.sb(es, "cosS", [64, 512], F32)
            sinT = P.sb(es, "sinS", [64, 512], F32)
            biasS = P.sb(es, "biasS", [128, 4], F32)
            esink = P.sb(es, "esink", [128, 4], F32)
            msk = P.sb(es, "mskS", [128, NB, 2, 128], BF16)
            qT = P.sb(es, "qTS", [64, 4, T], BF16, nsub=4)
            kT = P.sb(es, "kTS", [64, 2, 2560], BF16)
            Vt = P.sb(es, "VtS", [128, NK, 2, 65], BF16, nsub=NK)
            ta = P.sb(es, "sta", [64, 512], F32)
            tb_ = P.sb(es, "stb", [64, 512], F32)
            kvo = [P.sb(es, "kvo%d" % i, [128, 256], F32) for i in range(2)]
            cch = P.sb(es, "cchS", [128, 4, 2, 64], F32)
            PT = [P.sb(es, "PTS%d" % i, [128, 256], BF16) for i in range(2)]
            oa = P.sb(es, "oaS", [128, 256], F32)
            rden = P.sb(es, "rdenS", [128, 1], F32)
            self.load_w(wb, l, 1472, 512)
            P.dma("pool", wb.v(s_[:, :, 512:896]), d["w_in"][l, :, 6112:6496].rearrange("(k p) c -> p k c", p=128))
            P.dma("sp", biasS.v(), d["bias_s"][:, :])
            P.dma("sp", esink.v(), d["swa_sink"][l])
            P.I("act", "activation", esink.v(), esink.v(), AF.Exp)
            P.dma("sp", msk.v(), d["mask_s"][:, :, :, :])
            P.I("pool", "memset", Vt.v(), 1.0)
            import os
            STOP = int(os.environ.get("KSWA_STOP", "99"))
            if STOP <= 1:
                S.barrier(); return
            for tg in range(4):
                tsl = slice(tg * 512, (tg + 1) * 512)
                P.dma("sp", cosT.v(), d["rope_s"][0, :, tsl])
                P.dma("sp", sinT.v(), d["rope_s"][1, :, tsl])
                for h in range(4):
                    pq = self.fm_group(wb, [(h * 64, 64), (512 + h * 64, 64)], tg)
                    self.rope_evac(qT.v(s_[:, h, tsl], h), pq[0], pq[1], cosT.v(), sinT.v(), ta.v(), tb_.v())
                for j in range(2):
                    pk = self.fm_group(wb, [(256 + j * 64, 64), (768 + j * 64, 64)], tg)
                    self.rope_evac(kT.v(s_[:, j, 512 + tg * 512:512 + (tg + 1) * 512]), pk[0], pk[1], cosT.v(), sinT.v(), ta.v(), tb_.v())
            if STOP <= 2:
                S.barrier(); return
            for tb in range(NB):
                pt = ps[self.psi % 6]; self.psi += 1
                for k in range(8):
                    P.mm(pt.v(s_[:, 0:256]), self.uT.v(s_[:, k, tb * 128:(tb + 1) * 128], tb), wb.v(s_[:, k, 256:512]), start=(k == 0), stop=(k == 7))
                ko = kvo[tb % 2]
                P.I("act", "activation", ko.v(), pt.v(s_[:, 0:256]), AF.Copy)
                P.dma("sp", self.dout["swakv_o"][l, tb * 128:(tb + 1) * 128, :], ko.v())
                P.I("dve", "tensor_copy", Vt.v(s_[:, 4 + tb, :, 0:64], 4 + tb), ko.v(s_[:, 128:256]).m(lambda a: a.rearrange("p (j c) -> p j c", j=2)))
            if STOP <= 3:
                S.barrier(); return
            for j in range(2):
                P.dma("sp", cch.v(s_[:, :, j, :]), d["cache_swak"][l, j].rearrange("(c p) e -> p c e", p=128))
            for c in range(4):
                for j in range(2):
                    p2 = ps[self.psi % 6]; self.psi += 1
                    P.I("pe", "transpose", p2.v(s_[0:64, 0:128]), cch.v(s_[:, c, j, :]), ident.v())
                    P.I("act", "activation", kT.v(s_[:, j, c * 128:(c + 1) * 128]), p2.v(s_[0:64, 0:128]), AF.Copy)
            cchv = P.sb(es, "cchV", [128, 4, 2, 64], F32)
            for j in range(2):
                P.dma("sp", cchv.v(s_[:, :, j, :]), d["cache_swav"][l, j].rearrange("(c p) e -> p c e", p=128))
            for c in range(4):
                P.I("dve", "tensor_copy", Vt.v(s_[:, c, :, 0:64], c), cchv.v(s_[:, c, :, :]))
            n = 0
            import os
            for tb in range(NB if "noatt" not in os.environ.get("KSKIP", "") else 0):
                qsl = slice(tb * 128, (tb + 1) * 128)
                for j in range(2):
                    po = [ps[6], ps[7]]
                    kcs = [(4 + tb + dd, dd) for dd in (-1, 0, 1) if 0 <= tb + dd < NB] + [(c, 2) for c in range(4)]
                    for i, (kc, kind) in enumerate(kcs):
                        ksl = slice(kc * 128, (kc + 1) * 128)
                        pt = ps[self.psi % 6]; self.psi += 1
                        for g in range(2):
                            P.mm(pt.v(s_[:, g * 128:(g + 1) * 128]), kT.v(s_[:, j, ksl]), qT.v(s_[:, 2 * j + g, qsl], 2 * j + g), start=True, stop=True)
                        pT = PT[n % 2]; n += 1
                        if kind == 2:
                            P.I("act", "activation", pT.v(), pt.v(s_[:, 0:256]), AF.Exp, bias=biasS.v(s_[:, 0:1]), scale=SWA_SCALE)
                        else:
                            P.I("act", "activation", pT.v(), pt.v(s_[:, 0:256]), AF.Exp, scale=SWA_SCALE)
                            if kind != 0:
                                mi = 0 if kind == -1 else 1
                                for g in range(2):
                                    P.I("dve", "tensor_tensor", pT.v(s_[:, g * 128:(g + 1) * 128]), pT.v(s_[:, g * 128:(g + 1) * 128]), msk.v(s_[:, tb, mi, :]), op=ALU.mult)
                        for g in range(2):
                            P.mm(po[g].v(s_[:, 0:65]), pT.v(s_[:, g * 128:(g + 1) * 128]), Vt.v(s_[:, kc, j, :], kc), start=(i == 0), stop=(i == len(kcs) - 1))
                    for g in range(2):
                        hh = 2 * j + g
                        P.I("dve", "tensor_tensor", rden.v(), po[g].v(s_[:, 64:65]), esink.v(s_[:, hh:hh + 1]), op=ALU.add)
                        P.I("dve", "reciprocal", rden.v(), rden.v())
                        P.I("dve", "tensor_scalar_mul", oa.v(s_[:, hh * 64:(hh + 1) * 64]), po[g].v(s_[:, 0:64]), rden.v())
                for c in range(2):
                    p2 = ps[self.psi % 6]; self.psi += 1
                    P.I("pe", "transpose", p2.v(s_[:, 0:128]), oa.v(s_[:, c * 128:(c + 1) * 128]), ident.v())
                    P.I("act", "activation", self.brT[3].v(s_[:, c, tb * 128:(tb + 1) * 128], tb), p2.v(s_[:, 0:128]), AF.Copy)
            S.barrier()

    def fnet(self, l):
        P = self; S = self.S
        d = self.din
        with ExitStack() as es:
            wb = P.sb(es, "wbF", [128, 8, 256], BF16)
            fT = P.sb(es, "fT", [128, 2, T], BF16, nsub=4)
            cd = P.sb(es, "cdft", [128, 2, 128], BF16)
            A = P.sb(es, "fA", [128, NB, 256], BF16, nsub=NB)
            B = P.sb(es, "fB", [128, NB, 256], BF16, nsub=NB)
            tc_ = [P.sb(es, "dc%d" % i, [128, 512], BF16) for i in range(4)]
            ts_ = [P.sb(es, "ds%d" % i, [128, 512], BF16) for i in range(4)]
            self.load_w(wb, l, 416, 256)
            P.dma("sp", cd.v(), d["cdft"].rearrange("a p c -> p a c"))
            for c in range(2):
                def ev(tg, pv, c=c):
                    P.I("act", "activation", fT.v(s_[:, c, tg * 512:(tg + 1) * 512], tg), pv, AF.Copy)
                self.proj_fm(wb, c * 128, 128, ev)
            for tb in range(NB):
                pa = self.ps[self.psi % 8]; self.psi += 1
                for c in range(2):
                    P.mm(pa.v(s_[:, c * 128:(c + 1) * 128]), fT.v(s_[:, c, tb * 128:(tb + 1) * 128], tb // 4), cd.v(s_[:, 0, :]), start=True, stop=True)
                    P.mm(pa.v(s_[:, 256 + c * 128:256 + (c + 1) * 128]), fT.v(s_[:, c, tb * 128:(tb + 1) * 128], tb // 4), cd.v(s_[:, 1, :]), start=True, stop=True)
                P.I("act", "activation", A.v(s_[:, tb, :], tb), pa.v(s_[:, 0:256]), AF.Copy)
                P.I("act", "activation", B.v(s_[:, tb, :], tb), pa.v(s_[:, 256:512]), AF.Copy)
            n = 0
            for tg in range(4):
                p0 = self.ps[self.psi % 8]; p1 = self.ps[(self.psi + 1) % 8]; self.psi += 2
                for tb in range(NB):
                    ct = tc_[n % 4]; st = ts_[n % 4]; n += 1
                    P.dma("sp", ct.v(), d["dft_c"][tb * 128:(tb + 1) * 128, tg * 512:(tg + 1) * 512])
                    P.dma("act", st.v(), d["dft_s"][tb * 128:(tb + 1) * 128, tg * 512:(tg + 1) * 512])
                    for c, pp in ((0, p0), (1, p1)):
                        P.mm(pp.v(), A.v(s_[:, tb, c * 128:(c + 1) * 128], tb), ct.v(), start=(tb == 0), stop=False)
                        P.mm(pp.v(), B.v(s_[:, tb, c * 128:(c + 1) * 128], tb), st.v(), start=False, stop=(tb == NB - 1))
                for c, pp in ((0, p0), (1, p1)):
                    P.I("act" if c == 0 else "dve", "activation" if c == 0 else "tensor_copy", self.brT[1].v(s_[:, c, tg * 512:(tg + 1) * 512], range(tg * 4, tg * 4 + 4)),
                        pp.v(), *((AF.Copy,) if c == 0 else ()))
            S.barrier()

    def merge(self, l, dbg):
        P = self; S = self.S
        d = self.din
        x, uT, ps, mcol, ident, ones = self.x, self.uT, self.ps, self.mcol, self.ident, self.ones
        with ExitStack() as es:
            wbr = P.sb(es, "wbr", [128, 8, D], BF16)
            wo = P.sb(es, "wo", [128, 8, D], BF16)
            wg = [P.sb(es, "wg%d" % i, [128, 8, 512], BF16) for i in range(1)]
            G = P.sb(es, "Gacc", [128, 512], F32)
            GT = P.sb(es, "GT", [128, 8, 512], BF16, nsub=8)
            sg = [P.sb(es, "sg%d" % i, [128, 512], F32) for i in range(2)]
            gbc = P.sb(es, "g1bc", [128, D], F32)
            lng = P.sb(es, "ln1g", [128, D], F32)
            lnb = P.sb(es, "ln1b", [128, D], F32)
            dg = P.sb(es, "dgm", [128, 128], F32)
            xt = [P.sb(es, "xt%d" % i, [128, D], F32) for i in range(1)]
            tmps = [self.ln_tmp(es, "g%d" % i) for i in range(2)]
            P.dma("pool", wbr.v(), d["w_branch"][l].rearrange("b (k p) d -> p (b k) d", p=128))
            P.dma("pool", wo.v(), d["w_out"][l].rearrange("(k p) d -> p k d", p=128))
            P.dma("sp", lng.v(), d["ln"][l, 0, :].partition_broadcast(128))
            P.dma("sp", lnb.v(), d["ln"][l, 1, :].partition_broadcast(128))
            for k in range(8):
                P.I("dve", "tensor_scalar_mul", dg.v(), ident.v(), mcol.v(s_[:, l, 16 + k:17 + k], l))
                pt = ps[self.psi % 8]; self.psi += 1
                P.mm(pt.v(s_[:, 0:128]), ones.v(), dg.v(), start=True, stop=True)
                P.I("act", "activation", gbc.v(s_[:, k * 128:(k + 1) * 128]), pt.v(s_[:, 0:128]), AF.Copy)
            n = 0
            for tg in range(4):
                tsub = range(tg * 4, tg * 4 + 4)
                tsl = slice(tg * 512, (tg + 1) * 512)
                for dc in range(8):
                    w = wg[0]; n += 1
                    P.dma("pool", w.v(), d["w_gate"][l, dc])
                    for b in range(4):
                        pg = ps[self.psi % 8]; pp = ps[(self.psi + 1) % 8]; self.psi += 2
                        for k in range(8):
                            P.mm(pg.v(), w.v(s_[:, k, b * 128:(b + 1) * 128]), uT.v(s_[:, k, tsl], tsub), start=(k == 0), stop=(k == 7))
                        for kc in range(2):
                            P.mm(pp.v(), wbr.v(s_[:, b * 2 + kc, dc * 128:(dc + 1) * 128]), self.brT[b].v(s_[:, kc, tsl], tsub),
                                 start=(kc == 0), stop=(kc == 1))
                        sgt = sg[b % 2]
                        P.I("act", "activation", sgt.v(), pg.v(), AF.Sigmoid)
                        if b == 0:
                            P.I("dve", "tensor_tensor", G.v(), sgt.v(), pp.v(), op=ALU.mult)
                        else:
                            P.I("dve", "tensor_tensor", sgt.v(), sgt.v(), pp.v(), op=ALU.mult)
                            if b < 3:
                                P.I("pool", "tensor_tensor", G.v(), G.v(), sgt.v(), op=ALU.add)
                            else:
                                P.I("pool", "tensor_tensor", GT.v(s_[:, dc, :], dc), G.v(), sgt.v(), op=ALU.add)
                for j in range(4):
                    tb = tg * 4 + j
                    xtb = xt[0]
                    for hf in range(2):
                        pm = ps[self.psi % 8]; self.psi += 1
                        for k in range(8):
                            P.mm(pm.v(), GT.v(s_[:, k, j * 128:(j + 1) * 128], k), wo.v(s_[:, k, hf * 512:(hf + 1) * 512]), start=(k == 0), stop=(k == 7))
                        hs = slice(hf * 512, (hf + 1) * 512)
                        P.I("dve", "tensor_tensor", xtb.v(s_[:, hs]), pm.v(), gbc.v(s_[:, hs]), op=ALU.mult)
                        P.I("dve", "scalar_tensor_tensor", xtb.v(s_[:, hs]), x.v(s_[:, tb, hs], tb), ALPHA, xtb.v(s_[:, hs]), op0=ALU.mult, op1=ALU.add)
                    self.ln_block(tmps[tb % 2], xtb.v(), x.v(s_[:, tb, :], tb))
                    P.I("pool", "tensor_tensor", x.v(s_[:, tb, :], tb), x.v(s_[:, tb, :], tb), lng.v(), op=ALU.mult)
                    P.I("pool", "tensor_tensor", x.v(s_[:, tb, :], tb), x.v(s_[:, tb, :], tb), lnb.v(), op=ALU.add)
            S.barrier()

    def post_ffn(self, l, yacc_is_x=True):
        P = self; S = self.S
        d = self.din
        x = self.x
        with ExitStack() as es:
            lng = P.sb(es, "ln2g", [128, D], F32)
            lnb = P.sb(es, "ln2b", [128, D], F32)
            xt = [P.sb(es, "xq%d" % i, [128, D], F32) for i in range(2)]
            tmps = [self.ln_tmp(es, "q%d" % i) for i in range(2)]
            P.dma("sp", lng.v(), d["ln"][l, 2, :].partition_broadcast(128))
            P.dma("sp", lnb.v(), d["ln"][l, 3, :].partition_broadcast(128))
            for tb in range(NB):
                xtb = xt[tb % 2]
                P.I("act", "activation", xtb.v(), x.v(s_[:, tb, :], tb), AF.Copy)
                self.ln_block(tmps[tb % 2], xtb.v(), x.v(s_[:, tb, :], tb))
                P.I("pool", "tensor_tensor", x.v(s_[:, tb, :], tb), x.v(s_[:, tb, :], tb), lng.v(), op=ALU.mult)
                P.I("pool", "tensor_tensor", x.v(s_[:, tb, :], tb), x.v(s_[:, tb, :], tb), lnb.v(), op=ALU.add)
            S.barrier()

    def layer(self, l, dbg):
        P = self; S = self.S
        self.psi = 0
        for g4 in range(16):
            P.dma("pool", self.ubf.v(s_[l, g4 * 4:(g4 + 1) * 4], l * 16 + g4), self.din["peer_uT"][l, g4 * 4:(g4 + 1) * 4].rearrange("g p k e -> g p (k e)"))
            P.dma("pool", self.vbf.v(s_[l, g4 * 4:(g4 + 1) * 4], l * 16 + g4), self.din["peer_v"][l, g4 * 4:(g4 + 1) * 4].rearrange("g p j d -> g p (j d)"))
        with ExitStack() as esl:
            self.uT = P.sb(esl, "uT", [128, 8, T], BF16, nsub=NB)
            self.mod_to_uT(l, 0)
            self.brT = [P.sb(esl, "brT%d" % b, [128, 2, T], BF16, nsub=NB) for b in range(4)]
            import os
            skip = os.environ.get("KSKIP", "")
            for b, nm in ((0, "mla"), (3, "swa"), (1, "fnet"), (2, "gla")):
                if nm in skip:
                    for tb4 in range(4):
                        P.I("pool", "memset", self.brT[b].v(s_[:, :, tb4 * 512:(tb4 + 1) * 512], range(tb4 * 4, tb4 * 4 + 4)), 0.0)
            if "mla" not in skip:
                self.mla(l, dbg)
            if "swa" not in skip:
                self.swa(l, dbg)
            else:
                self.swa_kv_only(l)
            if "fnet" not in skip:
                self.fnet(l)
            if "gla" not in skip:
                self.gla(l, dbg)
            self.merge(l, dbg)
            S.barrier()
        import os
        if "peer" in os.environ.get("KSKIP", ""):
            for tb in range(NB):
                P.I("act", "activation", self.x.v(s_[:, tb, :], tb), self.x.v(s_[:, tb, :], tb), AF.Copy, scale=ALPHA)
        else:
            self.peer(l, dbg)
        self.post_ffn(l)

    def peer(self, l, dbg):
        P = self; S = self.S
        d = self.din
        x, ps, mcol, ident, ones, iota, bm = self.x, self.ps, self.mcol, self.ident, self.ones, self.iota, self.bm
        NCH = 128
        TBS = 256
        with ExitStack() as es:
            u2Ts = [P.sb(es, "u2T%d" % i, [128, 8, TBS], BF16, nsub=2) for i in range(2)]
            lt = self.ln_tmp(es, "P")
            wq = P.sb(es, "wq", [128, 8, 256], BF16)
            kT = P.sb(es, "keysT", [128, 16, 128], BF16)
            gbc = P.sb(es, "g2bc", [128, D], F32)
            dg = P.sb(es, "dg2", [128, 128], F32)
            qpT = P.sb(es, "qpT", [128, 16, 128], BF16, nsub=16)
            sc = P.sb(es, "psc", [128, 16, 128], F32, nsub=16)
            vtop = P.sb(es, "vtop", [128, 16, 16], F32, nsub=16)
            itop = P.sb(es, "itop", [128, 16, 16], U32, nsub=16)
            idx1f = P.sb(es, "idx1f", [128, 128], F32)
            idx2f = P.sb(es, "idx2f", [128, 128], F32)
            idxTs = P.sb(es, "idxT", [128, 2, 2, 128], F32, nsub=2)
            cand = P.sb(es, "cand", [128, 8, 256], F32, nsub=8)
            t8a = P.sb(es, "t8a", [128, 8, 8], F32, nsub=8)
            t8b = P.sb(es, "t8b", [128, 8, 8], F32, nsub=8)
            nmx = P.sb(es, "nmx", [128, 8], F32)
            zz = P.sb(es, "pz", [128, 8], F32)
            wCTs = P.sb(es, "wCT", [128, 2, 128, 16], BF16, nsub=2)
            O1 = P.sb(es, "O1", [128, 4, 128], BF16)
            O2 = P.sb(es, "O2", [128, 4, 128], BF16)
            Cbd = P.sb(es, "Cbd", [128, 4, 128], BF16)
            tmpS = [P.sb(es, "ptmp%d" % i, [128, 4, 128], BF16) for i in range(2)]
            WtT = P.sb(es, "WtT", [128, TBS, 128], BF16, nsub=TBS // 4)
            Ut = [P.sb(es, "Ut%d" % i, [128, 8, 256], BF16) for i in range(2)]
            Vt = [P.sb(es, "Vt%d" % i, [128, 2, D], BF16) for i in range(2)]
            actS = [P.sb(es, "pact%d" % i, [128, TBS], BF16) for i in range(2)]
            GS = [P.sb(es, "pG%d" % i, [128, TBS], BF16) for i in range(2)]
            P.dma("pool", kT.v(), d["peer_keysT"][l].rearrange("h q c k -> c (h q) k"))
            for k in range(8):
                P.I("dve", "tensor_scalar_mul", dg.v(), ident.v(), mcol.v(s_[:, l, 40 + k:41 + k], l))
                pt = ps[self.psi % 4]; self.psi += 1
                P.mm(pt.v(s_[:, 0:128]), ones.v(), dg.v(), start=True, stop=True)
                P.I("act", "activation", gbc.v(s_[:, k * 128:(k + 1) * 128]), pt.v(s_[:, 0:128]), AF.Copy)
            py = [ps[4], ps[5], ps[6], ps[7]]
            esc = lambda a: a.rearrange("p (h a) b -> p h (a b)", a=2)
            def sel_a(sb_):
                u2T = u2Ts[sb_ % 2]
                for sub in range(2):
                    tb = sb_ * 2 + sub
                    usl = slice(sub * 128, (sub + 1) * 128)
                    xnv = cand.v(s_[:, 0:4, :], [0, 1, 2, 3]).m(lambda a: a.rearrange("p a b -> p (a b)"))
                    self.ln_block(lt, x.v(s_[:, tb, :], tb), xnv)
                    yield
                    for half in range(2):
                        pt = ps[2 + self.psi % 2]; self.psi += 1
                        for j in range(4):
                            k = half * 4 + j
                            P.I("pe", "transpose", pt.v(s_[:, j * 128:(j + 1) * 128]), xnv.m(lambda a, k=k: a[:, k * 128:(k + 1) * 128]), ident.v())
                        for j in range(4):
                            k = half * 4 + j
                            P.I("dve", "tensor_scalar", u2T.v(s_[:, k, usl], sub), pt.v(s_[:, j * 128:(j + 1) * 128]),
                                mcol.v(s_[:, l, 32 + k:33 + k], l), mcol.v(s_[:, l, 24 + k:25 + k], l), op0=ALU.mult, op1=ALU.add)
                    for c4 in range(4):
                        pt = ps[2 + self.psi % 2]; self.psi += 1
                        for j in range(4):
                            c = c4 * 4 + j
                            if j % 2 == 0:
                                P.dma("pool", wq.v(), d["w_peer_q"][l, c // 2])
                            for k in range(8):
                                P.mm(pt.v(s_[:, j * 128:(j + 1) * 128]), wq.v(s_[:, k, (j % 2) * 128:(j % 2) * 128 + 128]), u2T.v(s_[:, k, usl], sub), start=(k == 0), stop=(k == 7))
                        P.I("act", "activation", qpT.v(s_[:, c4 * 4:(c4 + 1) * 4, :], range(c4 * 4, c4 * 4 + 4)), pt.v().m(lambda a: a.rearrange("p (j t) -> p j t", j=4)), AF.Copy)
                        yield
                    for c4 in range(4):
                        pt = ps[2 + self.psi % 2]; self.psi += 1
                        for j in range(4):
                            c = c4 * 4 + j
                            P.mm(pt.v(s_[:, j * 128:(j + 1) * 128]), qpT.v(s_[:, c, :], c), kT.v(s_[:, c, :]), start=True, stop=True)
                        P.I("act", "activation", sc.v(s_[:, c4 * 4:(c4 + 1) * 4, :], range(c4 * 4, c4 * 4 + 4)), pt.v().m(lambda a: a.rearrange("p (j t) -> p j t", j=4)), AF.Copy)
                        yield
                    wkc = lambda c: cand.v(s_[:, c // 2, (c % 2) * 128:(c % 2) * 128 + 128], c // 2)
                    for c in range(16):
                        P.I("dve", "max", vtop.v(s_[:, c, 0:8], c), sc.v(s_[:, c, :], c))
                    yield
                    for c in range(16):
                        P.I("dve", "max_index", itop.v(s_[:, c, 0:8], c), vtop.v(s_[:, c, 0:8], c), sc.v(s_[:, c, :], c))
                    yield
                    for c in range(16):
                        P.I("dve", "match_replace", wkc(c), vtop.v(s_[:, c, 0:8], c), sc.v(s_[:, c, :], c), -1e30)
                    yield
                    for c in range(16):
                        P.I("dve", "max", vtop.v(s_[:, c, 8:16], c), wkc(c))
                    yield
                    for c in range(16):
                        P.I("dve", "max_index", itop.v(s_[:, c, 8:16], c), vtop.v(s_[:, c, 8:16], c), wkc(c))
                    yield
                    v4 = lambda a: a.rearrange("p (h q) r -> p h q r", q=2)
                    P.I("dve", "tensor_copy", idx1f.v().m(lambda a: a.rearrange("p (h r) -> p h r", h=8)), itop.v().m(lambda a: v4(a)[:, :, 0, :]))
                    P.I("dve", "tensor_copy", idx2f.v().m(lambda a: a.rearrange("p (h r) -> p h r", h=8)), itop.v().m(lambda a: v4(a)[:, :, 1, :]))
                    P.I("dve", "tensor_tensor", cand.v().m(lambda a: a.rearrange("p h (a b) -> p h a b", a=16)),
                        vtop.v().m(lambda a: v4(a)[:, :, 0, :].unsqueeze(3).to_broadcast([128, 8, 16, 16])),
                        vtop.v().m(lambda a: v4(a)[:, :, 1, :].unsqueeze(2).to_broadcast([128, 8, 16, 16])), op=ALU.add)
                    yield
                    wkh = lambda h: sc.v(s_[:, 2 * h:2 * h + 2, :], [2 * h, 2 * h + 1]).m(lambda a: a.rearrange("p a b -> p (a b)"))
                    for h in range(8):
                        P.I("dve", "max", t8a.v(s_[:, h, :], h), cand.v(s_[:, h, :], h))
                    for h in range(8):
                        P.I("dve", "match_replace", wkh(h), t8a.v(s_[:, h, :], h), cand.v(s_[:, h, :], h), -1e30)
                    for h in range(8):
                        P.I("dve", "max", t8b.v(s_[:, h, :], h), wkh(h))
                    yield
                    P.I("dve", "tensor_scalar_mul", nmx.v(), t8a.v(s_[:, :, 0]), -1.0)
                    for h in range(8):
                        ev = sc.v(s_[:, 2 * h:2 * h + 2, :], [2 * h, 2 * h + 1]).m(lambda a: a.rearrange("p a b -> p (a b)"))
                        P.I("act", "activation", ev, cand.v(s_[:, h, :], h), AF.Exp, bias=nmx.v(s_[:, h:h + 1]), scale=1.0)
                        P.I("dve", "scalar_tensor_tensor", ev, cand.v(s_[:, h, :], h), t8b.v(s_[:, h, 7:8], h), ev, op0=ALU.is_ge, op1=ALU.mult)
                    yield
                    P.I("dve", "tensor_reduce", zz.v(), sc.v().m(esc), axis=AX.X, op=ALU.add)
                    P.I("dve", "reciprocal", zz.v(), zz.v())
                    P.I("dve", "tensor_tensor", sc.v().m(esc), sc.v().m(esc), zz.v().m(lambda a: a.unsqueeze(2).to_broadcast([128, 8, 256])), op=ALU.mult)
                    yield
                    pt = ps[2 + self.psi % 2]; self.psi += 1
                    P.I("pe", "transpose", pt.v(s_[:, 0:128]), idx1f.v(), ident.v())
                    P.I("pe", "transpose", pt.v(s_[:, 128:256]), idx2f.v(), ident.v())
                    P.I("act", "activation", idxTs.v(s_[:, sub], sub), pt.v(s_[:, 0:256]).m(lambda a: a.rearrange("p (a t) -> p a t", a=2)), AF.Copy)
                    for r4 in range(4):
                        yield
                        pt = ps[2 + self.psi % 2]; self.psi += 1
                        for j in range(4):
                            r2 = r4 * 4 + j
                            P.I("pe", "transpose", pt.v(s_[:, j * 128:(j + 1) * 128]),
                                sc.v().m(lambda a, r2=r2: a.rearrange("p c (a b) -> p (c a) b", b=16)[:, :, r2]), ident.v())
                        P.I("act", "activation", wCTs.v(s_[:, sub, :, r4 * 4:(r4 + 1) * 4], sub).m(lambda a: a.rearrange("p t j -> p j t")),
                            pt.v().m(lambda a: a.rearrange("p (j t) -> p j t", j=4)), AF.Copy)

                yield
            def expand(sb_):
                for sub in range(2):
                    for sbk in range(32):
                        t0 = sbk * 4
                        P.I("dve", "tensor_tensor", O1.v(), iota.v().m(lambda a: a.unsqueeze(1).to_broadcast([128, 4, 128])),
                            idxTs.v(s_[:, sub, 0, t0:t0 + 4], sub).m(lambda a: a.unsqueeze(2).to_broadcast([128, 4, 128])), op=ALU.is_equal)
                        P.I("dve", "tensor_tensor", O2.v(), iota.v().m(lambda a: a.unsqueeze(1).to_broadcast([128, 4, 128])),
                            idxTs.v(s_[:, sub, 1, t0:t0 + 4], sub).m(lambda a: a.unsqueeze(2).to_broadcast([128, 4, 128])), op=ALU.is_equal)
                        P.I("pool", "tensor_tensor", Cbd.v().m(lambda a: a.rearrange("p t (h r) -> p t h r", h=8)),
                            wCTs.v(s_[:, sub, t0:t0 + 4, :], sub).m(lambda a: a.unsqueeze(2).to_broadcast([128, 4, 8, 16])),
                            bm.v().m(lambda a: a.unsqueeze(1).unsqueeze(3).to_broadcast([128, 4, 8, 16])), op=ALU.mult)
                        for g4 in range(1):
                            pt = ps[self.psi % 4]; self.psi += 1
                            tS = tmpS[sbk % 2]
                            for j in range(4):
                                tt = g4 * 4 + j
                                P.mm(pt.v(s_[:, j * 128:(j + 1) * 128]), Cbd.v(s_[:, tt, :]), O1.v(s_[:, tt, :]), start=True, stop=True)
                            P.I("act", "activation", tS.v(), pt.v().m(lambda a: a.rearrange("p (j i) -> p j i", j=4)), AF.Copy)
                            pt2 = ps[self.psi % 4]; self.psi += 1
                            for j in range(4):
                                tt = g4 * 4 + j
                                P.mm(pt2.v(s_[:, j * 128:(j + 1) * 128]), O2.v(s_[:, tt, :]), tS.v(s_[:, j, :]), start=True, stop=True)
                            ta = sub * 128 + t0 + g4 * 4
                            P.I("dve" if sbk % 2 == 0 else "act", "tensor_copy" if sbk % 2 == 0 else "activation", WtT.v(s_[:, ta:ta + 4, :], ta // 4),
                                pt2.v().m(lambda a: a.rearrange("p (j i) -> p j i", j=4)), *(() if sbk % 2 == 0 else (AF.Copy,)))

            def expert(sb_, gen):
                u2T = u2Ts[sb_ % 2]
                def emitU(c):
                    c2, j = c // 2, c % 2
                    if j == 0:
                        ut = Ut[c2 % 2]; vt = Vt[c2 % 2]
                        P.dma("sp", ut.v().m(lambda a: a.rearrange("p k e -> p (k e)")), self.ubf.v(s_[l, c2], l * 16 + c2 // 4))
                        P.dma("sp", vt.v().m(lambda a: a.rearrange("p j d -> p (j d)")), self.vbf.v(s_[l, c2], l * 16 + c2 // 4))
                    ut = Ut[c2 % 2]
                    pa = ps[c % 2]
                    for k in range(8):
                        P.mm(pa.v(s_[:, 0:TBS]), ut.v(s_[:, k, j * 128:(j + 1) * 128]), u2T.v(s_[:, k, :]), start=(k == 0), stop=(k == 7))
                def emitMV(c):
                    c2, j = c // 2, c % 2
                    vt = Vt[c2 % 2]
                    pa = ps[c % 2]
                    aS = actS[c % 2]; gS = GS[c % 2]
                    P.I("act", "activation", aS.v(), pa.v(s_[:, 0:TBS]), AF.Gelu)
                    P.I("dve", "tensor_tensor", gS.v(), aS.v(), WtT.v(s_[:, :, c]), op=ALU.mult)
                    for sub in range(2):
                        for hf in range(2):
                            P.mm(py[sub * 2 + hf].v(), gS.v(s_[:, sub * 128:(sub + 1) * 128]), vt.v(s_[:, j, hf * 512:(hf + 1) * 512]), start=(c == 0), stop=(c == NCH - 1))
                emitU(0)
                for c in range(NCH):
                    if c + 1 < NCH:
                        emitU(c + 1)
                    emitMV(c)
                    if gen is not None and c % 2 == 1:
                        next(gen, None)
                if gen is not None:
                    for _ in gen:
                        pass

            def finalize(sb_):
                for sub in range(2):
                    tb = sb_ * 2 + sub
                    for hf in range(2):
                        hs = slice(hf * 512, (hf + 1) * 512)
                        yv = cand.v(s_[:, 0:2, :], [0, 1]).m(lambda a: a.rearrange("p a b -> p (a b)"))
                        P.I("dve", "tensor_tensor", yv, py[sub * 2 + hf].v(), gbc.v(s_[:, hs]), op=ALU.mult)
                        P.I("dve", "scalar_tensor_tensor", x.v(s_[:, tb, hs], tb), x.v(s_[:, tb, hs], tb), ALPHA, yv, op0=ALU.mult, op1=ALU.add)

            NSB = T // TBS
            for _ in sel_a(0):
                pass
            expand(0)
            for sb_ in range(NSB):
                gen = sel_a(sb_ + 1) if sb_ + 1 < NSB else None
                expert(sb_, gen)
                finalize(sb_)
                if sb_ + 1 < NSB:
                    expand(sb_ + 1)
            S.barrier()

def _bf(a):
    return np.ascontiguousarray(a).astype(ml_dtypes.bfloat16)


def host_consts(kind):
    c = {}
    c["ident"] = np.eye(128, dtype=np.float32)
    c["bm"] = np.ascontiguousarray((np.arange(128)[:, None] // 16 == np.arange(8)[None, :]).astype(np.float32))
    seqlen = T if kind == "sample" else 256
    n = np.arange(seqlen)
    ang = 2.0 * np.pi * np.outer(n, n) / seqlen
    sc = 1.0 / np.sqrt(seqlen * 64.0)
    cb = np.cos(ang) * sc
    sbm = -np.sin(ang) * sc
    Cf = np.zeros((T, T), np.float64)
    Sf = np.zeros((T, T), np.float64)
    for i in range(T // seqlen):
        sl = slice(i * seqlen, (i + 1) * seqlen)
        Cf[sl, sl] = cb
        Sf[sl, sl] = sbm
    c["dft_c"] = _bf(Cf.astype(np.float32))
    c["dft_s"] = _bf(Sf.astype(np.float32))
    m = np.arange(64)
    a2 = 2.0 * np.pi * np.outer(m, m) / 64.0
    cc = np.zeros((2, 128, 128), np.float64)
    for g in range(2):
        cc[0, g * 64:(g + 1) * 64, g * 64:(g + 1) * 64] = np.cos(a2)
        cc[1, g * 64:(g + 1) * 64, g * 64:(g + 1) * 64] = np.sin(a2)
    c["cdft"] = _bf(cc.astype(np.float32))
    t = np.arange(T)
    rows = (t // 64).astype(np.float64); cols = (t % 64).astype(np.float64)
    for nm, R in (("rope_m", 32), ("rope_s", 64)):
        half = R // 2; q = R // 4
        tab = np.zeros((2, R, T), np.float64)
        for dd in range(R):
            pos = rows if dd < half else cols
            fi = dd % q
            freq = 10000.0 ** (-(2.0 * fi) / half)
            ang = pos * freq
            if kind == "sample":
                tab[0, dd] = np.cos(ang)
                tab[1, dd] = np.sin(ang) * (-1.0 if (dd // q) % 2 == 0 else 1.0)
            else:
                tab[0, dd] = 1.0
        c[nm] = np.ascontiguousarray(tab.astype(np.float32))
    bm_ = np.zeros((160,), np.float32)
    if kind == "prompt":
        for qu in range(8):
            for kc in range(20):
                ok = kc >= 4 and (kc - 4) // 2 == qu
                bm_[qu * 20 + kc] = 0.0 if ok else NEG
    c["bias_m"] = np.ascontiguousarray(np.broadcast_to(bm_[None, :], (128, 160)))
    bs_ = np.zeros((128, 4), np.float32)
    if kind == "prompt":
        bs_[:, 0] = NEG
    c["bias_s"] = bs_
    mk = np.zeros((128, NB, 2, 128), np.float32)
    kk = np.arange(128)[:, None]; qq = np.arange(128)[None, :]
    for tb in range(NB):
        if kind == "sample":
            mk[:, tb, 0, :] = (kk >= qq)
            mk[:, tb, 1, :] = (kk <= qq)
        else:
            mk[:, tb, 0, :] = 1.0 if tb % 2 == 1 else 0.0
            mk[:, tb, 1, :] = 1.0 if tb % 2 == 0 else 0.0
    c["mask_s"] = _bf(mk)
    tt = np.arange(128)[:, None]; tp = np.arange(128)[None, :]
    same = (tt // 64) == (tp // 64)
    cc_ = -1.0 / 16.0
    gm = np.zeros((5, 128, 128), np.float32)
    gm[0] = cc_ * (same & (tt <= tp))
    gm[1] = cc_ * (same & (tt > tp))
    gm[2] = cc_ * (same & (tt >= tp))
    gm[3] = cc_ * (same & (tt < tp))
    gm[4, :, 0] = cc_ * (np.arange(128) < 64)
    gm[4, :, 1] = cc_ * (np.arange(128) >= 64)
    c["gla_mats"] = gm
    c["gla_mask"] = _bf(np.stack([(same & (tt <= tp)), (same & (tt >= tp))]).astype(np.float32))
    kp = np.ones((128, 2, 32), np.float32)
    if kind == "prompt":
        for n_ in range(32):
            if n_ % 4 == 0:
                kp[:, 0, n_] = 0.0
            if n_ % 4 == 3:
                kp[:, 1, n_] = 0.0
    c["gla_keep"] = kp
    return c


def perm_swap(R):
    q = R // 4
    return np.array([d + q if (d // q) % 2 == 0 else d - q for d in range(R)])


def host_weights(inp):
    w = {}
    w["w_ada"] = np.ascontiguousarray(inp["w_ada"], dtype=np.float32)
    w["b_adaT"] = np.ascontiguousarray(inp["b_ada"].reshape(L, 48, 128).transpose(0, 2, 1), dtype=np.float32)
    w_in = np.asarray(inp["w_in"], dtype=np.float32)
    p32 = perm_swap(32); p64 = perm_swap(64)
    kr = w_in[:, :, 384:416][:, :, p32]
    sq = w_in[:, :, 1472:1728].reshape(L, D, 4, 64)[:, :, :, p64].reshape(L, D, 256)
    sk = w_in[:, :, 1728:1856].reshape(L, D, 2, 64)[:, :, :, p64].reshape(L, D, 128)
    w["w_in"] = np.ascontiguousarray(np.concatenate([w_in, kr, sq, sk], axis=2))
    w["w_gate"] = np.ascontiguousarray(w_in[:, :, 1984:6080].reshape(L, 8, 128, 4, 8, 128).transpose(0, 4, 2, 1, 3, 5).reshape(L, 8, 128, 8, 512))
    w["w_branch"] = np.ascontiguousarray(inp["w_branch"], dtype=np.float32)
    w["w_out"] = np.ascontiguousarray(inp["w_out"], dtype=np.float32)
    w_uq = np.asarray(inp["w_uq"], dtype=np.float32)
    uq_sw = w_uq.reshape(L, 256, 4, 96)[:, :, :, 64:96][:, :, :, p32].reshape(L, 256, 128)
    w["w_uq"] = np.ascontiguousarray(np.concatenate([w_uq, uq_sw], axis=2))
    w["w_ukv"] = np.ascontiguousarray(inp["w_ukv"], dtype=np.float32)
    w["mla_q_norm"] = np.ascontiguousarray(np.asarray(inp["mla_q_norm"], dtype=np.float32).reshape(L, 2, 128).transpose(0, 2, 1))
    w["mla_kv_norm"] = np.ascontiguousarray(inp["mla_kv_norm"], dtype=np.float32)
    w["w_gla_a"] = np.ascontiguousarray(np.stack([inp["w_gla_a_fwd"], inp["w_gla_a_bwd"]], axis=1), dtype=np.float32)
    w["b_gla_a"] = np.ascontiguousarray(np.stack([inp["b_gla_a_fwd"], inp["b_gla_a_bwd"]], axis=1), dtype=np.float32)
    w["gla_norm"] = np.ascontiguousarray(inp["gla_norm"], dtype=np.float32)
    w["swa_sink"] = np.ascontiguousarray(np.broadcast_to(np.asarray(inp["swa_sink"], dtype=np.float32)[:, None, :], (L, 128, 4)))
    w["w_peer_q"] = np.ascontiguousarray(np.asarray(inp["w_peer_q"], dtype=np.float32).reshape(L, 8, 128, 8, 256).transpose(0, 3, 2, 1, 4))
    w["peer_keysT"] = np.ascontiguousarray(np.asarray(inp["peer_keys"], dtype=np.float32).transpose(0, 1, 2, 4, 3))
    w["peer_uT"] = np.ascontiguousarray(np.asarray(inp["peer_u"], dtype=np.float32).reshape(L, 64, 256, 8, 128).transpose(0, 1, 4, 3, 2))
    w["peer_v"] = np.ascontiguousarray(np.asarray(inp["peer_v"], dtype=np.float32).reshape(L, 64, 2, 128, D).transpose(0, 1, 3, 2, 4))
    w["ln"] = np.ascontiguousarray(np.stack([inp["ln1_g"], inp["ln1_b"], inp["ln2_g"], inp["ln2_b"]], axis=1), dtype=np.float32)
    return w


def core_inputs(inp, core, W, CS, CP):
    m = dict(W)
    if core < 2:
        m.update(CS)
        m["x"] = np.ascontiguousarray(inp["x_sample"][core], dtype=np.float32)
        cond = np.asarray(inp["c"][core], dtype=np.float32)
        m["cache_ckv"] = np.ascontiguousarray(inp["cache_mla_ckv"][core], dtype=np.float32)
        m["cache_krope"] = np.ascontiguousarray(inp["cache_mla_krope"][core], dtype=np.float32)
        m["cache_swak"] = np.ascontiguousarray(inp["cache_swa_k"][core], dtype=np.float32)
        m["cache_swav"] = np.ascontiguousarray(inp["cache_swa_v"][core], dtype=np.float32)
        m["gla_init"] = np.ascontiguousarray(np.asarray(inp["state_gla"][core], dtype=np.float32).reshape(L, 2, 128, 64))
    else:
        m.update(CP)
        j = core - 2 if core < 6 else 0
        m["x"] = np.ascontiguousarray(np.asarray(inp["x_prompt"][8 * j:8 * j + 8], dtype=np.float32).reshape(T, D))
        cond = np.asarray(inp["c_ctx"], dtype=np.float32)
        m["cache_ckv"] = np.zeros((L, 512, 128), np.float32)
        m["cache_krope"] = np.zeros((L, 512, 32), np.float32)
        m["cache_swak"] = np.zeros((L, 2, 512, 64), np.float32)
        m["cache_swav"] = np.zeros((L, 2, 512, 64), np.float32)
        m["gla_init"] = np.zeros((L, 2, 128, 64), np.float32)
    m["condT"] = np.ascontiguousarray(cond.reshape(8, 128).T)
    return m


_CACHE = {}


def kernel(**inputs):
    cores = inputs.pop("_cores", list(range(8)))
    debug = inputs.pop("_debug", None)
    stop_after = inputs.pop("_stop_after", None)
    prog = Prog(debug=debug, stop_after=stop_after)
    nc = prog.build()
    W = host_weights(inputs)
    CS = host_consts("sample")
    CP = host_consts("prompt")
    in_maps = [core_inputs(inputs, c, W, CS, CP) for c in cores]
    import os as _os
    if _os.environ.get("KTRACE"):
        res = run_bass_kernel_spmd(nc, in_maps, core_ids=list(range(len(cores))), trace=True)
        print("EXEC_TIME_NS", res.exec_time_ns)
        globals()["_LAST_RES"] = res
    else:
        res = run_bass_kernel_spmd(nc, in_maps, core_ids=list(range(len(cores))))
    R = res.results
    if debug is not None:
        return R
    y_sample = np.stack([R[0]["y"], R[1]["y"]], axis=0)
    y_prompt = np.concatenate([R[2 + j]["y"].reshape(8, 256, D) for j in range(4)], axis=0)
    ckv = np.concatenate([R[2 + j]["ckv_o"].reshape(L, 8, 256, 128).transpose(1, 0, 2, 3) for j in range(4)], axis=0)
    kr = np.concatenate([R[2 + j]["krope_o"].reshape(L, 8, 256, 32).transpose(1, 0, 2, 3) for j in range(4)], axis=0)
    kvs = [R[2 + j]["swakv_o"].reshape(L, 8, 256, 2, 2, 64) for j in range(4)]
    sk = np.concatenate([a[:, :, :, 0].transpose(1, 0, 3, 2, 4) for a in kvs], axis=0)
    sv = np.concatenate([a[:, :, :, 1].transpose(1, 0, 3, 2, 4) for a in kvs], axis=0)
    gl = np.concatenate([R[2 + j]["gla_o"].reshape(L, 8, 2, 4, 32, 64).transpose(1, 0, 2, 3, 4, 5) for j in range(4)], axis=0)
    f = lambda a: np.ascontiguousarray(a, dtype=np.float32)
    return (f(y_prompt), f(y_sample), f(ckv), f(kr), f(sk), f(sv), f(gl))
```

```python
import numpy as np
import ml_dtypes
from contextlib import ExitStack
import concourse.bass as bass
import concourse.mybir as mybir
from concourse.bass_utils import run_bass_kernel_spmd

F32 = mybir.dt.float32
BF16 = mybir.dt.bfloat16
U32 = mybir.dt.uint32
AF = mybir.ActivationFunctionType
ALU = mybir.AluOpType
AX = mybir.AxisListType
s_ = np.s_

ENGS = ("pe", "act", "dve", "pool", "sp")
SAME_ENGINE_SYNC = True
import os as _os0
SES_ALL = not bool(_os0.environ.get("KNOSES"))

T = 2048
NB = 16
D = 1024
L = 2
ALPHA = (2.0 * L) ** 0.25
EPS = 1e-6
NEG = -30000.0
MLA_SCALE = 96.0 ** -0.5
SWA_SCALE = 64.0 ** -0.5
WIN_EXT = 6496


class V:
    def __init__(self, ap, toks):
        self.ap = ap
        self.toks = toks

    def m(self, fn):
        return V(fn(self.ap), self.toks)


class Buf:
    def __init__(self, name, t, nsub=1):
        self.name = name
        self.t = t
        self.nsub = nsub

    def tok(self, subs=None):
        if subs is None:
            return [(self.name, s) for s in range(self.nsub)]
        if isinstance(subs, int):
            subs = [subs]
        return [(self.name, s) for s in subs]

    def v(self, key=None, subs=None):
        ap = self.t[:] if key is None else self.t[key]
        return V(ap, self.tok(subs))


class Sched:
    def __init__(self, nc, es, nd=24):
        self.nc = nc
        self.ops = {e: [] for e in ENGS}
        self.cnt = {e: 0 for e in ENGS}
        self.known = {e: {} for e in ENGS}
        self.nd = nd
        self.dma_tot = [0] * nd
        self.dma_rr = 0
        self.last_w = {}
        self.readers = {}
        self.sem = {e: es.enter_context(nc.semaphore("sem_" + e)) for e in ENGS if e != "sp"}
        self.dsem = [es.enter_context(nc.semaphore("dsem%d" % i)) for i in range(nd)]
        self.milestones = {e: set() for e in ENGS}

    def _need(self, eng, dep, waits):
        kind, key, val = dep
        if kind == "eng" and key == eng:
            if eng in ("pe", "sp") or (eng in ("act", "dve") and not SES_ALL) or not SAME_ENGINE_SYNC:
                return
        k = (kind, key)
        if self.known[eng].get(k, 0) >= val:
            return
        self.known[eng][k] = val
        waits.append((kind, key, val))
        if kind == "eng":
            self.milestones[key].add(val)

    def _deps(self, eng, reads, writes):
        waits = []
        for t in reads:
            lw = self.last_w.get(t)
            if lw is not None:
                self._need(eng, lw, waits)
        for t in writes:
            lw = self.last_w.get(t)
            if lw is not None:
                self._need(eng, lw, waits)
            for r in self.readers.get(t, ()):
                self._need(eng, r, waits)
        return waits

    def _commit(self, me, reads, writes):
        for t in reads:
            self.readers.setdefault(t, []).append(me)
        for t in writes:
            self.last_w[t] = me
            self.readers[t] = []

    def op(self, eng, fn, reads=(), writes=()):
        reads = list(reads); writes = list(writes)
        waits = self._deps(eng, reads, writes)
        self.cnt[eng] += 1
        me = ("eng", eng, self.cnt[eng])
        self.ops[eng].append((waits, fn, ("eng", self.cnt[eng])))
        self._commit(me, reads, writes)

    def dma(self, eng, fn, reads=(), writes=()):
        reads = list(reads); writes = list(writes)
        i = self.dma_rr
        self.dma_rr = (i + 1) % self.nd
        waits = []
        if self.dma_tot[i] > 0:
            self._need(eng, ("dma", i, self.dma_tot[i]), waits)
        waits += self._deps(eng, reads, writes)
        self.dma_tot[i] += 16
        me = ("dma", i, self.dma_tot[i])
        self.cnt[eng] += 1
        self.ops[eng].append((waits, fn, ("dma", i)))
        self._commit(me, reads, writes)

    def _last_seq(self, e):
        for w, fn, inc in reversed(self.ops[e]):
            if inc is not None and inc[0] == "eng":
                return inc[1]
        return 0

    def barrier(self):
        lasts = {e: self._last_seq(e) for e in ENGS}
        for e in ENGS:
            waits = []
            for e2 in ENGS:
                if e2 != e and e2 != "sp" and lasts[e2] > 0:
                    self._need(e, ("eng", e2, lasts[e2]), waits)
            for i in range(self.nd):
                if self.dma_tot[i] > 0:
                    self._need(e, ("dma", i, self.dma_tot[i]), waits)
            if waits:
                self.ops[e].append((waits, None, None))
        self.last_w = {}
        self.readers = {}

    def finish(self):
        self.barrier()

    def emit(self, blk):
        rank = {}
        for e in ENGS:
            ms = sorted(self.milestones[e])
            rank[e] = {s: i + 1 for i, s in enumerate(ms)}

        def run(e, eng):
            for waits, fn, inc in self.ops[e]:
                for kind, key, val in waits:
                    if kind == "eng":
                        eng.wait_ge(self.sem[key], rank[key][val])
                    else:
                        eng.wait_ge(self.dsem[key], val)
                if fn is None:
                    continue
                ins = fn(eng)
                if inc[0] == "dma":
                    ins.then_inc(self.dsem[inc[1]], 16)
                elif inc[1] in rank[e]:
                    ins.then_inc(self.sem[e], 1)

        blk.sync(lambda eng: run("sp", eng))
        blk.scalar(lambda eng: run("act", eng))
        blk.vector(lambda eng: run("dve", eng))
        blk.gpsimd(lambda eng: run("pool", eng))
        blk.tensor(lambda eng: run("pe", eng))


class Prog:
    def __init__(self, debug=None, stop_after=None):
        self.debug = debug or []
        self.stop_after = stop_after
        self.nc = bass.Bass("TRN2", target_bir_lowering=False)
        self.din = {}
        self.dout = {}

    def inp(self, name, shape, dt=F32):
        self.din[name] = self.nc.dram_tensor(name, list(shape), dt, kind="ExternalInput").ap()
        return self.din[name]

    def outp(self, name, shape, dt=F32):
        self.dout[name] = self.nc.dram_tensor(name, list(shape), dt, kind="ExternalOutput").ap()
        return self.dout[name]

    def sb(self, es, name, shape, dt=F32, nsub=1):
        self.uid = getattr(self, "uid", 0) + 1
        name = "%s_u%d" % (name, self.uid)
        return Buf(name, es.enter_context(self.nc.sbuf_tensor(name, list(shape), dt)), nsub)

    def I(self, eng, meth, out, *args, **kw):
        def conv(a):
            return a.ap if isinstance(a, V) else a
        reads = []
        writes = list(out.toks)
        for a in list(args) + list(kw.values()):
            if isinstance(a, V):
                reads += a.toks
        if "accum_out" in kw:
            writes += kw["accum_out"].toks
        a2 = [conv(a) for a in args]
        k2 = {k: conv(v) for k, v in kw.items()}
        o = out.ap
        self.S.op(eng, lambda e: getattr(e, meth)(o, *a2, **k2), reads, writes)

    def dma(self, q, out, in_):
        reads = in_.toks if isinstance(in_, V) else []
        writes = out.toks if isinstance(out, V) else []
        o = out.ap if isinstance(out, V) else out
        i = in_.ap if isinstance(in_, V) else in_
        self.S.dma(q, lambda e: e.dma_start(out=o, in_=i), reads, writes)

    def mm(self, out, lhsT, rhs, start, stop):
        self.I("pe", "matmul", out, lhsT=lhsT, rhs=rhs, start=start, stop=stop)

    def build(self):
        nc = self.nc
        P = self
        inp = self.inp
        x_d = inp("x", [T, D])
        condT_d = inp("condT", [128, 8])
        w_ada_d = inp("w_ada", [L, D, 6 * D])
        b_adaT_d = inp("b_adaT", [L, 128, 48])
        w_in_d = inp("w_in", [L, D, WIN_EXT])
        w_branch_d = inp("w_branch", [L, 4, 256, D])
        w_out_d = inp("w_out", [L, D, D])
        inp("w_gate", [L, 8, 128, 8, 512])
        ln_d = inp("ln", [L, 4, D])
        dft_c_d = inp("dft_c", [T, T], BF16)
        dft_s_d = inp("dft_s", [T, T], BF16)
        cdft_d = inp("cdft", [2, 128, 128], BF16)
        ident_d = inp("ident", [128, 128])
        inp("w_peer_q", [L, 8, 128, 8, 256])
        inp("w_uq", [L, 256, 512]); inp("w_ukv", [L, 128, 512]); inp("mla_q_norm", [L, 128, 2]); inp("mla_kv_norm", [L, 128])
        inp("gla_mats", [5, 128, 128]); inp("gla_mask", [2, 128, 128], BF16); inp("gla_keep", [128, 2, 32]); inp("gla_init", [L, 2, 128, 64])
        inp("w_gla_a", [L, 2, 16, 128]); inp("b_gla_a", [L, 2, 128]); inp("gla_norm", [L, 64])
        inp("rope_m", [2, 32, T]); inp("rope_s", [2, 64, T]); inp("bias_m", [128, 160]); inp("bias_s", [128, 4])
        inp("mask_s", [128, NB, 2, 128], BF16); inp("swa_sink", [L, 128, 4])
        inp("cache_ckv", [L, 512, 128]); inp("cache_krope", [L, 512, 32]); inp("cache_swak", [L, 2, 512, 64]); inp("cache_swav", [L, 2, 512, 64])
        self.outp("ckv_o", [L, T, 128]); self.outp("krope_o", [L, T, 32]); self.outp("swakv_o", [L, T, 256]); self.outp("gla_o", [L, 8, 2, 128, 64])
        inp("peer_keysT", [L, 8, 2, 128, 128])
        inp("peer_uT", [L, 64, 128, 8, 256])
        inp("peer_v", [L, 64, 128, 2, D])
        y_d = self.outp("y", [T, D])
        self.ubf = Buf("ubf", nc.dram_tensor("peer_u_bf", [L, 64, 128, 8 * 256], BF16, kind="Internal").ap(), nsub=L * 16)
        self.vbf = Buf("vbf", nc.dram_tensor("peer_v_bf", [L, 64, 128, 2 * D], BF16, kind="Internal").ap(), nsub=L * 16)
        dbg = {}
        for name, shape in self.debug:
            dbg[name] = self.outp(name, shape, F32)

        with ExitStack() as es:
            self.S = S = Sched(nc, es)
            sb = lambda *a, **k: P.sb(es, *a, **k)
            x = sb("xres", [128, NB, D], F32, nsub=NB)
            ident = sb("identS", [128, 128], F32)
            ones = sb("onesS", [128, 128], F32)
            mcol = sb("mcol", [128, L, 48], F32, nsub=L)
            condT = sb("condTS", [128, 8], F32)
            ps = [Buf("ps%d" % i, es.enter_context(nc.psum_tensor("ps%d" % i, [128, 512], F32))) for i in range(8)]
            self.x, self.ps, self.ident, self.ones, self.mcol = x, ps, ident, ones, mcol
            iota = sb("iotaS", [128, 128], F32)
            P.I("pool", "iota", iota.v(), pattern=[[1, 128]], base=0, channel_multiplier=0, allow_small_or_imprecise_dtypes=True)
            bm = sb("bmS", [128, 8], F32)
            P.dma("sp", bm.v(), inp("bm", [128, 8])[:, :])
            self.iota, self.bm = iota, bm

            for tb in range(NB):
                P.dma("sp", x.v(s_[:, tb, :], tb), x_d[tb * 128:(tb + 1) * 128, :])
            P.dma("sp", ident.v(), ident_d[:, :])
            P.I("pool", "memset", ones.v(), 1.0)
            P.dma("sp", condT.v(), condT_d[:, :])

            with ExitStack() as es0:
                scond = P.sb(es0, "scond", [128, 8], F32)
                wa = [P.sb(es0, "wa%d" % i, [128, 8, 768], F32) for i in range(1)]
                badaT = P.sb(es0, "badaT", [128, L, 48], F32)
                P.I("act", "activation", scond.v(), condT.v(), AF.Silu)
                P.dma("sp", badaT.v(), b_adaT_d.rearrange("l p j -> p l j"))
                n = 0
                for l in range(L):
                    for cg in range(8):
                        pt = ps[cg % 2]
                        wt = wa[0]
                        P.dma("sp", wt.v(), w_ada_d[l, :, cg * 768:(cg + 1) * 768].rearrange("(k p) c -> p k c", p=128))
                        for j in range(6):
                            for k in range(8):
                                P.mm(pt.v(s_[:, j:j + 1]), wt.v(s_[:, k, j * 128:(j + 1) * 128]), scond.v(s_[:, k:k + 1]),
                                     start=(k == 0), stop=(k == 7))
                        P.I("dve", "tensor_tensor", mcol.v(s_[:, l, cg * 6:(cg + 1) * 6], l), pt.v(s_[:, 0:6]),
                            badaT.v(s_[:, l, cg * 6:(cg + 1) * 6]), op=ALU.add)
                    for a in (8, 32):
                        P.I("dve", "tensor_scalar_add", mcol.v(s_[:, l, a:a + 8], l), mcol.v(s_[:, l, a:a + 8], l), 1.0)
                S.barrier()
            if "mcol" in dbg:
                P.dma("pool", dbg["mcol"], mcol.v())

            for l in range(L):
                self.layer(l, dbg)
                if self.stop_after == ("layer", l):
                    break

            for tb in range(NB):
                P.dma("sp", y_d[tb * 128:(tb + 1) * 128, :], x.v(s_[:, tb, :], tb))
            S.finish()
            blk = es.enter_context(nc.Block())
            S.emit(blk)
        return nc

    def ln_block(self, tmp, src, dst):
        P = self
        st, mv, rstd, nmr = tmp
        P.I("dve", "bn_stats", st.v(s_[:, 0, :]), src.m(lambda a: a[:, 0:512]))
        P.I("dve", "bn_stats", st.v(s_[:, 1, :]), src.m(lambda a: a[:, 512:1024]))
        P.I("dve", "bn_aggr", mv.v(), st.v())
        P.I("act", "activation", rstd.v(), mv.v(s_[:, 1:2]), AF.Sqrt, bias=EPS, scale=1.0)
        P.I("dve", "reciprocal", rstd.v(), rstd.v())
        P.I("dve", "scalar_tensor_tensor", nmr.v(), mv.v(s_[:, 0:1]), -1.0, rstd.v(), op0=ALU.mult, op1=ALU.mult)
        P.I("act", "activation", dst, src, AF.Identity, bias=nmr.v(), scale=rstd.v())

    def ln_tmp(self, es, tag):
        return (self.sb(es, "st" + tag, [128, 2, 6]), self.sb(es, "mv" + tag, [128, 2]),
                self.sb(es, "rstd" + tag, [128, 1]), self.sb(es, "nmr" + tag, [128, 1]))

    def mod_to_uT(self, l, which):
        P = self; S = self.S
        x, uT, ps, mcol, ident = self.x, self.uT, self.ps, self.mcol, self.ident
        sh0 = 0 if which == 0 else 24
        sc0 = 8 if which == 0 else 32
        with ExitStack() as es:
            tmps = [self.ln_tmp(es, "m%d" % i) for i in range(2)]
            xn = [P.sb(es, "xn%d" % i, [128, D]) for i in range(2)]
            for tb in range(NB):
                xnb = xn[tb % 2]
                self.ln_block(tmps[tb % 2], x.v(s_[:, tb, :], tb), xnb.v())
                for half in range(2):
                    pt = ps[(tb * 2 + half) % 4]
                    for j in range(4):
                        k = half * 4 + j
                        P.I("pe", "transpose", pt.v(s_[:, j * 128:(j + 1) * 128]), xnb.v(s_[:, k * 128:(k + 1) * 128]), ident.v())
                    for j in range(4):
                        k = half * 4 + j
                        eng = "dve" if j % 2 == 0 else "pool"
                        eng = "dve"
                        P.I(eng, "tensor_scalar", uT.v(s_[:, k, tb * 128:(tb + 1) * 128], tb), pt.v(s_[:, j * 128:(j + 1) * 128]),
                            mcol.v(s_[:, l, sc0 + k:sc0 + k + 1], l), mcol.v(s_[:, l, sh0 + k:sh0 + k + 1], l), op0=ALU.mult, op1=ALU.add)
            S.barrier()

    def load_w(self, wb, l, c0, n):
        w_in_d = self.din["w_in"]
        self.dma("pool", wb.v(s_[:, :, 0:n]), w_in_d[l, :, c0:c0 + n].rearrange("(k p) c -> p k c", p=128))

    def proj_fm(self, wb, wc0, m, dst_fn, pbase=0):
        P = self
        for tg in range(4):
            pt = self.ps[self.psi % 8]; self.psi += 1
            for k in range(8):
                P.mm(pt.v(s_[0:m, :]), wb.v(s_[:, k, wc0:wc0 + m]), self.uT.v(s_[:, k, tg * 512:(tg + 1) * 512], range(tg * 4, tg * 4 + 4)),
                     start=(k == 0), stop=(k == 7))
            dst_fn(tg, pt.v(s_[0:m, :]))

    def proj_tm(self, wb, wc0, n, dst_fn):
        P = self
        for tb in range(NB):
            pt = self.ps[self.psi % 8]; self.psi += 1
            for k in range(8):
                P.mm(pt.v(s_[:, 0:n]), self.uT.v(s_[:, k, tb * 128:(tb + 1) * 128], tb), wb.v(s_[:, k, wc0:wc0 + n]),
                     start=(k == 0), stop=(k == 7))
            dst_fn(tb, pt.v(s_[:, 0:n]))


    def rope_evac(self, dst, pa, pb, cosv, sinv, tmpa, tmpb):
        P = self
        P.I("dve", "tensor_tensor", tmpa, pa, cosv, op=ALU.mult)
        P.I("dve", "tensor_tensor", tmpb, pb, sinv, op=ALU.mult)
        P.I("pool", "tensor_tensor", dst, tmpa, tmpb, op=ALU.add)

    def fm_group(self, wb, specs, tg):
        P = self
        outs = []
        for (wc0, m) in specs:
            pt = self.ps[self.psi % 6]; self.psi += 1
            for k in range(8):
                P.mm(pt.v(s_[0:m, :]), wb.v(s_[:, k, wc0:wc0 + m]), self.uT.v(s_[:, k, tg * 512:(tg + 1) * 512], range(tg * 4, tg * 4 + 4)),
                     start=(k == 0), stop=(k == 7))
            outs.append(pt.v(s_[0:m, :]))
        return outs

    def mla(self, l, dbg):
        P = self; S = self.S
        d = self.din
        ps, ident, ones = self.ps, self.ident, self.ones
        NK = 20
        with ExitStack() as es:
            wb = P.sb(es, "wbM", [128, 8, 448], BF16)
            wuq = P.sb(es, "wuq", [128, 2, 512], BF16)
            wukv = P.sb(es, "wukv", [128, 2, 256], BF16)
            gq = P.sb(es, "gq", [128, 2], F32)
            gkv = P.sb(es, "gkv", [128, 128], F32)
            cosT = P.sb(es, "cosM", [32, 512], F32)
            sinT = P.sb(es, "sinM", [32, 512], F32)
            biasM = P.sb(es, "biasM", [128, 8 * NK], F32)
            qnT = P.sb(es, "qnT", [128, 2, T], BF16, nsub=4)
            rs = P.sb(es, "qrs", [128, 512], F32)
            qno = P.sb(es, "qno", [64, T], BF16, nsub=4)
            qro = P.sb(es, "qro", [32, T], BF16, nsub=4)
            kno = P.sb(es, "kno", [64, 2560], BF16)
            kro = P.sb(es, "kro", [32, 2560], BF16)
            ckvT = P.sb(es, "ckvT", [128, 2560], BF16, nsub=NK)
            Vt = P.sb(es, "VtM", [128, NK, 4, 65], BF16, nsub=NK)
            ta = P.sb(es, "rta", [128, 512], F32)
            tb_ = P.sb(es, "rtb", [128, 512], F32)
            kvt = P.sb(es, "kvt", [128, 160], F32)
            ckt = [P.sb(es, "ckt%d" % i, [128, 128], F32) for i in range(2)]
            ss = P.sb(es, "kss", [128, 1], F32)
            junk = P.sb(es, "kjunk", [128, 128], F32)
            PT = [P.sb(es, "PT%d" % i, [128, 256], BF16) for i in range(2)]
            oacc = P.sb(es, "oacc", [128, NB, 128], BF16, nsub=NB)
            identb = P.sb(es, "identb", [128, 128], BF16)
            P.I("dve", "tensor_copy", identb.v(), ident.v())
            rden = P.sb(es, "rden", [128, 1], F32)
            cch = P.sb(es, "cch", [128, 4, 160], F32)
            self.load_w(wb, l, 0, 416)
            P.dma("pool", wb.v(s_[:, :, 416:448]), d["w_in"][l, :, 6080:6112].rearrange("(k p) c -> p k c", p=128))
            P.dma("pool", wuq.v(), d["w_uq"][l].rearrange("(k p) c -> p k c", p=128))
            for two in range(2):
                P.dma("pool", wukv.v(s_[:, two, :]).m(lambda a: a.rearrange("p (h c) -> p h c", h=4)), d["w_ukv"][l].rearrange("p (h two c) -> p two h c", h=4, two=2)[:, two, :, :])
            P.dma("sp", gq.v(), d["mla_q_norm"][l])
            P.dma("sp", gkv.v(), d["mla_kv_norm"][l].partition_broadcast(128))
            P.dma("sp", biasM.v(), d["bias_m"][:, :])
            P.I("pool", "memset", Vt.v(), 1.0)
            for tg in range(4):
                tsl = slice(tg * 512, (tg + 1) * 512)
                P.dma("sp", cosT.v(), d["rope_m"][0, :, tsl])
                P.dma("sp", sinT.v(), d["rope_m"][1, :, tsl])
                pq = self.fm_group(wb, [(0, 128), (128, 128)], tg)
                P.I("act", "activation", ta.v(), pq[0], AF.Square)
                P.I("act", "activation", tb_.v(), pq[1], AF.Square)
                pt = ps[self.psi % 6]; self.psi += 1
                P.mm(pt.v(), ones.v(), ta.v(), start=True, stop=False)
                P.mm(pt.v(), ones.v(), tb_.v(), start=False, stop=True)
                P.I("act", "activation", rs.v(), pt.v(), AF.Sqrt, bias=EPS, scale=1.0 / 256)
                P.I("dve", "reciprocal", rs.v(), rs.v())
                for c in range(2):
                    P.I("dve", "scalar_tensor_tensor", qnT.v(s_[:, c, tsl], tg), pq[c], gq.v(s_[:, c:c + 1]), rs.v(), op0=ALU.mult, op1=ALU.mult)
                pk = self.fm_group(wb, [(384, 32), (416, 32)], tg)
                self.rope_evac(kro.v(s_[:, 512 + tg * 512:512 + (tg + 1) * 512]), pk[0], pk[1], cosT.v(), sinT.v(),
                               ta.v(s_[0:32, :]), tb_.v(s_[0:32, :]))
            for tb in range(NB):
                pt = ps[self.psi % 6]; self.psi += 1
                for k in range(8):
                    P.mm(pt.v(s_[:, 0:160]), self.uT.v(s_[:, k, tb * 128:(tb + 1) * 128], tb), wb.v(s_[:, k, 256:416]), start=(k == 0), stop=(k == 7))
                P.I("act", "activation", kvt.v(), pt.v(s_[:, 0:160]), AF.Copy)
                ck = ckt[tb % 2]
                P.I("act", "activation", junk.v(), kvt.v(s_[:, 0:128]), AF.Square, accum_out=ss.v())
                P.I("act", "activation", ss.v(), ss.v(), AF.Sqrt, bias=EPS, scale=1.0 / 128)
                P.I("dve", "reciprocal", ss.v(), ss.v())
                P.I("dve", "scalar_tensor_tensor", ck.v(), kvt.v(s_[:, 0:128]), ss.v(), gkv.v(), op0=ALU.mult, op1=ALU.mult)
                P.dma("sp", self.dout["ckv_o"][l, tb * 128:(tb + 1) * 128, :], ck.v())
                P.dma("sp", self.dout["krope_o"][l, tb * 128:(tb + 1) * 128, :], kvt.v(s_[:, 128:160]))
                p2 = ps[self.psi % 6]; self.psi += 1
                P.I("pe", "transpose", p2.v(s_[:, 0:128]), ck.v(), ident.v())
                P.I("act", "activation", ckvT.v(s_[:, 512 + tb * 128:512 + (tb + 1) * 128], 4 + tb), p2.v(s_[:, 0:128]), AF.Copy)
            P.dma("sp", cch.v(s_[:, :, 0:128]), d["cache_ckv"][l].rearrange("(j p) c -> p j c", p=128))
            P.dma("sp", cch.v(s_[:, :, 128:160]), d["cache_krope"][l].rearrange("(j p) c -> p j c", p=128))
            for j in range(4):
                p2 = ps[self.psi % 6]; self.psi += 1
                P.I("pe", "transpose", p2.v(s_[:, 0:128]), cch.v(s_[:, j, 0:128]), ident.v())
                P.I("act", "activation", ckvT.v(s_[:, j * 128:(j + 1) * 128], j), p2.v(s_[:, 0:128]), AF.Copy)
                p3 = ps[self.psi % 6]; self.psi += 1
                P.I("pe", "transpose", p3.v(s_[0:32, 0:128]), cch.v(s_[:, j, 128:160]), ident.v())
                P.I("act", "activation", kro.v(s_[:, j * 128:(j + 1) * 128]), p3.v(s_[0:32, 0:128]), AF.Copy)
            for kc in range(NK):
                pt = ps[self.psi % 6]; self.psi += 1
                P.mm(pt.v(s_[:, 0:256]), ckvT.v(s_[:, kc * 128:(kc + 1) * 128], kc), wukv.v(s_[:, 1, :]), start=True, stop=True)
                P.I("dve", "tensor_copy", Vt.v(s_[:, kc, :, 0:64], kc), pt.v(s_[:, 0:256]).m(lambda a: a.rearrange("p (h c) -> p h c", h=4)))
            n = 0
            for h in range(4):
                for tg in range(4):
                    tsl = slice(tg * 512, (tg + 1) * 512)
                    P.dma("sp", cosT.v(), d["rope_m"][0, :, tsl])
                    P.dma("sp", sinT.v(), d["rope_m"][1, :, tsl])
                    pn = ps[self.psi % 6]; pa = ps[(self.psi + 1) % 6]; pb = ps[(self.psi + 2) % 6]; self.psi += 3
                    for c in range(2):
                        P.mm(pn.v(s_[0:64, :]), wuq.v(s_[:, c, h * 96:h * 96 + 64]), qnT.v(s_[:, c, tsl], tg), start=(c == 0), stop=(c == 1))
                    for c in range(2):
                        P.mm(pa.v(s_[0:32, :]), wuq.v(s_[:, c, h * 96 + 64:h * 96 + 96]), qnT.v(s_[:, c, tsl], tg), start=(c == 0), stop=(c == 1))
                    for c in range(2):
                        P.mm(pb.v(s_[0:32, :]), wuq.v(s_[:, c, 384 + h * 32:384 + h * 32 + 32]), qnT.v(s_[:, c, tsl], tg), start=(c == 0), stop=(c == 1))
                    P.I("act", "activation", qno.v(s_[:, tsl], tg), pn.v(s_[0:64, :]), AF.Copy)
                    self.rope_evac(qro.v(s_[:, tsl], tg), pa.v(s_[0:32, :]), pb.v(s_[0:32, :]), cosT.v(), sinT.v(),
                                   ta.v(s_[0:32, :]), tb_.v(s_[0:32, :]))
                for g5 in range(5):
                    gsl = slice(g5 * 512, (g5 + 1) * 512)
                    pt = ps[self.psi % 6]; self.psi += 1
                    P.mm(pt.v(s_[0:64, :]), wukv.v(s_[:, 0, h * 64:(h + 1) * 64]), ckvT.v(s_[:, gsl], range(g5 * 4, g5 * 4 + 4)), start=True, stop=True)
                    P.I("act", "activation", kno.v(s_[:, gsl]), pt.v(s_[0:64, :]), AF.Copy)
                for qu in range(8):
                    qsl = slice(qu * 256, (qu + 1) * 256)
                    po = [ps[6], ps[7]]
                    def emitS(kc):
                        ksl = slice(kc * 128, (kc + 1) * 128)
                        pt = ps[self.psi % 6]; self.psi += 1
                        P.mm(pt.v(s_[:, 0:256]), kno.v(s_[:, ksl]), qno.v(s_[:, qsl], qu // 2), start=True, stop=False)
                        P.mm(pt.v(s_[:, 0:256]), kro.v(s_[:, ksl]), qro.v(s_[:, qsl], qu // 2), start=False, stop=True)
                        return pt
                    pts = {0: emitS(0)}
                    for kc in range(NK):
                        if kc + 1 < NK:
                            pts[kc + 1] = emitS(kc + 1)
                        pt = pts.pop(kc)
                        pT = PT[n % 2]; n += 1
                        P.I("act", "activation", pT.v(), pt.v(s_[:, 0:256]), AF.Exp, bias=biasM.v(s_[:, qu * NK + kc:qu * NK + kc + 1]), scale=MLA_SCALE)
                        for qb in range(2):
                            P.mm(po[qb].v(s_[:, 0:65]), pT.v(s_[:, qb * 128:(qb + 1) * 128]), Vt.v(s_[:, kc, h, :], kc), start=(kc == 0), stop=(kc == NK - 1))
                    for qb in range(2):
                        tb = qu * 2 + qb
                        P.I("dve", "reciprocal", rden.v(), po[qb].v(s_[:, 64:65]))
                        P.I("dve", "tensor_scalar_mul", oacc.v(s_[:, tb, (h % 2) * 64:(h % 2) * 64 + 64], tb), po[qb].v(s_[:, 0:64]), rden.v())
                if h % 2 == 1:
                    for tb in range(NB):
                        p2 = ps[self.psi % 6]; self.psi += 1
                        P.mm(p2.v(s_[:, 0:128]), oacc.v(s_[:, tb, :], tb), identb.v(), start=True, stop=True)
                        P.I("act", "activation", self.brT[0].v(s_[:, h // 2, tb * 128:(tb + 1) * 128], tb), p2.v(s_[:, 0:128]), AF.Copy)
            S.barrier()

    def gla_block(self, l, tb, wb, afT, qT, kT, w2, b2, onesb, ktm, e1t, spt, eq, ek, el, mats, vtm, kl, gcol, qeT, keT, LNQ):
        P = self; ps = self.ps
        lsl = slice((tb % 4) * 128, (tb % 4) * 128 + 128)
        bsl = slice(tb * 128, (tb + 1) * 128)
        pt = ps[self.psi % 6]; self.psi += 1
        for k in range(8):
            P.mm(pt.v(s_[:, 0:384]), self.uT.v(s_[:, k, bsl], tb), wb.v(s_[:, k, 128:512]), start=(k == 0), stop=(k == 7))
        P.I("dve", "tensor_copy", ktm.v(), pt.v(s_[:, 0:128]))
        P.I("dve", "tensor_copy", vtm.v(s_[:, tb, :], tb), pt.v(s_[:, 128:384]))
        for dr in range(2):
            pz = ps[self.psi % 6]; self.psi += 1
            P.mm(pz.v(s_[:, 0:128]), afT.v(s_[:, dr, lsl]), w2.v(s_[:, dr, :]), start=True, stop=False)
            P.mm(pz.v(s_[:, 0:128]), onesb.v(), b2.v(s_[:, dr, :]), start=False, stop=True)
            P.I("act", "activation", e1t.v(), pz.v(s_[:, 0:128]), AF.Exp, scale=-1.0)
            P.I("act", "activation", spt.v(), e1t.v(), AF.Ln, bias=1.0, scale=1.0)
            mi = 0 if dr == 0 else 2
            if self.G1 <= 3:
                continue
            pl = ps[self.psi % 6]; self.psi += 1
            P.mm(pl.v(s_[:, 0:128]), mats.v(s_[:, mi + 1, :]), spt.v(), start=True, stop=True)
            P.I("act", "activation", el.v(), pl.v(s_[:, 0:128]), AF.Exp)
            P.I("dve", "tensor_tensor", kl[dr].v(s_[:, tb, :], tb), ktm.v(), el.v(), op=ALU.mult)
            for hp in range(2):
                pc = ps[self.psi % 6]; pg = ps[(self.psi + 1) % 6]; self.psi += 2
                P.mm(pc.v(s_[0:64, 0:128]), spt.v(s_[:, hp * 64:(hp + 1) * 64]), mats.v(s_[:, mi, :]), start=True, stop=True)
                P.mm(pg.v(s_[0:64, 0:2]), spt.v(s_[:, hp * 64:(hp + 1) * 64]), mats.v(s_[:, 4, 0:2]), start=True, stop=True)
                P.I("act", "activation", eq.v(s_[:, hp, :]), pc.v(s_[0:64, 0:128]), AF.Exp, bias=LNQ, scale=1.0)
                P.I("act", "activation", ek.v(s_[:, hp, :]), pc.v(s_[0:64, 0:128]), AF.Exp, scale=-1.0)
                P.I("act", "activation", gcol.v(s_[:, dr, hp, 2 * tb:2 * tb + 2], dr), pg.v(s_[0:64, 0:2]), AF.Exp)
            P.I("dve", "tensor_tensor", qeT[dr].v(s_[:, :, bsl], tb), qT.v(s_[:, :, lsl]), eq.v(), op=ALU.mult)
            P.I("dve", "tensor_tensor", keT[dr].v(s_[:, :, bsl], tb), kT.v(s_[:, :, lsl]), ek.v(), op=ALU.mult)

    def gla(self, l, dbg):
        P = self; S = self.S
        d = self.din
        ps, ident, ones = self.ps, self.ident, self.ones
        LNQ = float(np.log(32.0 ** -0.5))
        with ExitStack() as es:
            qeT = [P.sb(es, "qeT%d" % i, [64, 2, T], BF16, nsub=NB) for i in range(2)]
            keT = [P.sb(es, "keT%d" % i, [64, 2, T], BF16, nsub=NB) for i in range(2)]
            kl = [P.sb(es, "kl%d" % i, [128, NB, 128], BF16, nsub=NB) for i in range(2)]
            gcol = P.sb(es, "gcol", [64, 2, 2, 32], F32, nsub=2)
            vtm = P.sb(es, "vtm", [128, NB, 256], BF16, nsub=NB)
            mats = P.sb(es, "gmats", [128, 5, 128], F32)
            gmask = P.sb(es, "gmask", [128, 2, 128], BF16)
            keep = P.sb(es, "gkeep", [128, 2, 32], F32)
            identb = P.sb(es, "identbG", [128, 128], BF16)
            P.dma("sp", mats.v(), d["gla_mats"].rearrange("a p c -> p a c"))
            P.dma("sp", gmask.v(), d["gla_mask"].rearrange("a p c -> p a c"))
            P.dma("sp", keep.v(), d["gla_keep"][:, :, :])
            P.I("dve", "tensor_copy", identb.v(), ident.v())
            with ExitStack() as e1:
                wb = P.sb(e1, "wbG", [128, 8, 800], BF16)
                afT = P.sb(e1, "afT", [16, 2, 512], BF16)
                qT = P.sb(e1, "gqT", [64, 2, 512], BF16)
                kT = P.sb(e1, "gkT", [64, 2, 512], BF16)
                w2 = P.sb(e1, "gw2", [16, 2, 128], BF16)
                b2 = P.sb(e1, "gb2", [1, 2, 128], BF16)
                onesb = P.sb(e1, "onesb", [1, 128], BF16)
                ktm = P.sb(e1, "ktm", [128, 128], F32)
                e1t = P.sb(e1, "ge1", [128, 128], F32)
                spt = P.sb(e1, "gsp", [128, 128], F32)
                eq = P.sb(e1, "geq", [64, 2, 128], F32)
                ek = P.sb(e1, "gek", [64, 2, 128], F32)
                el = P.sb(e1, "gel", [128, 128], F32)
                self.load_w(wb, l, 672, 800)
                P.dma("pool", w2.v(), d["w_gla_a"][l].rearrange("a r c -> r a c"))
                P.dma("pool", b2.v(), d["b_gla_a"][l:l + 1, :, :])
                P.I("pool", "memset", onesb.v(), 1.0)
                import os
                G1 = int(os.environ.get("KGLA_G1", "99"))
                for tg in range(4 if G1 >= 2 else 0):
                    tsl = slice(tg * 512, (tg + 1) * 512)
                    pq = self.fm_group(wb, [(0, 64), (64, 64), (128, 64), (192, 64)], tg)
                    P.I("act", "activation", qT.v(s_[:, 0, :]), pq[0], AF.Copy)
                    P.I("dve", "tensor_copy", qT.v(s_[:, 1, :]), pq[1])
                    P.I("act", "activation", kT.v(s_[:, 0, :]), pq[2], AF.Copy)
                    P.I("dve", "tensor_copy", kT.v(s_[:, 1, :]), pq[3])
                    pq = self.fm_group(wb, [(768, 16), (784, 16)], tg)
                    P.I("act", "activation", afT.v(s_[:, 0, :]), pq[0], AF.Copy)
                    P.I("dve", "tensor_copy", afT.v(s_[:, 1, :]), pq[1])
                    for tb in range(tg * 4, tg * 4 + 4) if G1 >= 3 else []:
                        self.G1 = G1
                        self.gla_block(l, tb, wb, afT, qT, kT, w2, b2, onesb, ktm, e1t, spt, eq, ek, el, mats, vtm, kl, gcol, qeT, keT, LNQ)
                S.barrier()
            import os
            GSTOP = int(os.environ.get("KGLA_STOP", "99"))
            if GSTOP <= 1:
                return
            with ExitStack() as e2:
                of = P.sb(e2, "gof", [128, NB, 256], F32, nsub=NB)
                St = P.sb(e2, "gS", [64, 2, 64], F32)
                tmpS = P.sb(e2, "gtmp", [64, 2, 64], F32)
                Sb = [P.sb(e2, "gSb%d" % i, [64, 2, 64], BF16) for i in range(2)]
                att = [P.sb(e2, "gatt%d" % i, [128, 128], BF16) for i in range(2)]
                na = 0
                for dr in range(2):
                    P.dma("sp", St.v(), d["gla_init"][l, dr].rearrange("(a p) c -> p a c", p=64))
                    blks = range(NB) if dr == 0 else range(NB - 1, -1, -1)
                    for tb in blks:
                        bsl = slice(tb * 128, (tb + 1) * 128)
                        halves = (0, 1) if dr == 0 else (1, 0)
                        for hf in halves:
                            n = 2 * tb + hf
                            r0 = hf * 64
                            P.I("act", "activation", Sb[hf].v(), St.v(), AF.Copy)
                            pu = ps[self.psi % 6]; self.psi += 1
                            for h in range(4):
                                P.mm(pu.v(s_[(h % 2) * 32:(h % 2) * 32 + 32, (h // 2) * 64:(h // 2) * 64 + 64]), kl[dr].v(s_[r0:r0 + 64, tb, h * 32:(h + 1) * 32], tb),
                                     vtm.v(s_[r0:r0 + 64, tb, h * 64:(h + 1) * 64], tb), start=True, stop=True)
                            for hp in range(2):
                                P.I("dve", "scalar_tensor_tensor", tmpS.v(s_[:, hp, :]), St.v(s_[:, hp, :]), gcol.v(s_[:, dr, hp, n:n + 1], dr), pu.v(s_[0:64, hp * 64:(hp + 1) * 64]), op0=ALU.mult, op1=ALU.add)
                            if (dr == 0 and n % 4 == 3) or (dr == 1 and n % 4 == 0):
                                P.dma("sp", self.dout["gla_o"][l, n // 4, dr].rearrange("(a p) c -> p a c", p=64), tmpS.v())
                            nn = n + 1 if dr == 0 else n - 1
                            if 0 <= nn < 32:
                                P.I("dve", "tensor_scalar_mul", St.v(), tmpS.v(), keep.v(s_[0:64, dr, nn:nn + 1]))
                        po = ps[6 + (tb % 2)]
                        for h in range(4):
                            hs = slice((h % 2) * 32, (h % 2) * 32 + 32); hp = h // 2
                            pa = ps[self.psi % 6]; self.psi += 1
                            P.mm(pa.v(s_[:, 0:128]), keT[dr].v(s_[hs, hp, bsl], tb), qeT[dr].v(s_[hs, hp, bsl], tb), start=True, stop=True)
                            at = att[na % 2]; na += 1
                            P.I("dve", "tensor_tensor", at.v(), pa.v(s_[:, 0:128]), gmask.v(s_[:, dr, :]), op=ALU.mult)
                            oc = slice(h * 64, (h + 1) * 64)
                            P.mm(po.v(s_[:, oc]), at.v(), vtm.v(s_[:, tb, oc], tb), start=True, stop=False)
                            P.mm(po.v(s_[0:64, oc]), qeT[dr].v(s_[hs, hp, tb * 128:tb * 128 + 64], tb), Sb[0].v(s_[hs, hp, :]), start=False, stop=False)
                            P.mm(po.v(s_[64:128, oc]), qeT[dr].v(s_[hs, hp, tb * 128 + 64:tb * 128 + 128], tb), Sb[1].v(s_[hs, hp, :]), start=False, stop=True)
                        if dr == 0:
                            P.I("act", "activation", of.v(s_[:, tb, :], tb), po.v(s_[:, 0:256]), AF.Copy)
                        else:
                            P.I("dve", "tensor_tensor", of.v(s_[:, tb, :], tb), of.v(s_[:, tb, :], tb), po.v(s_[:, 0:256]), op=ALU.add)
                if GSTOP <= 2:
                    S.barrier(); return
                wg = P.sb(e2, "wgG", [128, 8, 256], BF16)
                gn = P.sb(e2, "ggn", [128, 64], F32)
                ss4 = P.sb(e2, "gss", [128, 4], F32)
                junk = P.sb(e2, "gjunk", [128, 64], F32)
                sg = P.sb(e2, "gsil", [128, 256], F32)
                ob = P.sb(e2, "gob", [128, 256], BF16)
                self.load_w(wg, l, 1184, 256)
                P.dma("sp", gn.v(), d["gla_norm"][l].partition_broadcast(128))
                for tb in range(NB):
                    bsl = slice(tb * 128, (tb + 1) * 128)
                    for h in range(4):
                        P.I("act", "activation", junk.v(), of.v(s_[:, tb, h * 64:(h + 1) * 64], tb), AF.Square, accum_out=ss4.v(s_[:, h:h + 1]))
                    P.I("act", "activation", ss4.v(), ss4.v(), AF.Sqrt, bias=EPS, scale=1.0 / 64)
                    P.I("dve", "reciprocal", ss4.v(), ss4.v())
                    ov = of.v(s_[:, tb, :], tb).m(lambda a: a.rearrange("p (h c) -> p h c", h=4))
                    P.I("dve", "tensor_tensor", ov, ov, ss4.v().m(lambda a: a.unsqueeze(2).to_broadcast([128, 4, 64])), op=ALU.mult)
                    P.I("dve", "tensor_tensor", ov, ov, gn.v().m(lambda a: a.unsqueeze(1).to_broadcast([128, 4, 64])), op=ALU.mult)
                    pg = ps[self.psi % 6]; self.psi += 1
                    for k in range(8):
                        P.mm(pg.v(s_[:, 0:256]), self.uT.v(s_[:, k, bsl], tb), wg.v(s_[:, k, :]), start=(k == 0), stop=(k == 7))
                    P.I("act", "activation", sg.v(), pg.v(s_[:, 0:256]), AF.Silu)
                    P.I("dve", "tensor_tensor", ob.v(), of.v(s_[:, tb, :], tb), sg.v(), op=ALU.mult)
                    for c in range(2):
                        p2 = ps[self.psi % 6]; self.psi += 1
                        P.mm(p2.v(s_[:, 0:128]), ob.v(s_[:, c * 128:(c + 1) * 128]), identb.v(), start=True, stop=True)
                        P.I("act", "activation", self.brT[2].v(s_[:, c, bsl], tb), p2.v(s_[:, 0:128]), AF.Copy)
                S.barrier()

    def swa_kv_only(self, l):
        P = self; S = self.S
        ps = self.ps
        with ExitStack() as es:
            wb = P.sb(es, "wbS2", [128, 8, 256], BF16)
            kvo = [P.sb(es, "kvo2_%d" % i, [128, 256], F32) for i in range(2)]
            self.load_w(wb, l, 1728, 256)
            for tb in range(NB):
                pt = ps[self.psi % 6]; self.psi += 1
                for k in range(8):
                    P.mm(pt.v(s_[:, 0:256]), self.uT.v(s_[:, k, tb * 128:(tb + 1) * 128], tb), wb.v(s_[:, k, 0:256]), start=(k == 0), stop=(k == 7))
                ko = kvo[tb % 2]
                P.I("act", "activation", ko.v(), pt.v(s_[:, 0:256]), AF.Copy)
                P.dma("sp", self.dout["swakv_o"][l, tb * 128:(tb + 1) * 128, :], ko.v())
            S.barrier()

    def swa(self, l, dbg):
        P = self; S = self.S
        d = self.din
        ps, ident, ones = self.ps, self.ident, self.ones
        NK = 20
        with ExitStack() as es:
            wb = P.sb(es, "wbS", [128, 8, 896], BF16)
            cosT = P.sb(es, "cosS", [64, 512], F32)
            sinT = P.sb(es, "sinS", [64, 512], F32)
            biasS = P.sb(es, "biasS", [128, 4], F32)
            esink = P.sb(es, "esink", [128, 4], F32)
            msk = P.sb(es, "mskS", [128, NB, 2, 128], BF16)
            qT = P.sb(es, "qTS", [64, 4, T], BF16, nsub=4)
            kT = P.sb(es, "kTS", [64, 2, 2560], BF16)
            Vt = P.sb(es, "VtS", [128, NK, 2, 65], BF16, nsub=NK)
            ta = P.sb(es, "sta", [64, 512], F32)
            tb_ = P.sb(es, "stb", [64, 512], F32)
            kvo = [P.sb(es, "kvo%d" % i, [128, 256], F32) for i in range(2)]
            cch = P.sb(es, "cchS", [128, 4, 2, 64], F32)
            PT = [P.sb(es, "PTS%d" % i, [128, 256], BF16) for i in range(2)]
            oa = P.sb(es, "oaS", [128, 256], F32)
            rden = P.sb(es, "rdenS", [128, 1], F32)
            self.load_w(wb, l, 1472, 512)
            P.dma("pool", wb.v(s_[:, :, 512:896]), d["w_in"][l, :, 6112:6496].rearrange("(k p) c -> p k c", p=128))
            P.dma("sp", biasS.v(), d["bias_s"][:, :])
            P.dma("sp", esink.v(), d["swa_sink"][l])
            P.I("act", "activation", esink.v(), esink.v(), AF.Exp)
            P.dma("sp", msk.v(), d["mask_s"][:, :, :, :])
            P.I("pool", "memset", Vt.v(), 1.0)
            import os
            STOP = int(os.environ.get("KSWA_STOP", "99"))
            if STOP <= 1:
                S.barrier(); return
            for tg in range(4):
                tsl = slice(tg * 512, (tg + 1) * 512)
                P.dma("sp", cosT.v(), d["rope_s"][0, :, tsl])
                P.dma("sp", sinT.v(), d["rope_s"][1, :, tsl])
                for h in range(4):
                    pq = self.fm_group(wb, [(h * 64, 64), (512 + h * 64, 64)], tg)
                    self.rope_evac(qT.v(s_[:, h, tsl], h), pq[0], pq[1], cosT.v(), sinT.v(), ta.v(), tb_.v())
                for j in range(2):
                    pk = self.fm_group(wb, [(256 + j * 64, 64), (768 + j * 64, 64)], tg)
                    self.rope_evac(kT.v(s_[:, j, 512 + tg * 512:512 + (tg + 1) * 512]), pk[0], pk[1], cosT.v(), sinT.v(), ta.v(), tb_.v())
            if STOP <= 2:
                S.barrier(); return
            for tb in range(NB):
                pt = ps[self.psi % 6]; self.psi += 1
                for k in range(8):
                    P.mm(pt.v(s_[:, 0:256]), self.uT.v(s_[:, k, tb * 128:(tb + 1) * 128], tb), wb.v(s_[:, k, 256:512]), start=(k == 0), stop=(k == 7))
                ko = kvo[tb % 2]
                P.I("act", "activation", ko.v(), pt.v(s_[:, 0:256]), AF.Copy)
                P.dma("sp", self.dout["swakv_o"][l, tb * 128:(tb + 1) * 128, :], ko.v())
                P.I("dve", "tensor_copy", Vt.v(s_[:, 4 + tb, :, 0:64], 4 + tb), ko.v(s_[:, 128:256]).m(lambda a: a.rearrange("p (j c) -> p j c", j=2)))
            if STOP <= 3:
                S.barrier(); return
            for j in range(2):
                P.dma("sp", cch.v(s_[:, :, j, :]), d["cache_swak"][l, j].rearrange("(c p) e -> p c e", p=128))
            for c in range(4):
                for j in range(2):
                    p2 = ps[self.psi % 6]; self.psi += 1
                    P.I("pe", "transpose", p2.v(s_[0:64, 0:128]), cch.v(s_[:, c, j, :]), ident.v())
                    P.I("act", "activation", kT.v(s_[:, j, c * 128:(c + 1) * 128]), p2.v(s_[0:64, 0:128]), AF.Copy)
            cchv = P.sb(es, "cchV", [128, 4, 2, 64], F32)
            for j in range(2):
                P.dma("sp", cchv.v(s_[:, :, j, :]), d["cache_swav"][l, j].rearrange("(c p) e -> p c e", p=128))
            for c in range(4):
                P.I("dve", "tensor_copy", Vt.v(s_[:, c, :, 0:64], c), cchv.v(s_[:, c, :, :]))
            n = 0
            import os
            for tb in range(NB if "noatt" not in os.environ.get("KSKIP", "") else 0):
                qsl = slice(tb * 128, (tb + 1) * 128)
                for j in range(2):
                    po = [ps[6], ps[7]]
                    kcs = [(4 + tb + dd, dd) for dd in (-1, 0, 1) if 0 <= tb + dd < NB] + [(c, 2) for c in range(4)]
                    def emitS(i):
                        kc, kind = kcs[i]
                        ksl = slice(kc * 128, (kc + 1) * 128)
                        pt = ps[self.psi % 6]; self.psi += 1
                        for g in range(2):
                            P.mm(pt.v(s_[:, g * 128:(g + 1) * 128]), kT.v(s_[:, j, ksl]), qT.v(s_[:, 2 * j + g, qsl], 2 * j + g), start=True, stop=True)
                        return pt
                    pts = {0: emitS(0)}
                    for i, (kc, kind) in enumerate(kcs):
                        if i + 1 < len(kcs):
                            pts[i + 1] = emitS(i + 1)
                        pt = pts.pop(i)
                        pT = PT[n % 2]; n += 1
                        if kind == 2:
                            P.I("act", "activation", pT.v(), pt.v(s_[:, 0:256]), AF.Exp, bias=biasS.v(s_[:, 0:1]), scale=SWA_SCALE)
                        else:
                            P.I("act", "activation", pT.v(), pt.v(s_[:, 0:256]), AF.Exp, scale=SWA_SCALE)
                            if kind != 0:
                                mi = 0 if kind == -1 else 1
                                for g in range(2):
                                    P.I("dve", "tensor_tensor", pT.v(s_[:, g * 128:(g + 1) * 128]), pT.v(s_[:, g * 128:(g + 1) * 128]), msk.v(s_[:, tb, mi, :]), op=ALU.mult)
                        for g in range(2):
                            P.mm(po[g].v(s_[:, 0:65]), pT.v(s_[:, g * 128:(g + 1) * 128]), Vt.v(s_[:, kc, j, :], kc), start=(i == 0), stop=(i == len(kcs) - 1))
                    for g in range(2):
                        hh = 2 * j + g
                        P.I("dve", "tensor_tensor", rden.v(), po[g].v(s_[:, 64:65]), esink.v(s_[:, hh:hh + 1]), op=ALU.add)
                        P.I("dve", "reciprocal", rden.v(), rden.v())
                        P.I("dve", "tensor_scalar_mul", oa.v(s_[:, hh * 64:(hh + 1) * 64]), po[g].v(s_[:, 0:64]), rden.v())
                for c in range(2):
                    p2 = ps[self.psi % 6]; self.psi += 1
                    P.I("pe", "transpose", p2.v(s_[:, 0:128]), oa.v(s_[:, c * 128:(c + 1) * 128]), ident.v())
                    P.I("act", "activation", self.brT[3].v(s_[:, c, tb * 128:(tb + 1) * 128], tb), p2.v(s_[:, 0:128]), AF.Copy)
            S.barrier()

    def fnet(self, l):
        P = self; S = self.S
        d = self.din
        with ExitStack() as es:
            wb = P.sb(es, "wbF", [128, 8, 256], BF16)
            fT = P.sb(es, "fT", [128, 2, T], BF16, nsub=4)
            cd = P.sb(es, "cdft", [128, 2, 128], BF16)
            A = P.sb(es, "fA", [128, NB, 256], BF16, nsub=NB)
            B = P.sb(es, "fB", [128, NB, 256], BF16, nsub=NB)
            tc_ = [P.sb(es, "dc%d" % i, [128, 512], BF16) for i in range(4)]
            ts_ = [P.sb(es, "ds%d" % i, [128, 512], BF16) for i in range(4)]
            self.load_w(wb, l, 416, 256)
            P.dma("sp", cd.v(), d["cdft"].rearrange("a p c -> p a c"))
            for c in range(2):
                def ev(tg, pv, c=c):
                    P.I("act", "activation", fT.v(s_[:, c, tg * 512:(tg + 1) * 512], tg), pv, AF.Copy)
                self.proj_fm(wb, c * 128, 128, ev)
            for tb in range(NB):
                pa = self.ps[self.psi % 8]; self.psi += 1
                for c in range(2):
                    P.mm(pa.v(s_[:, c * 128:(c + 1) * 128]), fT.v(s_[:, c, tb * 128:(tb + 1) * 128], tb // 4), cd.v(s_[:, 0, :]), start=True, stop=True)
                    P.mm(pa.v(s_[:, 256 + c * 128:256 + (c + 1) * 128]), fT.v(s_[:, c, tb * 128:(tb + 1) * 128], tb // 4), cd.v(s_[:, 1, :]), start=True, stop=True)
                P.I("act", "activation", A.v(s_[:, tb, :], tb), pa.v(s_[:, 0:256]), AF.Copy)
                P.I("act", "activation", B.v(s_[:, tb, :], tb), pa.v(s_[:, 256:512]), AF.Copy)
            n = 0
            for tg in range(4):
                p0 = self.ps[self.psi % 8]; p1 = self.ps[(self.psi + 1) % 8]; self.psi += 2
                for tb in range(NB):
                    ct = tc_[n % 4]; st = ts_[n % 4]; n += 1
                    P.dma("sp", ct.v(), d["dft_c"][tb * 128:(tb + 1) * 128, tg * 512:(tg + 1) * 512])
                    P.dma("act", st.v(), d["dft_s"][tb * 128:(tb + 1) * 128, tg * 512:(tg + 1) * 512])
                    for c, pp in ((0, p0), (1, p1)):
                        P.mm(pp.v(), A.v(s_[:, tb, c * 128:(c + 1) * 128], tb), ct.v(), start=(tb == 0), stop=False)
                        P.mm(pp.v(), B.v(s_[:, tb, c * 128:(c + 1) * 128], tb), st.v(), start=False, stop=(tb == NB - 1))
                for c, pp in ((0, p0), (1, p1)):
                    P.I("act" if c == 0 else "dve", "activation" if c == 0 else "tensor_copy", self.brT[1].v(s_[:, c, tg * 512:(tg + 1) * 512], range(tg * 4, tg * 4 + 4)),
                        pp.v(), *((AF.Copy,) if c == 0 else ()))
            S.barrier()

    def merge(self, l, dbg):
        P = self; S = self.S
        d = self.din
        x, uT, ps, mcol, ident, ones = self.x, self.uT, self.ps, self.mcol, self.ident, self.ones
        with ExitStack() as es:
            wbr = P.sb(es, "wbr", [128, 8, D], BF16)
            wo = P.sb(es, "wo", [128, 8, D], BF16)
            wg = [P.sb(es, "wg%d" % i, [128, 8, 512], BF16) for i in range(1)]
            G = P.sb(es, "Gacc", [128, 512], F32)
            GT = P.sb(es, "GT", [128, 8, 512], BF16, nsub=8)
            sg = [P.sb(es, "sg%d" % i, [128, 512], F32) for i in range(2)]
            gbc = P.sb(es, "g1bc", [128, D], F32)
            lng = P.sb(es, "ln1g", [128, D], F32)
            lnb = P.sb(es, "ln1b", [128, D], F32)
            dg = P.sb(es, "dgm", [128, 128], F32)
            xt = [P.sb(es, "xt%d" % i, [128, D], F32) for i in range(1)]
            tmps = [self.ln_tmp(es, "g%d" % i) for i in range(2)]
            P.dma("pool", wbr.v(), d["w_branch"][l].rearrange("b (k p) d -> p (b k) d", p=128))
            P.dma("pool", wo.v(), d["w_out"][l].rearrange("(k p) d -> p k d", p=128))
            P.dma("sp", lng.v(), d["ln"][l, 0, :].partition_broadcast(128))
            P.dma("sp", lnb.v(), d["ln"][l, 1, :].partition_broadcast(128))
            for k in range(8):
                P.I("dve", "tensor_scalar_mul", dg.v(), ident.v(), mcol.v(s_[:, l, 16 + k:17 + k], l))
                pt = ps[self.psi % 8]; self.psi += 1
                P.mm(pt.v(s_[:, 0:128]), ones.v(), dg.v(), start=True, stop=True)
                P.I("act", "activation", gbc.v(s_[:, k * 128:(k + 1) * 128]), pt.v(s_[:, 0:128]), AF.Copy)
            n = 0
            for tg in range(4):
                tsub = range(tg * 4, tg * 4 + 4)
                tsl = slice(tg * 512, (tg + 1) * 512)
                for dc in range(8):
                    w = wg[0]; n += 1
                    P.dma("pool", w.v(), d["w_gate"][l, dc])
                    for b in range(4):
                        pg = ps[self.psi % 8]; pp = ps[(self.psi + 1) % 8]; self.psi += 2
                        for k in range(8):
                            P.mm(pg.v(), w.v(s_[:, k, b * 128:(b + 1) * 128]), uT.v(s_[:, k, tsl], tsub), start=(k == 0), stop=(k == 7))
                        for kc in range(2):
                            P.mm(pp.v(), wbr.v(s_[:, b * 2 + kc, dc * 128:(dc + 1) * 128]), self.brT[b].v(s_[:, kc, tsl], tsub),
                                 start=(kc == 0), stop=(kc == 1))
                        sgt = sg[b % 2]
                        P.I("act", "activation", sgt.v(), pg.v(), AF.Sigmoid)
                        if b == 0:
                            P.I("dve", "tensor_tensor", G.v(), sgt.v(), pp.v(), op=ALU.mult)
                        else:
                            P.I("dve", "tensor_tensor", sgt.v(), sgt.v(), pp.v(), op=ALU.mult)
                            if b < 3:
                                P.I("pool", "tensor_tensor", G.v(), G.v(), sgt.v(), op=ALU.add)
                            else:
                                P.I("pool", "tensor_tensor", GT.v(s_[:, dc, :], dc), G.v(), sgt.v(), op=ALU.add)
                for j in range(4):
                    tb = tg * 4 + j
                    xtb = xt[0]
                    for hf in range(2):
                        pm = ps[self.psi % 8]; self.psi += 1
                        for k in range(8):
                            P.mm(pm.v(), GT.v(s_[:, k, j * 128:(j + 1) * 128], k), wo.v(s_[:, k, hf * 512:(hf + 1) * 512]), start=(k == 0), stop=(k == 7))
                        hs = slice(hf * 512, (hf + 1) * 512)
                        P.I("dve", "tensor_tensor", xtb.v(s_[:, hs]), pm.v(), gbc.v(s_[:, hs]), op=ALU.mult)
                        P.I("dve", "scalar_tensor_tensor", xtb.v(s_[:, hs]), x.v(s_[:, tb, hs], tb), ALPHA, xtb.v(s_[:, hs]), op0=ALU.mult, op1=ALU.add)
                    self.ln_block(tmps[tb % 2], xtb.v(), x.v(s_[:, tb, :], tb))
                    P.I("pool", "tensor_tensor", x.v(s_[:, tb, :], tb), x.v(s_[:, tb, :], tb), lng.v(), op=ALU.mult)
                    P.I("pool", "tensor_tensor", x.v(s_[:, tb, :], tb), x.v(s_[:, tb, :], tb), lnb.v(), op=ALU.add)
            S.barrier()

    def post_ffn(self, l, yacc_is_x=True):
        P = self; S = self.S
        d = self.din
        x = self.x
        with ExitStack() as es:
            lng = P.sb(es, "ln2g", [128, D], F32)
            lnb = P.sb(es, "ln2b", [128, D], F32)
            xt = [P.sb(es, "xq%d" % i, [128, D], F32) for i in range(2)]
            tmps = [self.ln_tmp(es, "q%d" % i) for i in range(2)]
            P.dma("sp", lng.v(), d["ln"][l, 2, :].partition_broadcast(128))
            P.dma("sp", lnb.v(), d["ln"][l, 3, :].partition_broadcast(128))
            for tb in range(NB):
                xtb = xt[tb % 2]
                P.I("act", "activation", xtb.v(), x.v(s_[:, tb, :], tb), AF.Copy)
                self.ln_block(tmps[tb % 2], xtb.v(), x.v(s_[:, tb, :], tb))
                P.I("pool", "tensor_tensor", x.v(s_[:, tb, :], tb), x.v(s_[:, tb, :], tb), lng.v(), op=ALU.mult)
                P.I("pool", "tensor_tensor", x.v(s_[:, tb, :], tb), x.v(s_[:, tb, :], tb), lnb.v(), op=ALU.add)
            S.barrier()

    def layer(self, l, dbg):
        P = self; S = self.S
        self.psi = 0
        for g4 in range(16):
            P.dma("pool", self.ubf.v(s_[l, g4 * 4:(g4 + 1) * 4], l * 16 + g4), self.din["peer_uT"][l, g4 * 4:(g4 + 1) * 4].rearrange("g p k e -> g p (k e)"))
            P.dma("pool", self.vbf.v(s_[l, g4 * 4:(g4 + 1) * 4], l * 16 + g4), self.din["peer_v"][l, g4 * 4:(g4 + 1) * 4].rearrange("g p j d -> g p (j d)"))
        with ExitStack() as esl:
            self.uT = P.sb(esl, "uT", [128, 8, T], BF16, nsub=NB)
            self.mod_to_uT(l, 0)
            self.brT = [P.sb(esl, "brT%d" % b, [128, 2, T], BF16, nsub=NB) for b in range(4)]
            import os
            skip = os.environ.get("KSKIP", "")
            for b, nm in ((0, "mla"), (3, "swa"), (1, "fnet"), (2, "gla")):
                if nm in skip:
                    for tb4 in range(4):
                        P.I("pool", "memset", self.brT[b].v(s_[:, :, tb4 * 512:(tb4 + 1) * 512], range(tb4 * 4, tb4 * 4 + 4)), 0.0)
            if "mla" not in skip:
                self.mla(l, dbg)
            if "swa" not in skip:
                self.swa(l, dbg)
            else:
                self.swa_kv_only(l)
            if "fnet" not in skip:
                self.fnet(l)
            if "gla" not in skip:
                self.gla(l, dbg)
            self.merge(l, dbg)
            S.barrier()
        import os
        if "peer" in os.environ.get("KSKIP", ""):
            for tb in range(NB):
                P.I("act", "activation", self.x.v(s_[:, tb, :], tb), self.x.v(s_[:, tb, :], tb), AF.Copy, scale=ALPHA)
        else:
            self.peer(l, dbg)
        self.post_ffn(l)

    def peer(self, l, dbg):
        P = self; S = self.S
        d = self.din
        x, ps, mcol, ident, ones, iota, bm = self.x, self.ps, self.mcol, self.ident, self.ones, self.iota, self.bm
        NCH = 128
        TBS = 256
        with ExitStack() as es:
            u2Ts = [P.sb(es, "u2T%d" % i, [128, 8, TBS], BF16, nsub=2) for i in range(2)]
            lt = self.ln_tmp(es, "P")
            wq = P.sb(es, "wq", [128, 8, 256], BF16)
            kT = P.sb(es, "keysT", [128, 16, 128], BF16)
            gbc = P.sb(es, "g2bc", [128, D], F32)
            dg = P.sb(es, "dg2", [128, 128], F32)
            qpT = P.sb(es, "qpT", [128, 16, 128], BF16, nsub=16)
            sc = P.sb(es, "psc", [128, 16, 128], F32, nsub=16)
            vtop = P.sb(es, "vtop", [128, 16, 16], F32, nsub=16)
            itop = P.sb(es, "itop", [128, 16, 16], U32, nsub=16)
            idx1f = P.sb(es, "idx1f", [128, 128], F32)
            idx2f = P.sb(es, "idx2f", [128, 128], F32)
            idxTs = P.sb(es, "idxT", [128, 2, 2, 128], F32, nsub=2)
            cand = P.sb(es, "cand", [128, 8, 256], F32, nsub=8)
            t8a = P.sb(es, "t8a", [128, 8, 8], F32, nsub=8)
            t8b = P.sb(es, "t8b", [128, 8, 8], F32, nsub=8)
            nmx = P.sb(es, "nmx", [128, 8], F32)
            zz = P.sb(es, "pz", [128, 8], F32)
            wCTs = P.sb(es, "wCT", [128, 2, 128, 16], BF16, nsub=2)
            O1 = P.sb(es, "O1", [128, 4, 128], BF16)
            O2 = P.sb(es, "O2", [128, 4, 128], BF16)
            Cbd = P.sb(es, "Cbd", [128, 4, 128], BF16)
            tmpS = [P.sb(es, "ptmp%d" % i, [128, 4, 128], BF16) for i in range(2)]
            WtT = P.sb(es, "WtT", [128, TBS, 128], BF16, nsub=TBS // 4)
            Ut = [P.sb(es, "Ut%d" % i, [128, 8, 256], BF16) for i in range(2)]
            Vt = [P.sb(es, "Vt%d" % i, [128, 2, D], BF16) for i in range(2)]
            actS = [P.sb(es, "pact%d" % i, [128, TBS], BF16) for i in range(2)]
            GS = [P.sb(es, "pG%d" % i, [128, TBS], BF16) for i in range(2)]
            P.dma("pool", kT.v(), d["peer_keysT"][l].rearrange("h q c k -> c (h q) k"))
            for k in range(8):
                P.I("dve", "tensor_scalar_mul", dg.v(), ident.v(), mcol.v(s_[:, l, 40 + k:41 + k], l))
                pt = ps[self.psi % 4]; self.psi += 1
                P.mm(pt.v(s_[:, 0:128]), ones.v(), dg.v(), start=True, stop=True)
                P.I("act", "activation", gbc.v(s_[:, k * 128:(k + 1) * 128]), pt.v(s_[:, 0:128]), AF.Copy)
            py = [ps[4], ps[5], ps[6], ps[7]]
            esc = lambda a: a.rearrange("p (h a) b -> p h (a b)", a=2)
            def sel_a(sb_):
                u2T = u2Ts[sb_ % 2]
                for sub in range(2):
                    tb = sb_ * 2 + sub
                    usl = slice(sub * 128, (sub + 1) * 128)
                    xnv = cand.v(s_[:, 0:4, :], [0, 1, 2, 3]).m(lambda a: a.rearrange("p a b -> p (a b)"))
                    self.ln_block(lt, x.v(s_[:, tb, :], tb), xnv)
                    yield
                    for half in range(2):
                        pt = ps[2 + self.psi % 2]; self.psi += 1
                        for j in range(4):
                            k = half * 4 + j
                            P.I("pe", "transpose", pt.v(s_[:, j * 128:(j + 1) * 128]), xnv.m(lambda a, k=k: a[:, k * 128:(k + 1) * 128]), ident.v())
                        for j in range(4):
                            k = half * 4 + j
                            P.I("dve", "tensor_scalar", u2T.v(s_[:, k, usl], sub), pt.v(s_[:, j * 128:(j + 1) * 128]),
                                mcol.v(s_[:, l, 32 + k:33 + k], l), mcol.v(s_[:, l, 24 + k:25 + k], l), op0=ALU.mult, op1=ALU.add)
                    for c4 in range(4):
                        pt = ps[2 + self.psi % 2]; self.psi += 1
                        for j in range(4):
                            c = c4 * 4 + j
                            if j % 2 == 0:
                                P.dma("pool", wq.v(), d["w_peer_q"][l, c // 2])
                            for k in range(8):
                                P.mm(pt.v(s_[:, j * 128:(j + 1) * 128]), wq.v(s_[:, k, (j % 2) * 128:(j % 2) * 128 + 128]), u2T.v(s_[:, k, usl], sub), start=(k == 0), stop=(k == 7))
                        P.I("act", "activation", qpT.v(s_[:, c4 * 4:(c4 + 1) * 4, :], range(c4 * 4, c4 * 4 + 4)), pt.v().m(lambda a: a.rearrange("p (j t) -> p j t", j=4)), AF.Copy)
                        yield
                    for c4 in range(4):
                        pt = ps[2 + self.psi % 2]; self.psi += 1
                        for j in range(4):
                            c = c4 * 4 + j
                            P.mm(pt.v(s_[:, j * 128:(j + 1) * 128]), qpT.v(s_[:, c, :], c), kT.v(s_[:, c, :]), start=True, stop=True)
                        P.I("act", "activation", sc.v(s_[:, c4 * 4:(c4 + 1) * 4, :], range(c4 * 4, c4 * 4 + 4)), pt.v().m(lambda a: a.rearrange("p (j t) -> p j t", j=4)), AF.Copy)
                        yield
                    wkc = lambda c: cand.v(s_[:, c // 2, (c % 2) * 128:(c % 2) * 128 + 128], c // 2)
                    for c in range(16):
                        P.I("dve", "max", vtop.v(s_[:, c, 0:8], c), sc.v(s_[:, c, :], c))
                    yield
                    for c in range(16):
                        P.I("dve", "max_index", itop.v(s_[:, c, 0:8], c), vtop.v(s_[:, c, 0:8], c), sc.v(s_[:, c, :], c))
                    yield
                    for c in range(16):
                        P.I("dve", "match_replace", wkc(c), vtop.v(s_[:, c, 0:8], c), sc.v(s_[:, c, :], c), -1e30)
                    yield
                    for c in range(16):
                        P.I("dve", "max", vtop.v(s_[:, c, 8:16], c), wkc(c))
                    yield
                    for c in range(16):
                        P.I("dve", "max_index", itop.v(s_[:, c, 8:16], c), vtop.v(s_[:, c, 8:16], c), wkc(c))
                    yield
                    v4 = lambda a: a.rearrange("p (h q) r -> p h q r", q=2)
                    P.I("dve", "tensor_copy", idx1f.v().m(lambda a: a.rearrange("p (h r) -> p h r", h=8)), itop.v().m(lambda a: v4(a)[:, :, 0, :]))
                    P.I("dve", "tensor_copy", idx2f.v().m(lambda a: a.rearrange("p (h r) -> p h r", h=8)), itop.v().m(lambda a: v4(a)[:, :, 1, :]))
                    P.I("dve", "tensor_tensor", cand.v().m(lambda a: a.rearrange("p h (a b) -> p h a b", a=16)),
                        vtop.v().m(lambda a: v4(a)[:, :, 0, :].unsqueeze(3).to_broadcast([128, 8, 16, 16])),
                        vtop.v().m(lambda a: v4(a)[:, :, 1, :].unsqueeze(2).to_broadcast([128, 8, 16, 16])), op=ALU.add)
                    yield
                    wkh = lambda h: sc.v(s_[:, 2 * h:2 * h + 2, :], [2 * h, 2 * h + 1]).m(lambda a: a.rearrange("p a b -> p (a b)"))
                    for h in range(8):
                        P.I("dve", "max", t8a.v(s_[:, h, :], h), cand.v(s_[:, h, :], h))
                    for h in range(8):
                        P.I("dve", "match_replace", wkh(h), t8a.v(s_[:, h, :], h), cand.v(s_[:, h, :], h), -1e30)
                    for h in range(8):
                        P.I("dve", "max", t8b.v(s_[:, h, :], h), wkh(h))
                    yield
                    P.I("dve", "tensor_scalar_mul", nmx.v(), t8a.v(s_[:, :, 0]), -1.0)
                    for h in range(8):
                        ev = sc.v(s_[:, 2 * h:2 * h + 2, :], [2 * h, 2 * h + 1]).m(lambda a: a.rearrange("p a b -> p (a b)"))
                        P.I("act", "activation", ev, cand.v(s_[:, h, :], h), AF.Exp, bias=nmx.v(s_[:, h:h + 1]), scale=1.0)
                        P.I("dve", "scalar_tensor_tensor", ev, cand.v(s_[:, h, :], h), t8b.v(s_[:, h, 7:8], h), ev, op0=ALU.is_ge, op1=ALU.mult)
                    yield
                    P.I("dve", "tensor_reduce", zz.v(), sc.v().m(esc), axis=AX.X, op=ALU.add)
                    P.I("dve", "reciprocal", zz.v(), zz.v())
                    P.I("dve", "tensor_tensor", sc.v().m(esc), sc.v().m(esc), zz.v().m(lambda a: a.unsqueeze(2).to_broadcast([128, 8, 256])), op=ALU.mult)
                    yield
                    pt = ps[2 + self.psi % 2]; self.psi += 1
                    P.I("pe", "transpose", pt.v(s_[:, 0:128]), idx1f.v(), ident.v())
                    P.I("pe", "transpose", pt.v(s_[:, 128:256]), idx2f.v(), ident.v())
                    P.I("act", "activation", idxTs.v(s_[:, sub], sub), pt.v(s_[:, 0:256]).m(lambda a: a.rearrange("p (a t) -> p a t", a=2)), AF.Copy)
                    for r4 in range(4):
                        yield
                        pt = ps[2 + self.psi % 2]; self.psi += 1
                        for j in range(4):
                            r2 = r4 * 4 + j
                            P.I("pe", "transpose", pt.v(s_[:, j * 128:(j + 1) * 128]),
                                sc.v().m(lambda a, r2=r2: a.rearrange("p c (a b) -> p (c a) b", b=16)[:, :, r2]), ident.v())
                        P.I("act", "activation", wCTs.v(s_[:, sub, :, r4 * 4:(r4 + 1) * 4], sub).m(lambda a: a.rearrange("p t j -> p j t")),
                            pt.v().m(lambda a: a.rearrange("p (j t) -> p j t", j=4)), AF.Copy)

                yield
            def expand(sb_):
                for sub in range(2):
                    for sbk in range(32):
                        t0 = sbk * 4
                        P.I("dve", "tensor_tensor", O1.v(), iota.v().m(lambda a: a.unsqueeze(1).to_broadcast([128, 4, 128])),
                            idxTs.v(s_[:, sub, 0, t0:t0 + 4], sub).m(lambda a: a.unsqueeze(2).to_broadcast([128, 4, 128])), op=ALU.is_equal)
                        P.I("dve", "tensor_tensor", O2.v(), iota.v().m(lambda a: a.unsqueeze(1).to_broadcast([128, 4, 128])),
                            idxTs.v(s_[:, sub, 1, t0:t0 + 4], sub).m(lambda a: a.unsqueeze(2).to_broadcast([128, 4, 128])), op=ALU.is_equal)
                        P.I("pool", "tensor_tensor", Cbd.v().m(lambda a: a.rearrange("p t (h r) -> p t h r", h=8)),
                            wCTs.v(s_[:, sub, t0:t0 + 4, :], sub).m(lambda a: a.unsqueeze(2).to_broadcast([128, 4, 8, 16])),
                            bm.v().m(lambda a: a.unsqueeze(1).unsqueeze(3).to_broadcast([128, 4, 8, 16])), op=ALU.mult)
                        for g4 in range(1):
                            pt = ps[self.psi % 4]; self.psi += 1
                            tS = tmpS[sbk % 2]
                            for j in range(4):
                                tt = g4 * 4 + j
                                P.mm(pt.v(s_[:, j * 128:(j + 1) * 128]), Cbd.v(s_[:, tt, :]), O1.v(s_[:, tt, :]), start=True, stop=True)
                            P.I("act", "activation", tS.v(), pt.v().m(lambda a: a.rearrange("p (j i) -> p j i", j=4)), AF.Copy)
                            pt2 = ps[self.psi % 4]; self.psi += 1
                            for j in range(4):
                                tt = g4 * 4 + j
                                P.mm(pt2.v(s_[:, j * 128:(j + 1) * 128]), O2.v(s_[:, tt, :]), tS.v(s_[:, j, :]), start=True, stop=True)
                            ta = sub * 128 + t0 + g4 * 4
                            P.I("dve" if sbk % 2 == 0 else "act", "tensor_copy" if sbk % 2 == 0 else "activation", WtT.v(s_[:, ta:ta + 4, :], ta // 4),
                                pt2.v().m(lambda a: a.rearrange("p (j i) -> p j i", j=4)), *(() if sbk % 2 == 0 else (AF.Copy,)))

            def expert(sb_, gen):
                u2T = u2Ts[sb_ % 2]
                def emitU(c):
                    c2, j = c // 2, c % 2
                    if j == 0:
                        ut = Ut[c2 % 2]; vt = Vt[c2 % 2]
                        P.dma("sp", ut.v().m(lambda a: a.rearrange("p k e -> p (k e)")), self.ubf.v(s_[l, c2], l * 16 + c2 // 4))
                        P.dma("sp", vt.v().m(lambda a: a.rearrange("p j d -> p (j d)")), self.vbf.v(s_[l, c2], l * 16 + c2 // 4))
                    ut = Ut[c2 % 2]
                    pa = ps[c % 2]
                    for k in range(8):
                        P.mm(pa.v(s_[:, 0:TBS]), ut.v(s_[:, k, j * 128:(j + 1) * 128]), u2T.v(s_[:, k, :]), start=(k == 0), stop=(k == 7))
                def emitMV(c):
                    c2, j = c // 2, c % 2
                    vt = Vt[c2 % 2]
                    pa = ps[c % 2]
                    aS = actS[c % 2]; gS = GS[c % 2]
                    P.I("act", "activation", aS.v(), pa.v(s_[:, 0:TBS]), AF.Gelu)
                    P.I("dve", "tensor_tensor", gS.v(), aS.v(), WtT.v(s_[:, :, c]), op=ALU.mult)
                    for sub in range(2):
                        for hf in range(2):
                            P.mm(py[sub * 2 + hf].v(), gS.v(s_[:, sub * 128:(sub + 1) * 128]), vt.v(s_[:, j, hf * 512:(hf + 1) * 512]), start=(c == 0), stop=(c == NCH - 1))
                emitU(0)
                for c in range(NCH):
                    if c + 1 < NCH:
                        emitU(c + 1)
                    emitMV(c)
                    if gen is not None and c % 2 == 1:
                        next(gen, None)
                if gen is not None:
                    for _ in gen:
                        pass

            def finalize(sb_):
                for sub in range(2):
                    tb = sb_ * 2 + sub
                    for hf in range(2):
                        hs = slice(hf * 512, (hf + 1) * 512)
                        yv = cand.v(s_[:, 0:2, :], [0, 1]).m(lambda a: a.rearrange("p a b -> p (a b)"))
                        P.I("dve", "tensor_tensor", yv, py[sub * 2 + hf].v(), gbc.v(s_[:, hs]), op=ALU.mult)
                        P.I("dve", "scalar_tensor_tensor", x.v(s_[:, tb, hs], tb), x.v(s_[:, tb, hs], tb), ALPHA, yv, op0=ALU.mult, op1=ALU.add)

            NSB = T // TBS
            for _ in sel_a(0):
                pass
            expand(0)
            for sb_ in range(NSB):
                gen = sel_a(sb_ + 1) if sb_ + 1 < NSB else None
                expert(sb_, gen)
                finalize(sb_)
                if sb_ + 1 < NSB:
                    expand(sb_ + 1)
            S.barrier()

def _bf(a):
    return np.ascontiguousarray(a).astype(ml_dtypes.bfloat16)


def host_consts(kind):
    c = {}
    c["ident"] = np.eye(128, dtype=np.float32)
    c["bm"] = np.ascontiguousarray((np.arange(128)[:, None] // 16 == np.arange(8)[None, :]).astype(np.float32))
    seqlen = T if kind == "sample" else 256
    n = np.arange(seqlen)
    ang = 2.0 * np.pi * np.outer(n, n) / seqlen
    sc = 1.0 / np.sqrt(seqlen * 64.0)
    cb = np.cos(ang) * sc
    sbm = -np.sin(ang) * sc
    Cf = np.zeros((T, T), np.float64)
    Sf = np.zeros((T, T), np.float64)
    for i in range(T // seqlen):
        sl = slice(i * seqlen, (i + 1) * seqlen)
        Cf[sl, sl] = cb
        Sf[sl, sl] = sbm
    c["dft_c"] = _bf(Cf.astype(np.float32))
    c["dft_s"] = _bf(Sf.astype(np.float32))
    m = np.arange(64)
    a2 = 2.0 * np.pi * np.outer(m, m) / 64.0
    cc = np.zeros((2, 128, 128), np.float64)
    for g in range(2):
        cc[0, g * 64:(g + 1) * 64, g * 64:(g + 1) * 64] = np.cos(a2)
        cc[1, g * 64:(g + 1) * 64, g * 64:(g + 1) * 64] = np.sin(a2)
    c["cdft"] = _bf(cc.astype(np.float32))
    t = np.arange(T)
    rows = (t // 64).astype(np.float64); cols = (t % 64).astype(np.float64)
    for nm, R in (("rope_m", 32), ("rope_s", 64)):
        half = R // 2; q = R // 4
        tab = np.zeros((2, R, T), np.float64)
        for dd in range(R):
            pos = rows if dd < half else cols
            fi = dd % q
            freq = 10000.0 ** (-(2.0 * fi) / half)
            ang = pos * freq
            if kind == "sample":
                tab[0, dd] = np.cos(ang)
                tab[1, dd] = np.sin(ang) * (-1.0 if (dd // q) % 2 == 0 else 1.0)
            else:
                tab[0, dd] = 1.0
        c[nm] = np.ascontiguousarray(tab.astype(np.float32))
    bm_ = np.zeros((160,), np.float32)
    if kind == "prompt":
        for qu in range(8):
            for kc in range(20):
                ok = kc >= 4 and (kc - 4) // 2 == qu
                bm_[qu * 20 + kc] = 0.0 if ok else NEG
    c["bias_m"] = np.ascontiguousarray(np.broadcast_to(bm_[None, :], (128, 160)))
    bs_ = np.zeros((128, 4), np.float32)
    if kind == "prompt":
        bs_[:, 0] = NEG
    c["bias_s"] = bs_
    mk = np.zeros((128, NB, 2, 128), np.float32)
    kk = np.arange(128)[:, None]; qq = np.arange(128)[None, :]
    for tb in range(NB):
        if kind == "sample":
            mk[:, tb, 0, :] = (kk >= qq)
            mk[:, tb, 1, :] = (kk <= qq)
        else:
            mk[:, tb, 0, :] = 1.0 if tb % 2 == 1 else 0.0
            mk[:, tb, 1, :] = 1.0 if tb % 2 == 0 else 0.0
    c["mask_s"] = _bf(mk)
    tt = np.arange(128)[:, None]; tp = np.arange(128)[None, :]
    same = (tt // 64) == (tp // 64)
    cc_ = -1.0 / 16.0
    gm = np.zeros((5, 128, 128), np.float32)
    gm[0] = cc_ * (same & (tt <= tp))
    gm[1] = cc_ * (same & (tt > tp))
    gm[2] = cc_ * (same & (tt >= tp))
    gm[3] = cc_ * (same & (tt < tp))
    gm[4, :, 0] = cc_ * (np.arange(128) < 64)
    gm[4, :, 1] = cc_ * (np.arange(128) >= 64)
    c["gla_mats"] = gm
    c["gla_mask"] = _bf(np.stack([(same & (tt <= tp)), (same & (tt >= tp))]).astype(np.float32))
    kp = np.ones((128, 2, 32), np.float32)
    if kind == "prompt":
        for n_ in range(32):
            if n_ % 4 == 0:
                kp[:, 0, n_] = 0.0
            if n_ % 4 == 3:
                kp[:, 1, n_] = 0.0
    c["gla_keep"] = kp
    return c


def perm_swap(R):
    q = R // 4
    return np.array([d + q if (d // q) % 2 == 0 else d - q for d in range(R)])


def host_weights(inp):
    w = {}
    w["w_ada"] = np.ascontiguousarray(inp["w_ada"], dtype=np.float32)
    w["b_adaT"] = np.ascontiguousarray(inp["b_ada"].reshape(L, 48, 128).transpose(0, 2, 1), dtype=np.float32)
    w_in = np.asarray(inp["w_in"], dtype=np.float32)
    p32 = perm_swap(32); p64 = perm_swap(64)
    kr = w_in[:, :, 384:416][:, :, p32]
    sq = w_in[:, :, 1472:1728].reshape(L, D, 4, 64)[:, :, :, p64].reshape(L, D, 256)
    sk = w_in[:, :, 1728:1856].reshape(L, D, 2, 64)[:, :, :, p64].reshape(L, D, 128)
    w["w_in"] = np.ascontiguousarray(np.concatenate([w_in, kr, sq, sk], axis=2))
    w["w_gate"] = np.ascontiguousarray(w_in[:, :, 1984:6080].reshape(L, 8, 128, 4, 8, 128).transpose(0, 4, 2, 1, 3, 5).reshape(L, 8, 128, 8, 512))
    w["w_branch"] = np.ascontiguousarray(inp["w_branch"], dtype=np.float32)
    w["w_out"] = np.ascontiguousarray(inp["w_out"], dtype=np.float32)
    w_uq = np.asarray(inp["w_uq"], dtype=np.float32)
    uq_sw = w_uq.reshape(L, 256, 4, 96)[:, :, :, 64:96][:, :, :, p32].reshape(L, 256, 128)
    w["w_uq"] = np.ascontiguousarray(np.concatenate([w_uq, uq_sw], axis=2))
    w["w_ukv"] = np.ascontiguousarray(inp["w_ukv"], dtype=np.float32)
    w["mla_q_norm"] = np.ascontiguousarray(np.asarray(inp["mla_q_norm"], dtype=np.float32).reshape(L, 2, 128).transpose(0, 2, 1))
    w["mla_kv_norm"] = np.ascontiguousarray(inp["mla_kv_norm"], dtype=np.float32)
    w["w_gla_a"] = np.ascontiguousarray(np.stack([inp["w_gla_a_fwd"], inp["w_gla_a_bwd"]], axis=1), dtype=np.float32)
    w["b_gla_a"] = np.ascontiguousarray(np.stack([inp["b_gla_a_fwd"], inp["b_gla_a_bwd"]], axis=1), dtype=np.float32)
    w["gla_norm"] = np.ascontiguousarray(inp["gla_norm"], dtype=np.float32)
    w["swa_sink"] = np.ascontiguousarray(np.broadcast_to(np.asarray(inp["swa_sink"], dtype=np.float32)[:, None, :], (L, 128, 4)))
    w["w_peer_q"] = np.ascontiguousarray(np.asarray(inp["w_peer_q"], dtype=np.float32).reshape(L, 8, 128, 8, 256).transpose(0, 3, 2, 1, 4))
    w["peer_keysT"] = np.ascontiguousarray(np.asarray(inp["peer_keys"], dtype=np.float32).transpose(0, 1, 2, 4, 3))
    w["peer_uT"] = np.ascontiguousarray(np.asarray(inp["peer_u"], dtype=np.float32).reshape(L, 64, 256, 8, 128).transpose(0, 1, 4, 3, 2))
    w["peer_v"] = np.ascontiguousarray(np.asarray(inp["peer_v"], dtype=np.float32).reshape(L, 64, 2, 128, D).transpose(0, 1, 3, 2, 4))
    w["ln"] = np.ascontiguousarray(np.stack([inp["ln1_g"], inp["ln1_b"], inp["ln2_g"], inp["ln2_b"]], axis=1), dtype=np.float32)
    return w


def core_inputs(inp, core, W, CS, CP):
    m = dict(W)
    if core < 2:
        m.update(CS)
        m["x"] = np.ascontiguousarray(inp["x_sample"][core], dtype=np.float32)
        cond = np.asarray(inp["c"][core], dtype=np.float32)
        m["cache_ckv"] = np.ascontiguousarray(inp["cache_mla_ckv"][core], dtype=np.float32)
        m["cache_krope"] = np.ascontiguousarray(inp["cache_mla_krope"][core], dtype=np.float32)
        m["cache_swak"] = np.ascontiguousarray(inp["cache_swa_k"][core], dtype=np.float32)
        m["cache_swav"] = np.ascontiguousarray(inp["cache_swa_v"][core], dtype=np.float32)
        m["gla_init"] = np.ascontiguousarray(np.asarray(inp["state_gla"][core], dtype=np.float32).reshape(L, 2, 128, 64))
    else:
        m.update(CP)
        j = core - 2 if core < 6 else 0
        m["x"] = np.ascontiguousarray(np.asarray(inp["x_prompt"][8 * j:8 * j + 8], dtype=np.float32).reshape(T, D))
        cond = np.asarray(inp["c_ctx"], dtype=np.float32)
        m["cache_ckv"] = np.zeros((L, 512, 128), np.float32)
        m["cache_krope"] = np.zeros((L, 512, 32), np.float32)
        m["cache_swak"] = np.zeros((L, 2, 512, 64), np.float32)
        m["cache_swav"] = np.zeros((L, 2, 512, 64), np.float32)
        m["gla_init"] = np.zeros((L, 2, 128, 64), np.float32)
    m["condT"] = np.ascontiguousarray(cond.reshape(8, 128).T)
    return m


_CACHE = {}


def kernel(**inputs):
    cores = inputs.pop("_cores", list(range(8)))
    debug = inputs.pop("_debug", None)
    stop_after = inputs.pop("_stop_after", None)
    prog = Prog(debug=debug, stop_after=stop_after)
    nc = prog.build()
    W = host_weights(inputs)
    CS = host_consts("sample")
    CP = host_consts("prompt")
    in_maps = [core_inputs(inputs, c, W, CS, CP) for c in cores]
    import os as _os
    if _os.environ.get("KTRACE"):
        res = run_bass_kernel_spmd(nc, in_maps, core_ids=list(range(len(cores))), trace=True)
        print("EXEC_TIME_NS", res.exec_time_ns)
        globals()["_LAST_RES"] = res
    else:
        res = run_bass_kernel_spmd(nc, in_maps, core_ids=list(range(len(cores))))
    R = res.results
    if debug is not None:
        return R
    y_sample = np.stack([R[0]["y"], R[1]["y"]], axis=0)
    y_prompt = np.concatenate([R[2 + j]["y"].reshape(8, 256, D) for j in range(4)], axis=0)
    ckv = np.concatenate([R[2 + j]["ckv_o"].reshape(L, 8, 256, 128).transpose(1, 0, 2, 3) for j in range(4)], axis=0)
    kr = np.concatenate([R[2 + j]["krope_o"].reshape(L, 8, 256, 32).transpose(1, 0, 2, 3) for j in range(4)], axis=0)
    kvs = [R[2 + j]["swakv_o"].reshape(L, 8, 256, 2, 2, 64) for j in range(4)]
    sk = np.concatenate([a[:, :, :, 0].transpose(1, 0, 3, 2, 4) for a in kvs], axis=0)
    sv = np.concatenate([a[:, :, :, 1].transpose(1, 0, 3, 2, 4) for a in kvs], axis=0)
    gl = np.concatenate([R[2 + j]["gla_o"].reshape(L, 8, 2, 4, 32, 64).transpose(1, 0, 2, 3, 4, 5) for j in range(4)], axis=0)
    f = lambda a: np.ascontiguousarray(a, dtype=np.float32)
    return (f(y_prompt), f(y_sample), f(ckv), f(kr), f(sk), f(sv), f(gl))
```

```python
import numpy as np
import ml_dtypes
from contextlib import ExitStack
import concourse.bass as bass
import concourse.mybir as mybir
from concourse.bass_utils import run_bass_kernel_spmd

F32 = mybir.dt.float32
BF16 = mybir.dt.bfloat16
U32 = mybir.dt.uint32
AF = mybir.ActivationFunctionType
ALU = mybir.AluOpType
AX = mybir.AxisListType
s_ = np.s_

ENGS = ("pe", "act", "dve", "pool", "sp")
SAME_ENGINE_SYNC = True
import os as _os0
SES_ALL = not bool(_os0.environ.get("KNOSES"))

T = 2048
NB = 16
D = 1024
L = 2
ALPHA = (2.0 * L) ** 0.25
EPS = 1e-6
NEG = -30000.0
MLA_SCALE = 96.0 ** -0.5
SWA_SCALE = 64.0 ** -0.5
WIN_EXT = 6496


class V:
    def __init__(self, ap, toks):
        self.ap = ap
        self.toks = toks

    def m(self, fn):
        return V(fn(self.ap), self.toks)


class Buf:
    def __init__(self, name, t, nsub=1):
        self.name = name
        self.t = t
        self.nsub = nsub

    def tok(self, subs=None):
        if subs is None:
            return [(self.name, s) for s in range(self.nsub)]
        if isinstance(subs, int):
            subs = [subs]
        return [(self.name, s) for s in subs]

    def v(self, key=None, subs=None):
        ap = self.t[:] if key is None else self.t[key]
        return V(ap, self.tok(subs))


class Sched:
    def __init__(self, nc, es, nd=24):
        self.nc = nc
        self.ops = {e: [] for e in ENGS}
        self.cnt = {e: 0 for e in ENGS}
        self.known = {e: {} for e in ENGS}
        self.nd = nd
        self.dma_tot = [0] * nd
        self.dma_rr = 0
        self.last_w = {}
        self.readers = {}
        self.sem = {e: es.enter_context(nc.semaphore("sem_" + e)) for e in ENGS if e != "sp"}
        self.dsem = [es.enter_context(nc.semaphore("dsem%d" % i)) for i in range(nd)]
        self.milestones = {e: set() for e in ENGS}

    def _need(self, eng, dep, waits):
        kind, key, val = dep
        if kind == "eng" and key == eng:
            if eng in ("pe", "sp") or (eng in ("act", "dve") and not SES_ALL) or not SAME_ENGINE_SYNC:
                return
        k = (kind, key)
        if self.known[eng].get(k, 0) >= val:
            return
        self.known[eng][k] = val
        waits.append((kind, key, val))
        if kind == "eng":
            self.milestones[key].add(val)

    def _deps(self, eng, reads, writes):
        waits = []
        for t in reads:
            lw = self.last_w.get(t)
            if lw is not None:
                self._need(eng, lw, waits)
        for t in writes:
            lw = self.last_w.get(t)
            if lw is not None:
                self._need(eng, lw, waits)
            for r in self.readers.get(t, ()):
                self._need(eng, r, waits)
        return waits

    def _commit(self, me, reads, writes):
        for t in reads:
            self.readers.setdefault(t, []).append(me)
        for t in writes:
            self.last_w[t] = me
            self.readers[t] = []

    def op(self, eng, fn, reads=(), writes=()):
        reads = list(reads); writes = list(writes)
        waits = self._deps(eng, reads, writes)
        self.cnt[eng] += 1
        me = ("eng", eng, self.cnt[eng])
        self.ops[eng].append((waits, fn, ("eng", self.cnt[eng])))
        self._commit(me, reads, writes)

    def dma(self, eng, fn, reads=(), writes=()):
        reads = list(reads); writes = list(writes)
        i = self.dma_rr
        self.dma_rr = (i + 1) % self.nd
        waits = []
        if self.dma_tot[i] > 0:
            self._need(eng, ("dma", i, self.dma_tot[i]), waits)
        waits += self._deps(eng, reads, writes)
        self.dma_tot[i] += 16
        me = ("dma", i, self.dma_tot[i])
        self.cnt[eng] += 1
        self.ops[eng].append((waits, fn, ("dma", i)))
        self._commit(me, reads, writes)

    def _last_seq(self, e):
        for w, fn, inc in reversed(self.ops[e]):
            if inc is not None and inc[0] == "eng":
                return inc[1]
        return 0

    def barrier(self):
        lasts = {e: self._last_seq(e) for e in ENGS}
        for e in ENGS:
            waits = []
            for e2 in ENGS:
                if e2 != e and e2 != "sp" and lasts[e2] > 0:
                    self._need(e, ("eng", e2, lasts[e2]), waits)
            for i in range(self.nd):
                if self.dma_tot[i] > 0:
                    self._need(e, ("dma", i, self.dma_tot[i]), waits)
            if waits:
                self.ops[e].append((waits, None, None))
        self.last_w = {}
        self.readers = {}

    def finish(self):
        self.barrier()

    def emit(self, blk):
        rank = {}
        for e in ENGS:
            ms = sorted(self.milestones[e])
            rank[e] = {s: i + 1 for i, s in enumerate(ms)}

        def run(e, eng):
            for waits, fn, inc in self.ops[e]:
                for kind, key, val in waits:
                    if kind == "eng":
                        eng.wait_ge(self.sem[key], rank[key][val])
                    else:
                        eng.wait_ge(self.dsem[key], val)
                if fn is None:
                    continue
                ins = fn(eng)
                if inc[0] == "dma":
                    ins.then_inc(self.dsem[inc[1]], 16)
                elif inc[1] in rank[e]:
                    ins.then_inc(self.sem[e], 1)

        blk.sync(lambda eng: run("sp", eng))
        blk.scalar(lambda eng: run("act", eng))
        blk.vector(lambda eng: run("dve", eng))
        blk.gpsimd(lambda eng: run("pool", eng))
        blk.tensor(lambda eng: run("pe", eng))


class Prog:
    def __init__(self, debug=None, stop_after=None):
        self.debug = debug or []
        self.stop_after = stop_after
        self.nc = bass.Bass("TRN2", target_bir_lowering=False)
        self.din = {}
        self.dout = {}

    def inp(self, name, shape, dt=F32):
        self.din[name] = self.nc.dram_tensor(name, list(shape), dt, kind="ExternalInput").ap()
        return self.din[name]

    def outp(self, name, shape, dt=F32):
        self.dout[name] = self.nc.dram_tensor(name, list(shape), dt, kind="ExternalOutput").ap()
        return self.dout[name]

    def sb(self, es, name, shape, dt=F32, nsub=1):
        self.uid = getattr(self, "uid", 0) + 1
        name = "%s_u%d" % (name, self.uid)
        return Buf(name, es.enter_context(self.nc.sbuf_tensor(name, list(shape), dt)), nsub)

    def I(self, eng, meth, out, *args, **kw):
        def conv(a):
            return a.ap if isinstance(a, V) else a
        reads = []
        writes = list(out.toks)
        for a in list(args) + list(kw.values()):
            if isinstance(a, V):
                reads += a.toks
        if "accum_out" in kw:
            writes += kw["accum_out"].toks
        a2 = [conv(a) for a in args]
        k2 = {k: conv(v) for k, v in kw.items()}
        o = out.ap
        self.S.op(eng, lambda e: getattr(e, meth)(o, *a2, **k2), reads, writes)

    def dma(self, q, out, in_):
        reads = in_.toks if isinstance(in_, V) else []
        writes = out.toks if isinstance(out, V) else []
        o = out.ap if isinstance(out, V) else out
        i = in_.ap if isinstance(in_, V) else in_
        self.S.dma(q, lambda e: e.dma_start(out=o, in_=i), reads, writes)

    def mm(self, out, lhsT, rhs, start, stop):
        self.I("pe", "matmul", out, lhsT=lhsT, rhs=rhs, start=start, stop=stop)

    def build(self):
        nc = self.nc
        P = self
        inp = self.inp
        x_d = inp("x", [T, D])
        condT_d = inp("condT", [128, 8])
        w_ada_d = inp("w_ada", [L, D, 6 * D])
        b_adaT_d = inp("b_adaT", [L, 128, 48])
        w_in_d = inp("w_in", [L, D, WIN_EXT])
        w_branch_d = inp("w_branch", [L, 4, 256, D])
        w_out_d = inp("w_out", [L, D, D])
        inp("w_gate", [L, 8, 128, 8, 512])
        ln_d = inp("ln", [L, 4, D])
        dft_c_d = inp("dft_c", [T, T], BF16)
        dft_s_d = inp("dft_s", [T, T], BF16)
        cdft_d = inp("cdft", [2, 128, 128], BF16)
        ident_d = inp("ident", [128, 128])
        inp("w_peer_q", [L, 8, 128, 8, 256])
        inp("w_uq", [L, 256, 512]); inp("w_ukv", [L, 128, 512]); inp("mla_q_norm", [L, 128, 2]); inp("mla_kv_norm", [L, 128])
        inp("gla_mats", [5, 128, 128]); inp("gla_mask", [2, 128, 128], BF16); inp("gla_keep", [128, 2, 32]); inp("gla_init", [L, 2, 128, 64])
        inp("w_gla_a", [L, 2, 16, 128]); inp("b_gla_a", [L, 2, 128]); inp("gla_norm", [L, 64])
        inp("rope_m", [2, 32, T]); inp("rope_s", [2, 64, T]); inp("bias_m", [128, 160]); inp("bias_s", [128, 4])
        inp("mask_s", [128, NB, 2, 128], BF16); inp("swa_sink", [L, 128, 4])
        inp("cache_ckv", [L, 512, 128]); inp("cache_krope", [L, 512, 32]); inp("cache_swak", [L, 2, 512, 64]); inp("cache_swav", [L, 2, 512, 64])
        self.outp("ckv_o", [L, T, 128]); self.outp("krope_o", [L, T, 32]); self.outp("swakv_o", [L, T, 256]); self.outp("gla_o", [L, 8, 2, 128, 64])
        inp("peer_keysT", [L, 8, 2, 128, 128])
        inp("peer_uT", [L, 64, 128, 8, 256])
        inp("peer_v", [L, 64, 128, 2, D])
        y_d = self.outp("y", [T, D])
        self.ubf = Buf("ubf", nc.dram_tensor("peer_u_bf", [L, 64, 128, 8 * 256], BF16, kind="Internal").ap(), nsub=L * 16)
        self.vbf = Buf("vbf", nc.dram_tensor("peer_v_bf", [L, 64, 128, 2 * D], BF16, kind="Internal").ap(), nsub=L * 16)
        dbg = {}
        for name, shape in self.debug:
            dbg[name] = self.outp(name, shape, F32)

        with ExitStack() as es:
            self.S = S = Sched(nc, es)
            sb = lambda *a, **k: P.sb(es, *a, **k)
            x = sb("xres", [128, NB, D], F32, nsub=NB)
            ident = sb("identS", [128, 128], F32)
            ones = sb("onesS", [128, 128], F32)
            mcol = sb("mcol", [128, L, 48], F32, nsub=L)
            condT = sb("condTS", [128, 8], F32)
            ps = [Buf("ps%d" % i, es.enter_context(nc.psum_tensor("ps%d" % i, [128, 512], F32))) for i in range(8)]
            self.x, self.ps, self.ident, self.ones, self.mcol = x, ps, ident, ones, mcol
            iota = sb("iotaS", [128, 128], F32)
            P.I("pool", "iota", iota.v(), pattern=[[1, 128]], base=0, channel_multiplier=0, allow_small_or_imprecise_dtypes=True)
            bm = sb("bmS", [128, 8], F32)
            P.dma("sp", bm.v(), inp("bm", [128, 8])[:, :])
            self.iota, self.bm = iota, bm

            for tb in range(NB):
                P.dma("sp", x.v(s_[:, tb, :], tb), x_d[tb * 128:(tb + 1) * 128, :])
            P.dma("sp", ident.v(), ident_d[:, :])
            P.I("pool", "memset", ones.v(), 1.0)
            P.dma("sp", condT.v(), condT_d[:, :])

            with ExitStack() as es0:
                scond = P.sb(es0, "scond", [128, 8], F32)
                wa = [P.sb(es0, "wa%d" % i, [128, 8, 768], F32) for i in range(1)]
                badaT = P.sb(es0, "badaT", [128, L, 48], F32)
                P.I("act", "activation", scond.v(), condT.v(), AF.Silu)
                P.dma("sp", badaT.v(), b_adaT_d.rearrange("l p j -> p l j"))
                n = 0
                for l in range(L):
                    for cg in range(8):
                        pt = ps[cg % 2]
                        wt = wa[0]
                        P.dma("sp", wt.v(), w_ada_d[l, :, cg * 768:(cg + 1) * 768].rearrange("(k p) c -> p k c", p=128))
                        for j in range(6):
                            for k in range(8):
                                P.mm(pt.v(s_[:, j:j + 1]), wt.v(s_[:, k, j * 128:(j + 1) * 128]), scond.v(s_[:, k:k + 1]),
                                     start=(k == 0), stop=(k == 7))
                        P.I("dve", "tensor_tensor", mcol.v(s_[:, l, cg * 6:(cg + 1) * 6], l), pt.v(s_[:, 0:6]),
                            badaT.v(s_[:, l, cg * 6:(cg + 1) * 6]), op=ALU.add)
                    for a in (8, 32):
                        P.I("dve", "tensor_scalar_add", mcol.v(s_[:, l, a:a + 8], l), mcol.v(s_[:, l, a:a + 8], l), 1.0)
                S.barrier()
            if "mcol" in dbg:
                P.dma("pool", dbg["mcol"], mcol.v())

            for l in range(L):
                self.layer(l, dbg)
                if self.stop_after == ("layer", l):
                    break

            for tb in range(NB):
                P.dma("sp", y_d[tb * 128:(tb + 1) * 128, :], x.v(s_[:, tb, :], tb))
            S.finish()
            blk = es.enter_context(nc.Block())
            S.emit(blk)
        return nc

    def ln_block(self, tmp, src, dst):
        P = self
        st, mv, rstd, nmr = tmp
        P.I("dve", "bn_stats", st.v(s_[:, 0, :]), src.m(lambda a: a[:, 0:512]))
        P.I("dve", "bn_stats", st.v(s_[:, 1, :]), src.m(lambda a: a[:, 512:1024]))
        P.I("dve", "bn_aggr", mv.v(), st.v())
        P.I("act", "activation", rstd.v(), mv.v(s_[:, 1:2]), AF.Sqrt, bias=EPS, scale=1.0)
        P.I("dve", "reciprocal", rstd.v(), rstd.v())
        P.I("dve", "scalar_tensor_tensor", nmr.v(), mv.v(s_[:, 0:1]), -1.0, rstd.v(), op0=ALU.mult, op1=ALU.mult)
        P.I("act", "activation", dst, src, AF.Identity, bias=nmr.v(), scale=rstd.v())

    def ln_tmp(self, es, tag):
        return (self.sb(es, "st" + tag, [128, 2, 6]), self.sb(es, "mv" + tag, [128, 2]),
                self.sb(es, "rstd" + tag, [128, 1]), self.sb(es, "nmr" + tag, [128, 1]))

    def mod_to_uT(self, l, which):
        P = self; S = self.S
        x, uT, ps, mcol, ident = self.x, self.uT, self.ps, self.mcol, self.ident
        sh0 = 0 if which == 0 else 24
        sc0 = 8 if which == 0 else 32
        with ExitStack() as es:
            tmps = [self.ln_tmp(es, "m%d" % i) for i in range(2)]
            xn = [P.sb(es, "xn%d" % i, [128, D]) for i in range(2)]
            for tb in range(NB):
                xnb = xn[tb % 2]
                self.ln_block(tmps[tb % 2], x.v(s_[:, tb, :], tb), xnb.v())
                for half in range(2):
                    pt = ps[(tb * 2 + half) % 4]
                    for j in range(4):
                        k = half * 4 + j
                        P.I("pe", "transpose", pt.v(s_[:, j * 128:(j + 1) * 128]), xnb.v(s_[:, k * 128:(k + 1) * 128]), ident.v())
                    for j in range(4):
                        k = half * 4 + j
                        eng = "dve" if j % 2 == 0 else "pool"
                        eng = "dve"
                        P.I(eng, "tensor_scalar", uT.v(s_[:, k, tb * 128:(tb + 1) * 128], tb), pt.v(s_[:, j * 128:(j + 1) * 128]),
                            mcol.v(s_[:, l, sc0 + k:sc0 + k + 1], l), mcol.v(s_[:, l, sh0 + k:sh0 + k + 1], l), op0=ALU.mult, op1=ALU.add)
            S.barrier()

    def load_w(self, wb, l, c0, n):
        w_in_d = self.din["w_in"]
        self.dma("pool", wb.v(s_[:, :, 0:n]), w_in_d[l, :, c0:c0 + n].rearrange("(k p) c -> p k c", p=128))

    def proj_fm(self, wb, wc0, m, dst_fn, pbase=0):
        P = self
        for tg in range(4):
            pt = self.ps[self.psi % 8]; self.psi += 1
            for k in range(8):
                P.mm(pt.v(s_[0:m, :]), wb.v(s_[:, k, wc0:wc0 + m]), self.uT.v(s_[:, k, tg * 512:(tg + 1) * 512], range(tg * 4, tg * 4 + 4)),
                     start=(k == 0), stop=(k == 7))
            dst_fn(tg, pt.v(s_[0:m, :]))

    def proj_tm(self, wb, wc0, n, dst_fn):
        P = self
        for tb in range(NB):
            pt = self.ps[self.psi % 8]; self.psi += 1
            for k in range(8):
                P.mm(pt.v(s_[:, 0:n]), self.uT.v(s_[:, k, tb * 128:(tb + 1) * 128], tb), wb.v(s_[:, k, wc0:wc0 + n]),
                     start=(k == 0), stop=(k == 7))
            dst_fn(tb, pt.v(s_[:, 0:n]))


    def rope_evac(self, dst, pa, pb, cosv, sinv, tmpa, tmpb):
        P = self
        P.I("dve", "tensor_tensor", tmpa, pa, cosv, op=ALU.mult)
        P.I("dve", "tensor_tensor", tmpb, pb, sinv, op=ALU.mult)
        P.I("pool", "tensor_tensor", dst, tmpa, tmpb, op=ALU.add)

    def fm_group(self, wb, specs, tg):
        P = self
        outs = []
        for (wc0, m) in specs:
            pt = self.ps[self.psi % 6]; self.psi += 1
            for k in range(8):
                P.mm(pt.v(s_[0:m, :]), wb.v(s_[:, k, wc0:wc0 + m]), self.uT.v(s_[:, k, tg * 512:(tg + 1) * 512], range(tg * 4, tg * 4 + 4)),
                     start=(k == 0), stop=(k == 7))
            outs.append(pt.v(s_[0:m, :]))
        return outs

    def mla(self, l, dbg):
        P = self; S = self.S
        d = self.din
        ps, ident, ones = self.ps, self.ident, self.ones
        NK = 20
        with ExitStack() as es:
            wb = P.sb(es, "wbM", [128, 8, 448], BF16)
            wuq = P.sb(es, "wuq", [128, 2, 512], BF16)
            wukv = P.sb(es, "wukv", [128, 2, 256], BF16)
            gq = P.sb(es, "gq", [128, 2], F32)
            gkv = P.sb(es, "gkv", [128, 128], F32)
            cosT = P.sb(es, "cosM", [32, 512], F32)
            sinT = P.sb(es, "sinM", [32, 512], F32)
            biasM = P.sb(es, "biasM", [128, 8 * NK], F32)
            qnT = P.sb(es, "qnT", [128, 2, T], BF16, nsub=4)
            rs = P.sb(es, "qrs", [128, 512], F32)
            qno = P.sb(es, "qno", [64, T], BF16, nsub=4)
            qro = P.sb(es, "qro", [32, T], BF16, nsub=4)
            kno = P.sb(es, "kno", [64, 2560], BF16)
            kro = P.sb(es, "kro", [32, 2560], BF16)
            ckvT = P.sb(es, "ckvT", [128, 2560], BF16, nsub=NK)
            Vt = P.sb(es, "VtM", [128, NK, 4, 65], BF16, nsub=NK)
            ta = P.sb(es, "rta", [128, 512], F32)
            tb_ = P.sb(es, "rtb", [128, 512], F32)
            kvt = P.sb(es, "kvt", [128, 160], F32)
            ckt = [P.sb(es, "ckt%d" % i, [128, 128], F32) for i in range(2)]
            ss = P.sb(es, "kss", [128, 1], F32)
            junk = P.sb(es, "kjunk", [128, 128], F32)
            PT = [P.sb(es, "PT%d" % i, [128, 256], BF16) for i in range(2)]
            oacc = P.sb(es, "oacc", [128, NB, 128], BF16, nsub=NB)
            identb = P.sb(es, "identb", [128, 128], BF16)
            P.I("dve", "tensor_copy", identb.v(), ident.v())
            rden = P.sb(es, "rden", [128, 1], F32)
            cch = P.sb(es, "cch", [128, 4, 160], F32)
            self.load_w(wb, l, 0, 416)
            P.dma("pool", wb.v(s_[:, :, 416:448]), d["w_in"][l, :, 6080:6112].rearrange("(k p) c -> p k c", p=128))
            P.dma("pool", wuq.v(), d["w_uq"][l].rearrange("(k p) c -> p k c", p=128))
            for two in range(2):
                P.dma("pool", wukv.v(s_[:, two, :]).m(lambda a: a.rearrange("p (h c) -> p h c", h=4)), d["w_ukv"][l].rearrange("p (h two c) -> p two h c", h=4, two=2)[:, two, :, :])
            P.dma("sp", gq.v(), d["mla_q_norm"][l])
            P.dma("sp", gkv.v(), d["mla_kv_norm"][l].partition_broadcast(128))
            P.dma("sp", biasM.v(), d["bias_m"][:, :])
            P.I("pool", "memset", Vt.v(), 1.0)
            for tg in range(4):
                tsl = slice(tg * 512, (tg + 1) * 512)
                P.dma("sp", cosT.v(), d["rope_m"][0, :, tsl])
                P.dma("sp", sinT.v(), d["rope_m"][1, :, tsl])
                pq = self.fm_group(wb, [(0, 128), (128, 128)], tg)
                P.I("act", "activation", ta.v(), pq[0], AF.Square)
                P.I("act", "activation", tb_.v(), pq[1], AF.Square)
                pt = ps[self.psi % 6]; self.psi += 1
                P.mm(pt.v(), ones.v(), ta.v(), start=True, stop=False)
                P.mm(pt.v(), ones.v(), tb_.v(), start=False, stop=True)
                P.I("act", "activation", rs.v(), pt.v(), AF.Sqrt, bias=EPS, scale=1.0 / 256)
                P.I("dve", "reciprocal", rs.v(), rs.v())
                for c in range(2):
                    P.I("dve", "scalar_tensor_tensor", qnT.v(s_[:, c, tsl], tg), pq[c], gq.v(s_[:, c:c + 1]), rs.v(), op0=ALU.mult, op1=ALU.mult)
                pk = self.fm_group(wb, [(384, 32), (416, 32)], tg)
                self.rope_evac(kro.v(s_[:, 512 + tg * 512:512 + (tg + 1) * 512]), pk[0], pk[1], cosT.v(), sinT.v(),
                               ta.v(s_[0:32, :]), tb_.v(s_[0:32, :]))
            for tb in range(NB):
                pt = ps[self.psi % 6]; self.psi += 1
                for k in range(8):
                    P.mm(pt.v(s_[:, 0:160]), self.uT.v(s_[:, k, tb * 128:(tb + 1) * 128], tb), wb.v(s_[:, k, 256:416]), start=(k == 0), stop=(k == 7))
                P.I("act", "activation", kvt.v(), pt.v(s_[:, 0:160]), AF.Copy)
                ck = ckt[tb % 2]
                P.I("act", "activation", junk.v(), kvt.v(s_[:, 0:128]), AF.Square, accum_out=ss.v())
                P.I("act", "activation", ss.v(), ss.v(), AF.Sqrt, bias=EPS, scale=1.0 / 128)
                P.I("dve", "reciprocal", ss.v(), ss.v())
                P.I("dve", "scalar_tensor_tensor", ck.v(), kvt.v(s_[:, 0:128]), ss.v(), gkv.v(), op0=ALU.mult, op1=ALU.mult)
                P.dma("sp", self.dout["ckv_o"][l, tb * 128:(tb + 1) * 128, :], ck.v())
                P.dma("sp", self.dout["krope_o"][l, tb * 128:(tb + 1) * 128, :], kvt.v(s_[:, 128:160]))
                p2 = ps[self.psi % 6]; self.psi += 1
                P.I("pe", "transpose", p2.v(s_[:, 0:128]), ck.v(), ident.v())
                P.I("act", "activation", ckvT.v(s_[:, 512 + tb * 128:512 + (tb + 1) * 128], 4 + tb), p2.v(s_[:, 0:128]), AF.Copy)
            P.dma("sp", cch.v(s_[:, :, 0:128]), d["cache_ckv"][l].rearrange("(j p) c -> p j c", p=128))
            P.dma("sp", cch.v(s_[:, :, 128:160]), d["cache_krope"][l].rearrange("(j p) c -> p j c", p=128))
            for j in range(4):
                p2 = ps[self.psi % 6]; self.psi += 1
                P.I("pe", "transpose", p2.v(s_[:, 0:128]), cch.v(s_[:, j, 0:128]), ident.v())
                P.I("act", "activation", ckvT.v(s_[:, j * 128:(j + 1) * 128], j), p2.v(s_[:, 0:128]), AF.Copy)
                p3 = ps[self.psi % 6]; self.psi += 1
                P.I("pe", "transpose", p3.v(s_[0:32, 0:128]), cch.v(s_[:, j, 128:160]), ident.v())
                P.I("act", "activation", kro.v(s_[:, j * 128:(j + 1) * 128]), p3.v(s_[0:32, 0:128]), AF.Copy)
            for kc in range(NK):
                pt = ps[self.psi % 6]; self.psi += 1
                P.mm(pt.v(s_[:, 0:256]), ckvT.v(s_[:, kc * 128:(kc + 1) * 128], kc), wukv.v(s_[:, 1, :]), start=True, stop=True)
                P.I("dve", "tensor_copy", Vt.v(s_[:, kc, :, 0:64], kc), pt.v(s_[:, 0:256]).m(lambda a: a.rearrange("p (h c) -> p h c", h=4)))
            n = 0
            for h in range(4):
                for tg in range(4):
                    tsl = slice(tg * 512, (tg + 1) * 512)
                    P.dma("sp", cosT.v(), d["rope_m"][0, :, tsl])
                    P.dma("sp", sinT.v(), d["rope_m"][1, :, tsl])
                    pn = ps[self.psi % 6]; pa = ps[(self.psi + 1) % 6]; pb = ps[(self.psi + 2) % 6]; self.psi += 3
                    for c in range(2):
                        P.mm(pn.v(s_[0:64, :]), wuq.v(s_[:, c, h * 96:h * 96 + 64]), qnT.v(s_[:, c, tsl], tg), start=(c == 0), stop=(c == 1))
                    for c in range(2):
                        P.mm(pa.v(s_[0:32, :]), wuq.v(s_[:, c, h * 96 + 64:h * 96 + 96]), qnT.v(s_[:, c, tsl], tg), start=(c == 0), stop=(c == 1))
                    for c in range(2):
                        P.mm(pb.v(s_[0:32, :]), wuq.v(s_[:, c, 384 + h * 32:384 + h * 32 + 32]), qnT.v(s_[:, c, tsl], tg), start=(c == 0), stop=(c == 1))
                    P.I("act", "activation", qno.v(s_[:, tsl], tg), pn.v(s_[0:64, :]), AF.Copy)
                    self.rope_evac(qro.v(s_[:, tsl], tg), pa.v(s_[0:32, :]), pb.v(s_[0:32, :]), cosT.v(), sinT.v(),
                                   ta.v(s_[0:32, :]), tb_.v(s_[0:32, :]))
                for g5 in range(5):
                    gsl = slice(g5 * 512, (g5 + 1) * 512)
                    pt = ps[self.psi % 6]; self.psi += 1
                    P.mm(pt.v(s_[0:64, :]), wukv.v(s_[:, 0, h * 64:(h + 1) * 64]), ckvT.v(s_[:, gsl], range(g5 * 4, g5 * 4 + 4)), start=True, stop=True)
                    P.I("act", "activation", kno.v(s_[:, gsl]), pt.v(s_[0:64, :]), AF.Copy)
                for qu in range(8):
                    qsl = slice(qu * 256, (qu + 1) * 256)
                    po = [ps[6], ps[7]]
                    def emitS(kc):
                        ksl = slice(kc * 128, (kc + 1) * 128)
                        pt = ps[self.psi % 6]; self.psi += 1
                        P.mm(pt.v(s_[:, 0:256]), kno.v(s_[:, ksl]), qno.v(s_[:, qsl], qu // 2), start=True, stop=False)
                        P.mm(pt.v(s_[:, 0:256]), kro.v(s_[:, ksl]), qro.v(s_[:, qsl], qu // 2), start=False, stop=True)
                        return pt
                    pts = {0: emitS(0)}
                    for kc in range(NK):
                        if kc + 1 < NK:
                            pts[kc + 1] = emitS(kc + 1)
                        pt = pts.pop(kc)
                        pT = PT[n % 2]; n += 1
                        P.I("act", "activation", pT.v(), pt.v(s_[:, 0:256]), AF.Exp, bias=biasM.v(s_[:, qu * NK + kc:qu * NK + kc + 1]), scale=MLA_SCALE)
                        for qb in range(2):
                            P.mm(po[qb].v(s_[:, 0:65]), pT.v(s_[:, qb * 128:(qb + 1) * 128]), Vt.v(s_[:, kc, h, :], kc), start=(kc == 0), stop=(kc == NK - 1))
                    for qb in range(2):
                        tb = qu * 2 + qb
                        P.I("dve", "reciprocal", rden.v(), po[qb].v(s_[:, 64:65]))
                        P.I("dve", "tensor_scalar_mul", oacc.v(s_[:, tb, (h % 2) * 64:(h % 2) * 64 + 64], tb), po[qb].v(s_[:, 0:64]), rden.v())
                if h % 2 == 1:
                    for tb in range(NB):
                        p2 = ps[self.psi % 6]; self.psi += 1
                        P.mm(p2.v(s_[:, 0:128]), oacc.v(s_[:, tb, :], tb), identb.v(), start=True, stop=True)
                        P.I("act", "activation", self.brT[0].v(s_[:, h // 2, tb * 128:(tb + 1) * 128], tb), p2.v(s_[:, 0:128]), AF.Copy)
            S.barrier()

    def gla_block(self, l, tb, wb, afT, qT, kT, w2, b2, onesb, ktm, e1t, spt, eq, ek, el, mats, vtm, kl, gcol, qeT, keT, LNQ):
        P = self; ps = self.ps
        lsl = slice((tb % 4) * 128, (tb % 4) * 128 + 128)
        bsl = slice(tb * 128, (tb + 1) * 128)
        pt = ps[self.psi % 6]; self.psi += 1
        for k in range(8):
            P.mm(pt.v(s_[:, 0:384]), self.uT.v(s_[:, k, bsl], tb), wb.v(s_[:, k, 128:512]), start=(k == 0), stop=(k == 7))
        P.I("dve", "tensor_copy", ktm.v(), pt.v(s_[:, 0:128]))
        P.I("dve", "tensor_copy", vtm.v(s_[:, tb, :], tb), pt.v(s_[:, 128:384]))
        for dr in range(2):
            pz = ps[self.psi % 6]; self.psi += 1
            P.mm(pz.v(s_[:, 0:128]), afT.v(s_[:, dr, lsl]), w2.v(s_[:, dr, :]), start=True, stop=False)
            P.mm(pz.v(s_[:, 0:128]), onesb.v(), b2.v(s_[:, dr, :]), start=False, stop=True)
            P.I("act", "activation", e1t.v(), pz.v(s_[:, 0:128]), AF.Exp, scale=-1.0)
            P.I("act", "activation", spt.v(), e1t.v(), AF.Ln, bias=1.0, scale=1.0)
            mi = 0 if dr == 0 else 2
            if self.G1 <= 3:
                continue
            pl = ps[self.psi % 6]; self.psi += 1
            P.mm(pl.v(s_[:, 0:128]), mats.v(s_[:, mi + 1, :]), spt.v(), start=True, stop=True)
            P.I("act", "activation", el.v(), pl.v(s_[:, 0:128]), AF.Exp)
            P.I("dve", "tensor_tensor", kl[dr].v(s_[:, tb, :], tb), ktm.v(), el.v(), op=ALU.mult)
            for hp in range(2):
                pc = ps[self.psi % 6]; pg = ps[(self.psi + 1) % 6]; self.psi += 2
                P.mm(pc.v(s_[0:64, 0:128]), spt.v(s_[:, hp * 64:(hp + 1) * 64]), mats.v(s_[:, mi, :]), start=True, stop=True)
                P.mm(pg.v(s_[0:64, 0:2]), spt.v(s_[:, hp * 64:(hp + 1) * 64]), mats.v(s_[:, 4, 0:2]), start=True, stop=True)
                P.I("act", "activation", eq.v(s_[:, hp, :]), pc.v(s_[0:64, 0:128]), AF.Exp, bias=LNQ, scale=1.0)
                P.I("act", "activation", ek.v(s_[:, hp, :]), pc.v(s_[0:64, 0:128]), AF.Exp, scale=-1.0)
                P.I("act", "activation", gcol.v(s_[:, dr, hp, 2 * tb:2 * tb + 2], dr), pg.v(s_[0:64, 0:2]), AF.Exp)
            P.I("dve", "tensor_tensor", qeT[dr].v(s_[:, :, bsl], tb), qT.v(s_[:, :, lsl]), eq.v(), op=ALU.mult)
            P.I("dve", "tensor_tensor", keT[dr].v(s_[:, :, bsl], tb), kT.v(s_[:, :, lsl]), ek.v(), op=ALU.mult)

    def gla(self, l, dbg):
        P = self; S = self.S
        d = self.din
        ps, ident, ones = self.ps, self.ident, self.ones
        LNQ = float(np.log(32.0 ** -0.5))
        with ExitStack() as es:
            qeT = [P.sb(es, "qeT%d" % i, [64, 2, T], BF16, nsub=NB) for i in range(2)]
            keT = [P.sb(es, "keT%d" % i, [64, 2, T], BF16, nsub=NB) for i in range(2)]
            kl = [P.sb(es, "kl%d" % i, [128, NB, 128], BF16, nsub=NB) for i in range(2)]
            gcol = P.sb(es, "gcol", [64, 2, 2, 32], F32, nsub=2)
            vtm = P.sb(es, "vtm", [128, NB, 256], BF16, nsub=NB)
            mats = P.sb(es, "gmats", [128, 5, 128], F32)
            gmask = P.sb(es, "gmask", [128, 2, 128], BF16)
            keep = P.sb(es, "gkeep", [128, 2, 32], F32)
            identb = P.sb(es, "identbG", [128, 128], BF16)
            P.dma("sp", mats.v(), d["gla_mats"].rearrange("a p c -> p a c"))
            P.dma("sp", gmask.v(), d["gla_mask"].rearrange("a p c -> p a c"))
            P.dma("sp", keep.v(), d["gla_keep"][:, :, :])
            P.I("dve", "tensor_copy", identb.v(), ident.v())
            with ExitStack() as e1:
                wb = P.sb(e1, "wbG", [128, 8, 800], BF16)
                afT = P.sb(e1, "afT", [16, 2, 512], BF16)
                qT = P.sb(e1, "gqT", [64, 2, 512], BF16)
                kT = P.sb(e1, "gkT", [64, 2, 512], BF16)
                w2 = P.sb(e1, "gw2", [16, 2, 128], BF16)
                b2 = P.sb(e1, "gb2", [1, 2, 128], BF16)
                onesb = P.sb(e1, "onesb", [1, 128], BF16)
                ktm = P.sb(e1, "ktm", [128, 128], F32)
                e1t = P.sb(e1, "ge1", [128, 128], F32)
                spt = P.sb(e1, "gsp", [128, 128], F32)
                eq = P.sb(e1, "geq", [64, 2, 128], F32)
                ek = P.sb(e1, "gek", [64, 2, 128], F32)
                el = P.sb(e1, "gel", [128, 128], F32)
                self.load_w(wb, l, 672, 800)
                P.dma("pool", w2.v(), d["w_gla_a"][l].rearrange("a r c -> r a c"))
                P.dma("pool", b2.v(), d["b_gla_a"][l:l + 1, :, :])
                P.I("pool", "memset", onesb.v(), 1.0)
                import os
                G1 = int(os.environ.get("KGLA_G1", "99"))
                for tg in range(4 if G1 >= 2 else 0):
                    tsl = slice(tg * 512, (tg + 1) * 512)
                    pq = self.fm_group(wb, [(0, 64), (64, 64), (128, 64), (192, 64)], tg)
                    P.I("act", "activation", qT.v(s_[:, 0, :]), pq[0], AF.Copy)
                    P.I("dve", "tensor_copy", qT.v(s_[:, 1, :]), pq[1])
                    P.I("act", "activation", kT.v(s_[:, 0, :]), pq[2], AF.Copy)
                    P.I("dve", "tensor_copy", kT.v(s_[:, 1, :]), pq[3])
                    pq = self.fm_group(wb, [(768, 16), (784, 16)], tg)
                    P.I("act", "activation", afT.v(s_[:, 0, :]), pq[0], AF.Copy)
                    P.I("dve", "tensor_copy", afT.v(s_[:, 1, :]), pq[1])
                    for tb in range(tg * 4, tg * 4 + 4) if G1 >= 3 else []:
                        self.G1 = G1
                        self.gla_block(l, tb, wb, afT, qT, kT, w2, b2, onesb, ktm, e1t, spt, eq, ek, el, mats, vtm, kl, gcol, qeT, keT, LNQ)
                S.barrier()
            import os
            GSTOP = int(os.environ.get("KGLA_STOP", "99"))
            if GSTOP <= 1:
                return
            with ExitStack() as e2:
                of = P.sb(e2, "gof", [128, NB, 256], F32, nsub=NB)
                St = P.sb(e2, "gS", [64, 2, 64], F32)
                tmpS = P.sb(e2, "gtmp", [64, 2, 64], F32)
                Sb = [P.sb(e2, "gSb%d" % i, [64, 2, 64], BF16) for i in range(2)]
                att = [P.sb(e2, "gatt%d" % i, [128, 128], BF16) for i in range(2)]
                na = 0
                for dr in range(2):
                    P.dma("sp", St.v(), d["gla_init"][l, dr].rearrange("(a p) c -> p a c", p=64))
                    blks = range(NB) if dr == 0 else range(NB - 1, -1, -1)
                    for tb in blks:
                        bsl = slice(tb * 128, (tb + 1) * 128)
                        halves = (0, 1) if dr == 0 else (1, 0)
                        for hf in halves:
                            n = 2 * tb + hf
                            r0 = hf * 64
                            P.I("act", "activation", Sb[hf].v(), St.v(), AF.Copy)
                            pu = ps[self.psi % 6]; self.psi += 1
                            for h in range(4):
                                P.mm(pu.v(s_[(h % 2) * 32:(h % 2) * 32 + 32, (h // 2) * 64:(h // 2) * 64 + 64]), kl[dr].v(s_[r0:r0 + 64, tb, h * 32:(h + 1) * 32], tb),
                                     vtm.v(s_[r0:r0 + 64, tb, h * 64:(h + 1) * 64], tb), start=True, stop=True)
                            for hp in range(2):
                                P.I("dve", "scalar_tensor_tensor", tmpS.v(s_[:, hp, :]), St.v(s_[:, hp, :]), gcol.v(s_[:, dr, hp, n:n + 1], dr), pu.v(s_[0:64, hp * 64:(hp + 1) * 64]), op0=ALU.mult, op1=ALU.add)
                            if (dr == 0 and n % 4 == 3) or (dr == 1 and n % 4 == 0):
                                P.dma("sp", self.dout["gla_o"][l, n // 4, dr].rearrange("(a p) c -> p a c", p=64), tmpS.v())
                            nn = n + 1 if dr == 0 else n - 1
                            if 0 <= nn < 32:
                                P.I("dve", "tensor_scalar_mul", St.v(), tmpS.v(), keep.v(s_[0:64, dr, nn:nn + 1]))
                        po = ps[6 + (tb % 2)]
                        for h in range(4):
                            hs = slice((h % 2) * 32, (h % 2) * 32 + 32); hp = h // 2
                            pa = ps[self.psi % 6]; self.psi += 1
                            P.mm(pa.v(s_[:, 0:128]), keT[dr].v(s_[hs, hp, bsl], tb), qeT[dr].v(s_[hs, hp, bsl], tb), start=True, stop=True)
                            at = att[na % 2]; na += 1
                            P.I("dve", "tensor_tensor", at.v(), pa.v(s_[:, 0:128]), gmask.v(s_[:, dr, :]), op=ALU.mult)
                            oc = slice(h * 64, (h + 1) * 64)
                            P.mm(po.v(s_[:, oc]), at.v(), vtm.v(s_[:, tb, oc], tb), start=True, stop=False)
                            P.mm(po.v(s_[0:64, oc]), qeT[dr].v(s_[hs, hp, tb * 128:tb * 128 + 64], tb), Sb[0].v(s_[hs, hp, :]), start=False, stop=False)
                            P.mm(po.v(s_[64:128, oc]), qeT[dr].v(s_[hs, hp, tb * 128 + 64:tb * 128 + 128], tb), Sb[1].v(s_[hs, hp, :]), start=False, stop=True)
                        if dr == 0:
                            P.I("act", "activation", of.v(s_[:, tb, :], tb), po.v(s_[:, 0:256]), AF.Copy)
                        else:
                            P.I("dve", "tensor_tensor", of.v(s_[:, tb, :], tb), of.v(s_[:, tb, :], tb), po.v(s_[:, 0:256]), op=ALU.add)
                if GSTOP <= 2:
                    S.barrier(); return
                wg = P.sb(e2, "wgG", [128, 8, 256], BF16)
                gn = P.sb(e2, "ggn", [128, 64], F32)
                ss4 = P.sb(e2, "gss", [128, 4], F32)
                junk = P.sb(e2, "gjunk", [128, 64], F32)
                sg = P.sb(e2, "gsil", [128, 256], F32)
                ob = P.sb(e2, "gob", [128, 256], BF16)
                self.load_w(wg, l, 1184, 256)
                P.dma("sp", gn.v(), d["gla_norm"][l].partition_broadcast(128))
                for tb in range(NB):
                    bsl = slice(tb * 128, (tb + 1) * 128)
                    for h in range(4):
                        P.I("act", "activation", junk.v(), of.v(s_[:, tb, h * 64:(h + 1) * 64], tb), AF.Square, accum_out=ss4.v(s_[:, h:h + 1]))
                    P.I("act", "activation", ss4.v(), ss4.v(), AF.Sqrt, bias=EPS, scale=1.0 / 64)
                    P.I("dve", "reciprocal", ss4.v(), ss4.v())
                    ov = of.v(s_[:, tb, :], tb).m(lambda a: a.rearrange("p (h c) -> p h c", h=4))
                    P.I("dve", "tensor_tensor", ov, ov, ss4.v().m(lambda a: a.unsqueeze(2).to_broadcast([128, 4, 64])), op=ALU.mult)
                    P.I("dve", "tensor_tensor", ov, ov, gn.v().m(lambda a: a.unsqueeze(1).to_broadcast([128, 4, 64])), op=ALU.mult)
                    pg = ps[self.psi % 6]; self.psi += 1
                    for k in range(8):
                        P.mm(pg.v(s_[:, 0:256]), self.uT.v(s_[:, k, bsl], tb), wg.v(s_[:, k, :]), start=(k == 0), stop=(k == 7))
                    P.I("act", "activation", sg.v(), pg.v(s_[:, 0:256]), AF.Silu)
                    P.I("dve", "tensor_tensor", ob.v(), of.v(s_[:, tb, :], tb), sg.v(), op=ALU.mult)
                    for c in range(2):
                        p2 = ps[self.psi % 6]; self.psi += 1
                        P.mm(p2.v(s_[:, 0:128]), ob.v(s_[:, c * 128:(c + 1) * 128]), identb.v(), start=True, stop=True)
                        P.I("act", "activation", self.brT[2].v(s_[:, c, bsl], tb), p2.v(s_[:, 0:128]), AF.Copy)
                S.barrier()

    def swa_kv_only(self, l):
        P = self; S = self.S
        ps = self.ps
        with ExitStack() as es:
            wb = P.sb(es, "wbS2", [128, 8, 256], BF16)
            kvo = [P.sb(es, "kvo2_%d" % i, [128, 256], F32) for i in range(2)]
            self.load_w(wb, l, 1728, 256)
            for tb in range(NB):
                pt = ps[self.psi % 6]; self.psi += 1
                for k in range(8):
                    P.mm(pt.v(s_[:, 0:256]), self.uT.v(s_[:, k, tb * 128:(tb + 1) * 128], tb), wb.v(s_[:, k, 0:256]), start=(k == 0), stop=(k == 7))
                ko = kvo[tb % 2]
                P.I("act", "activation", ko.v(), pt.v(s_[:, 0:256]), AF.Copy)
                P.dma("sp", self.dout["swakv_o"][l, tb * 128:(tb + 1) * 128, :], ko.v())
            S.barrier()

    def swa(self, l, dbg):
        P = self; S = self.S
        d = self.din
        ps, ident, ones = self.ps, self.ident, self.ones
        NK = 20
        with ExitStack() as es:
            wb = P.sb(es, "wbS", [128, 8, 896], BF16)
            cosT = P.sb(es, "cosS", [64, 512], F32)
            sinT = P.sb(es, "sinS", [64, 512], F32)
            biasS = P.sb(es, "biasS", [128, 4], F32)
            esink = P.sb(es, "esink", [128, 4], F32)
            msk = P.sb(es, "mskS", [128, NB, 2, 128], BF16)
            qT = P.sb(es, "qTS", [64, 4, T], BF16, nsub=4)
            kT = P.sb(es, "kTS", [64, 2, 2560], BF16)
            Vt = P.sb(es, "VtS", [128, NK, 2, 65], BF16, nsub=NK)
            ta = P.sb(es, "sta", [64, 512], F32)
            tb_ = P.sb(es, "stb", [64, 512], F32)
            kvo = [P.sb(es, "kvo%d" % i, [128, 256], F32) for i in range(2)]
            cch = P.sb(es, "cchS", [128, 4, 2, 64], F32)
            PT = [P.sb(es, "PTS%d" % i, [128, 256], BF16) for i in range(2)]
            oa = P.sb(es, "oaS", [128, 256], F32)
            rden = P.sb(es, "rdenS", [128, 1], F32)
            self.load_w(wb, l, 1472, 512)
            P.dma("pool", wb.v(s_[:, :, 512:896]), d["w_in"][l, :, 6112:6496].rearrange("(k p) c -> p k c", p=128))
            P.dma("sp", biasS.v(), d["bias_s"][:, :])
            P.dma("sp", esink.v(), d["swa_sink"][l])
            P.I("act", "activation", esink.v(), esink.v(), AF.Exp)
            P.dma("sp", msk.v(), d["mask_s"][:, :, :, :])
            P.I("pool", "memset", Vt.v(), 1.0)
            import os
            STOP = int(os.environ.get("KSWA_STOP", "99"))
            if STOP <= 1:
                S.barrier(); return
            for tg in range(4):
                tsl = slice(tg * 512, (tg + 1) * 512)
                P.dma("sp", cosT.v(), d["rope_s"][0, :, tsl])
                P.dma("sp", sinT.v(), d["rope_s"][1, :, tsl])
                for h in range(4):
                    pq = self.fm_group(wb, [(h * 64, 64), (512 + h * 64, 64)], tg)
                    self.rope_evac(qT.v(s_[:, h, tsl], h), pq[0], pq[1], cosT.v(), sinT.v(), ta.v(), tb_.v())
                for j in range(2):
                    pk = self.fm_group(wb, [(256 + j * 64, 64), (768 + j * 64, 64)], tg)
                    self.rope_evac(kT.v(s_[:, j, 512 + tg * 512:512 + (tg + 1) * 512]), pk[0], pk[1], cosT.v(), sinT.v(), ta.v(), tb_.v())
            if STOP <= 2:
                S.barrier(); return
            for tb in range(NB):
                pt = ps[self.psi % 6]; self.psi += 1
                for k in range(8):
                    P.mm(pt.v(s_[:, 0:256]), self.uT.v(s_[:, k, tb * 128:(tb + 1) * 128], tb), wb.v(s_[:, k, 256:512]), start=(k == 0), stop=(k == 7))
                ko = kvo[tb % 2]
                P.I("act", "activation", ko.v(), pt.v(s_[:, 0:256]), AF.Copy)
                P.dma("sp", self.dout["swakv_o"][l, tb * 128:(tb + 1) * 128, :], ko.v())
                P.I("dve", "tensor_copy", Vt.v(s_[:, 4 + tb, :, 0:64], 4 + tb), ko.v(s_[:, 128:256]).m(lambda a: a.rearrange("p (j c) -> p j c", j=2)))
            if STOP <= 3:
                S.barrier(); return
            for j in range(2):
                P.dma("sp", cch.v(s_[:, :, j, :]), d["cache_swak"][l, j].rearrange("(c p) e -> p c e", p=128))
            for c in range(4):
                for j in range(2):
                    p2 = ps[self.psi % 6]; self.psi += 1
                    P.I("pe", "transpose", p2.v(s_[0:64, 0:128]), cch.v(s_[:, c, j, :]), ident.v())
                    P.I("act", "activation", kT.v(s_[:, j, c * 128:(c + 1) * 128]), p2.v(s_[0:64, 0:128]), AF.Copy)
            cchv = P.sb(es, "cchV", [128, 4, 2, 64], F32)
            for j in range(2):
                P.dma("sp", cchv.v(s_[:, :, j, :]), d["cache_swav"][l, j].rearrange("(c p) e -> p c e", p=128))
            for c in range(4):
                P.I("dve", "tensor_copy", Vt.v(s_[:, c, :, 0:64], c), cchv.v(s_[:, c, :, :]))
            n = 0
            import os
            for tb in range(NB if "noatt" not in os.environ.get("KSKIP", "") else 0):
                qsl = slice(tb * 128, (tb + 1) * 128)
                for j in range(2):
                    po = [ps[6], ps[7]]
                    kcs = [(4 + tb + dd, dd) for dd in (-1, 0, 1) if 0 <= tb + dd < NB] + [(c, 2) for c in range(4)]
                    def emitS(i):
                        kc, kind = kcs[i]
                        ksl = slice(kc * 128, (kc + 1) * 128)
                        pt = ps[self.psi % 6]; self.psi += 1
                        for g in range(2):
                            P.mm(pt.v(s_[:, g * 128:(g + 1) * 128]), kT.v(s_[:, j, ksl]), qT.v(s_[:, 2 * j + g, qsl], 2 * j + g), start=True, stop=True)
                        return pt
                    pts = {0: emitS(0)}
                    for i, (kc, kind) in enumerate(kcs):
                        if i + 1 < len(kcs):
                            pts[i + 1] = emitS(i + 1)
                        pt = pts.pop(i)
                        pT = PT[n % 2]; n += 1
                        if kind == 2:
                            P.I("act", "activation", pT.v(), pt.v(s_[:, 0:256]), AF.Exp, bias=biasS.v(s_[:, 0:1]), scale=SWA_SCALE)
                        else:
                            P.I("act", "activation", pT.v(), pt.v(s_[:, 0:256]), AF.Exp, scale=SWA_SCALE)
                            if kind != 0:
                                mi = 0 if kind == -1 else 1
                                for g in range(2):
                                    P.I("dve", "tensor_tensor", pT.v(s_[:, g * 128:(g + 1) * 128]), pT.v(s_[:, g * 128:(g + 1) * 128]), msk.v(s_[:, tb, mi, :]), op=ALU.mult)
                        for g in range(2):
                            P.mm(po[g].v(s_[:, 0:65]), pT.v(s_[:, g * 128:(g + 1) * 128]), Vt.v(s_[:, kc, j, :], kc), start=(i == 0), stop=(i == len(kcs) - 1))
                    for g in range(2):
                        hh = 2 * j + g
                        P.I("dve", "tensor_tensor", rden.v(), po[g].v(s_[:, 64:65]), esink.v(s_[:, hh:hh + 1]), op=ALU.add)
                        P.I("dve", "reciprocal", rden.v(), rden.v())
                        P.I("dve", "tensor_scalar_mul", oa.v(s_[:, hh * 64:(hh + 1) * 64]), po[g].v(s_[:, 0:64]), rden.v())
                for c in range(2):
                    p2 = ps[self.psi % 6]; self.psi += 1
                    P.I("pe", "transpose", p2.v(s_[:, 0:128]), oa.v(s_[:, c * 128:(c + 1) * 128]), ident.v())
                    P.I("act", "activation", self.brT[3].v(s_[:, c, tb * 128:(tb + 1) * 128], tb), p2.v(s_[:, 0:128]), AF.Copy)
            S.barrier()

    def fnet(self, l):
        P = self; S = self.S
        d = self.din
        with ExitStack() as es:
            wb = P.sb(es, "wbF", [128, 8, 256], BF16)
            fT = P.sb(es, "fT", [128, 2, T], BF16, nsub=4)
            cd = P.sb(es, "cdft", [128, 2, 128], BF16)
            A = P.sb(es, "fA", [128, NB, 256], BF16, nsub=NB)
            B = P.sb(es, "fB", [128, NB, 256], BF16, nsub=NB)
            tc_ = [P.sb(es, "dc%d" % i, [128, 512], BF16) for i in range(4)]
            ts_ = [P.sb(es, "ds%d" % i, [128, 512], BF16) for i in range(4)]
            self.load_w(wb, l, 416, 256)
            P.dma("sp", cd.v(), d["cdft"].rearrange("a p c -> p a c"))
            for c in range(2):
                def ev(tg, pv, c=c):
                    P.I("act", "activation", fT.v(s_[:, c, tg * 512:(tg + 1) * 512], tg), pv, AF.Copy)
                self.proj_fm(wb, c * 128, 128, ev)
            for tb in range(NB):
                pa = self.ps[self.psi % 8]; self.psi += 1
                for c in range(2):
                    P.mm(pa.v(s_[:, c * 128:(c + 1) * 128]), fT.v(s_[:, c, tb * 128:(tb + 1) * 128], tb // 4), cd.v(s_[:, 0, :]), start=True, stop=True)
                    P.mm(pa.v(s_[:, 256 + c * 128:256 + (c + 1) * 128]), fT.v(s_[:, c, tb * 128:(tb + 1) * 128], tb // 4), cd.v(s_[:, 1, :]), start=True, stop=True)
                P.I("act", "activation", A.v(s_[:, tb, :], tb), pa.v(s_[:, 0:256]), AF.Copy)
                P.I("act", "activation", B.v(s_[:, tb, :], tb), pa.v(s_[:, 256:512]), AF.Copy)
            n = 0
            for tg in range(4):
                p0 = self.ps[self.psi % 8]; p1 = self.ps[(self.psi + 1) % 8]; self.psi += 2
                for tb in range(NB):
                    ct = tc_[n % 4]; st = ts_[n % 4]; n += 1
                    P.dma("sp", ct.v(), d["dft_c"][tb * 128:(tb + 1) * 128, tg * 512:(tg + 1) * 512])
                    P.dma("act", st.v(), d["dft_s"][tb * 128:(tb + 1) * 128, tg * 512:(tg + 1) * 512])
                    for c, pp in ((0, p0), (1, p1)):
                        P.mm(pp.v(), A.v(s_[:, tb, c * 128:(c + 1) * 128], tb), ct.v(), start=(tb == 0), stop=False)
                        P.mm(pp.v(), B.v(s_[:, tb, c * 128:(c + 1) * 128], tb), st.v(), start=False, stop=(tb == NB - 1))
                for c, pp in ((0, p0), (1, p1)):
                    P.I("act" if c == 0 else "dve", "activation" if c == 0 else "tensor_copy", self.brT[1].v(s_[:, c, tg * 512:(tg + 1) * 512], range(tg * 4, tg * 4 + 4)),
                        pp.v(), *((AF.Copy,) if c == 0 else ()))
            S.barrier()

    def merge(self, l, dbg):
        P = self; S = self.S
        d = self.din
        x, uT, ps, mcol, ident, ones = self.x, self.uT, self.ps, self.mcol, self.ident, self.ones
        with ExitStack() as es:
            wbr = P.sb(es, "wbr", [128, 8, D], BF16)
            wo = P.sb(es, "wo", [128, 8, D], BF16)
            wg = [P.sb(es, "wg%d" % i, [128, 8, 512], BF16) for i in range(1)]
            G = P.sb(es, "Gacc", [128, 512], F32)
            GT = P.sb(es, "GT", [128, 8, 512], BF16, nsub=8)
            sg = [P.sb(es, "sg%d" % i, [128, 512], F32) for i in range(2)]
            gbc = P.sb(es, "g1bc", [128, D], F32)
            lng = P.sb(es, "ln1g", [128, D], F32)
            lnb = P.sb(es, "ln1b", [128, D], F32)
            dg = P.sb(es, "dgm", [128, 128], F32)
            xt = [P.sb(es, "xt%d" % i, [128, D], F32) for i in range(1)]
            tmps = [self.ln_tmp(es, "g%d" % i) for i in range(2)]
            P.dma("pool", wbr.v(), d["w_branch"][l].rearrange("b (k p) d -> p (b k) d", p=128))
            P.dma("pool", wo.v(), d["w_out"][l].rearrange("(k p) d -> p k d", p=128))
            P.dma("sp", lng.v(), d["ln"][l, 0, :].partition_broadcast(128))
            P.dma("sp", lnb.v(), d["ln"][l, 1, :].partition_broadcast(128))
            for k in range(8):
                P.I("dve", "tensor_scalar_mul", dg.v(), ident.v(), mcol.v(s_[:, l, 16 + k:17 + k], l))
                pt = ps[self.psi % 8]; self.psi += 1
                P.mm(pt.v(s_[:, 0:128]), ones.v(), dg.v(), start=True, stop=True)
                P.I("act", "activation", gbc.v(s_[:, k * 128:(k + 1) * 128]), pt.v(s_[:, 0:128]), AF.Copy)
            n = 0
            for tg in range(4):
                tsub = range(tg * 4, tg * 4 + 4)
                tsl = slice(tg * 512, (tg + 1) * 512)
                for dc in range(8):
                    w = wg[0]; n += 1
                    P.dma("pool", w.v(), d["w_gate"][l, dc])
                    for b in range(4):
                        pg = ps[self.psi % 8]; pp = ps[(self.psi + 1) % 8]; self.psi += 2
                        for k in range(8):
                            P.mm(pg.v(), w.v(s_[:, k, b * 128:(b + 1) * 128]), uT.v(s_[:, k, tsl], tsub), start=(k == 0), stop=(k == 7))
                        for kc in range(2):
                            P.mm(pp.v(), wbr.v(s_[:, b * 2 + kc, dc * 128:(dc + 1) * 128]), self.brT[b].v(s_[:, kc, tsl], tsub),
                                 start=(kc == 0), stop=(kc == 1))
                        sgt = sg[b % 2]
                        P.I("act", "activation", sgt.v(), pg.v(), AF.Sigmoid)
                        if b == 0:
                            P.I("dve", "tensor_tensor", G.v(), sgt.v(), pp.v(), op=ALU.mult)
                        else:
                            P.I("dve", "tensor_tensor", sgt.v(), sgt.v(), pp.v(), op=ALU.mult)
                            if b < 3:
                                P.I("pool", "tensor_tensor", G.v(), G.v(), sgt.v(), op=ALU.add)
                            else:
                                P.I("pool", "tensor_tensor", GT.v(s_[:, dc, :], dc), G.v(), sgt.v(), op=ALU.add)
                for j in range(4):
                    tb = tg * 4 + j
                    xtb = xt[0]
                    for hf in range(2):
                        pm = ps[self.psi % 8]; self.psi += 1
                        for k in range(8):
                            P.mm(pm.v(), GT.v(s_[:, k, j * 128:(j + 1) * 128], k), wo.v(s_[:, k, hf * 512:(hf + 1) * 512]), start=(k == 0), stop=(k == 7))
                        hs = slice(hf * 512, (hf + 1) * 512)
                        P.I("dve", "tensor_tensor", xtb.v(s_[:, hs]), pm.v(), gbc.v(s_[:, hs]), op=ALU.mult)
                        P.I("dve", "scalar_tensor_tensor", xtb.v(s_[:, hs]), x.v(s_[:, tb, hs], tb), ALPHA, xtb.v(s_[:, hs]), op0=ALU.mult, op1=ALU.add)
                    self.ln_block(tmps[tb % 2], xtb.v(), x.v(s_[:, tb, :], tb))
                    P.I("pool", "tensor_tensor", x.v(s_[:, tb, :], tb), x.v(s_[:, tb, :], tb), lng.v(), op=ALU.mult)
                    P.I("pool", "tensor_tensor", x.v(s_[:, tb, :], tb), x.v(s_[:, tb, :], tb), lnb.v(), op=ALU.add)
            S.barrier()

    def post_ffn(self, l, yacc_is_x=True):
        P = self; S = self.S
        d = self.din
        x = self.x
        with ExitStack() as es:
            lng = P.sb(es, "ln2g", [128, D], F32)
            lnb = P.sb(es, "ln2b", [128, D], F32)
            xt = [P.sb(es, "xq%d" % i, [128, D], F32) for i in range(2)]
            tmps = [self.ln_tmp(es, "q%d" % i) for i in range(2)]
            P.dma("sp", lng.v(), d["ln"][l, 2, :].partition_broadcast(128))
            P.dma("sp", lnb.v(), d["ln"][l, 3, :].partition_broadcast(128))
            for tb in range(NB):
                xtb = xt[tb % 2]
                P.I("act", "activation", xtb.v(), x.v(s_[:, tb, :], tb), AF.Copy)
                self.ln_block(tmps[tb % 2], xtb.v(), x.v(s_[:, tb, :], tb))
                P.I("pool", "tensor_tensor", x.v(s_[:, tb, :], tb), x.v(s_[:, tb, :], tb), lng.v(), op=ALU.mult)
                P.I("pool", "tensor_tensor", x.v(s_[:, tb, :], tb), x.v(s_[:, tb, :], tb), lnb.v(), op=ALU.add)
            S.barrier()

    def layer(self, l, dbg):
        P = self; S = self.S
        self.psi = 0
        for g4 in range(16):
            P.dma("pool", self.ubf.v(s_[l, g4 * 4:(g4 + 1) * 4], l * 16 + g4), self.din["peer_uT"][l, g4 * 4:(g4 + 1) * 4].rearrange("g p k e -> g p (k e)"))
            P.dma("pool", self.vbf.v(s_[l, g4 * 4:(g4 + 1) * 4], l * 16 + g4), self.din["peer_v"][l, g4 * 4:(g4 + 1) * 4].rearrange("g p j d -> g p (j d)"))
        with ExitStack() as esl:
            self.uT = P.sb(esl, "uT", [128, 8, T], BF16, nsub=NB)
            self.mod_to_uT(l, 0)
            self.brT = [P.sb(esl, "brT%d" % b, [128, 2, T], BF16, nsub=NB) for b in range(4)]
            import os
            skip = os.environ.get("KSKIP", "")
            for b, nm in ((0, "mla"), (3, "swa"), (1, "fnet"), (2, "gla")):
                if nm in skip:
                    for tb4 in range(4):
                        P.I("pool", "memset", self.brT[b].v(s_[:, :, tb4 * 512:(tb4 + 1) * 512], range(tb4 * 4, tb4 * 4 + 4)), 0.0)
            if "mla" not in skip:
                self.mla(l, dbg)
            if "swa" not in skip:
                self.swa(l, dbg)
            else:
                self.swa_kv_only(l)
            if "fnet" not in skip:
                self.fnet(l)
            if "gla" not in skip:
                self.gla(l, dbg)
            self.merge(l, dbg)
            S.barrier()
        import os
        if "peer" in os.environ.get("KSKIP", ""):
            for tb in range(NB):
                P.I("act", "activation", self.x.v(s_[:, tb, :], tb), self.x.v(s_[:, tb, :], tb), AF.Copy, scale=ALPHA)
        else:
            self.peer(l, dbg)
        self.post_ffn(l)

    def peer(self, l, dbg):
        P = self; S = self.S
        d = self.din
        x, ps, mcol, ident, ones, iota, bm = self.x, self.ps, self.mcol, self.ident, self.ones, self.iota, self.bm
        NCH = 128
        TBS = 256
        with ExitStack() as es:
            u2Ts = [P.sb(es, "u2T%d" % i, [128, 8, TBS], BF16, nsub=2) for i in range(2)]
            lt = self.ln_tmp(es, "P")
            wq = P.sb(es, "wq", [128, 8, 256], BF16)
            kT = P.sb(es, "keysT", [128, 16, 128], BF16)
            gbc = P.sb(es, "g2bc", [128, D], BF16)
            qpT = P.sb(es, "qpT", [128, 16, 128], BF16, nsub=16)
            sc = P.sb(es, "psc", [128, 16, 128], F32, nsub=16)
            vtop = P.sb(es, "vtop", [128, 16, 16], F32, nsub=16)
            itop = P.sb(es, "itop", [128, 16, 16], U32, nsub=16)
            idx1f = P.sb(es, "idx1f", [128, 128], F32)
            dg = idx1f
            idx2f = P.sb(es, "idx2f", [128, 128], F32)
            idxTs = P.sb(es, "idxT", [128, 2, 2, 128], F32, nsub=2)
            cand = P.sb(es, "cand", [128, 8, 256], F32, nsub=8)
            t8a = P.sb(es, "t8a", [128, 8, 8], F32, nsub=8)
            t8b = P.sb(es, "t8b", [128, 8, 8], F32, nsub=8)
            nmx = P.sb(es, "nmx", [128, 8], F32)
            zz = P.sb(es, "pz", [128, 8], F32)
            wCTs = P.sb(es, "wCT", [128, 2, 128, 16], BF16, nsub=2)
            O1s = [P.sb(es, "O1_%d" % i, [128, 4, 128], BF16) for i in range(2)]
            O2s = [P.sb(es, "O2_%d" % i, [128, 4, 128], BF16) for i in range(2)]
            Cbds = [P.sb(es, "Cbd_%d" % i, [128, 4, 128], BF16) for i in range(2)]
            tmpS = [P.sb(es, "ptmp%d" % i, [128, 4, 128], BF16) for i in range(2)]
            WtT = P.sb(es, "WtT", [128, TBS, 128], BF16, nsub=TBS // 4)
            Ut = [P.sb(es, "Ut%d" % i, [128, 8, 256], BF16) for i in range(2)]
            Vt = [P.sb(es, "Vt%d" % i, [128, 2, D], BF16) for i in range(2)]
            actS = [P.sb(es, "pact%d" % i, [128, TBS], BF16) for i in range(2)]
            GS = [P.sb(es, "pG%d" % i, [128, TBS], BF16) for i in range(2)]
            P.dma("pool", kT.v(), d["peer_keysT"][l].rearrange("h q c k -> c (h q) k"))
            for k in range(8):
                P.I("dve", "tensor_scalar_mul", dg.v(), ident.v(), mcol.v(s_[:, l, 40 + k:41 + k], l))
                pt = ps[self.psi % 4]; self.psi += 1
                P.mm(pt.v(s_[:, 0:128]), ones.v(), dg.v(), start=True, stop=True)
                P.I("act", "activation", gbc.v(s_[:, k * 128:(k + 1) * 128]), pt.v(s_[:, 0:128]), AF.Copy)
            py = [ps[4], ps[5], ps[6], ps[7]]
            esc = lambda a: a.rearrange("p (h a) b -> p h (a b)", a=2)
            def sel_a(sb_):
                u2T = u2Ts[sb_ % 2]
                for sub in range(2):
                    tb = sb_ * 2 + sub
                    usl = slice(sub * 128, (sub + 1) * 128)
                    xnv = cand.v(s_[:, 0:4, :], [0, 1, 2, 3]).m(lambda a: a.rearrange("p a b -> p (a b)"))
                    self.ln_block(lt, x.v(s_[:, tb, :], tb), xnv)
                    yield
                    for half in range(2):
                        pt = ps[2 + self.psi % 2]; self.psi += 1
                        for j in range(4):
                            k = half * 4 + j
                            P.I("pe", "transpose", pt.v(s_[:, j * 128:(j + 1) * 128]), xnv.m(lambda a, k=k: a[:, k * 128:(k + 1) * 128]), ident.v())
                        for j in range(4):
                            k = half * 4 + j
                            P.I("dve", "tensor_scalar", u2T.v(s_[:, k, usl], sub), pt.v(s_[:, j * 128:(j + 1) * 128]),
                                mcol.v(s_[:, l, 32 + k:33 + k], l), mcol.v(s_[:, l, 24 + k:25 + k], l), op0=ALU.mult, op1=ALU.add)
                    for c4 in range(4):
                        pt = ps[2 + self.psi % 2]; self.psi += 1
                        for j in range(4):
                            c = c4 * 4 + j
                            if j % 2 == 0:
                                P.dma("pool", wq.v(), d["w_peer_q"][l, c // 2])
                            for k in range(8):
                                P.mm(pt.v(s_[:, j * 128:(j + 1) * 128]), wq.v(s_[:, k, (j % 2) * 128:(j % 2) * 128 + 128]), u2T.v(s_[:, k, usl], sub), start=(k == 0), stop=(k == 7))
                        P.I("act", "activation", qpT.v(s_[:, c4 * 4:(c4 + 1) * 4, :], range(c4 * 4, c4 * 4 + 4)), pt.v().m(lambda a: a.rearrange("p (j t) -> p j t", j=4)), AF.Copy)
                        yield
                    for c4 in range(4):
                        pt = ps[2 + self.psi % 2]; self.psi += 1
                        for j in range(4):
                            c = c4 * 4 + j
                            P.mm(pt.v(s_[:, j * 128:(j + 1) * 128]), qpT.v(s_[:, c, :], c), kT.v(s_[:, c, :]), start=True, stop=True)
                        P.I("act", "activation", sc.v(s_[:, c4 * 4:(c4 + 1) * 4, :], range(c4 * 4, c4 * 4 + 4)), pt.v().m(lambda a: a.rearrange("p (j t) -> p j t", j=4)), AF.Copy)
                        yield
                    wkc = lambda c: cand.v(s_[:, c // 2, (c % 2) * 128:(c % 2) * 128 + 128], c // 2)
                    for c in range(16):
                        P.I("dve", "max", vtop.v(s_[:, c, 0:8], c), sc.v(s_[:, c, :], c))
                    yield
                    for c in range(16):
                        P.I("dve", "max_index", itop.v(s_[:, c, 0:8], c), vtop.v(s_[:, c, 0:8], c), sc.v(s_[:, c, :], c))
                    yield
                    for c in range(16):
                        P.I("dve", "match_replace", wkc(c), vtop.v(s_[:, c, 0:8], c), sc.v(s_[:, c, :], c), -1e30)
                    yield
                    for c in range(16):
                        P.I("dve", "max", vtop.v(s_[:, c, 8:16], c), wkc(c))
                    yield
                    for c in range(16):
                        P.I("dve", "max_index", itop.v(s_[:, c, 8:16], c), vtop.v(s_[:, c, 8:16], c), wkc(c))
                    yield
                    v4 = lambda a: a.rearrange("p (h q) r -> p h q r", q=2)
                    P.I("dve", "tensor_copy", idx1f.v().m(lambda a: a.rearrange("p (h r) -> p h r", h=8)), itop.v().m(lambda a: v4(a)[:, :, 0, :]))
                    P.I("dve", "tensor_copy", idx2f.v().m(lambda a: a.rearrange("p (h r) -> p h r", h=8)), itop.v().m(lambda a: v4(a)[:, :, 1, :]))
                    P.I("dve", "tensor_tensor", cand.v().m(lambda a: a.rearrange("p h (a b) -> p h a b", a=16)),
                        vtop.v().m(lambda a: v4(a)[:, :, 0, :].unsqueeze(3).to_broadcast([128, 8, 16, 16])),
                        vtop.v().m(lambda a: v4(a)[:, :, 1, :].unsqueeze(2).to_broadcast([128, 8, 16, 16])), op=ALU.add)
                    yield
                    wkh = lambda h: sc.v(s_[:, 2 * h:2 * h + 2, :], [2 * h, 2 * h + 1]).m(lambda a: a.rearrange("p a b -> p (a b)"))
                    for h in range(8):
                        P.I("dve", "max", t8a.v(s_[:, h, :], h), cand.v(s_[:, h, :], h))
                    for h in range(8):
                        P.I("dve", "match_replace", wkh(h), t8a.v(s_[:, h, :], h), cand.v(s_[:, h, :], h), -1e30)
                    for h in range(8):
                        P.I("dve", "max", t8b.v(s_[:, h, :], h), wkh(h))
                    yield
                    P.I("dve", "tensor_scalar_mul", nmx.v(), t8a.v(s_[:, :, 0]), -1.0)
                    for h in range(8):
                        ev = sc.v(s_[:, 2 * h:2 * h + 2, :], [2 * h, 2 * h + 1]).m(lambda a: a.rearrange("p a b -> p (a b)"))
                        P.I("act", "activation", ev, cand.v(s_[:, h, :], h), AF.Exp, bias=nmx.v(s_[:, h:h + 1]), scale=1.0)
                        P.I("dve", "scalar_tensor_tensor", ev, cand.v(s_[:, h, :], h), t8b.v(s_[:, h, 7:8], h), ev, op0=ALU.is_ge, op1=ALU.mult)
                    yield
                    P.I("dve", "tensor_reduce", zz.v(), sc.v().m(esc), axis=AX.X, op=ALU.add)
                    P.I("dve", "reciprocal", zz.v(), zz.v())
                    P.I("dve", "tensor_tensor", sc.v().m(esc), sc.v().m(esc), zz.v().m(lambda a: a.unsqueeze(2).to_broadcast([128, 8, 256])), op=ALU.mult)
                    yield
                    pt = ps[2 + self.psi % 2]; self.psi += 1
                    P.I("pe", "transpose", pt.v(s_[:, 0:128]), idx1f.v(), ident.v())
                    P.I("pe", "transpose", pt.v(s_[:, 128:256]), idx2f.v(), ident.v())
                    P.I("act", "activation", idxTs.v(s_[:, sub], sub), pt.v(s_[:, 0:256]).m(lambda a: a.rearrange("p (a t) -> p a t", a=2)), AF.Copy)
                    for r4 in range(4):
                        yield
                        pt = ps[2 + self.psi % 2]; self.psi += 1
                        for j in range(4):
                            r2 = r4 * 4 + j
                            P.I("pe", "transpose", pt.v(s_[:, j * 128:(j + 1) * 128]),
                                sc.v().m(lambda a, r2=r2: a.rearrange("p c (a b) -> p (c a) b", b=16)[:, :, r2]), ident.v())
                        P.I("act", "activation", wCTs.v(s_[:, sub, :, r4 * 4:(r4 + 1) * 4], sub).m(lambda a: a.rearrange("p t j -> p j t")),
                            pt.v().m(lambda a: a.rearrange("p (j t) -> p j t", j=4)), AF.Copy)

                yield
            def expand(sb_):
                items = [(sub, sbk) for sub in range(2) for sbk in range(32)]
                def genO(i):
                    sub, sbk = items[i]
                    t0 = sbk * 4
                    o1, o2, cb = O1s[i % 2], O2s[i % 2], Cbds[i % 2]
                    P.I("dve", "tensor_tensor", o1.v(), iota.v().m(lambda a: a.unsqueeze(1).to_broadcast([128, 4, 128])),
                        idxTs.v(s_[:, sub, 0, t0:t0 + 4], sub).m(lambda a: a.unsqueeze(2).to_broadcast([128, 4, 128])), op=ALU.is_equal)
                    P.I("dve", "tensor_tensor", o2.v(), iota.v().m(lambda a: a.unsqueeze(1).to_broadcast([128, 4, 128])),
                        idxTs.v(s_[:, sub, 1, t0:t0 + 4], sub).m(lambda a: a.unsqueeze(2).to_broadcast([128, 4, 128])), op=ALU.is_equal)
                    P.I("pool", "tensor_tensor", cb.v().m(lambda a: a.rearrange("p t (h r) -> p t h r", h=8)),
                        wCTs.v(s_[:, sub, t0:t0 + 4, :], sub).m(lambda a: a.unsqueeze(2).to_broadcast([128, 4, 8, 16])),
                        bm.v().m(lambda a: a.unsqueeze(1).unsqueeze(3).to_broadcast([128, 4, 8, 16])), op=ALU.mult)
                def mm1(i):
                    o1, cb = O1s[i % 2], Cbds[i % 2]
                    pt = ps[self.psi % 4]; self.psi += 1
                    for j in range(4):
                        P.mm(pt.v(s_[:, j * 128:(j + 1) * 128]), cb.v(s_[:, j, :]), o1.v(s_[:, j, :]), start=True, stop=True)
                    P.I("act", "activation", tmpS[i % 2].v(), pt.v().m(lambda a: a.rearrange("p (j i) -> p j i", j=4)), AF.Copy)
                def mm2(i):
                    sub, sbk = items[i]
                    o2 = O2s[i % 2]
                    tS = tmpS[i % 2]
                    pt2 = ps[self.psi % 4]; self.psi += 1
                    for j in range(4):
                        P.mm(pt2.v(s_[:, j * 128:(j + 1) * 128]), o2.v(s_[:, j, :]), tS.v(s_[:, j, :]), start=True, stop=True)
                    ta = sub * 128 + sbk * 4
                    P.I("dve", "tensor_copy", WtT.v(s_[:, ta:ta + 4, :], ta // 4), pt2.v().m(lambda a: a.rearrange("p (j i) -> p j i", j=4)))
                genO(0); mm1(0)
                for i in range(len(items)):
                    if i + 1 < len(items):
                        genO(i + 1); mm1(i + 1)
                    mm2(i)

            def expert(sb_, gen):
                u2T = u2Ts[sb_ % 2]
                def emitU(c):
                    c2, j = c // 2, c % 2
                    if j == 0:
                        ut = Ut[c2 % 2]; vt = Vt[c2 % 2]
                        P.dma("sp", ut.v().m(lambda a: a.rearrange("p k e -> p (k e)")), self.ubf.v(s_[l, c2], l * 16 + c2 // 4))
                        P.dma("sp", vt.v().m(lambda a: a.rearrange("p j d -> p (j d)")), self.vbf.v(s_[l, c2], l * 16 + c2 // 4))
                    ut = Ut[c2 % 2]
                    pa = ps[c % 2]
                    for k in range(8):
                        P.mm(pa.v(s_[:, 0:TBS]), ut.v(s_[:, k, j * 128:(j + 1) * 128]), u2T.v(s_[:, k, :]), start=(k == 0), stop=(k == 7))
                def emitMV(c):
                    c2, j = c // 2, c % 2
                    vt = Vt[c2 % 2]
                    pa = ps[c % 2]
                    aS = actS[c % 2]; gS = GS[c % 2]
                    P.I("act", "activation", aS.v(), pa.v(s_[:, 0:TBS]), AF.Gelu)
                    P.I("dve", "tensor_tensor", gS.v(), aS.v(), WtT.v(s_[:, :, c]), op=ALU.mult)
                    for sub in range(2):
                        for hf in range(2):
                            P.mm(py[sub * 2 + hf].v(), gS.v(s_[:, sub * 128:(sub + 1) * 128]), vt.v(s_[:, j, hf * 512:(hf + 1) * 512]), start=(c == 0), stop=(c == NCH - 1))
                emitU(0)
                for c in range(NCH):
                    if c + 1 < NCH:
                        emitU(c + 1)
                    emitMV(c)
                    if gen is not None and c % 2 == 1:
                        next(gen, None)
                if gen is not None:
                    for _ in gen:
                        pass

            def finalize(sb_):
                for sub in range(2):
                    tb = sb_ * 2 + sub
                    for hf in range(2):
                        hs = slice(hf * 512, (hf + 1) * 512)
                        yv = cand.v(s_[:, 0:2, :], [0, 1]).m(lambda a: a.rearrange("p a b -> p (a b)"))
                        P.I("dve", "tensor_tensor", yv, py[sub * 2 + hf].v(), gbc.v(s_[:, hs]), op=ALU.mult)
                        P.I("dve", "scalar_tensor_tensor", x.v(s_[:, tb, hs], tb), x.v(s_[:, tb, hs], tb), ALPHA, yv, op0=ALU.mult, op1=ALU.add)

            NSB = T // TBS
            for _ in sel_a(0):
                pass
            expand(0)
            for sb_ in range(NSB):
                gen = sel_a(sb_ + 1) if sb_ + 1 < NSB else None
                expert(sb_, gen)
                finalize(sb_)
                if sb_ + 1 < NSB:
                    expand(sb_ + 1)
            S.barrier()

def _bf(a):
    return np.ascontiguousarray(a).astype(ml_dtypes.bfloat16)


def host_consts(kind):
    c = {}
    c["ident"] = np.eye(128, dtype=np.float32)
    c["bm"] = np.ascontiguousarray((np.arange(128)[:, None] // 16 == np.arange(8)[None, :]).astype(np.float32))
    seqlen = T if kind == "sample" else 256
    n = np.arange(seqlen)
    ang = 2.0 * np.pi * np.outer(n, n) / seqlen
    sc = 1.0 / np.sqrt(seqlen * 64.0)
    cb = np.cos(ang) * sc
    sbm = -np.sin(ang) * sc
    Cf = np.zeros((T, T), np.float64)
    Sf = np.zeros((T, T), np.float64)
    for i in range(T // seqlen):
        sl = slice(i * seqlen, (i + 1) * seqlen)
        Cf[sl, sl] = cb
        Sf[sl, sl] = sbm
    c["dft_c"] = _bf(Cf.astype(np.float32))
    c["dft_s"] = _bf(Sf.astype(np.float32))
    m = np.arange(64)
    a2 = 2.0 * np.pi * np.outer(m, m) / 64.0
    cc = np.zeros((2, 128, 128), np.float64)
    for g in range(2):
        cc[0, g * 64:(g + 1) * 64, g * 64:(g + 1) * 64] = np.cos(a2)
        cc[1, g * 64:(g + 1) * 64, g * 64:(g + 1) * 64] = np.sin(a2)
    c["cdft"] = _bf(cc.astype(np.float32))
    t = np.arange(T)
    rows = (t // 64).astype(np.float64); cols = (t % 64).astype(np.float64)
    for nm, R in (("rope_m", 32), ("rope_s", 64)):
        half = R // 2; q = R // 4
        tab = np.zeros((2, R, T), np.float64)
        for dd in range(R):
            pos = rows if dd < half else cols
            fi = dd % q
            freq = 10000.0 ** (-(2.0 * fi) / half)
            ang = pos * freq
            if kind == "sample":
                tab[0, dd] = np.cos(ang)
                tab[1, dd] = np.sin(ang) * (-1.0 if (dd // q) % 2 == 0 else 1.0)
            else:
                tab[0, dd] = 1.0
        c[nm] = np.ascontiguousarray(tab.astype(np.float32))
    bm_ = np.zeros((160,), np.float32)
    if kind == "prompt":
        for qu in range(8):
            for kc in range(20):
                ok = kc >= 4 and (kc - 4) // 2 == qu
                bm_[qu * 20 + kc] = 0.0 if ok else NEG
    c["bias_m"] = np.ascontiguousarray(np.broadcast_to(bm_[None, :], (128, 160)))
    bs_ = np.zeros((128, 4), np.float32)
    if kind == "prompt":
        bs_[:, 0] = NEG
    c["bias_s"] = bs_
    mk = np.zeros((128, NB, 2, 128), np.float32)
    kk = np.arange(128)[:, None]; qq = np.arange(128)[None, :]
    for tb in range(NB):
        if kind == "sample":
            mk[:, tb, 0, :] = (kk >= qq)
            mk[:, tb, 1, :] = (kk <= qq)
        else:
            mk[:, tb, 0, :] = 1.0 if tb % 2 == 1 else 0.0
            mk[:, tb, 1, :] = 1.0 if tb % 2 == 0 else 0.0
    c["mask_s"] = _bf(mk)
    tt = np.arange(128)[:, None]; tp = np.arange(128)[None, :]
    same = (tt // 64) == (tp // 64)
    cc_ = -1.0 / 16.0
    gm = np.zeros((5, 128, 128), np.float32)
    gm[0] = cc_ * (same & (tt <= tp))
    gm[1] = cc_ * (same & (tt > tp))
    gm[2] = cc_ * (same & (tt >= tp))
    gm[3] = cc_ * (same & (tt < tp))
    gm[4, :, 0] = cc_ * (np.arange(128) < 64)
    gm[4, :, 1] = cc_ * (np.arange(128) >= 64)
    c["gla_mats"] = gm
    c["gla_mask"] = _bf(np.stack([(same & (tt <= tp)), (same & (tt >= tp))]).astype(np.float32))
    kp = np.ones((128, 2, 32), np.float32)
    if kind == "prompt":
        for n_ in range(32):
            if n_ % 4 == 0:
                kp[:, 0, n_] = 0.0
            if n_ % 4 == 3:
                kp[:, 1, n_] = 0.0
    c["gla_keep"] = kp
    return c


def perm_swap(R):
    q = R // 4
    return np.array([d + q if (d // q) % 2 == 0 else d - q for d in range(R)])


def host_weights(inp):
    w = {}
    w["w_ada"] = np.ascontiguousarray(inp["w_ada"], dtype=np.float32)
    w["b_adaT"] = np.ascontiguousarray(inp["b_ada"].reshape(L, 48, 128).transpose(0, 2, 1), dtype=np.float32)
    w_in = np.asarray(inp["w_in"], dtype=np.float32)
    p32 = perm_swap(32); p64 = perm_swap(64)
    kr = w_in[:, :, 384:416][:, :, p32]
    sq = w_in[:, :, 1472:1728].reshape(L, D, 4, 64)[:, :, :, p64].reshape(L, D, 256)
    sk = w_in[:, :, 1728:1856].reshape(L, D, 2, 64)[:, :, :, p64].reshape(L, D, 128)
    w["w_in"] = np.ascontiguousarray(np.concatenate([w_in, kr, sq, sk], axis=2))
    w["w_gate"] = np.ascontiguousarray(w_in[:, :, 1984:6080].reshape(L, 8, 128, 4, 8, 128).transpose(0, 4, 2, 1, 3, 5).reshape(L, 8, 128, 8, 512))
    w["w_branch"] = np.ascontiguousarray(inp["w_branch"], dtype=np.float32)
    w["w_out"] = np.ascontiguousarray(inp["w_out"], dtype=np.float32)
    w_uq = np.asarray(inp["w_uq"], dtype=np.float32)
    uq_sw = w_uq.reshape(L, 256, 4, 96)[:, :, :, 64:96][:, :, :, p32].reshape(L, 256, 128)
    w["w_uq"] = np.ascontiguousarray(np.concatenate([w_uq, uq_sw], axis=2))
    w["w_ukv"] = np.ascontiguousarray(inp["w_ukv"], dtype=np.float32)
    w["mla_q_norm"] = np.ascontiguousarray(np.asarray(inp["mla_q_norm"], dtype=np.float32).reshape(L, 2, 128).transpose(0, 2, 1))
    w["mla_kv_norm"] = np.ascontiguousarray(inp["mla_kv_norm"], dtype=np.float32)
    w["w_gla_a"] = np.ascontiguousarray(np.stack([inp["w_gla_a_fwd"], inp["w_gla_a_bwd"]], axis=1), dtype=np.float32)
    w["b_gla_a"] = np.ascontiguousarray(np.stack([inp["b_gla_a_fwd"], inp["b_gla_a_bwd"]], axis=1), dtype=np.float32)
    w["gla_norm"] = np.ascontiguousarray(inp["gla_norm"], dtype=np.float32)
    w["swa_sink"] = np.ascontiguousarray(np.broadcast_to(np.asarray(inp["swa_sink"], dtype=np.float32)[:, None, :], (L, 128, 4)))
    w["w_peer_q"] = np.ascontiguousarray(np.asarray(inp["w_peer_q"], dtype=np.float32).reshape(L, 8, 128, 8, 256).transpose(0, 3, 2, 1, 4))
    w["peer_keysT"] = np.ascontiguousarray(np.asarray(inp["peer_keys"], dtype=np.float32).transpose(0, 1, 2, 4, 3))
    w["peer_uT"] = np.ascontiguousarray(np.asarray(inp["peer_u"], dtype=np.float32).reshape(L, 64, 256, 8, 128).transpose(0, 1, 4, 3, 2))
    w["peer_v"] = np.ascontiguousarray(np.asarray(inp["peer_v"], dtype=np.float32).reshape(L, 64, 2, 128, D).transpose(0, 1, 3, 2, 4))
    w["ln"] = np.ascontiguousarray(np.stack([inp["ln1_g"], inp["ln1_b"], inp["ln2_g"], inp["ln2_b"]], axis=1), dtype=np.float32)
    return w


def core_inputs(inp, core, W, CS, CP):
    m = dict(W)
    if core < 2:
        m.update(CS)
        m["x"] = np.ascontiguousarray(inp["x_sample"][core], dtype=np.float32)
        cond = np.asarray(inp["c"][core], dtype=np.float32)
        m["cache_ckv"] = np.ascontiguousarray(inp["cache_mla_ckv"][core], dtype=np.float32)
        m["cache_krope"] = np.ascontiguousarray(inp["cache_mla_krope"][core], dtype=np.float32)
        m["cache_swak"] = np.ascontiguousarray(inp["cache_swa_k"][core], dtype=np.float32)
        m["cache_swav"] = np.ascontiguousarray(inp["cache_swa_v"][core], dtype=np.float32)
        m["gla_init"] = np.ascontiguousarray(np.asarray(inp["state_gla"][core], dtype=np.float32).reshape(L, 2, 128, 64))
    else:
        m.update(CP)
        j = core - 2 if core < 6 else 0
        m["x"] = np.ascontiguousarray(np.asarray(inp["x_prompt"][8 * j:8 * j + 8], dtype=np.float32).reshape(T, D))
        cond = np.asarray(inp["c_ctx"], dtype=np.float32)
        m["cache_ckv"] = np.zeros((L, 512, 128), np.float32)
        m["cache_krope"] = np.zeros((L, 512, 32), np.float32)
        m["cache_swak"] = np.zeros((L, 2, 512, 64), np.float32)
        m["cache_swav"] = np.zeros((L, 2, 512, 64), np.float32)
        m["gla_init"] = np.zeros((L, 2, 128, 64), np.float32)
    m["condT"] = np.ascontiguousarray(cond.reshape(8, 128).T)
    return m


_CACHE = {}


def kernel(**inputs):
    cores = inputs.pop("_cores", list(range(8)))
    debug = inputs.pop("_debug", None)
    stop_after = inputs.pop("_stop_after", None)
    prog = Prog(debug=debug, stop_after=stop_after)
    nc = prog.build()
    W = host_weights(inputs)
    CS = host_consts("sample")
    CP = host_consts("prompt")
    in_maps = [core_inputs(inputs, c, W, CS, CP) for c in cores]
    import os as _os
    if _os.environ.get("KTRACE"):
        res = run_bass_kernel_spmd(nc, in_maps, core_ids=list(range(len(cores))), trace=True)
        print("EXEC_TIME_NS", res.exec_time_ns)
        globals()["_LAST_RES"] = res
    else:
        res = run_bass_kernel_spmd(nc, in_maps, core_ids=list(range(len(cores))))
    R = res.results
    if debug is not None:
        return R
    y_sample = np.stack([R[0]["y"], R[1]["y"]], axis=0)
    y_prompt = np.concatenate([R[2 + j]["y"].reshape(8, 256, D) for j in range(4)], axis=0)
    ckv = np.concatenate([R[2 + j]["ckv_o"].reshape(L, 8, 256, 128).transpose(1, 0, 2, 3) for j in range(4)], axis=0)
    kr = np.concatenate([R[2 + j]["krope_o"].reshape(L, 8, 256, 32).transpose(1, 0, 2, 3) for j in range(4)], axis=0)
    kvs = [R[2 + j]["swakv_o"].reshape(L, 8, 256, 2, 2, 64) for j in range(4)]
    sk = np.concatenate([a[:, :, :, 0].transpose(1, 0, 3, 2, 4) for a in kvs], axis=0)
    sv = np.concatenate([a[:, :, :, 1].transpose(1, 0, 3, 2, 4) for a in kvs], axis=0)
    gl = np.concatenate([R[2 + j]["gla_o"].reshape(L, 8, 2, 4, 32, 64).transpose(1, 0, 2, 3, 4, 5) for j in range(4)], axis=0)
    f = lambda a: np.ascontiguousarray(a, dtype=np.float32)
    return (f(y_prompt), f(y_sample), f(ckv), f(kr), f(sk), f(sv), f(gl))
```

```python
import numpy as np
import ml_dtypes
from contextlib import ExitStack
import concourse.bass as bass
import concourse.mybir as mybir
from concourse.bass_utils import run_bass_kernel_spmd

F32 = mybir.dt.float32
BF16 = mybir.dt.bfloat16
U32 = mybir.dt.uint32
AF = mybir.ActivationFunctionType
ALU = mybir.AluOpType
AX = mybir.AxisListType
s_ = np.s_

ENGS = ("pe", "act", "dve", "pool", "sp")
SAME_ENGINE_SYNC = True
import os as _os0
SES_ALL = not bool(_os0.environ.get("KNOSES"))

T = 2048
NB = 16
D = 1024
L = 2
ALPHA = (2.0 * L) ** 0.25
EPS = 1e-6
NEG = -30000.0
MLA_SCALE = 96.0 ** -0.5
SWA_SCALE = 64.0 ** -0.5
WIN_EXT = 6496


class V:
    def __init__(self, ap, toks):
        self.ap = ap
        self.toks = toks

    def m(self, fn):
        return V(fn(self.ap), self.toks)


class Buf:
    def __init__(self, name, t, nsub=1):
        self.name = name
        self.t = t
        self.nsub = nsub

    def tok(self, subs=None):
        if subs is None:
            return [(self.name, s) for s in range(self.nsub)]
        if isinstance(subs, int):
            subs = [subs]
        return [(self.name, s) for s in subs]

    def v(self, key=None, subs=None):
        ap = self.t[:] if key is None else self.t[key]
        return V(ap, self.tok(subs))


class Sched:
    def __init__(self, nc, es, nd=24):
        self.nc = nc
        self.ops = {e: [] for e in ENGS}
        self.cnt = {e: 0 for e in ENGS}
        self.known = {e: {} for e in ENGS}
        self.nd = nd
        self.dma_tot = [0] * nd
        self.dma_rr = 0
        self.last_w = {}
        self.readers = {}
        self.sem = {e: es.enter_context(nc.semaphore("sem_" + e)) for e in ENGS if e != "sp"}
        self.dsem = [es.enter_context(nc.semaphore("dsem%d" % i)) for i in range(nd)]
        self.milestones = {e: set() for e in ENGS}

    def _need(self, eng, dep, waits):
        kind, key, val = dep
        if kind == "eng" and key == eng:
            if eng in ("pe", "sp") or (eng in ("act", "dve") and not SES_ALL) or not SAME_ENGINE_SYNC:
                return
        k = (kind, key)
        if self.known[eng].get(k, 0) >= val:
            return
        self.known[eng][k] = val
        waits.append((kind, key, val))
        if kind == "eng":
            self.milestones[key].add(val)

    def _deps(self, eng, reads, writes):
        waits = []
        for t in reads:
            lw = self.last_w.get(t)
            if lw is not None:
                self._need(eng, lw, waits)
        for t in writes:
            lw = self.last_w.get(t)
            if lw is not None:
                self._need(eng, lw, waits)
            for r in self.readers.get(t, ()):
                self._need(eng, r, waits)
        return waits

    def _commit(self, me, reads, writes):
        for t in reads:
            self.readers.setdefault(t, []).append(me)
        for t in writes:
            self.last_w[t] = me
            self.readers[t] = []

    def op(self, eng, fn, reads=(), writes=()):
        reads = list(reads); writes = list(writes)
        waits = self._deps(eng, reads, writes)
        self.cnt[eng] += 1
        me = ("eng", eng, self.cnt[eng])
        self.ops[eng].append((waits, fn, ("eng", self.cnt[eng])))
        self._commit(me, reads, writes)

    def dma(self, eng, fn, reads=(), writes=()):
        reads = list(reads); writes = list(writes)
        i = self.dma_rr
        self.dma_rr = (i + 1) % self.nd
        waits = []
        if self.dma_tot[i] > 0:
            self._need(eng, ("dma", i, self.dma_tot[i]), waits)
        waits += self._deps(eng, reads, writes)
        self.dma_tot[i] += 16
        me = ("dma", i, self.dma_tot[i])
        self.cnt[eng] += 1
        self.ops[eng].append((waits, fn, ("dma", i)))
        self._commit(me, reads, writes)

    def _last_seq(self, e):
        for w, fn, inc in reversed(self.ops[e]):
            if inc is not None and inc[0] == "eng":
                return inc[1]
        return 0

    def barrier(self):
        lasts = {e: self._last_seq(e) for e in ENGS}
        for e in ENGS:
            waits = []
            for e2 in ENGS:
                if e2 != e and e2 != "sp" and lasts[e2] > 0:
                    self._need(e, ("eng", e2, lasts[e2]), waits)
            for i in range(self.nd):
                if self.dma_tot[i] > 0:
                    self._need(e, ("dma", i, self.dma_tot[i]), waits)
            if waits:
                self.ops[e].append((waits, None, None))
        self.last_w = {}
        self.readers = {}

    def finish(self):
        self.barrier()

    def emit(self, blk):
        rank = {}
        for e in ENGS:
            ms = sorted(self.milestones[e])
            rank[e] = {s: i + 1 for i, s in enumerate(ms)}

        def run(e, eng):
            for waits, fn, inc in self.ops[e]:
                for kind, key, val in waits:
                    if kind == "eng":
                        eng.wait_ge(self.sem[key], rank[key][val])
                    else:
                        eng.wait_ge(self.dsem[key], val)
                if fn is None:
                    continue
                ins = fn(eng)
                if inc[0] == "dma":
                    ins.then_inc(self.dsem[inc[1]], 16)
                elif inc[1] in rank[e]:
                    ins.then_inc(self.sem[e], 1)

        blk.sync(lambda eng: run("sp", eng))
        blk.scalar(lambda eng: run("act", eng))
        blk.vector(lambda eng: run("dve", eng))
        blk.gpsimd(lambda eng: run("pool", eng))
        blk.tensor(lambda eng: run("pe", eng))


class Prog:
    def __init__(self, debug=None, stop_after=None):
        self.debug = debug or []
        self.stop_after = stop_after
        self.nc = bass.Bass("TRN2", target_bir_lowering=False)
        self.din = {}
        self.dout = {}

    def inp(self, name, shape, dt=F32):
        self.din[name] = self.nc.dram_tensor(name, list(shape), dt, kind="ExternalInput").ap()
        return self.din[name]

    def outp(self, name, shape, dt=F32):
        self.dout[name] = self.nc.dram_tensor(name, list(shape), dt, kind="ExternalOutput").ap()
        return self.dout[name]

    def sb(self, es, name, shape, dt=F32, nsub=1):
        self.uid = getattr(self, "uid", 0) + 1
        name = "%s_u%d" % (name, self.uid)
        return Buf(name, es.enter_context(self.nc.sbuf_tensor(name, list(shape), dt)), nsub)

    def I(self, eng, meth, out, *args, **kw):
        def conv(a):
            return a.ap if isinstance(a, V) else a
        reads = []
        writes = list(out.toks)
        for a in list(args) + list(kw.values()):
            if isinstance(a, V):
                reads += a.toks
        if "accum_out" in kw:
            writes += kw["accum_out"].toks
        a2 = [conv(a) for a in args]
        k2 = {k: conv(v) for k, v in kw.items()}
        o = out.ap
        self.S.op(eng, lambda e: getattr(e, meth)(o, *a2, **k2), reads, writes)

    def dma(self, q, out, in_):
        reads = in_.toks if isinstance(in_, V) else []
        writes = out.toks if isinstance(out, V) else []
        o = out.ap if isinstance(out, V) else out
        i = in_.ap if isinstance(in_, V) else in_
        self.S.dma(q, lambda e: e.dma_start(out=o, in_=i), reads, writes)

    def mm(self, out, lhsT, rhs, start, stop):
        self.I("pe", "matmul", out, lhsT=lhsT, rhs=rhs, start=start, stop=stop)

    def build(self):
        nc = self.nc
        P = self
        inp = self.inp
        x_d = inp("x", [T, D])
        condT_d = inp("condT", [128, 8])
        w_ada_d = inp("w_ada", [L, D, 6 * D])
        b_adaT_d = inp("b_adaT", [L, 128, 48])
        w_in_d = inp("w_in", [L, D, WIN_EXT])
        w_branch_d = inp("w_branch", [L, 4, 256, D])
        w_out_d = inp("w_out", [L, D, D])
        inp("w_gate", [L, 8, 128, 8, 512])
        ln_d = inp("ln", [L, 4, D])
        dft_c_d = inp("dft_c", [T, T], BF16)
        dft_s_d = inp("dft_s", [T, T], BF16)
        cdft_d = inp("cdft", [2, 128, 128], BF16)
        ident_d = inp("ident", [128, 128])
        inp("w_peer_q", [L, 8, 128, 8, 256])
        inp("w_uq", [L, 256, 512]); inp("w_ukv", [L, 128, 512]); inp("mla_q_norm", [L, 128, 2]); inp("mla_kv_norm", [L, 128])
        inp("gla_mats", [5, 128, 128]); inp("gla_mask", [2, 128, 128], BF16); inp("gla_keep", [128, 2, 32]); inp("gla_init", [L, 2, 128, 64])
        inp("w_gla_a", [L, 2, 16, 128]); inp("b_gla_a", [L, 2, 128]); inp("gla_norm", [L, 64])
        inp("rope_m", [2, 32, T]); inp("rope_s", [2, 64, T]); inp("bias_m", [128, 160]); inp("bias_s", [128, 4])
        inp("mask_s", [128, NB, 2, 128], BF16); inp("swa_sink", [L, 128, 4])
        inp("cache_ckv", [L, 512, 128]); inp("cache_krope", [L, 512, 32]); inp("cache_swak", [L, 2, 512, 64]); inp("cache_swav", [L, 2, 512, 64])
        self.outp("ckv_o", [L, T, 128]); self.outp("krope_o", [L, T, 32]); self.outp("swakv_o", [L, T, 256]); self.outp("gla_o", [L, 8, 2, 128, 64])
        inp("peer_keysT", [L, 8, 2, 128, 128])
        inp("peer_uT", [L, 64, 128, 8, 256])
        inp("peer_v", [L, 64, 128, 2, D])
        y_d = self.outp("y", [T, D])
        self.ubf = Buf("ubf", nc.dram_tensor("peer_u_bf", [L, 64, 128, 8 * 256], BF16, kind="Internal").ap(), nsub=L * 16)
        self.vbf = Buf("vbf", nc.dram_tensor("peer_v_bf", [L, 64, 128, 2 * D], BF16, kind="Internal").ap(), nsub=L * 16)
        dbg = {}
        for name, shape in self.debug:
            dbg[name] = self.outp(name, shape, F32)

        with ExitStack() as es:
            self.S = S = Sched(nc, es)
            sb = lambda *a, **k: P.sb(es, *a, **k)
            x = sb("xres", [128, NB, D], F32, nsub=NB)
            ident = sb("identS", [128, 128], F32)
            ones = sb("onesS", [128, 128], F32)
            mcol = sb("mcol", [128, L, 48], F32, nsub=L)
            condT = sb("condTS", [128, 8], F32)
            ps = [Buf("ps%d" % i, es.enter_context(nc.psum_tensor("ps%d" % i, [128, 512], F32))) for i in range(8)]
            self.x, self.ps, self.ident, self.ones, self.mcol = x, ps, ident, ones, mcol
            iota = sb("iotaS", [128, 128], F32)
            P.I("pool", "iota", iota.v(), pattern=[[1, 128]], base=0, channel_multiplier=0, allow_small_or_imprecise_dtypes=True)
            bm = sb("bmS", [128, 8], F32)
            P.dma("sp", bm.v(), inp("bm", [128, 8])[:, :])
            self.iota, self.bm = iota, bm

            for tb in range(NB):
                P.dma("sp", x.v(s_[:, tb, :], tb), x_d[tb * 128:(tb + 1) * 128, :])
            P.dma("sp", ident.v(), ident_d[:, :])
            P.I("pool", "memset", ones.v(), 1.0)
            P.dma("sp", condT.v(), condT_d[:, :])

            with ExitStack() as es0:
                scond = P.sb(es0, "scond", [128, 8], F32)
                wa = [P.sb(es0, "wa%d" % i, [128, 8, 768], F32) for i in range(2)]
                badaT = P.sb(es0, "badaT", [128, L, 48], F32)
                P.I("act", "activation", scond.v(), condT.v(), AF.Silu)
                P.dma("sp", badaT.v(), b_adaT_d.rearrange("l p j -> p l j"))
                n = 0
                for l in range(L):
                    for cg in range(8):
                        pt = ps[cg % 2]
                        wt = wa[n % 2]; n += 1
                        P.dma("sp", wt.v(), w_ada_d[l, :, cg * 768:(cg + 1) * 768].rearrange("(k p) c -> p k c", p=128))
                        for j in range(6):
                            for k in range(8):
                                P.mm(pt.v(s_[:, j:j + 1]), wt.v(s_[:, k, j * 128:(j + 1) * 128]), scond.v(s_[:, k:k + 1]),
                                     start=(k == 0), stop=(k == 7))
                        P.I("dve", "tensor_tensor", mcol.v(s_[:, l, cg * 6:(cg + 1) * 6], l), pt.v(s_[:, 0:6]),
                            badaT.v(s_[:, l, cg * 6:(cg + 1) * 6]), op=ALU.add)
                    for a in (8, 32):
                        P.I("dve", "tensor_scalar_add", mcol.v(s_[:, l, a:a + 8], l), mcol.v(s_[:, l, a:a + 8], l), 1.0)
                S.barrier()
            if "mcol" in dbg:
                P.dma("pool", dbg["mcol"], mcol.v())

            for l in range(L):
                self.layer(l, dbg)
                if self.stop_after == ("layer", l):
                    break

            for tb in range(NB):
                P.dma("sp", y_d[tb * 128:(tb + 1) * 128, :], x.v(s_[:, tb, :], tb))
            S.finish()
            blk = es.enter_context(nc.Block())
            S.emit(blk)
        return nc

    def ln_block(self, tmp, src, dst):
        P = self
        st, mv, rstd, nmr = tmp
        P.I("dve", "bn_stats", st.v(s_[:, 0, :]), src.m(lambda a: a[:, 0:512]))
        P.I("dve", "bn_stats", st.v(s_[:, 1, :]), src.m(lambda a: a[:, 512:1024]))
        P.I("dve", "bn_aggr", mv.v(), st.v())
        P.I("act", "activation", rstd.v(), mv.v(s_[:, 1:2]), AF.Sqrt, bias=EPS, scale=1.0)
        P.I("dve", "reciprocal", rstd.v(), rstd.v())
        P.I("dve", "scalar_tensor_tensor", nmr.v(), mv.v(s_[:, 0:1]), -1.0, rstd.v(), op0=ALU.mult, op1=ALU.mult)
        P.I("act", "activation", dst, src, AF.Identity, bias=nmr.v(), scale=rstd.v())

    def ln_tmp(self, es, tag):
        return (self.sb(es, "st" + tag, [128, 2, 6]), self.sb(es, "mv" + tag, [128, 2]),
                self.sb(es, "rstd" + tag, [128, 1]), self.sb(es, "nmr" + tag, [128, 1]))

    def mod_to_uT(self, l, which):
        P = self; S = self.S
        x, uT, ps, mcol, ident = self.x, self.uT, self.ps, self.mcol, self.ident
        sh0 = 0 if which == 0 else 24
        sc0 = 8 if which == 0 else 32
        with ExitStack() as es:
            tmps = [self.ln_tmp(es, "m%d" % i) for i in range(2)]
            xn = [P.sb(es, "xn%d" % i, [128, D]) for i in range(2)]
            for tb in range(NB):
                xnb = xn[tb % 2]
                self.ln_block(tmps[tb % 2], x.v(s_[:, tb, :], tb), xnb.v())
                for half in range(2):
                    pt = ps[(tb * 2 + half) % 4]
                    for j in range(4):
                        k = half * 4 + j
                        P.I("pe", "transpose", pt.v(s_[:, j * 128:(j + 1) * 128]), xnb.v(s_[:, k * 128:(k + 1) * 128]), ident.v())
                    for j in range(4):
                        k = half * 4 + j
                        eng = "dve" if j % 2 == 0 else "pool"
                        eng = "dve"
                        P.I(eng, "tensor_scalar", uT.v(s_[:, k, tb * 128:(tb + 1) * 128], tb), pt.v(s_[:, j * 128:(j + 1) * 128]),
                            mcol.v(s_[:, l, sc0 + k:sc0 + k + 1], l), mcol.v(s_[:, l, sh0 + k:sh0 + k + 1], l), op0=ALU.mult, op1=ALU.add)
            S.barrier()

    def load_w(self, wb, l, c0, n):
        w_in_d = self.din["w_in"]
        self.dma("pool", wb.v(s_[:, :, 0:n]), w_in_d[l, :, c0:c0 + n].rearrange("(k p) c -> p k c", p=128))

    def proj_fm(self, wb, wc0, m, dst_fn, pbase=0):
        P = self
        for tg in range(4):
            pt = self.ps[self.psi % 8]; self.psi += 1
            for k in range(8):
                P.mm(pt.v(s_[0:m, :]), wb.v(s_[:, k, wc0:wc0 + m]), self.uT.v(s_[:, k, tg * 512:(tg + 1) * 512], range(tg * 4, tg * 4 + 4)),
                     start=(k == 0), stop=(k == 7))
            dst_fn(tg, pt.v(s_[0:m, :]))

    def proj_tm(self, wb, wc0, n, dst_fn):
        P = self
        for tb in range(NB):
            pt = self.ps[self.psi % 8]; self.psi += 1
            for k in range(8):
                P.mm(pt.v(s_[:, 0:n]), self.uT.v(s_[:, k, tb * 128:(tb + 1) * 128], tb), wb.v(s_[:, k, wc0:wc0 + n]),
                     start=(k == 0), stop=(k == 7))
            dst_fn(tb, pt.v(s_[:, 0:n]))


    def rope_evac(self, dst, pa, pb, cosv, sinv, tmpa, tmpb):
        P = self
        P.I("dve", "tensor_tensor", tmpa, pa, cosv, op=ALU.mult)
        P.I("dve", "tensor_tensor", tmpb, pb, sinv, op=ALU.mult)
        P.I("pool", "tensor_tensor", dst, tmpa, tmpb, op=ALU.add)

    def fm_group(self, wb, specs, tg):
        P = self
        outs = []
        for (wc0, m) in specs:
            pt = self.ps[self.psi % 6]; self.psi += 1
            for k in range(8):
                P.mm(pt.v(s_[0:m, :]), wb.v(s_[:, k, wc0:wc0 + m]), self.uT.v(s_[:, k, tg * 512:(tg + 1) * 512], range(tg * 4, tg * 4 + 4)),
                     start=(k == 0), stop=(k == 7))
            outs.append(pt.v(s_[0:m, :]))
        return outs

    def mla(self, l, dbg):
        P = self; S = self.S
        d = self.din
        ps, ident, ones = self.ps, self.ident, self.ones
        NK = 20
        with ExitStack() as es:
            wb = P.sb(es, "wbM", [128, 8, 448], BF16)
            wuq = P.sb(es, "wuq", [128, 2, 512], BF16)
            wukv = P.sb(es, "wukv", [128, 2, 256], BF16)
            gq = P.sb(es, "gq", [128, 2], F32)
            gkv = P.sb(es, "gkv", [128, 128], F32)
            cosT = P.sb(es, "cosM", [32, 512], F32)
            sinT = P.sb(es, "sinM", [32, 512], F32)
            biasM = P.sb(es, "biasM", [128, 8 * NK], F32)
            qnT = P.sb(es, "qnT", [128, 2, T], BF16, nsub=4)
            rs = P.sb(es, "qrs", [128, 512], F32)
            qno = P.sb(es, "qno", [64, T], BF16, nsub=4)
            qro = P.sb(es, "qro", [32, T], BF16, nsub=4)
            kno = P.sb(es, "kno", [64, 2560], BF16)
            kro = P.sb(es, "kro", [32, 2560], BF16)
            ckvT = P.sb(es, "ckvT", [128, 2560], BF16, nsub=NK)
            Vt = P.sb(es, "VtM", [128, NK, 4, 65], BF16, nsub=NK)
            ta = P.sb(es, "rta", [128, 512], F32)
            tb_ = P.sb(es, "rtb", [128, 512], F32)
            kvt = P.sb(es, "kvt", [128, 160], F32)
            ckt = [P.sb(es, "ckt%d" % i, [128, 128], F32) for i in range(2)]
            ss = P.sb(es, "kss", [128, 1], F32)
            junk = P.sb(es, "kjunk", [128, 128], F32)
            PT = [P.sb(es, "PT%d" % i, [128, 256], BF16) for i in range(2)]
            oacc = P.sb(es, "oacc", [128, NB, 128], BF16, nsub=NB)
            identb = P.sb(es, "identb", [128, 128], BF16)
            P.I("dve", "tensor_copy", identb.v(), ident.v())
            rden = P.sb(es, "rden", [128, 1], F32)
            cch = P.sb(es, "cch", [128, 4, 160], F32)
            self.load_w(wb, l, 0, 416)
            P.dma("pool", wb.v(s_[:, :, 416:448]), d["w_in"][l, :, 6080:6112].rearrange("(k p) c -> p k c", p=128))
            P.dma("pool", wuq.v(), d["w_uq"][l].rearrange("(k p) c -> p k c", p=128))
            for two in range(2):
                P.dma("pool", wukv.v(s_[:, two, :]).m(lambda a: a.rearrange("p (h c) -> p h c", h=4)), d["w_ukv"][l].rearrange("p (h two c) -> p two h c", h=4, two=2)[:, two, :, :])
            P.dma("sp", gq.v(), d["mla_q_norm"][l])
            P.dma("sp", gkv.v(), d["mla_kv_norm"][l].partition_broadcast(128))
            P.dma("sp", biasM.v(), d["bias_m"][:, :])
            P.I("pool", "memset", Vt.v(), 1.0)
            for tg in range(4):
                tsl = slice(tg * 512, (tg + 1) * 512)
                P.dma("sp", cosT.v(), d["rope_m"][0, :, tsl])
                P.dma("sp", sinT.v(), d["rope_m"][1, :, tsl])
                pq = self.fm_group(wb, [(0, 128), (128, 128)], tg)
                P.I("act", "activation", ta.v(), pq[0], AF.Square)
                P.I("act", "activation", tb_.v(), pq[1], AF.Square)
                pt = ps[self.psi % 6]; self.psi += 1
                P.mm(pt.v(), ones.v(), ta.v(), start=True, stop=False)
                P.mm(pt.v(), ones.v(), tb_.v(), start=False, stop=True)
                P.I("act", "activation", rs.v(), pt.v(), AF.Sqrt, bias=EPS, scale=1.0 / 256)
                P.I("dve", "reciprocal", rs.v(), rs.v())
                for c in range(2):
                    P.I("dve", "scalar_tensor_tensor", qnT.v(s_[:, c, tsl], tg), pq[c], gq.v(s_[:, c:c + 1]), rs.v(), op0=ALU.mult, op1=ALU.mult)
                pk = self.fm_group(wb, [(384, 32), (416, 32)], tg)
                self.rope_evac(kro.v(s_[:, 512 + tg * 512:512 + (tg + 1) * 512]), pk[0], pk[1], cosT.v(), sinT.v(),
                               ta.v(s_[0:32, :]), tb_.v(s_[0:32, :]))
            for tb in range(NB):
                pt = ps[self.psi % 6]; self.psi += 1
                for k in range(8):
                    P.mm(pt.v(s_[:, 0:160]), self.uT.v(s_[:, k, tb * 128:(tb + 1) * 128], tb), wb.v(s_[:, k, 256:416]), start=(k == 0), stop=(k == 7))
                P.I("act", "activation", kvt.v(), pt.v(s_[:, 0:160]), AF.Copy)
                ck = ckt[tb % 2]
                P.I("act", "activation", junk.v(), kvt.v(s_[:, 0:128]), AF.Square, accum_out=ss.v())
                P.I("act", "activation", ss.v(), ss.v(), AF.Sqrt, bias=EPS, scale=1.0 / 128)
                P.I("dve", "reciprocal", ss.v(), ss.v())
                P.I("dve", "scalar_tensor_tensor", ck.v(), kvt.v(s_[:, 0:128]), ss.v(), gkv.v(), op0=ALU.mult, op1=ALU.mult)
                P.dma("sp", self.dout["ckv_o"][l, tb * 128:(tb + 1) * 128, :], ck.v())
                P.dma("sp", self.dout["krope_o"][l, tb * 128:(tb + 1) * 128, :], kvt.v(s_[:, 128:160]))
                p2 = ps[self.psi % 6]; self.psi += 1
                P.I("pe", "transpose", p2.v(s_[:, 0:128]), ck.v(), ident.v())
                P.I("act", "activation", ckvT.v(s_[:, 512 + tb * 128:512 + (tb + 1) * 128], 4 + tb), p2.v(s_[:, 0:128]), AF.Copy)
            P.dma("sp", cch.v(s_[:, :, 0:128]), d["cache_ckv"][l].rearrange("(j p) c -> p j c", p=128))
            P.dma("sp", cch.v(s_[:, :, 128:160]), d["cache_krope"][l].rearrange("(j p) c -> p j c", p=128))
            for j in range(4):
                p2 = ps[self.psi % 6]; self.psi += 1
                P.I("pe", "transpose", p2.v(s_[:, 0:128]), cch.v(s_[:, j, 0:128]), ident.v())
                P.I("act", "activation", ckvT.v(s_[:, j * 128:(j + 1) * 128], j), p2.v(s_[:, 0:128]), AF.Copy)
                p3 = ps[self.psi % 6]; self.psi += 1
                P.I("pe", "transpose", p3.v(s_[0:32, 0:128]), cch.v(s_[:, j, 128:160]), ident.v())
                P.I("act", "activation", kro.v(s_[:, j * 128:(j + 1) * 128]), p3.v(s_[0:32, 0:128]), AF.Copy)
            for kc in range(NK):
                pt = ps[self.psi % 6]; self.psi += 1
                P.mm(pt.v(s_[:, 0:256]), ckvT.v(s_[:, kc * 128:(kc + 1) * 128], kc), wukv.v(s_[:, 1, :]), start=True, stop=True)
                P.I("dve", "tensor_copy", Vt.v(s_[:, kc, :, 0:64], kc), pt.v(s_[:, 0:256]).m(lambda a: a.rearrange("p (h c) -> p h c", h=4)))
            n = 0
            for h in range(4):
                for tg in range(4):
                    tsl = slice(tg * 512, (tg + 1) * 512)
                    P.dma("sp", cosT.v(), d["rope_m"][0, :, tsl])
                    P.dma("sp", sinT.v(), d["rope_m"][1, :, tsl])
                    pn = ps[self.psi % 6]; pa = ps[(self.psi + 1) % 6]; pb = ps[(self.psi + 2) % 6]; self.psi += 3
                    for c in range(2):
                        P.mm(pn.v(s_[0:64, :]), wuq.v(s_[:, c, h * 96:h * 96 + 64]), qnT.v(s_[:, c, tsl], tg), start=(c == 0), stop=(c == 1))
                    for c in range(2):
                        P.mm(pa.v(s_[0:32, :]), wuq.v(s_[:, c, h * 96 + 64:h * 96 + 96]), qnT.v(s_[:, c, tsl], tg), start=(c == 0), stop=(c == 1))
                    for c in range(2):
                        P.mm(pb.v(s_[0:32, :]), wuq.v(s_[:, c, 384 + h * 32:384 + h * 32 + 32]), qnT.v(s_[:, c, tsl], tg), start=(c == 0), stop=(c == 1))
                    P.I("act", "activation", qno.v(s_[:, tsl], tg), pn.v(s_[0:64, :]), AF.Copy)
                    self.rope_evac(qro.v(s_[:, tsl], tg), pa.v(s_[0:32, :]), pb.v(s_[0:32, :]), cosT.v(), sinT.v(),
                                   ta.v(s_[0:32, :]), tb_.v(s_[0:32, :]))
                for g5 in range(5):
                    gsl = slice(g5 * 512, (g5 + 1) * 512)
                    pt = ps[self.psi % 6]; self.psi += 1
                    P.mm(pt.v(s_[0:64, :]), wukv.v(s_[:, 0, h * 64:(h + 1) * 64]), ckvT.v(s_[:, gsl], range(g5 * 4, g5 * 4 + 4)), start=True, stop=True)
                    P.I("act", "activation", kno.v(s_[:, gsl]), pt.v(s_[0:64, :]), AF.Copy)
                for qu in range(8):
                    qsl = slice(qu * 256, (qu + 1) * 256)
                    po = [ps[6], ps[7]]
                    def emitS(kc):
                        ksl = slice(kc * 128, (kc + 1) * 128)
                        pt = ps[self.psi % 6]; self.psi += 1
                        P.mm(pt.v(s_[:, 0:256]), kno.v(s_[:, ksl]), qno.v(s_[:, qsl], qu // 2), start=True, stop=False)
                        P.mm(pt.v(s_[:, 0:256]), kro.v(s_[:, ksl]), qro.v(s_[:, qsl], qu // 2), start=False, stop=True)
                        return pt
                    pts = {0: emitS(0)}
                    for kc in range(NK):
                        if kc + 1 < NK:
                            pts[kc + 1] = emitS(kc + 1)
                        pt = pts.pop(kc)
                        pT = PT[n % 2]; n += 1
                        P.I("act", "activation", pT.v(), pt.v(s_[:, 0:256]), AF.Exp, bias=biasM.v(s_[:, qu * NK + kc:qu * NK + kc + 1]), scale=MLA_SCALE)
                        for qb in range(2):
                            P.mm(po[qb].v(s_[:, 0:65]), pT.v(s_[:, qb * 128:(qb + 1) * 128]), Vt.v(s_[:, kc, h, :], kc), start=(kc == 0), stop=(kc == NK - 1))
                    for qb in range(2):
                        tb = qu * 2 + qb
                        P.I("dve", "reciprocal", rden.v(), po[qb].v(s_[:, 64:65]))
                        P.I("dve", "tensor_scalar_mul", oacc.v(s_[:, tb, (h % 2) * 64:(h % 2) * 64 + 64], tb), po[qb].v(s_[:, 0:64]), rden.v())
                if h % 2 == 1:
                    for tb in range(NB):
                        p2 = ps[self.psi % 6]; self.psi += 1
                        P.mm(p2.v(s_[:, 0:128]), oacc.v(s_[:, tb, :], tb), identb.v(), start=True, stop=True)
                        P.I("act", "activation", self.brT[0].v(s_[:, h // 2, tb * 128:(tb + 1) * 128], tb), p2.v(s_[:, 0:128]), AF.Copy)
            S.barrier()

    def gla_block(self, l, tb, wb, afT, qT, kT, w2, b2, onesb, ktm, e1t, spt, eq, ek, el, mats, vtm, kl, gcol, qeT, keT, LNQ):
        P = self; ps = self.ps
        lsl = slice((tb % 4) * 128, (tb % 4) * 128 + 128)
        bsl = slice(tb * 128, (tb + 1) * 128)
        pt = ps[self.psi % 6]; self.psi += 1
        for k in range(8):
            P.mm(pt.v(s_[:, 0:384]), self.uT.v(s_[:, k, bsl], tb), wb.v(s_[:, k, 128:512]), start=(k == 0), stop=(k == 7))
        P.I("dve", "tensor_copy", ktm.v(), pt.v(s_[:, 0:128]))
        P.I("dve", "tensor_copy", vtm.v(s_[:, tb, :], tb), pt.v(s_[:, 128:384]))
        e1t_, spt_, eq_, ek_, el_ = e1t, spt, eq, ek, el
        for dr in range(2):
            e1t, spt, eq, ek, el = e1t_[dr], spt_[dr], eq_[dr], ek_[dr], el_[dr]
            pz = ps[self.psi % 6]; self.psi += 1
            P.mm(pz.v(s_[:, 0:128]), afT.v(s_[:, dr, lsl]), w2.v(s_[:, dr, :]), start=True, stop=False)
            P.mm(pz.v(s_[:, 0:128]), onesb.v(), b2.v(s_[:, dr, :]), start=False, stop=True)
            P.I("act", "activation", e1t.v(), pz.v(s_[:, 0:128]), AF.Exp, scale=-1.0)
            P.I("act", "activation", spt.v(), e1t.v(), AF.Ln, bias=1.0, scale=1.0)
            mi = 0 if dr == 0 else 2
            if self.G1 <= 3:
                continue
            pl = ps[self.psi % 6]; self.psi += 1
            P.mm(pl.v(s_[:, 0:128]), mats.v(s_[:, mi + 1, :]), spt.v(), start=True, stop=True)
            P.I("act", "activation", el.v(), pl.v(s_[:, 0:128]), AF.Exp)
            P.I("dve", "tensor_tensor", kl[dr].v(s_[:, tb, :], tb), ktm.v(), el.v(), op=ALU.mult)
            for hp in range(2):
                pc = ps[self.psi % 6]; pg = ps[(self.psi + 1) % 6]; self.psi += 2
                P.mm(pc.v(s_[0:64, 0:128]), spt.v(s_[:, hp * 64:(hp + 1) * 64]), mats.v(s_[:, mi, :]), start=True, stop=True)
                P.mm(pg.v(s_[0:64, 0:2]), spt.v(s_[:, hp * 64:(hp + 1) * 64]), mats.v(s_[:, 4, 0:2]), start=True, stop=True)
                P.I("act", "activation", eq.v(s_[:, hp, :]), pc.v(s_[0:64, 0:128]), AF.Exp, bias=LNQ, scale=1.0)
                P.I("act", "activation", ek.v(s_[:, hp, :]), pc.v(s_[0:64, 0:128]), AF.Exp, scale=-1.0)
                P.I("act", "activation", gcol.v(s_[:, dr, hp, 2 * tb:2 * tb + 2], dr), pg.v(s_[0:64, 0:2]), AF.Exp)
            P.I("dve", "tensor_tensor", qeT[dr].v(s_[:, :, bsl], tb), qT.v(s_[:, :, lsl]), eq.v(), op=ALU.mult)
            P.I("dve", "tensor_tensor", keT[dr].v(s_[:, :, bsl], tb), kT.v(s_[:, :, lsl]), ek.v(), op=ALU.mult)

    def gla(self, l, dbg):
        P = self; S = self.S
        d = self.din
        ps, ident, ones = self.ps, self.ident, self.ones
        LNQ = float(np.log(32.0 ** -0.5))
        with ExitStack() as es:
            qeT = [P.sb(es, "qeT%d" % i, [64, 2, T], BF16, nsub=NB) for i in range(2)]
            keT = [P.sb(es, "keT%d" % i, [64, 2, T], BF16, nsub=NB) for i in range(2)]
            kl = [P.sb(es, "kl%d" % i, [128, NB, 128], BF16, nsub=NB) for i in range(2)]
            gcol = P.sb(es, "gcol", [64, 2, 2, 32], F32, nsub=2)
            vtm = P.sb(es, "vtm", [128, NB, 256], BF16, nsub=NB)
            mats = P.sb(es, "gmats", [128, 5, 128], F32)
            gmask = P.sb(es, "gmask", [128, 2, 128], BF16)
            keep = P.sb(es, "gkeep", [128, 2, 32], F32)
            identb = P.sb(es, "identbG", [128, 128], BF16)
            P.dma("sp", mats.v(), d["gla_mats"].rearrange("a p c -> p a c"))
            P.dma("sp", gmask.v(), d["gla_mask"].rearrange("a p c -> p a c"))
            P.dma("sp", keep.v(), d["gla_keep"][:, :, :])
            P.I("dve", "tensor_copy", identb.v(), ident.v())
            with ExitStack() as e1:
                wb = P.sb(e1, "wbG", [128, 8, 800], BF16)
                afT = P.sb(e1, "afT", [16, 2, 512], BF16)
                qT = P.sb(e1, "gqT", [64, 2, 512], BF16)
                kT = P.sb(e1, "gkT", [64, 2, 512], BF16)
                w2 = P.sb(e1, "gw2", [16, 2, 128], BF16)
                b2 = P.sb(e1, "gb2", [1, 2, 128], BF16)
                onesb = P.sb(e1, "onesb", [1, 128], BF16)
                ktm = P.sb(e1, "ktm", [128, 128], F32)
                e1t = [P.sb(e1, "ge1%d" % i, [128, 128], F32) for i in range(2)]
                spt = [P.sb(e1, "gsp%d" % i, [128, 128], F32) for i in range(2)]
                eq = [P.sb(e1, "geq", [64, 2, 128], F32)] * 2
                ek = [P.sb(e1, "gek", [64, 2, 128], F32)] * 2
                el = [P.sb(e1, "gel", [128, 128], F32)] * 2
                self.load_w(wb, l, 672, 800)
                P.dma("pool", w2.v(), d["w_gla_a"][l].rearrange("a r c -> r a c"))
                P.dma("pool", b2.v(), d["b_gla_a"][l:l + 1, :, :])
                P.I("pool", "memset", onesb.v(), 1.0)
                import os
                G1 = int(os.environ.get("KGLA_G1", "99"))
                for tg in range(4 if G1 >= 2 else 0):
                    tsl = slice(tg * 512, (tg + 1) * 512)
                    pq = self.fm_group(wb, [(0, 64), (64, 64), (128, 64), (192, 64)], tg)
                    P.I("act", "activation", qT.v(s_[:, 0, :]), pq[0], AF.Copy)
                    P.I("dve", "tensor_copy", qT.v(s_[:, 1, :]), pq[1])
                    P.I("act", "activation", kT.v(s_[:, 0, :]), pq[2], AF.Copy)
                    P.I("dve", "tensor_copy", kT.v(s_[:, 1, :]), pq[3])
                    pq = self.fm_group(wb, [(768, 16), (784, 16)], tg)
                    P.I("act", "activation", afT.v(s_[:, 0, :]), pq[0], AF.Copy)
                    P.I("dve", "tensor_copy", afT.v(s_[:, 1, :]), pq[1])
                    for tb in range(tg * 4, tg * 4 + 4) if G1 >= 3 else []:
                        self.G1 = G1
                        self.gla_block(l, tb, wb, afT, qT, kT, w2, b2, onesb, ktm, e1t, spt, eq, ek, el, mats, vtm, kl, gcol, qeT, keT, LNQ)
                S.barrier()
            import os
            GSTOP = int(os.environ.get("KGLA_STOP", "99"))
            if GSTOP <= 1:
                return
            with ExitStack() as e2:
                of = P.sb(e2, "gof", [128, NB, 256], F32, nsub=NB)
                St = P.sb(e2, "gS", [64, 2, 64], F32)
                tmpS = P.sb(e2, "gtmp", [64, 2, 64], F32)
                Sb = [P.sb(e2, "gSb%d" % i, [64, 2, 64], BF16) for i in range(2)]
                att = [P.sb(e2, "gatt%d" % i, [128, 128], BF16) for i in range(2)]
                na = 0
                for dr in range(2):
                    P.dma("sp", St.v(), d["gla_init"][l, dr].rearrange("(a p) c -> p a c", p=64))
                    blks = range(NB) if dr == 0 else range(NB - 1, -1, -1)
                    for tb in blks:
                        bsl = slice(tb * 128, (tb + 1) * 128)
                        halves = (0, 1) if dr == 0 else (1, 0)
                        for hf in halves:
                            n = 2 * tb + hf
                            r0 = hf * 64
                            P.I("act", "activation", Sb[hf].v(), St.v(), AF.Copy)
                            pu = ps[self.psi % 6]; self.psi += 1
                            for h in range(4):
                                P.mm(pu.v(s_[(h % 2) * 32:(h % 2) * 32 + 32, (h // 2) * 64:(h // 2) * 64 + 64]), kl[dr].v(s_[r0:r0 + 64, tb, h * 32:(h + 1) * 32], tb),
                                     vtm.v(s_[r0:r0 + 64, tb, h * 64:(h + 1) * 64], tb), start=True, stop=True)
                            for hp in range(2):
                                P.I("dve", "scalar_tensor_tensor", tmpS.v(s_[:, hp, :]), St.v(s_[:, hp, :]), gcol.v(s_[:, dr, hp, n:n + 1], dr), pu.v(s_[0:64, hp * 64:(hp + 1) * 64]), op0=ALU.mult, op1=ALU.add)
                            if (dr == 0 and n % 4 == 3) or (dr == 1 and n % 4 == 0):
                                P.dma("sp", self.dout["gla_o"][l, n // 4, dr].rearrange("(a p) c -> p a c", p=64), tmpS.v())
                            nn = n + 1 if dr == 0 else n - 1
                            if 0 <= nn < 32:
                                P.I("dve", "tensor_scalar_mul", St.v(), tmpS.v(), keep.v(s_[0:64, dr, nn:nn + 1]))
                        po = ps[6 + (tb % 2)]
                        for h in range(4):
                            hs = slice((h % 2) * 32, (h % 2) * 32 + 32); hp = h // 2
                            pa = ps[self.psi % 6]; self.psi += 1
                            P.mm(pa.v(s_[:, 0:128]), keT[dr].v(s_[hs, hp, bsl], tb), qeT[dr].v(s_[hs, hp, bsl], tb), start=True, stop=True)
                            at = att[na % 2]; na += 1
                            P.I("dve", "tensor_tensor", at.v(), pa.v(s_[:, 0:128]), gmask.v(s_[:, dr, :]), op=ALU.mult)
                            oc = slice(h * 64, (h + 1) * 64)
                            P.mm(po.v(s_[:, oc]), at.v(), vtm.v(s_[:, tb, oc], tb), start=True, stop=False)
                            P.mm(po.v(s_[0:64, oc]), qeT[dr].v(s_[hs, hp, tb * 128:tb * 128 + 64], tb), Sb[0].v(s_[hs, hp, :]), start=False, stop=False)
                            P.mm(po.v(s_[64:128, oc]), qeT[dr].v(s_[hs, hp, tb * 128 + 64:tb * 128 + 128], tb), Sb[1].v(s_[hs, hp, :]), start=False, stop=True)
                        if dr == 0:
                            P.I("act", "activation", of.v(s_[:, tb, :], tb), po.v(s_[:, 0:256]), AF.Copy)
                        else:
                            P.I("dve", "tensor_tensor", of.v(s_[:, tb, :], tb), of.v(s_[:, tb, :], tb), po.v(s_[:, 0:256]), op=ALU.add)
                if GSTOP <= 2:
                    S.barrier(); return
                wg = P.sb(e2, "wgG", [128, 8, 256], BF16)
                gn = P.sb(e2, "ggn", [128, 64], F32)
                ss4 = P.sb(e2, "gss", [128, 4], F32)
                junk = P.sb(e2, "gjunk", [128, 64], F32)
                sg = P.sb(e2, "gsil", [128, 256], F32)
                ob = P.sb(e2, "gob", [128, 256], BF16)
                self.load_w(wg, l, 1184, 256)
                P.dma("sp", gn.v(), d["gla_norm"][l].partition_broadcast(128))
                for tb in range(NB):
                    bsl = slice(tb * 128, (tb + 1) * 128)
                    for h in range(4):
                        P.I("act", "activation", junk.v(), of.v(s_[:, tb, h * 64:(h + 1) * 64], tb), AF.Square, accum_out=ss4.v(s_[:, h:h + 1]))
                    P.I("act", "activation", ss4.v(), ss4.v(), AF.Sqrt, bias=EPS, scale=1.0 / 64)
                    P.I("dve", "reciprocal", ss4.v(), ss4.v())
                    ov = of.v(s_[:, tb, :], tb).m(lambda a: a.rearrange("p (h c) -> p h c", h=4))
                    P.I("dve", "tensor_tensor", ov, ov, ss4.v().m(lambda a: a.unsqueeze(2).to_broadcast([128, 4, 64])), op=ALU.mult)
                    P.I("dve", "tensor_tensor", ov, ov, gn.v().m(lambda a: a.unsqueeze(1).to_broadcast([128, 4, 64])), op=ALU.mult)
                    pg = ps[self.psi % 6]; self.psi += 1
                    for k in range(8):
                        P.mm(pg.v(s_[:, 0:256]), self.uT.v(s_[:, k, bsl], tb), wg.v(s_[:, k, :]), start=(k == 0), stop=(k == 7))
                    P.I("act", "activation", sg.v(), pg.v(s_[:, 0:256]), AF.Silu)
                    P.I("dve", "tensor_tensor", ob.v(), of.v(s_[:, tb, :], tb), sg.v(), op=ALU.mult)
                    for c in range(2):
                        p2 = ps[self.psi % 6]; self.psi += 1
                        P.mm(p2.v(s_[:, 0:128]), ob.v(s_[:, c * 128:(c + 1) * 128]), identb.v(), start=True, stop=True)
                        P.I("act", "activation", self.brT[2].v(s_[:, c, bsl], tb), p2.v(s_[:, 0:128]), AF.Copy)
                S.barrier()

    def swa_kv_only(self, l):
        P = self; S = self.S
        ps = self.ps
        with ExitStack() as es:
            wb = P.sb(es, "wbS2", [128, 8, 256], BF16)
            kvo = [P.sb(es, "kvo2_%d" % i, [128, 256], F32) for i in range(2)]
            self.load_w(wb, l, 1728, 256)
            for tb in range(NB):
                pt = ps[self.psi % 6]; self.psi += 1
                for k in range(8):
                    P.mm(pt.v(s_[:, 0:256]), self.uT.v(s_[:, k, tb * 128:(tb + 1) * 128], tb), wb.v(s_[:, k, 0:256]), start=(k == 0), stop=(k == 7))
                ko = kvo[tb % 2]
                P.I("act", "activation", ko.v(), pt.v(s_[:, 0:256]), AF.Copy)
                P.dma("sp", self.dout["swakv_o"][l, tb * 128:(tb + 1) * 128, :], ko.v())
            S.barrier()

    def swa(self, l, dbg):
        P = self; S = self.S
        d = self.din
        ps, ident, ones = self.ps, self.ident, self.ones
        NK = 20
        with ExitStack() as es:
            wb = P.sb(es, "wbS", [128, 8, 896], BF16)
            cosT = P.sb(es, "cosS", [64, 512], F32)
            sinT = P.sb(es, "sinS", [64, 512], F32)
            biasS = P.sb(es, "biasS", [128, 4], F32)
            esink = P.sb(es, "esink", [128, 4], F32)
            msk = P.sb(es, "mskS", [128, NB, 2, 128], BF16)
            qT = P.sb(es, "qTS", [64, 4, T], BF16, nsub=4)
            kT = P.sb(es, "kTS", [64, 2, 2560], BF16)
            Vt = P.sb(es, "VtS", [128, NK, 2, 65], BF16, nsub=NK)
            ta = P.sb(es, "sta", [64, 512], F32)
            tb_ = P.sb(es, "stb", [64, 512], F32)
            kvo = [P.sb(es, "kvo%d" % i, [128, 256], F32) for i in range(2)]
            cch = P.sb(es, "cchS", [128, 4, 2, 64], F32)
            PT = [P.sb(es, "PTS%d" % i, [128, 256], BF16) for i in range(2)]
            oa = P.sb(es, "oaS", [128, 256], F32)
            rden = P.sb(es, "rdenS", [128, 1], F32)
            self.load_w(wb, l, 1472, 512)
            P.dma("pool", wb.v(s_[:, :, 512:896]), d["w_in"][l, :, 6112:6496].rearrange("(k p) c -> p k c", p=128))
            P.dma("sp", biasS.v(), d["bias_s"][:, :])
            P.dma("sp", esink.v(), d["swa_sink"][l])
            P.I("act", "activation", esink.v(), esink.v(), AF.Exp)
            P.dma("sp", msk.v(), d["mask_s"][:, :, :, :])
            P.I("pool", "memset", Vt.v(), 1.0)
            import os
            STOP = int(os.environ.get("KSWA_STOP", "99"))
            if STOP <= 1:
                S.barrier(); return
            for tg in range(4):
                tsl = slice(tg * 512, (tg + 1) * 512)
                P.dma("sp", cosT.v(), d["rope_s"][0, :, tsl])
                P.dma("sp", sinT.v(), d["rope_s"][1, :, tsl])
                for h in range(4):
                    pq = self.fm_group(wb, [(h * 64, 64), (512 + h * 64, 64)], tg)
                    self.rope_evac(qT.v(s_[:, h, tsl], h), pq[0], pq[1], cosT.v(), sinT.v(), ta.v(), tb_.v())
                for j in range(2):
                    pk = self.fm_group(wb, [(256 + j * 64, 64), (768 + j * 64, 64)], tg)
                    self.rope_evac(kT.v(s_[:, j, 512 + tg * 512:512 + (tg + 1) * 512]), pk[0], pk[1], cosT.v(), sinT.v(), ta.v(), tb_.v())
            if STOP <= 2:
                S.barrier(); return
            for tb in range(NB):
                pt = ps[self.psi % 6]; self.psi += 1
                for k in range(8):
                    P.mm(pt.v(s_[:, 0:256]), self.uT.v(s_[:, k, tb * 128:(tb + 1) * 128], tb), wb.v(s_[:, k, 256:512]), start=(k == 0), stop=(k == 7))
                ko = kvo[tb % 2]
                P.I("act", "activation", ko.v(), pt.v(s_[:, 0:256]), AF.Copy)
                P.dma("sp", self.dout["swakv_o"][l, tb * 128:(tb + 1) * 128, :], ko.v())
                P.I("dve", "tensor_copy", Vt.v(s_[:, 4 + tb, :, 0:64], 4 + tb), ko.v(s_[:, 128:256]).m(lambda a: a.rearrange("p (j c) -> p j c", j=2)))
            if STOP <= 3:
                S.barrier(); return
            for j in range(2):
                P.dma("sp", cch.v(s_[:, :, j, :]), d["cache_swak"][l, j].rearrange("(c p) e -> p c e", p=128))
            for c in range(4):
                for j in range(2):
                    p2 = ps[self.psi % 6]; self.psi += 1
                    P.I("pe", "transpose", p2.v(s_[0:64, 0:128]), cch.v(s_[:, c, j, :]), ident.v())
                    P.I("act", "activation", kT.v(s_[:, j, c * 128:(c + 1) * 128]), p2.v(s_[0:64, 0:128]), AF.Copy)
            cchv = P.sb(es, "cchV", [128, 4, 2, 64], F32)
            for j in range(2):
                P.dma("sp", cchv.v(s_[:, :, j, :]), d["cache_swav"][l, j].rearrange("(c p) e -> p c e", p=128))
            for c in range(4):
                P.I("dve", "tensor_copy", Vt.v(s_[:, c, :, 0:64], c), cchv.v(s_[:, c, :, :]))
            n = 0
            import os
            for tb in range(NB if "noatt" not in os.environ.get("KSKIP", "") else 0):
                qsl = slice(tb * 128, (tb + 1) * 128)
                for j in range(2):
                    po = [ps[6], ps[7]]
                    kcs = [(4 + tb + dd, dd) for dd in (-1, 0, 1) if 0 <= tb + dd < NB] + [(c, 2) for c in range(4)]
                    def emitS(i):
                        kc, kind = kcs[i]
                        ksl = slice(kc * 128, (kc + 1) * 128)
                        pt = ps[self.psi % 6]; self.psi += 1
                        for g in range(2):
                            P.mm(pt.v(s_[:, g * 128:(g + 1) * 128]), kT.v(s_[:, j, ksl]), qT.v(s_[:, 2 * j + g, qsl], 2 * j + g), start=True, stop=True)
                        return pt
                    pts = {0: emitS(0)}
                    for i, (kc, kind) in enumerate(kcs):
                        if i + 1 < len(kcs):
                            pts[i + 1] = emitS(i + 1)
                        pt = pts.pop(i)
                        pT = PT[n % 2]; n += 1
                        if kind == 2:
                            P.I("act", "activation", pT.v(), pt.v(s_[:, 0:256]), AF.Exp, bias=biasS.v(s_[:, 0:1]), scale=SWA_SCALE)
                        else:
                            P.I("act", "activation", pT.v(), pt.v(s_[:, 0:256]), AF.Exp, scale=SWA_SCALE)
                            if kind != 0:
                                mi = 0 if kind == -1 else 1
                                for g in range(2):
                                    P.I("dve", "tensor_tensor", pT.v(s_[:, g * 128:(g + 1) * 128]), pT.v(s_[:, g * 128:(g + 1) * 128]), msk.v(s_[:, tb, mi, :]), op=ALU.mult)
                        for g in range(2):
                            P.mm(po[g].v(s_[:, 0:65]), pT.v(s_[:, g * 128:(g + 1) * 128]), Vt.v(s_[:, kc, j, :], kc), start=(i == 0), stop=(i == len(kcs) - 1))
                    for g in range(2):
                        hh = 2 * j + g
                        P.I("dve", "tensor_tensor", rden.v(), po[g].v(s_[:, 64:65]), esink.v(s_[:, hh:hh + 1]), op=ALU.add)
                        P.I("dve", "reciprocal", rden.v(), rden.v())
                        P.I("dve", "tensor_scalar_mul", oa.v(s_[:, hh * 64:(hh + 1) * 64]), po[g].v(s_[:, 0:64]), rden.v())
                for c in range(2):
                    p2 = ps[self.psi % 6]; self.psi += 1
                    P.I("pe", "transpose", p2.v(s_[:, 0:128]), oa.v(s_[:, c * 128:(c + 1) * 128]), ident.v())
                    P.I("act", "activation", self.brT[3].v(s_[:, c, tb * 128:(tb + 1) * 128], tb), p2.v(s_[:, 0:128]), AF.Copy)
            S.barrier()

    def fnet(self, l):
        P = self; S = self.S
        d = self.din
        with ExitStack() as es:
            wb = P.sb(es, "wbF", [128, 8, 256], BF16)
            fT = P.sb(es, "fT", [128, 2, T], BF16, nsub=4)
            cd = P.sb(es, "cdft", [128, 2, 128], BF16)
            A = P.sb(es, "fA", [128, NB, 256], BF16, nsub=NB)
            B = P.sb(es, "fB", [128, NB, 256], BF16, nsub=NB)
            tc_ = [P.sb(es, "dc%d" % i, [128, 512], BF16) for i in range(4)]
            ts_ = [P.sb(es, "ds%d" % i, [128, 512], BF16) for i in range(4)]
            self.load_w(wb, l, 416, 256)
            P.dma("sp", cd.v(), d["cdft"].rearrange("a p c -> p a c"))
            for c in range(2):
                def ev(tg, pv, c=c):
                    P.I("act", "activation", fT.v(s_[:, c, tg * 512:(tg + 1) * 512], tg), pv, AF.Copy)
                self.proj_fm(wb, c * 128, 128, ev)
            for tb in range(NB):
                pa = self.ps[self.psi % 8]; self.psi += 1
                for c in range(2):
                    P.mm(pa.v(s_[:, c * 128:(c + 1) * 128]), fT.v(s_[:, c, tb * 128:(tb + 1) * 128], tb // 4), cd.v(s_[:, 0, :]), start=True, stop=True)
                    P.mm(pa.v(s_[:, 256 + c * 128:256 + (c + 1) * 128]), fT.v(s_[:, c, tb * 128:(tb + 1) * 128], tb // 4), cd.v(s_[:, 1, :]), start=True, stop=True)
                P.I("act", "activation", A.v(s_[:, tb, :], tb), pa.v(s_[:, 0:256]), AF.Copy)
                P.I("act", "activation", B.v(s_[:, tb, :], tb), pa.v(s_[:, 256:512]), AF.Copy)
            n = 0
            for tg in range(4):
                p0 = self.ps[self.psi % 8]; p1 = self.ps[(self.psi + 1) % 8]; self.psi += 2
                for tb in range(NB):
                    ct = tc_[n % 4]; st = ts_[n % 4]; n += 1
                    P.dma("sp", ct.v(), d["dft_c"][tb * 128:(tb + 1) * 128, tg * 512:(tg + 1) * 512])
                    P.dma("act", st.v(), d["dft_s"][tb * 128:(tb + 1) * 128, tg * 512:(tg + 1) * 512])
                    for c, pp in ((0, p0), (1, p1)):
                        P.mm(pp.v(), A.v(s_[:, tb, c * 128:(c + 1) * 128], tb), ct.v(), start=(tb == 0), stop=False)
                        P.mm(pp.v(), B.v(s_[:, tb, c * 128:(c + 1) * 128], tb), st.v(), start=False, stop=(tb == NB - 1))
                for c, pp in ((0, p0), (1, p1)):
                    P.I("act" if c == 0 else "dve", "activation" if c == 0 else "tensor_copy", self.brT[1].v(s_[:, c, tg * 512:(tg + 1) * 512], range(tg * 4, tg * 4 + 4)),
                        pp.v(), *((AF.Copy,) if c == 0 else ()))
            S.barrier()

    def merge(self, l, dbg):
        P = self; S = self.S
        d = self.din
        x, uT, ps, mcol, ident, ones = self.x, self.uT, self.ps, self.mcol, self.ident, self.ones
        with ExitStack() as es:
            wbr = P.sb(es, "wbr", [128, 8, D], BF16)
            wo = P.sb(es, "wo", [128, 8, D], BF16)
            wg = [P.sb(es, "wg%d" % i, [128, 8, 512], BF16) for i in range(1)]
            G = P.sb(es, "Gacc", [128, 512], F32)
            GT = P.sb(es, "GT", [128, 8, 512], BF16, nsub=8)
            sg = [P.sb(es, "sg%d" % i, [128, 512], F32) for i in range(2)]
            gbc = P.sb(es, "g1bc", [128, D], F32)
            lng = P.sb(es, "ln1g", [128, D], F32)
            lnb = P.sb(es, "ln1b", [128, D], F32)
            dg = P.sb(es, "dgm", [128, 128], F32)
            xt = [P.sb(es, "xt%d" % i, [128, D], F32) for i in range(1)]
            tmps = [self.ln_tmp(es, "g%d" % i) for i in range(2)]
            P.dma("pool", wbr.v(), d["w_branch"][l].rearrange("b (k p) d -> p (b k) d", p=128))
            P.dma("pool", wo.v(), d["w_out"][l].rearrange("(k p) d -> p k d", p=128))
            P.dma("sp", lng.v(), d["ln"][l, 0, :].partition_broadcast(128))
            P.dma("sp", lnb.v(), d["ln"][l, 1, :].partition_broadcast(128))
            for k in range(8):
                P.I("dve", "tensor_scalar_mul", dg.v(), ident.v(), mcol.v(s_[:, l, 16 + k:17 + k], l))
                pt = ps[self.psi % 8]; self.psi += 1
                P.mm(pt.v(s_[:, 0:128]), ones.v(), dg.v(), start=True, stop=True)
                P.I("act", "activation", gbc.v(s_[:, k * 128:(k + 1) * 128]), pt.v(s_[:, 0:128]), AF.Copy)
            n = 0
            for tg in range(4):
                tsub = range(tg * 4, tg * 4 + 4)
                tsl = slice(tg * 512, (tg + 1) * 512)
                for dc in range(8):
                    w = wg[0]; n += 1
                    P.dma("pool", w.v(), d["w_gate"][l, dc])
                    for b in range(4):
                        pg = ps[self.psi % 8]; pp = ps[(self.psi + 1) % 8]; self.psi += 2
                        for k in range(8):
                            P.mm(pg.v(), w.v(s_[:, k, b * 128:(b + 1) * 128]), uT.v(s_[:, k, tsl], tsub), start=(k == 0), stop=(k == 7))
                        for kc in range(2):
                            P.mm(pp.v(), wbr.v(s_[:, b * 2 + kc, dc * 128:(dc + 1) * 128]), self.brT[b].v(s_[:, kc, tsl], tsub),
                                 start=(kc == 0), stop=(kc == 1))
                        sgt = sg[b % 2]
                        P.I("act", "activation", sgt.v(), pg.v(), AF.Sigmoid)
                        if b == 0:
                            P.I("dve", "tensor_tensor", G.v(), sgt.v(), pp.v(), op=ALU.mult)
                        else:
                            P.I("dve", "tensor_tensor", sgt.v(), sgt.v(), pp.v(), op=ALU.mult)
                            if b < 3:
                                P.I("pool", "tensor_tensor", G.v(), G.v(), sgt.v(), op=ALU.add)
                            else:
                                P.I("pool", "tensor_tensor", GT.v(s_[:, dc, :], dc), G.v(), sgt.v(), op=ALU.add)
                for j in range(4):
                    tb = tg * 4 + j
                    xtb = xt[0]
                    for hf in range(2):
                        pm = ps[self.psi % 8]; self.psi += 1
                        for k in range(8):
                            P.mm(pm.v(), GT.v(s_[:, k, j * 128:(j + 1) * 128], k), wo.v(s_[:, k, hf * 512:(hf + 1) * 512]), start=(k == 0), stop=(k == 7))
                        hs = slice(hf * 512, (hf + 1) * 512)
                        P.I("dve", "tensor_tensor", xtb.v(s_[:, hs]), pm.v(), gbc.v(s_[:, hs]), op=ALU.mult)
                        P.I("dve", "scalar_tensor_tensor", xtb.v(s_[:, hs]), x.v(s_[:, tb, hs], tb), ALPHA, xtb.v(s_[:, hs]), op0=ALU.mult, op1=ALU.add)
                    self.ln_block(tmps[tb % 2], xtb.v(), x.v(s_[:, tb, :], tb))
                    P.I("pool", "tensor_tensor", x.v(s_[:, tb, :], tb), x.v(s_[:, tb, :], tb), lng.v(), op=ALU.mult)
                    P.I("pool", "tensor_tensor", x.v(s_[:, tb, :], tb), x.v(s_[:, tb, :], tb), lnb.v(), op=ALU.add)
            S.barrier()

    def post_ffn(self, l, yacc_is_x=True):
        P = self; S = self.S
        d = self.din
        x = self.x
        with ExitStack() as es:
            lng = P.sb(es, "ln2g", [128, D], F32)
            lnb = P.sb(es, "ln2b", [128, D], F32)
            xt = [P.sb(es, "xq%d" % i, [128, D], F32) for i in range(2)]
            tmps = [self.ln_tmp(es, "q%d" % i) for i in range(2)]
            P.dma("sp", lng.v(), d["ln"][l, 2, :].partition_broadcast(128))
            P.dma("sp", lnb.v(), d["ln"][l, 3, :].partition_broadcast(128))
            for tb in range(NB):
                xtb = xt[tb % 2]
                P.I("act", "activation", xtb.v(), x.v(s_[:, tb, :], tb), AF.Copy)
                self.ln_block(tmps[tb % 2], xtb.v(), x.v(s_[:, tb, :], tb))
                P.I("pool", "tensor_tensor", x.v(s_[:, tb, :], tb), x.v(s_[:, tb, :], tb), lng.v(), op=ALU.mult)
                P.I("pool", "tensor_tensor", x.v(s_[:, tb, :], tb), x.v(s_[:, tb, :], tb), lnb.v(), op=ALU.add)
            S.barrier()

    def layer(self, l, dbg):
        P = self; S = self.S
        self.psi = 0
        for g4 in range(16):
            P.dma("pool", self.ubf.v(s_[l, g4 * 4:(g4 + 1) * 4], l * 16 + g4), self.din["peer_uT"][l, g4 * 4:(g4 + 1) * 4].rearrange("g p k e -> g p (k e)"))
            P.dma("pool", self.vbf.v(s_[l, g4 * 4:(g4 + 1) * 4], l * 16 + g4), self.din["peer_v"][l, g4 * 4:(g4 + 1) * 4].rearrange("g p j d -> g p (j d)"))
        with ExitStack() as esl:
            self.uT = P.sb(esl, "uT", [128, 8, T], BF16, nsub=NB)
            self.mod_to_uT(l, 0)
            self.brT = [P.sb(esl, "brT%d" % b, [128, 2, T], BF16, nsub=NB) for b in range(4)]
            import os
            skip = os.environ.get("KSKIP", "")
            for b, nm in ((0, "mla"), (3, "swa"), (1, "fnet"), (2, "gla")):
                if nm in skip:
                    for tb4 in range(4):
                        P.I("pool", "memset", self.brT[b].v(s_[:, :, tb4 * 512:(tb4 + 1) * 512], range(tb4 * 4, tb4 * 4 + 4)), 0.0)
            if "mla" not in skip:
                self.mla(l, dbg)
            if "swa" not in skip:
                self.swa(l, dbg)
            else:
                self.swa_kv_only(l)
            if "fnet" not in skip:
                self.fnet(l)
            if "gla" not in skip:
                self.gla(l, dbg)
            self.merge(l, dbg)
            S.barrier()
        import os
        if "peer" in os.environ.get("KSKIP", ""):
            for tb in range(NB):
                P.I("act", "activation", self.x.v(s_[:, tb, :], tb), self.x.v(s_[:, tb, :], tb), AF.Copy, scale=ALPHA)
        else:
            self.peer(l, dbg)
        self.post_ffn(l)

    def peer(self, l, dbg):
        P = self; S = self.S
        d = self.din
        x, ps, mcol, ident, ones, iota, bm = self.x, self.ps, self.mcol, self.ident, self.ones, self.iota, self.bm
        NCH = 128
        TBS = 256
        with ExitStack() as es:
            u2Ts = [P.sb(es, "u2T%d" % i, [128, 8, TBS], BF16, nsub=2) for i in range(2)]
            lt = self.ln_tmp(es, "P")
            wq = P.sb(es, "wq", [128, 8, 256], BF16)
            kT = P.sb(es, "keysT", [128, 16, 128], BF16)
            gbc = P.sb(es, "g2bc", [128, D], BF16)
            qpT = P.sb(es, "qpT", [128, 16, 128], BF16, nsub=16)
            sc = P.sb(es, "psc", [128, 16, 128], F32, nsub=16)
            vtop = P.sb(es, "vtop", [128, 16, 16], F32, nsub=16)
            itop = P.sb(es, "itop", [128, 16, 16], U32, nsub=16)
            idx1f = P.sb(es, "idx1f", [128, 128], F32)
            dg = idx1f
            idx2f = P.sb(es, "idx2f", [128, 128], F32)
            idxTs = P.sb(es, "idxT", [128, 2, 2, 128], F32, nsub=2)
            cand = P.sb(es, "cand", [128, 8, 256], F32, nsub=8)
            t8a = P.sb(es, "t8a", [128, 8, 8], F32, nsub=8)
            t8b = P.sb(es, "t8b", [128, 8, 8], F32, nsub=8)
            nmx = P.sb(es, "nmx", [128, 8], F32)
            zz = P.sb(es, "pz", [128, 8], F32)
            wCTs = P.sb(es, "wCT", [128, 2, 128, 16], BF16, nsub=2)
            O1s = [P.sb(es, "O1_%d" % i, [128, 4, 128], BF16) for i in range(2)]
            O2s = [P.sb(es, "O2_%d" % i, [128, 4, 128], BF16) for i in range(2)]
            Cbds = [P.sb(es, "Cbd_%d" % i, [128, 4, 128], BF16) for i in range(2)]
            tmpS = [P.sb(es, "ptmp%d" % i, [128, 4, 128], BF16) for i in range(2)]
            WtT = P.sb(es, "WtT", [128, TBS, 128], BF16, nsub=TBS // 4)
            Ut = [P.sb(es, "Ut%d" % i, [128, 8, 256], BF16) for i in range(2)]
            Vt = [P.sb(es, "Vt%d" % i, [128, 2, D], BF16) for i in range(2)]
            actS = [P.sb(es, "pact%d" % i, [128, TBS], BF16) for i in range(2)]
            GS = [P.sb(es, "pG%d" % i, [128, TBS], BF16) for i in range(2)]
            P.dma("pool", kT.v(), d["peer_keysT"][l].rearrange("h q c k -> c (h q) k"))
            for k in range(8):
                P.I("dve", "tensor_scalar_mul", dg.v(), ident.v(), mcol.v(s_[:, l, 40 + k:41 + k], l))
                pt = ps[self.psi % 4]; self.psi += 1
                P.mm(pt.v(s_[:, 0:128]), ones.v(), dg.v(), start=True, stop=True)
                P.I("act", "activation", gbc.v(s_[:, k * 128:(k + 1) * 128]), pt.v(s_[:, 0:128]), AF.Copy)
            py = [ps[4], ps[5], ps[6], ps[7]]
            esc = lambda a: a.rearrange("p (h a) b -> p h (a b)", a=2)
            def sel_a(sb_):
                u2T = u2Ts[sb_ % 2]
                for sub in range(2):
                    tb = sb_ * 2 + sub
                    usl = slice(sub * 128, (sub + 1) * 128)
                    xnv = cand.v(s_[:, 0:4, :], [0, 1, 2, 3]).m(lambda a: a.rearrange("p a b -> p (a b)"))
                    self.ln_block(lt, x.v(s_[:, tb, :], tb), xnv)
                    yield
                    for half in range(2):
                        pt = ps[2 + self.psi % 2]; self.psi += 1
                        for j in range(4):
                            k = half * 4 + j
                            P.I("pe", "transpose", pt.v(s_[:, j * 128:(j + 1) * 128]), xnv.m(lambda a, k=k: a[:, k * 128:(k + 1) * 128]), ident.v())
                        for j in range(4):
                            k = half * 4 + j
                            P.I("dve", "tensor_scalar", u2T.v(s_[:, k, usl], sub), pt.v(s_[:, j * 128:(j + 1) * 128]),
                                mcol.v(s_[:, l, 32 + k:33 + k], l), mcol.v(s_[:, l, 24 + k:25 + k], l), op0=ALU.mult, op1=ALU.add)
                    for c4 in range(4):
                        pt = ps[2 + self.psi % 2]; self.psi += 1
                        for j in range(4):
                            c = c4 * 4 + j
                            if j % 2 == 0:
                                P.dma("pool", wq.v(), d["w_peer_q"][l, c // 2])
                            for k in range(8):
                                P.mm(pt.v(s_[:, j * 128:(j + 1) * 128]), wq.v(s_[:, k, (j % 2) * 128:(j % 2) * 128 + 128]), u2T.v(s_[:, k, usl], sub), start=(k == 0), stop=(k == 7))
                        P.I("act", "activation", qpT.v(s_[:, c4 * 4:(c4 + 1) * 4, :], range(c4 * 4, c4 * 4 + 4)), pt.v().m(lambda a: a.rearrange("p (j t) -> p j t", j=4)), AF.Copy)
                        yield
                    for c4 in range(4):
                        pt = ps[2 + self.psi % 2]; self.psi += 1
                        for j in range(4):
                            c = c4 * 4 + j
                            P.mm(pt.v(s_[:, j * 128:(j + 1) * 128]), qpT.v(s_[:, c, :], c), kT.v(s_[:, c, :]), start=True, stop=True)
                        P.I("act", "activation", sc.v(s_[:, c4 * 4:(c4 + 1) * 4, :], range(c4 * 4, c4 * 4 + 4)), pt.v().m(lambda a: a.rearrange("p (j t) -> p j t", j=4)), AF.Copy)
                        yield
                    wkc = lambda c: cand.v(s_[:, c // 2, (c % 2) * 128:(c % 2) * 128 + 128], c // 2)
                    for c in range(16):
                        P.I("dve", "max", vtop.v(s_[:, c, 0:8], c), sc.v(s_[:, c, :], c))
                    yield
                    for c in range(16):
                        P.I("dve", "max_index", itop.v(s_[:, c, 0:8], c), vtop.v(s_[:, c, 0:8], c), sc.v(s_[:, c, :], c))
                    yield
                    for c in range(16):
                        P.I("dve", "match_replace", wkc(c), vtop.v(s_[:, c, 0:8], c), sc.v(s_[:, c, :], c), -1e30)
                    yield
                    for c in range(16):
                        P.I("dve", "max", vtop.v(s_[:, c, 8:16], c), wkc(c))
                    yield
                    for c in range(16):
                        P.I("dve", "max_index", itop.v(s_[:, c, 8:16], c), vtop.v(s_[:, c, 8:16], c), wkc(c))
                    yield
                    v4 = lambda a: a.rearrange("p (h q) r -> p h q r", q=2)
                    P.I("dve", "tensor_copy", idx1f.v().m(lambda a: a.rearrange("p (h r) -> p h r", h=8)), itop.v().m(lambda a: v4(a)[:, :, 0, :]))
                    P.I("dve", "tensor_copy", idx2f.v().m(lambda a: a.rearrange("p (h r) -> p h r", h=8)), itop.v().m(lambda a: v4(a)[:, :, 1, :]))
                    P.I("dve", "tensor_tensor", cand.v().m(lambda a: a.rearrange("p h (a b) -> p h a b", a=16)),
                        vtop.v().m(lambda a: v4(a)[:, :, 0, :].unsqueeze(3).to_broadcast([128, 8, 16, 16])),
                        vtop.v().m(lambda a: v4(a)[:, :, 1, :].unsqueeze(2).to_broadcast([128, 8, 16, 16])), op=ALU.add)
                    yield
                    wkh = lambda h: sc.v(s_[:, 2 * h:2 * h + 2, :], [2 * h, 2 * h + 1]).m(lambda a: a.rearrange("p a b -> p (a b)"))
                    for h in range(8):
                        P.I("dve", "max", t8a.v(s_[:, h, :], h), cand.v(s_[:, h, :], h))
                    for h in range(8):
                        P.I("dve", "match_replace", wkh(h), t8a.v(s_[:, h, :], h), cand.v(s_[:, h, :], h), -1e30)
                    for h in range(8):
                        P.I("dve", "max", t8b.v(s_[:, h, :], h), wkh(h))
                    yield
                    P.I("dve", "tensor_scalar_mul", nmx.v(), t8a.v(s_[:, :, 0]), -1.0)
                    for h in range(8):
                        ev = sc.v(s_[:, 2 * h:2 * h + 2, :], [2 * h, 2 * h + 1]).m(lambda a: a.rearrange("p a b -> p (a b)"))
                        P.I("act", "activation", ev, cand.v(s_[:, h, :], h), AF.Exp, bias=nmx.v(s_[:, h:h + 1]), scale=1.0)
                        P.I("dve", "scalar_tensor_tensor", ev, cand.v(s_[:, h, :], h), t8b.v(s_[:, h, 7:8], h), ev, op0=ALU.is_ge, op1=ALU.mult)
                    yield
                    P.I("dve", "tensor_reduce", zz.v(), sc.v().m(esc), axis=AX.X, op=ALU.add)
                    P.I("dve", "reciprocal", zz.v(), zz.v())
                    P.I("dve", "tensor_tensor", sc.v().m(esc), sc.v().m(esc), zz.v().m(lambda a: a.unsqueeze(2).to_broadcast([128, 8, 256])), op=ALU.mult)
                    yield
                    pt = ps[2 + self.psi % 2]; self.psi += 1
                    P.I("pe", "transpose", pt.v(s_[:, 0:128]), idx1f.v(), ident.v())
                    P.I("pe", "transpose", pt.v(s_[:, 128:256]), idx2f.v(), ident.v())
                    P.I("act", "activation", idxTs.v(s_[:, sub], sub), pt.v(s_[:, 0:256]).m(lambda a: a.rearrange("p (a t) -> p a t", a=2)), AF.Copy)
                    for r4 in range(4):
                        yield
                        pt = ps[2 + self.psi % 2]; self.psi += 1
                        for j in range(4):
                            r2 = r4 * 4 + j
                            P.I("pe", "transpose", pt.v(s_[:, j * 128:(j + 1) * 128]),
                                sc.v().m(lambda a, r2=r2: a.rearrange("p c (a b) -> p (c a) b", b=16)[:, :, r2]), ident.v())
                        P.I("act", "activation", wCTs.v(s_[:, sub, :, r4 * 4:(r4 + 1) * 4], sub).m(lambda a: a.rearrange("p t j -> p j t")),
                            pt.v().m(lambda a: a.rearrange("p (j t) -> p j t", j=4)), AF.Copy)

                yield
            def expand(sb_):
                items = [(sub, sbk) for sub in range(2) for sbk in range(32)]
                def genO(i):
                    sub, sbk = items[i]
                    t0 = sbk * 4
                    o1, o2, cb = O1s[i % 2], O2s[i % 2], Cbds[i % 2]
                    P.I("dve", "tensor_tensor", o1.v(), iota.v().m(lambda a: a.unsqueeze(1).to_broadcast([128, 4, 128])),
                        idxTs.v(s_[:, sub, 0, t0:t0 + 4], sub).m(lambda a: a.unsqueeze(2).to_broadcast([128, 4, 128])), op=ALU.is_equal)
                    P.I("dve", "tensor_tensor", o2.v(), iota.v().m(lambda a: a.unsqueeze(1).to_broadcast([128, 4, 128])),
                        idxTs.v(s_[:, sub, 1, t0:t0 + 4], sub).m(lambda a: a.unsqueeze(2).to_broadcast([128, 4, 128])), op=ALU.is_equal)
                    P.I("pool", "tensor_tensor", cb.v().m(lambda a: a.rearrange("p t (h r) -> p t h r", h=8)),
                        wCTs.v(s_[:, sub, t0:t0 + 4, :], sub).m(lambda a: a.unsqueeze(2).to_broadcast([128, 4, 8, 16])),
                        bm.v().m(lambda a: a.unsqueeze(1).unsqueeze(3).to_broadcast([128, 4, 8, 16])), op=ALU.mult)
                def mm1(i):
                    o1, cb = O1s[i % 2], Cbds[i % 2]
                    pt = ps[self.psi % 4]; self.psi += 1
                    for j in range(4):
                        P.mm(pt.v(s_[:, j * 128:(j + 1) * 128]), cb.v(s_[:, j, :]), o1.v(s_[:, j, :]), start=True, stop=True)
                    P.I("act", "activation", tmpS[i % 2].v(), pt.v().m(lambda a: a.rearrange("p (j i) -> p j i", j=4)), AF.Copy)
                def mm2(i):
                    sub, sbk = items[i]
                    o2 = O2s[i % 2]
                    tS = tmpS[i % 2]
                    pt2 = ps[self.psi % 4]; self.psi += 1
                    for j in range(4):
                        P.mm(pt2.v(s_[:, j * 128:(j + 1) * 128]), o2.v(s_[:, j, :]), tS.v(s_[:, j, :]), start=True, stop=True)
                    ta = sub * 128 + sbk * 4
                    P.I("dve", "tensor_copy", WtT.v(s_[:, ta:ta + 4, :], ta // 4), pt2.v().m(lambda a: a.rearrange("p (j i) -> p j i", j=4)))
                genO(0); mm1(0)
                for i in range(len(items)):
                    if i + 1 < len(items):
                        genO(i + 1); mm1(i + 1)
                    mm2(i)

            def expert(sb_, gen):
                u2T = u2Ts[sb_ % 2]
                def emitU(c):
                    c2, j = c // 2, c % 2
                    if j == 0:
                        ut = Ut[c2 % 2]; vt = Vt[c2 % 2]
                        P.dma("sp", ut.v().m(lambda a: a.rearrange("p k e -> p (k e)")), self.ubf.v(s_[l, c2], l * 16 + c2 // 4))
                        P.dma("sp", vt.v().m(lambda a: a.rearrange("p j d -> p (j d)")), self.vbf.v(s_[l, c2], l * 16 + c2 // 4))
                    ut = Ut[c2 % 2]
                    pa = ps[c % 2]
                    for k in range(8):
                        P.mm(pa.v(s_[:, 0:TBS]), ut.v(s_[:, k, j * 128:(j + 1) * 128]), u2T.v(s_[:, k, :]), start=(k == 0), stop=(k == 7))
                def emitMV(c):
                    c2, j = c // 2, c % 2
                    vt = Vt[c2 % 2]
                    pa = ps[c % 2]
                    aS = actS[c % 2]; gS = GS[c % 2]
                    P.I("act", "activation", aS.v(), pa.v(s_[:, 0:TBS]), AF.Gelu)
                    P.I("dve", "tensor_tensor", gS.v(), aS.v(), WtT.v(s_[:, :, c]), op=ALU.mult)
                    for sub in range(2):
                        for hf in range(2):
                            P.mm(py[sub * 2 + hf].v(), gS.v(s_[:, sub * 128:(sub + 1) * 128]), vt.v(s_[:, j, hf * 512:(hf + 1) * 512]), start=(c == 0), stop=(c == NCH - 1))
                emitU(0)
                for c in range(NCH):
                    if c + 1 < NCH:
                        emitU(c + 1)
                    emitMV(c)
                    if gen is not None and c % 2 == 1:
                        next(gen, None)
                if gen is not None:
                    for _ in gen:
                        pass

            def finalize(sb_):
                for sub in range(2):
                    tb = sb_ * 2 + sub
                    for hf in range(2):
                        hs = slice(hf * 512, (hf + 1) * 512)
                        yv = cand.v(s_[:, 0:2, :], [0, 1]).m(lambda a: a.rearrange("p a b -> p (a b)"))
                        P.I("dve", "tensor_tensor", yv, py[sub * 2 + hf].v(), gbc.v(s_[:, hs]), op=ALU.mult)
                        P.I("dve", "scalar_tensor_tensor", x.v(s_[:, tb, hs], tb), x.v(s_[:, tb, hs], tb), ALPHA, yv, op0=ALU.mult, op1=ALU.add)

            NSB = T // TBS
            for _ in sel_a(0):
                pass
            expand(0)
            for sb_ in range(NSB):
                gen = sel_a(sb_ + 1) if sb_ + 1 < NSB else None
                expert(sb_, gen)
                finalize(sb_)
                if sb_ + 1 < NSB:
                    expand(sb_ + 1)
            S.barrier()

def _bf(a):
    return np.ascontiguousarray(a).astype(ml_dtypes.bfloat16)


def host_consts(kind):
    c = {}
    c["ident"] = np.eye(128, dtype=np.float32)
    c["bm"] = np.ascontiguousarray((np.arange(128)[:, None] // 16 == np.arange(8)[None, :]).astype(np.float32))
    seqlen = T if kind == "sample" else 256
    n = np.arange(seqlen)
    ang = 2.0 * np.pi * np.outer(n, n) / seqlen
    sc = 1.0 / np.sqrt(seqlen * 64.0)
    cb = np.cos(ang) * sc
    sbm = -np.sin(ang) * sc
    Cf = np.zeros((T, T), np.float64)
    Sf = np.zeros((T, T), np.float64)
    for i in range(T // seqlen):
        sl = slice(i * seqlen, (i + 1) * seqlen)
        Cf[sl, sl] = cb
        Sf[sl, sl] = sbm
    c["dft_c"] = _bf(Cf.astype(np.float32))
    c["dft_s"] = _bf(Sf.astype(np.float32))
    m = np.arange(64)
    a2 = 2.0 * np.pi * np.outer(m, m) / 64.0
    cc = np.zeros((2, 128, 128), np.float64)
    for g in range(2):
        cc[0, g * 64:(g + 1) * 64, g * 64:(g + 1) * 64] = np.cos(a2)
        cc[1, g * 64:(g + 1) * 64, g * 64:(g + 1) * 64] = np.sin(a2)
    c["cdft"] = _bf(cc.astype(np.float32))
    t = np.arange(T)
    rows = (t // 64).astype(np.float64); cols = (t % 64).astype(np.float64)
    for nm, R in (("rope_m", 32), ("rope_s", 64)):
        half = R // 2; q = R // 4
        tab = np.zeros((2, R, T), np.float64)
        for dd in range(R):
            pos = rows if dd < half else cols
            fi = dd % q
            freq = 10000.0 ** (-(2.0 * fi) / half)
            ang = pos * freq
            if kind == "sample":
                tab[0, dd] = np.cos(ang)
                tab[1, dd] = np.sin(ang) * (-1.0 if (dd // q) % 2 == 0 else 1.0)
            else:
                tab[0, dd] = 1.0
        c[nm] = np.ascontiguousarray(tab.astype(np.float32))
    bm_ = np.zeros((160,), np.float32)
    if kind == "prompt":
        for qu in range(8):
            for kc in range(20):
                ok = kc >= 4 and (kc - 4) // 2 == qu
                bm_[qu * 20 + kc] = 0.0 if ok else NEG
    c["bias_m"] = np.ascontiguousarray(np.broadcast_to(bm_[None, :], (128, 160)))
    bs_ = np.zeros((128, 4), np.float32)
    if kind == "prompt":
        bs_[:, 0] = NEG
    c["bias_s"] = bs_
    mk = np.zeros((128, NB, 2, 128), np.float32)
    kk = np.arange(128)[:, None]; qq = np.arange(128)[None, :]
    for tb in range(NB):
        if kind == "sample":
            mk[:, tb, 0, :] = (kk >= qq)
            mk[:, tb, 1, :] = (kk <= qq)
        else:
            mk[:, tb, 0, :] = 1.0 if tb % 2 == 1 else 0.0
            mk[:, tb, 1, :] = 1.0 if tb % 2 == 0 else 0.0
    c["mask_s"] = _bf(mk)
    tt = np.arange(128)[:, None]; tp = np.arange(128)[None, :]
    same = (tt // 64) == (tp // 64)
    cc_ = -1.0 / 16.0
    gm = np.zeros((5, 128, 128), np.float32)
    gm[0] = cc_ * (same & (tt <= tp))
    gm[1] = cc_ * (same & (tt > tp))
    gm[2] = cc_ * (same & (tt >= tp))
    gm[3] = cc_ * (same & (tt < tp))
    gm[4, :, 0] = cc_ * (np.arange(128) < 64)
    gm[4, :, 1] = cc_ * (np.arange(128) >= 64)
    c["gla_mats"] = gm
    c["gla_mask"] = _bf(np.stack([(same & (tt <= tp)), (same & (tt >= tp))]).astype(np.float32))
    kp = np.ones((128, 2, 32), np.float32)
    if kind == "prompt":
        for n_ in range(32):
            if n_ % 4 == 0:
                kp[:, 0, n_] = 0.0
            if n_ % 4 == 3:
                kp[:, 1, n_] = 0.0
    c["gla_keep"] = kp
    return c


def perm_swap(R):
    q = R // 4
    return np.array([d + q if (d // q) % 2 == 0 else d - q for d in range(R)])


def host_weights(inp):
    w = {}
    w["w_ada"] = np.ascontiguousarray(inp["w_ada"], dtype=np.float32)
    w["b_adaT"] = np.ascontiguousarray(inp["b_ada"].reshape(L, 48, 128).transpose(0, 2, 1), dtype=np.float32)
    w_in = np.asarray(inp["w_in"], dtype=np.float32)
    p32 = perm_swap(32); p64 = perm_swap(64)
    kr = w_in[:, :, 384:416][:, :, p32]
    sq = w_in[:, :, 1472:1728].reshape(L, D, 4, 64)[:, :, :, p64].reshape(L, D, 256)
    sk = w_in[:, :, 1728:1856].reshape(L, D, 2, 64)[:, :, :, p64].reshape(L, D, 128)
    w["w_in"] = np.ascontiguousarray(np.concatenate([w_in, kr, sq, sk], axis=2))
    w["w_gate"] = np.ascontiguousarray(w_in[:, :, 1984:6080].reshape(L, 8, 128, 4, 8, 128).transpose(0, 4, 2, 1, 3, 5).reshape(L, 8, 128, 8, 512))
    w["w_branch"] = np.ascontiguousarray(inp["w_branch"], dtype=np.float32)
    w["w_out"] = np.ascontiguousarray(inp["w_out"], dtype=np.float32)
    w_uq = np.asarray(inp["w_uq"], dtype=np.float32)
    uq_sw = w_uq.reshape(L, 256, 4, 96)[:, :, :, 64:96][:, :, :, p32].reshape(L, 256, 128)
    w["w_uq"] = np.ascontiguousarray(np.concatenate([w_uq, uq_sw], axis=2))
    w["w_ukv"] = np.ascontiguousarray(inp["w_ukv"], dtype=np.float32)
    w["mla_q_norm"] = np.ascontiguousarray(np.asarray(inp["mla_q_norm"], dtype=np.float32).reshape(L, 2, 128).transpose(0, 2, 1))
    w["mla_kv_norm"] = np.ascontiguousarray(inp["mla_kv_norm"], dtype=np.float32)
    w["w_gla_a"] = np.ascontiguousarray(np.stack([inp["w_gla_a_fwd"], inp["w_gla_a_bwd"]], axis=1), dtype=np.float32)
    w["b_gla_a"] = np.ascontiguousarray(np.stack([inp["b_gla_a_fwd"], inp["b_gla_a_bwd"]], axis=1), dtype=np.float32)
    w["gla_norm"] = np.ascontiguousarray(inp["gla_norm"], dtype=np.float32)
    w["swa_sink"] = np.ascontiguousarray(np.broadcast_to(np.asarray(inp["swa_sink"], dtype=np.float32)[:, None, :], (L, 128, 4)))
    w["w_peer_q"] = np.ascontiguousarray(np.asarray(inp["w_peer_q"], dtype=np.float32).reshape(L, 8, 128, 8, 256).transpose(0, 3, 2, 1, 4))
    w["peer_keysT"] = np.ascontiguousarray(np.asarray(inp["peer_keys"], dtype=np.float32).transpose(0, 1, 2, 4, 3))
    w["peer_uT"] = np.ascontiguousarray(np.asarray(inp["peer_u"], dtype=np.float32).reshape(L, 64, 256, 8, 128).transpose(0, 1, 4, 3, 2))
    w["peer_v"] = np.ascontiguousarray(np.asarray(inp["peer_v"], dtype=np.float32).reshape(L, 64, 2, 128, D).transpose(0, 1, 3, 2, 4))
    w["ln"] = np.ascontiguousarray(np.stack([inp["ln1_g"], inp["ln1_b"], inp["ln2_g"], inp["ln2_b"]], axis=1), dtype=np.float32)
    return w


def core_inputs(inp, core, W, CS, CP):
    m = dict(W)
    if core < 2:
        m.update(CS)
        m["x"] = np.ascontiguousarray(inp["x_sample"][core], dtype=np.float32)
        cond = np.asarray(inp["c"][core], dtype=np.float32)
        m["cache_ckv"] = np.ascontiguousarray(inp["cache_mla_ckv"][core], dtype=np.float32)
        m["cache_krope"] = np.ascontiguousarray(inp["cache_mla_krope"][core], dtype=np.float32)
        m["cache_swak"] = np.ascontiguousarray(inp["cache_swa_k"][core], dtype=np.float32)
        m["cache_swav"] = np.ascontiguousarray(inp["cache_swa_v"][core], dtype=np.float32)
        m["gla_init"] = np.ascontiguousarray(np.asarray(inp["state_gla"][core], dtype=np.float32).reshape(L, 2, 128, 64))
    else:
        m.update(CP)
        j = core - 2 if core < 6 else 0
        m["x"] = np.ascontiguousarray(np.asarray(inp["x_prompt"][8 * j:8 * j + 8], dtype=np.float32).reshape(T, D))
        cond = np.asarray(inp["c_ctx"], dtype=np.float32)
        m["cache_ckv"] = np.zeros((L, 512, 128), np.float32)
        m["cache_krope"] = np.zeros((L, 512, 32), np.float32)
        m["cache_swak"] = np.zeros((L, 2, 512, 64), np.float32)
        m["cache_swav"] = np.zeros((L, 2, 512, 64), np.float32)
        m["gla_init"] = np.zeros((L, 2, 128, 64), np.float32)
    m["condT"] = np.ascontiguousarray(cond.reshape(8, 128).T)
    return m


_CACHE = {}


def kernel(**inputs):
    cores = inputs.pop("_cores", list(range(8)))
    debug = inputs.pop("_debug", None)
    stop_after = inputs.pop("_stop_after", None)
    prog = Prog(debug=debug, stop_after=stop_after)
    nc = prog.build()
    W = host_weights(inputs)
    CS = host_consts("sample")
    CP = host_consts("prompt")
    in_maps = [core_inputs(inputs, c, W, CS, CP) for c in cores]
    import os as _os
    if _os.environ.get("KTRACE"):
        res = run_bass_kernel_spmd(nc, in_maps, core_ids=list(range(len(cores))), trace=True)
        print("EXEC_TIME_NS", res.exec_time_ns)
        globals()["_LAST_RES"] = res
    else:
        res = run_bass_kernel_spmd(nc, in_maps, core_ids=list(range(len(cores))))
    R = res.results
    if debug is not None:
        return R
    y_sample = np.stack([R[0]["y"], R[1]["y"]], axis=0)
    y_prompt = np.concatenate([R[2 + j]["y"].reshape(8, 256, D) for j in range(4)], axis=0)
    ckv = np.concatenate([R[2 + j]["ckv_o"].reshape(L, 8, 256, 128).transpose(1, 0, 2, 3) for j in range(4)], axis=0)
    kr = np.concatenate([R[2 + j]["krope_o"].reshape(L, 8, 256, 32).transpose(1, 0, 2, 3) for j in range(4)], axis=0)
    kvs = [R[2 + j]["swakv_o"].reshape(L, 8, 256, 2, 2, 64) for j in range(4)]
    sk = np.concatenate([a[:, :, :, 0].transpose(1, 0, 3, 2, 4) for a in kvs], axis=0)
    sv = np.concatenate([a[:, :, :, 1].transpose(1, 0, 3, 2, 4) for a in kvs], axis=0)
    gl = np.concatenate([R[2 + j]["gla_o"].reshape(L, 8, 2, 4, 32, 64).transpose(1, 0, 2, 3, 4, 5) for j in range(4)], axis=0)
    f = lambda a: np.ascontiguousarray(a, dtype=np.float32)
    return (f(y_prompt), f(y_sample), f(ckv), f(kr), f(sk), f(sv), f(gl))
```

```python
import numpy as np
import ml_dtypes
from contextlib import ExitStack
import concourse.bass as bass
import concourse.mybir as mybir
from concourse.bass_utils import run_bass_kernel_spmd

F32 = mybir.dt.float32
BF16 = mybir.dt.bfloat16
U32 = mybir.dt.uint32
AF = mybir.ActivationFunctionType
ALU = mybir.AluOpType
AX = mybir.AxisListType
s_ = np.s_

ENGS = ("pe", "act", "dve", "pool", "sp")
SAME_ENGINE_SYNC = True
import os as _os0
SES_ALL = not bool(_os0.environ.get("KNOSES"))

T = 2048
NB = 16
D = 1024
L = 2
ALPHA = (2.0 * L) ** 0.25
EPS = 1e-6
NEG = -30000.0
MLA_SCALE = 96.0 ** -0.5
SWA_SCALE = 64.0 ** -0.5
WIN_EXT = 6496


class V:
    def __init__(self, ap, toks):
        self.ap = ap
        self.toks = toks

    def m(self, fn):
        return V(fn(self.ap), self.toks)


class Buf:
    def __init__(self, name, t, nsub=1):
        self.name = name
        self.t = t
        self.nsub = nsub

    def tok(self, subs=None):
        if subs is None:
            return [(self.name, s) for s in range(self.nsub)]
        if isinstance(subs, int):
            subs = [subs]
        return [(self.name, s) for s in subs]

    def v(self, key=None, subs=None):
        ap = self.t[:] if key is None else self.t[key]
        return V(ap, self.tok(subs))


class Sched:
    def __init__(self, nc, es, nd=24):
        self.nc = nc
        self.ops = {e: [] for e in ENGS}
        self.cnt = {e: 0 for e in ENGS}
        self.known = {e: {} for e in ENGS}
        self.nd = nd
        self.dma_tot = [0] * nd
        self.dma_rr = 0
        self.last_w = {}
        self.readers = {}
        self.sem = {e: es.enter_context(nc.semaphore("sem_" + e)) for e in ENGS if e != "sp"}
        self.dsem = [es.enter_context(nc.semaphore("dsem%d" % i)) for i in range(nd)]
        self.milestones = {e: set() for e in ENGS}

    def _need(self, eng, dep, waits):
        kind, key, val = dep
        if kind == "eng" and key == eng:
            if eng in ("pe", "sp") or (eng in ("act", "dve") and not SES_ALL) or not SAME_ENGINE_SYNC:
                return
        k = (kind, key)
        if self.known[eng].get(k, 0) >= val:
            return
        self.known[eng][k] = val
        waits.append((kind, key, val))
        if kind == "eng":
            self.milestones[key].add(val)

    def _deps(self, eng, reads, writes):
        waits = []
        for t in reads:
            lw = self.last_w.get(t)
            if lw is not None:
                self._need(eng, lw, waits)
        for t in writes:
            lw = self.last_w.get(t)
            if lw is not None:
                self._need(eng, lw, waits)
            for r in self.readers.get(t, ()):
                self._need(eng, r, waits)
        return waits

    def _commit(self, me, reads, writes):
        for t in reads:
            self.readers.setdefault(t, []).append(me)
        for t in writes:
            self.last_w[t] = me
            self.readers[t] = []

    def op(self, eng, fn, reads=(), writes=()):
        reads = list(reads); writes = list(writes)
        waits = self._deps(eng, reads, writes)
        self.cnt[eng] += 1
        me = ("eng", eng, self.cnt[eng])
        self.ops[eng].append((waits, fn, ("eng", self.cnt[eng])))
        self._commit(me, reads, writes)

    def dma(self, eng, fn, reads=(), writes=()):
        reads = list(reads); writes = list(writes)
        i = self.dma_rr
        self.dma_rr = (i + 1) % self.nd
        waits = []
        if self.dma_tot[i] > 0:
            self._need(eng, ("dma", i, self.dma_tot[i]), waits)
        waits += self._deps(eng, reads, writes)
        self.dma_tot[i] += 16
        me = ("dma", i, self.dma_tot[i])
        self.cnt[eng] += 1
        self.ops[eng].append((waits, fn, ("dma", i)))
        self._commit(me, reads, writes)

    def _last_seq(self, e):
        for w, fn, inc in reversed(self.ops[e]):
            if inc is not None and inc[0] == "eng":
                return inc[1]
        return 0

    def barrier(self):
        lasts = {e: self._last_seq(e) for e in ENGS}
        for e in ENGS:
            waits = []
            for e2 in ENGS:
                if e2 != e and e2 != "sp" and lasts[e2] > 0:
                    self._need(e, ("eng", e2, lasts[e2]), waits)
            for i in range(self.nd):
                if self.dma_tot[i] > 0:
                    self._need(e, ("dma", i, self.dma_tot[i]), waits)
            if waits:
                self.ops[e].append((waits, None, None))
        self.last_w = {}
        self.readers = {}

    def finish(self):
        self.barrier()

    def emit(self, blk):
        rank = {}
        for e in ENGS:
            ms = sorted(self.milestones[e])
            rank[e] = {s: i + 1 for i, s in enumerate(ms)}

        def run(e, eng):
            for waits, fn, inc in self.ops[e]:
                for kind, key, val in waits:
                    if kind == "eng":
                        eng.wait_ge(self.sem[key], rank[key][val])
                    else:
                        eng.wait_ge(self.dsem[key], val)
                if fn is None:
                    continue
                ins = fn(eng)
                if inc[0] == "dma":
                    ins.then_inc(self.dsem[inc[1]], 16)
                elif inc[1] in rank[e]:
                    ins.then_inc(self.sem[e], 1)

        blk.sync(lambda eng: run("sp", eng))
        blk.scalar(lambda eng: run("act", eng))
        blk.vector(lambda eng: run("dve", eng))
        blk.gpsimd(lambda eng: run("pool", eng))
        blk.tensor(lambda eng: run("pe", eng))


class Prog:
    def __init__(self, debug=None, stop_after=None):
        self.debug = debug or []
        self.stop_after = stop_after
        self.nc = bass.Bass("TRN2", target_bir_lowering=False)
        self.din = {}
        self.dout = {}

    def inp(self, name, shape, dt=F32):
        self.din[name] = self.nc.dram_tensor(name, list(shape), dt, kind="ExternalInput").ap()
        return self.din[name]

    def outp(self, name, shape, dt=F32):
        self.dout[name] = self.nc.dram_tensor(name, list(shape), dt, kind="ExternalOutput").ap()
        return self.dout[name]

    def sb(self, es, name, shape, dt=F32, nsub=1):
        self.uid = getattr(self, "uid", 0) + 1
        name = "%s_u%d" % (name, self.uid)
        return Buf(name, es.enter_context(self.nc.sbuf_tensor(name, list(shape), dt)), nsub)

    def I(self, eng, meth, out, *args, **kw):
        def conv(a):
            return a.ap if isinstance(a, V) else a
        reads = []
        writes = list(out.toks)
        for a in list(args) + list(kw.values()):
            if isinstance(a, V):
                reads += a.toks
        if "accum_out" in kw:
            writes += kw["accum_out"].toks
        a2 = [conv(a) for a in args]
        k2 = {k: conv(v) for k, v in kw.items()}
        o = out.ap
        self.S.op(eng, lambda e: getattr(e, meth)(o, *a2, **k2), reads, writes)

    def dma(self, q, out, in_):
        reads = in_.toks if isinstance(in_, V) else []
        writes = out.toks if isinstance(out, V) else []
        o = out.ap if isinstance(out, V) else out
        i = in_.ap if isinstance(in_, V) else in_
        self.S.dma(q, lambda e: e.dma_start(out=o, in_=i), reads, writes)

    def mm(self, out, lhsT, rhs, start, stop):
        self.I("pe", "matmul", out, lhsT=lhsT, rhs=rhs, start=start, stop=stop)

    def build(self):
        nc = self.nc
        P = self
        inp = self.inp
        x_d = inp("x", [T, D])
        condT_d = inp("condT", [128, 8])
        w_ada_d = inp("w_ada", [L, D, 6 * D])
        b_adaT_d = inp("b_adaT", [L, 128, 48])
        w_in_d = inp("w_in", [L, D, WIN_EXT])
        w_branch_d = inp("w_branch", [L, 4, 256, D])
        w_out_d = inp("w_out", [L, D, D])
        inp("w_gate", [L, 8, 128, 8, 512])
        ln_d = inp("ln", [L, 4, D])
        dft_c_d = inp("dft_c", [T, T], BF16)
        dft_s_d = inp("dft_s", [T, T], BF16)
        cdft_d = inp("cdft", [2, 128, 128], BF16)
        ident_d = inp("ident", [128, 128])
        inp("w_peer_q", [L, 8, 128, 8, 256])
        inp("w_uq", [L, 256, 512]); inp("w_ukv", [L, 128, 512]); inp("mla_q_norm", [L, 128, 2]); inp("mla_kv_norm", [L, 128])
        inp("gla_mats", [5, 128, 128]); inp("gla_mask", [2, 128, 128], BF16); inp("gla_keep", [128, 2, 32]); inp("gla_init", [L, 2, 128, 64])
        inp("w_gla_a", [L, 2, 16, 128]); inp("b_gla_a", [L, 2, 128]); inp("gla_norm", [L, 64])
        inp("rope_m", [2, 32, T]); inp("rope_s", [2, 64, T]); inp("bias_m", [128, 160]); inp("bias_s", [128, 4])
        inp("mask_s", [128, NB, 2, 128], BF16); inp("swa_sink", [L, 128, 4])
        inp("cache_ckv", [L, 512, 128]); inp("cache_krope", [L, 512, 32]); inp("cache_swak", [L, 2, 512, 64]); inp("cache_swav", [L, 2, 512, 64])
        self.outp("ckv_o", [L, T, 128]); self.outp("krope_o", [L, T, 32]); self.outp("swakv_o", [L, T, 256]); self.outp("gla_o", [L, 8, 2, 128, 64])
        inp("peer_keysT", [L, 8, 2, 128, 128])
        inp("peer_uT", [L, 64, 128, 8, 256])
        inp("peer_v", [L, 64, 128, 2, D])
        y_d = self.outp("y", [T, D])
        self.ubf = Buf("ubf", nc.dram_tensor("peer_u_bf", [L, 64, 128, 8 * 256], BF16, kind="Internal").ap(), nsub=L * 16)
        self.vbf = Buf("vbf", nc.dram_tensor("peer_v_bf", [L, 64, 128, 2 * D], BF16, kind="Internal").ap(), nsub=L * 16)
        dbg = {}
        for name, shape in self.debug:
            dbg[name] = self.outp(name, shape, F32)

        with ExitStack() as es:
            self.S = S = Sched(nc, es)
            sb = lambda *a, **k: P.sb(es, *a, **k)
            x = sb("xres", [128, NB, D], F32, nsub=NB)
            ident = sb("identS", [128, 128], F32)
            ones = sb("onesS", [128, 128], F32)
            mcol = sb("mcol", [128, L, 48], F32, nsub=L)
            condT = sb("condTS", [128, 8], F32)
            ps = [Buf("ps%d" % i, es.enter_context(nc.psum_tensor("ps%d" % i, [128, 512], F32))) for i in range(8)]
            self.x, self.ps, self.ident, self.ones, self.mcol = x, ps, ident, ones, mcol
            iota = sb("iotaS", [128, 128], F32)
            P.I("pool", "iota", iota.v(), pattern=[[1, 128]], base=0, channel_multiplier=0, allow_small_or_imprecise_dtypes=True)
            bm = sb("bmS", [128, 8], F32)
            P.dma("sp", bm.v(), inp("bm", [128, 8])[:, :])
            self.iota, self.bm = iota, bm

            for tb in range(NB):
                P.dma("sp", x.v(s_[:, tb, :], tb), x_d[tb * 128:(tb + 1) * 128, :])
            P.dma("sp", ident.v(), ident_d[:, :])
            P.I("pool", "memset", ones.v(), 1.0)
            P.dma("sp", condT.v(), condT_d[:, :])

            with ExitStack() as es0:
                scond = P.sb(es0, "scond", [128, 8], F32)
                wa = [P.sb(es0, "wa%d" % i, [128, 8, 768], F32) for i in range(2)]
                badaT = P.sb(es0, "badaT", [128, L, 48], F32)
                P.I("act", "activation", scond.v(), condT.v(), AF.Silu)
                P.dma("sp", badaT.v(), b_adaT_d.rearrange("l p j -> p l j"))
                n = 0
                for l in range(L):
                    for cg in range(8):
                        pt = ps[cg % 2]
                        wt = wa[n % 2]; n += 1
                        P.dma("sp", wt.v(), w_ada_d[l, :, cg * 768:(cg + 1) * 768].rearrange("(k p) c -> p k c", p=128))
                        for j in range(6):
                            for k in range(8):
                                P.mm(pt.v(s_[:, j:j + 1]), wt.v(s_[:, k, j * 128:(j + 1) * 128]), scond.v(s_[:, k:k + 1]),
                                     start=(k == 0), stop=(k == 7))
                        P.I("dve", "tensor_tensor", mcol.v(s_[:, l, cg * 6:(cg + 1) * 6], l), pt.v(s_[:, 0:6]),
                            badaT.v(s_[:, l, cg * 6:(cg + 1) * 6]), op=ALU.add)
                    for a in (8, 32):
                        P.I("dve", "tensor_scalar_add", mcol.v(s_[:, l, a:a + 8], l), mcol.v(s_[:, l, a:a + 8], l), 1.0)
                S.barrier()
            if "mcol" in dbg:
                P.dma("pool", dbg["mcol"], mcol.v())

            for l in range(L):
                self.layer(l, dbg)
                if self.stop_after == ("layer", l):
                    break

            for tb in range(NB):
                P.dma("sp", y_d[tb * 128:(tb + 1) * 128, :], x.v(s_[:, tb, :], tb))
            S.finish()
            blk = es.enter_context(nc.Block())
            S.emit(blk)
        return nc

    def ln_block(self, tmp, src, dst):
        P = self
        st, mv, rstd, nmr = tmp
        P.I("dve", "bn_stats", st.v(s_[:, 0, :]), src.m(lambda a: a[:, 0:512]))
        P.I("dve", "bn_stats", st.v(s_[:, 1, :]), src.m(lambda a: a[:, 512:1024]))
        P.I("dve", "bn_aggr", mv.v(), st.v())
        P.I("act", "activation", rstd.v(), mv.v(s_[:, 1:2]), AF.Sqrt, bias=EPS, scale=1.0)
        P.I("dve", "reciprocal", rstd.v(), rstd.v())
        P.I("dve", "scalar_tensor_tensor", nmr.v(), mv.v(s_[:, 0:1]), -1.0, rstd.v(), op0=ALU.mult, op1=ALU.mult)
        P.I("act", "activation", dst, src, AF.Identity, bias=nmr.v(), scale=rstd.v())

    def ln_tmp(self, es, tag):
        return (self.sb(es, "st" + tag, [128, 2, 6]), self.sb(es, "mv" + tag, [128, 2]),
                self.sb(es, "rstd" + tag, [128, 1]), self.sb(es, "nmr" + tag, [128, 1]))

    def mod_to_uT(self, l, which):
        P = self; S = self.S
        x, uT, ps, mcol, ident = self.x, self.uT, self.ps, self.mcol, self.ident
        sh0 = 0 if which == 0 else 24
        sc0 = 8 if which == 0 else 32
        with ExitStack() as es:
            tmps = [self.ln_tmp(es, "m%d" % i) for i in range(2)]
            xn = [P.sb(es, "xn%d" % i, [128, D]) for i in range(2)]
            for tb in range(NB):
                xnb = xn[tb % 2]
                self.ln_block(tmps[tb % 2], x.v(s_[:, tb, :], tb), xnb.v())
                for half in range(2):
                    pt = ps[(tb * 2 + half) % 4]
                    for j in range(4):
                        k = half * 4 + j
                        P.I("pe", "transpose", pt.v(s_[:, j * 128:(j + 1) * 128]), xnb.v(s_[:, k * 128:(k + 1) * 128]), ident.v())
                    for j in range(4):
                        k = half * 4 + j
                        eng = "dve" if j % 2 == 0 else "pool"
                        eng = "dve"
                        P.I(eng, "tensor_scalar", uT.v(s_[:, k, tb * 128:(tb + 1) * 128], tb), pt.v(s_[:, j * 128:(j + 1) * 128]),
                            mcol.v(s_[:, l, sc0 + k:sc0 + k + 1], l), mcol.v(s_[:, l, sh0 + k:sh0 + k + 1], l), op0=ALU.mult, op1=ALU.add)
            S.barrier()

    def load_w(self, wb, l, c0, n):
        w_in_d = self.din["w_in"]
        self.dma("pool", wb.v(s_[:, :, 0:n]), w_in_d[l, :, c0:c0 + n].rearrange("(k p) c -> p k c", p=128))

    def proj_fm(self, wb, wc0, m, dst_fn, pbase=0):
        P = self
        for tg in range(4):
            pt = self.ps[self.psi % 8]; self.psi += 1
            for k in range(8):
                P.mm(pt.v(s_[0:m, :]), wb.v(s_[:, k, wc0:wc0 + m]), self.uT.v(s_[:, k, tg * 512:(tg + 1) * 512], range(tg * 4, tg * 4 + 4)),
                     start=(k == 0), stop=(k == 7))
            dst_fn(tg, pt.v(s_[0:m, :]))

    def proj_tm(self, wb, wc0, n, dst_fn):
        P = self
        for tb in range(NB):
            pt = self.ps[self.psi % 8]; self.psi += 1
            for k in range(8):
                P.mm(pt.v(s_[:, 0:n]), self.uT.v(s_[:, k, tb * 128:(tb + 1) * 128], tb), wb.v(s_[:, k, wc0:wc0 + n]),
                     start=(k == 0), stop=(k == 7))
            dst_fn(tb, pt.v(s_[:, 0:n]))


    def rope_evac(self, dst, pa, pb, cosv, sinv, tmpa, tmpb):
        P = self
        P.I("dve", "tensor_tensor", tmpa, pa, cosv, op=ALU.mult)
        P.I("dve", "tensor_tensor", tmpb, pb, sinv, op=ALU.mult)
        P.I("pool", "tensor_tensor", dst, tmpa, tmpb, op=ALU.add)

    def fm_group(self, wb, specs, tg):
        P = self
        outs = []
        for (wc0, m) in specs:
            pt = self.ps[self.psi % 6]; self.psi += 1
            for k in range(8):
                P.mm(pt.v(s_[0:m, :]), wb.v(s_[:, k, wc0:wc0 + m]), self.uT.v(s_[:, k, tg * 512:(tg + 1) * 512], range(tg * 4, tg * 4 + 4)),
                     start=(k == 0), stop=(k == 7))
            outs.append(pt.v(s_[0:m, :]))
        return outs

    def mla(self, l, dbg):
        P = self; S = self.S
        d = self.din
        ps, ident, ones = self.ps, self.ident, self.ones
        NK = 20
        with ExitStack() as es:
            wb = P.sb(es, "wbM", [128, 8, 448], BF16)
            wuq = P.sb(es, "wuq", [128, 2, 512], BF16)
            wukv = P.sb(es, "wukv", [128, 2, 256], BF16)
            gq = P.sb(es, "gq", [128, 2], F32)
            gkv = P.sb(es, "gkv", [128, 128], F32)
            cosT = P.sb(es, "cosM", [32, 512], F32)
            sinT = P.sb(es, "sinM", [32, 512], F32)
            biasM = P.sb(es, "biasM", [128, 8 * NK], F32)
            qnT = P.sb(es, "qnT", [128, 2, T], BF16, nsub=4)
            rs = P.sb(es, "qrs", [128, 512], F32)
            qno = P.sb(es, "qno", [64, T], BF16, nsub=4)
            qro = P.sb(es, "qro", [32, T], BF16, nsub=4)
            kno = P.sb(es, "kno", [64, 2560], BF16)
            kro = P.sb(es, "kro", [32, 2560], BF16)
            ckvT = P.sb(es, "ckvT", [128, 2560], BF16, nsub=NK)
            Vt = P.sb(es, "VtM", [128, NK, 4, 65], BF16, nsub=NK)
            ta = P.sb(es, "rta", [128, 512], F32)
            tb_ = P.sb(es, "rtb", [128, 512], F32)
            kvt = P.sb(es, "kvt", [128, 160], F32)
            ckt = [P.sb(es, "ckt%d" % i, [128, 128], F32) for i in range(2)]
            ss = P.sb(es, "kss", [128, 1], F32)
            junk = P.sb(es, "kjunk", [128, 128], F32)
            PT = [P.sb(es, "PT%d" % i, [128, 256], BF16) for i in range(2)]
            oacc = P.sb(es, "oacc", [128, NB, 128], BF16, nsub=NB)
            identb = P.sb(es, "identb", [128, 128], BF16)
            P.I("dve", "tensor_copy", identb.v(), ident.v())
            rden = P.sb(es, "rden", [128, 1], F32)
            cch = P.sb(es, "cch", [128, 4, 160], F32)
            self.load_w(wb, l, 0, 416)
            P.dma("pool", wb.v(s_[:, :, 416:448]), d["w_in"][l, :, 6080:6112].rearrange("(k p) c -> p k c", p=128))
            P.dma("pool", wuq.v(), d["w_uq"][l].rearrange("(k p) c -> p k c", p=128))
            for two in range(2):
                P.dma("pool", wukv.v(s_[:, two, :]).m(lambda a: a.rearrange("p (h c) -> p h c", h=4)), d["w_ukv"][l].rearrange("p (h two c) -> p two h c", h=4, two=2)[:, two, :, :])
            P.dma("sp", gq.v(), d["mla_q_norm"][l])
            P.dma("sp", gkv.v(), d["mla_kv_norm"][l].partition_broadcast(128))
            P.dma("sp", biasM.v(), d["bias_m"][:, :])
            P.I("pool", "memset", Vt.v(), 1.0)
            for tg in range(4):
                tsl = slice(tg * 512, (tg + 1) * 512)
                P.dma("sp", cosT.v(), d["rope_m"][0, :, tsl])
                P.dma("sp", sinT.v(), d["rope_m"][1, :, tsl])
                pq = self.fm_group(wb, [(0, 128), (128, 128)], tg)
                P.I("act", "activation", ta.v(), pq[0], AF.Square)
                P.I("act", "activation", tb_.v(), pq[1], AF.Square)
                pt = ps[self.psi % 6]; self.psi += 1
                P.mm(pt.v(), ones.v(), ta.v(), start=True, stop=False)
                P.mm(pt.v(), ones.v(), tb_.v(), start=False, stop=True)
                P.I("act", "activation", rs.v(), pt.v(), AF.Sqrt, bias=EPS, scale=1.0 / 256)
                P.I("dve", "reciprocal", rs.v(), rs.v())
                for c in range(2):
                    P.I("dve", "scalar_tensor_tensor", qnT.v(s_[:, c, tsl], tg), pq[c], gq.v(s_[:, c:c + 1]), rs.v(), op0=ALU.mult, op1=ALU.mult)
                pk = self.fm_group(wb, [(384, 32), (416, 32)], tg)
                self.rope_evac(kro.v(s_[:, 512 + tg * 512:512 + (tg + 1) * 512]), pk[0], pk[1], cosT.v(), sinT.v(),
                               ta.v(s_[0:32, :]), tb_.v(s_[0:32, :]))
            for tb in range(NB):
                pt = ps[self.psi % 6]; self.psi += 1
                for k in range(8):
                    P.mm(pt.v(s_[:, 0:160]), self.uT.v(s_[:, k, tb * 128:(tb + 1) * 128], tb), wb.v(s_[:, k, 256:416]), start=(k == 0), stop=(k == 7))
                P.I("act", "activation", kvt.v(), pt.v(s_[:, 0:160]), AF.Copy)
                ck = ckt[tb % 2]
                P.I("act", "activation", junk.v(), kvt.v(s_[:, 0:128]), AF.Square, accum_out=ss.v())
                P.I("act", "activation", ss.v(), ss.v(), AF.Sqrt, bias=EPS, scale=1.0 / 128)
                P.I("dve", "reciprocal", ss.v(), ss.v())
                P.I("dve", "scalar_tensor_tensor", ck.v(), kvt.v(s_[:, 0:128]), ss.v(), gkv.v(), op0=ALU.mult, op1=ALU.mult)
                P.dma("sp", self.dout["ckv_o"][l, tb * 128:(tb + 1) * 128, :], ck.v())
                P.dma("sp", self.dout["krope_o"][l, tb * 128:(tb + 1) * 128, :], kvt.v(s_[:, 128:160]))
                p2 = ps[self.psi % 6]; self.psi += 1
                P.I("pe", "transpose", p2.v(s_[:, 0:128]), ck.v(), ident.v())
                P.I("act", "activation", ckvT.v(s_[:, 512 + tb * 128:512 + (tb + 1) * 128], 4 + tb), p2.v(s_[:, 0:128]), AF.Copy)
            P.dma("sp", cch.v(s_[:, :, 0:128]), d["cache_ckv"][l].rearrange("(j p) c -> p j c", p=128))
            P.dma("sp", cch.v(s_[:, :, 128:160]), d["cache_krope"][l].rearrange("(j p) c -> p j c", p=128))
            for j in range(4):
                p2 = ps[self.psi % 6]; self.psi += 1
                P.I("pe", "transpose", p2.v(s_[:, 0:128]), cch.v(s_[:, j, 0:128]), ident.v())
                P.I("act", "activation", ckvT.v(s_[:, j * 128:(j + 1) * 128], j), p2.v(s_[:, 0:128]), AF.Copy)
                p3 = ps[self.psi % 6]; self.psi += 1
                P.I("pe", "transpose", p3.v(s_[0:32, 0:128]), cch.v(s_[:, j, 128:160]), ident.v())
                P.I("act", "activation", kro.v(s_[:, j * 128:(j + 1) * 128]), p3.v(s_[0:32, 0:128]), AF.Copy)
            for kc in range(NK):
                pt = ps[self.psi % 6]; self.psi += 1
                P.mm(pt.v(s_[:, 0:256]), ckvT.v(s_[:, kc * 128:(kc + 1) * 128], kc), wukv.v(s_[:, 1, :]), start=True, stop=True)
                P.I("dve", "tensor_copy", Vt.v(s_[:, kc, :, 0:64], kc), pt.v(s_[:, 0:256]).m(lambda a: a.rearrange("p (h c) -> p h c", h=4)))
            n = 0
            for h in range(4):
                for tg in range(4):
                    tsl = slice(tg * 512, (tg + 1) * 512)
                    P.dma("sp", cosT.v(), d["rope_m"][0, :, tsl])
                    P.dma("sp", sinT.v(), d["rope_m"][1, :, tsl])
                    pn = ps[self.psi % 6]; pa = ps[(self.psi + 1) % 6]; pb = ps[(self.psi + 2) % 6]; self.psi += 3
                    for c in range(2):
                        P.mm(pn.v(s_[0:64, :]), wuq.v(s_[:, c, h * 96:h * 96 + 64]), qnT.v(s_[:, c, tsl], tg), start=(c == 0), stop=(c == 1))
                    for c in range(2):
                        P.mm(pa.v(s_[0:32, :]), wuq.v(s_[:, c, h * 96 + 64:h * 96 + 96]), qnT.v(s_[:, c, tsl], tg), start=(c == 0), stop=(c == 1))
                    for c in range(2):
                        P.mm(pb.v(s_[0:32, :]), wuq.v(s_[:, c, 384 + h * 32:384 + h * 32 + 32]), qnT.v(s_[:, c, tsl], tg), start=(c == 0), stop=(c == 1))
                    P.I("act", "activation", qno.v(s_[:, tsl], tg), pn.v(s_[0:64, :]), AF.Copy)
                    self.rope_evac(qro.v(s_[:, tsl], tg), pa.v(s_[0:32, :]), pb.v(s_[0:32, :]), cosT.v(), sinT.v(),
                                   ta.v(s_[0:32, :]), tb_.v(s_[0:32, :]))
                for g5 in range(5):
                    gsl = slice(g5 * 512, (g5 + 1) * 512)
                    pt = ps[self.psi % 6]; self.psi += 1
                    P.mm(pt.v(s_[0:64, :]), wukv.v(s_[:, 0, h * 64:(h + 1) * 64]), ckvT.v(s_[:, gsl], range(g5 * 4, g5 * 4 + 4)), start=True, stop=True)
                    P.I("act", "activation", kno.v(s_[:, gsl]), pt.v(s_[0:64, :]), AF.Copy)
                for qu in range(8):
                    qsl = slice(qu * 256, (qu + 1) * 256)
                    po = [ps[6], ps[7]]
                    def emitS(kc):
                        ksl = slice(kc * 128, (kc + 1) * 128)
                        pt = ps[self.psi % 6]; self.psi += 1
                        P.mm(pt.v(s_[:, 0:256]), kno.v(s_[:, ksl]), qno.v(s_[:, qsl], qu // 2), start=True, stop=False)
                        P.mm(pt.v(s_[:, 0:256]), kro.v(s_[:, ksl]), qro.v(s_[:, qsl], qu // 2), start=False, stop=True)
                        return pt
                    pts = {0: emitS(0)}
                    for kc in range(NK):
                        if kc + 1 < NK:
                            pts[kc + 1] = emitS(kc + 1)
                        pt = pts.pop(kc)
                        pT = PT[n % 2]; n += 1
                        P.I("act", "activation", pT.v(), pt.v(s_[:, 0:256]), AF.Exp, bias=biasM.v(s_[:, qu * NK + kc:qu * NK + kc + 1]), scale=MLA_SCALE)
                        for qb in range(2):
                            P.mm(po[qb].v(s_[:, 0:65]), pT.v(s_[:, qb * 128:(qb + 1) * 128]), Vt.v(s_[:, kc, h, :], kc), start=(kc == 0), stop=(kc == NK - 1))
                    for qb in range(2):
                        tb = qu * 2 + qb
                        P.I("dve", "reciprocal", rden.v(), po[qb].v(s_[:, 64:65]))
                        P.I("dve", "tensor_scalar_mul", oacc.v(s_[:, tb, (h % 2) * 64:(h % 2) * 64 + 64], tb), po[qb].v(s_[:, 0:64]), rden.v())
                if h % 2 == 1:
                    for tb in range(NB):
                        p2 = ps[self.psi % 6]; self.psi += 1
                        P.mm(p2.v(s_[:, 0:128]), oacc.v(s_[:, tb, :], tb), identb.v(), start=True, stop=True)
                        P.I("act", "activation", self.brT[0].v(s_[:, h // 2, tb * 128:(tb + 1) * 128], tb), p2.v(s_[:, 0:128]), AF.Copy)
            S.barrier()

    def gla_block(self, l, tb, wb, afT, qT, kT, w2, b2, onesb, ktm, e1t, spt, eq, ek, el, mats, vtm, kl, gcol, qeT, keT, LNQ):
        P = self; ps = self.ps
        lsl = slice((tb % 4) * 128, (tb % 4) * 128 + 128)
        bsl = slice(tb * 128, (tb + 1) * 128)
        pt = ps[self.psi % 6]; self.psi += 1
        for k in range(8):
            P.mm(pt.v(s_[:, 0:384]), self.uT.v(s_[:, k, bsl], tb), wb.v(s_[:, k, 128:512]), start=(k == 0), stop=(k == 7))
        P.I("dve", "tensor_copy", ktm.v(), pt.v(s_[:, 0:128]))
        P.I("dve", "tensor_copy", vtm.v(s_[:, tb, :], tb), pt.v(s_[:, 128:384]))
        e1t_, spt_, eq_, ek_, el_ = e1t, spt, eq, ek, el
        for dr in range(2):
            e1t, spt, eq, ek, el = e1t_[dr], spt_[dr], eq_[dr], ek_[dr], el_[dr]
            pz = ps[self.psi % 6]; self.psi += 1
            P.mm(pz.v(s_[:, 0:128]), afT.v(s_[:, dr, lsl]), w2.v(s_[:, dr, :]), start=True, stop=False)
            P.mm(pz.v(s_[:, 0:128]), onesb.v(), b2.v(s_[:, dr, :]), start=False, stop=True)
            P.I("act", "activation", e1t.v(), pz.v(s_[:, 0:128]), AF.Exp, scale=-1.0)
            P.I("act", "activation", spt.v(), e1t.v(), AF.Ln, bias=1.0, scale=1.0)
            mi = 0 if dr == 0 else 2
            if self.G1 <= 3:
                continue
            pl = ps[self.psi % 6]; self.psi += 1
            P.mm(pl.v(s_[:, 0:128]), mats.v(s_[:, mi + 1, :]), spt.v(), start=True, stop=True)
            P.I("act", "activation", el.v(), pl.v(s_[:, 0:128]), AF.Exp)
            P.I("dve", "tensor_tensor", kl[dr].v(s_[:, tb, :], tb), ktm.v(), el.v(), op=ALU.mult)
            for hp in range(2):
                pc = ps[self.psi % 6]; pg = ps[(self.psi + 1) % 6]; self.psi += 2
                P.mm(pc.v(s_[0:64, 0:128]), spt.v(s_[:, hp * 64:(hp + 1) * 64]), mats.v(s_[:, mi, :]), start=True, stop=True)
                P.mm(pg.v(s_[0:64, 0:2]), spt.v(s_[:, hp * 64:(hp + 1) * 64]), mats.v(s_[:, 4, 0:2]), start=True, stop=True)
                P.I("act", "activation", eq.v(s_[:, hp, :]), pc.v(s_[0:64, 0:128]), AF.Exp, bias=LNQ, scale=1.0)
                P.I("act", "activation", ek.v(s_[:, hp, :]), pc.v(s_[0:64, 0:128]), AF.Exp, scale=-1.0)
                P.I("act", "activation", gcol.v(s_[:, dr, hp, 2 * tb:2 * tb + 2], dr), pg.v(s_[0:64, 0:2]), AF.Exp)
            P.I("dve", "tensor_tensor", qeT[dr].v(s_[:, :, bsl], tb), qT.v(s_[:, :, lsl]), eq.v(), op=ALU.mult)
            P.I("dve", "tensor_tensor", keT[dr].v(s_[:, :, bsl], tb), kT.v(s_[:, :, lsl]), ek.v(), op=ALU.mult)

    def gla(self, l, dbg):
        P = self; S = self.S
        d = self.din
        ps, ident, ones = self.ps, self.ident, self.ones
        LNQ = float(np.log(32.0 ** -0.5))
        with ExitStack() as es:
            qeT = [P.sb(es, "qeT%d" % i, [64, 2, T], BF16, nsub=NB) for i in range(2)]
            keT = [P.sb(es, "keT%d" % i, [64, 2, T], BF16, nsub=NB) for i in range(2)]
            kl = [P.sb(es, "kl%d" % i, [128, NB, 128], BF16, nsub=NB) for i in range(2)]
            gcol = P.sb(es, "gcol", [64, 2, 2, 32], F32, nsub=2)
            vtm = P.sb(es, "vtm", [128, NB, 256], BF16, nsub=NB)
            mats = P.sb(es, "gmats", [128, 5, 128], F32)
            gmask = P.sb(es, "gmask", [128, 2, 128], BF16)
            keep = P.sb(es, "gkeep", [128, 2, 32], F32)
            identb = P.sb(es, "identbG", [128, 128], BF16)
            P.dma("sp", mats.v(), d["gla_mats"].rearrange("a p c -> p a c"))
            P.dma("sp", gmask.v(), d["gla_mask"].rearrange("a p c -> p a c"))
            P.dma("sp", keep.v(), d["gla_keep"][:, :, :])
            P.I("dve", "tensor_copy", identb.v(), ident.v())
            with ExitStack() as e1:
                wb = P.sb(e1, "wbG", [128, 8, 800], BF16)
                afT = P.sb(e1, "afT", [16, 2, 512], BF16)
                qT = P.sb(e1, "gqT", [64, 2, 512], BF16)
                kT = P.sb(e1, "gkT", [64, 2, 512], BF16)
                w2 = P.sb(e1, "gw2", [16, 2, 128], BF16)
                b2 = P.sb(e1, "gb2", [1, 2, 128], BF16)
                onesb = P.sb(e1, "onesb", [1, 128], BF16)
                ktm = P.sb(e1, "ktm", [128, 128], F32)
                e1t = [P.sb(e1, "ge1%d" % i, [128, 128], F32) for i in range(2)]
                spt = [P.sb(e1, "gsp%d" % i, [128, 128], F32) for i in range(2)]
                eq = [P.sb(e1, "geq", [64, 2, 128], F32)] * 2
                ek = [P.sb(e1, "gek", [64, 2, 128], F32)] * 2
                el = [P.sb(e1, "gel", [128, 128], F32)] * 2
                self.load_w(wb, l, 672, 800)
                P.dma("pool", w2.v(), d["w_gla_a"][l].rearrange("a r c -> r a c"))
                P.dma("pool", b2.v(), d["b_gla_a"][l:l + 1, :, :])
                P.I("pool", "memset", onesb.v(), 1.0)
                import os
                G1 = int(os.environ.get("KGLA_G1", "99"))
                for tg in range(4 if G1 >= 2 else 0):
                    tsl = slice(tg * 512, (tg + 1) * 512)
                    pq = self.fm_group(wb, [(0, 64), (64, 64), (128, 64), (192, 64)], tg)
                    P.I("act", "activation", qT.v(s_[:, 0, :]), pq[0], AF.Copy)
                    P.I("dve", "tensor_copy", qT.v(s_[:, 1, :]), pq[1])
                    P.I("act", "activation", kT.v(s_[:, 0, :]), pq[2], AF.Copy)
                    P.I("dve", "tensor_copy", kT.v(s_[:, 1, :]), pq[3])
                    pq = self.fm_group(wb, [(768, 16), (784, 16)], tg)
                    P.I("act", "activation", afT.v(s_[:, 0, :]), pq[0], AF.Copy)
                    P.I("dve", "tensor_copy", afT.v(s_[:, 1, :]), pq[1])
                    for tb in range(tg * 4, tg * 4 + 4) if G1 >= 3 else []:
                        self.G1 = G1
                        self.gla_block(l, tb, wb, afT, qT, kT, w2, b2, onesb, ktm, e1t, spt, eq, ek, el, mats, vtm, kl, gcol, qeT, keT, LNQ)
                S.barrier()
            import os
            GSTOP = int(os.environ.get("KGLA_STOP", "99"))
            if GSTOP <= 1:
                return
            with ExitStack() as e2:
                of = P.sb(e2, "gof", [128, NB, 256], F32, nsub=NB)
                St = P.sb(e2, "gS", [64, 2, 64], F32)
                tmpS = P.sb(e2, "gtmp", [64, 2, 64], F32)
                Sb = [P.sb(e2, "gSb%d" % i, [64, 2, 64], BF16) for i in range(2)]
                att = [P.sb(e2, "gatt%d" % i, [128, 128], BF16) for i in range(2)]
                na = 0
                for dr in range(2):
                    P.dma("sp", St.v(), d["gla_init"][l, dr].rearrange("(a p) c -> p a c", p=64))
                    blks = range(NB) if dr == 0 else range(NB - 1, -1, -1)
                    for tb in blks:
                        bsl = slice(tb * 128, (tb + 1) * 128)
                        halves = (0, 1) if dr == 0 else (1, 0)
                        for hf in halves:
                            n = 2 * tb + hf
                            r0 = hf * 64
                            P.I("act", "activation", Sb[hf].v(), St.v(), AF.Copy)
                            pu = ps[self.psi % 6]; self.psi += 1
                            for h in range(4):
                                P.mm(pu.v(s_[(h % 2) * 32:(h % 2) * 32 + 32, (h // 2) * 64:(h // 2) * 64 + 64]), kl[dr].v(s_[r0:r0 + 64, tb, h * 32:(h + 1) * 32], tb),
                                     vtm.v(s_[r0:r0 + 64, tb, h * 64:(h + 1) * 64], tb), start=True, stop=True)
                            for hp in range(2):
                                P.I("dve", "scalar_tensor_tensor", tmpS.v(s_[:, hp, :]), St.v(s_[:, hp, :]), gcol.v(s_[:, dr, hp, n:n + 1], dr), pu.v(s_[0:64, hp * 64:(hp + 1) * 64]), op0=ALU.mult, op1=ALU.add)
                            if (dr == 0 and n % 4 == 3) or (dr == 1 and n % 4 == 0):
                                P.dma("sp", self.dout["gla_o"][l, n // 4, dr].rearrange("(a p) c -> p a c", p=64), tmpS.v())
                            nn = n + 1 if dr == 0 else n - 1
                            if 0 <= nn < 32:
                                P.I("dve", "tensor_scalar_mul", St.v(), tmpS.v(), keep.v(s_[0:64, dr, nn:nn + 1]))
                        po = ps[6 + (tb % 2)]
                        for h in range(4):
                            hs = slice((h % 2) * 32, (h % 2) * 32 + 32); hp = h // 2
                            pa = ps[self.psi % 6]; self.psi += 1
                            P.mm(pa.v(s_[:, 0:128]), keT[dr].v(s_[hs, hp, bsl], tb), qeT[dr].v(s_[hs, hp, bsl], tb), start=True, stop=True)
                            at = att[na % 2]; na += 1
                            P.I("dve", "tensor_tensor", at.v(), pa.v(s_[:, 0:128]), gmask.v(s_[:, dr, :]), op=ALU.mult)
                            oc = slice(h * 64, (h + 1) * 64)
                            P.mm(po.v(s_[:, oc]), at.v(), vtm.v(s_[:, tb, oc], tb), start=True, stop=False)
                            P.mm(po.v(s_[0:64, oc]), qeT[dr].v(s_[hs, hp, tb * 128:tb * 128 + 64], tb), Sb[0].v(s_[hs, hp, :]), start=False, stop=False)
                            P.mm(po.v(s_[64:128, oc]), qeT[dr].v(s_[hs, hp, tb * 128 + 64:tb * 128 + 128], tb), Sb[1].v(s_[hs, hp, :]), start=False, stop=True)
                        if dr == 0:
                            P.I("act", "activation", of.v(s_[:, tb, :], tb), po.v(s_[:, 0:256]), AF.Copy)
                        else:
                            P.I("dve", "tensor_tensor", of.v(s_[:, tb, :], tb), of.v(s_[:, tb, :], tb), po.v(s_[:, 0:256]), op=ALU.add)
                if GSTOP <= 2:
                    S.barrier(); return
                wg = P.sb(e2, "wgG", [128, 8, 256], BF16)
                gn = P.sb(e2, "ggn", [128, 64], F32)
                ss4 = P.sb(e2, "gss", [128, 4], F32)
                junk = P.sb(e2, "gjunk", [128, 64], F32)
                sg = P.sb(e2, "gsil", [128, 256], F32)
                ob = P.sb(e2, "gob", [128, 256], BF16)
                self.load_w(wg, l, 1184, 256)
                P.dma("sp", gn.v(), d["gla_norm"][l].partition_broadcast(128))
                for tb in range(NB):
                    bsl = slice(tb * 128, (tb + 1) * 128)
                    for h in range(4):
                        P.I("act", "activation", junk.v(), of.v(s_[:, tb, h * 64:(h + 1) * 64], tb), AF.Square, accum_out=ss4.v(s_[:, h:h + 1]))
                    P.I("act", "activation", ss4.v(), ss4.v(), AF.Sqrt, bias=EPS, scale=1.0 / 64)
                    P.I("dve", "reciprocal", ss4.v(), ss4.v())
                    ov = of.v(s_[:, tb, :], tb).m(lambda a: a.rearrange("p (h c) -> p h c", h=4))
                    P.I("dve", "tensor_tensor", ov, ov, ss4.v().m(lambda a: a.unsqueeze(2).to_broadcast([128, 4, 64])), op=ALU.mult)
                    P.I("dve", "tensor_tensor", ov, ov, gn.v().m(lambda a: a.unsqueeze(1).to_broadcast([128, 4, 64])), op=ALU.mult)
                    pg = ps[self.psi % 6]; self.psi += 1
                    for k in range(8):
                        P.mm(pg.v(s_[:, 0:256]), self.uT.v(s_[:, k, bsl], tb), wg.v(s_[:, k, :]), start=(k == 0), stop=(k == 7))
                    P.I("act", "activation", sg.v(), pg.v(s_[:, 0:256]), AF.Silu)
                    P.I("dve", "tensor_tensor", ob.v(), of.v(s_[:, tb, :], tb), sg.v(), op=ALU.mult)
                    for c in range(2):
                        p2 = ps[self.psi % 6]; self.psi += 1
                        P.mm(p2.v(s_[:, 0:128]), ob.v(s_[:, c * 128:(c + 1) * 128]), identb.v(), start=True, stop=True)
                        P.I("act", "activation", self.brT[2].v(s_[:, c, bsl], tb), p2.v(s_[:, 0:128]), AF.Copy)
                S.barrier()

    def swa_kv_only(self, l):
        P = self; S = self.S
        ps = self.ps
        with ExitStack() as es:
            wb = P.sb(es, "wbS2", [128, 8, 256], BF16)
            kvo = [P.sb(es, "kvo2_%d" % i, [128, 256], F32) for i in range(2)]
            self.load_w(wb, l, 1728, 256)
            for tb in range(NB):
                pt = ps[self.psi % 6]; self.psi += 1
                for k in range(8):
                    P.mm(pt.v(s_[:, 0:256]), self.uT.v(s_[:, k, tb * 128:(tb + 1) * 128], tb), wb.v(s_[:, k, 0:256]), start=(k == 0), stop=(k == 7))
                ko = kvo[tb % 2]
                P.I("act", "activation", ko.v(), pt.v(s_[:, 0:256]), AF.Copy)
                P.dma("sp", self.dout["swakv_o"][l, tb * 128:(tb + 1) * 128, :], ko.v())
            S.barrier()

    def swa(self, l, dbg):
        P = self; S = self.S
        d = self.din
        ps, ident, ones = self.ps, self.ident, self.ones
        NK = 20
        with ExitStack() as es:
            wb = P.sb(es, "wbS", [128, 8, 896], BF16)
            cosT = P.sb(es, "cosS", [64, 512], F32)
            sinT = P.sb(es, "sinS", [64, 512], F32)
            biasS = P.sb(es, "biasS", [128, 4], F32)
            esink = P.sb(es, "esink", [128, 4], F32)
            msk = P.sb(es, "mskS", [128, NB, 2, 128], BF16)
            qT = P.sb(es, "qTS", [64, 4, T], BF16, nsub=4)
            kT = P.sb(es, "kTS", [64, 2, 2560], BF16)
            Vt = P.sb(es, "VtS", [128, NK, 2, 65], BF16, nsub=NK)
            ta = P.sb(es, "sta", [64, 512], F32)
            tb_ = P.sb(es, "stb", [64, 512], F32)
            kvo = [P.sb(es, "kvo%d" % i, [128, 256], F32) for i in range(2)]
            cch = P.sb(es, "cchS", [128, 4, 2, 64], F32)
            PT = [P.sb(es, "PTS%d" % i, [128, 256], BF16) for i in range(2)]
            oa = P.sb(es, "oaS", [128, 256], F32)
            rden = P.sb(es, "rdenS", [128, 1], F32)
            self.load_w(wb, l, 1472, 512)
            P.dma("pool", wb.v(s_[:, :, 512:896]), d["w_in"][l, :, 6112:6496].rearrange("(k p) c -> p k c", p=128))
            P.dma("sp", biasS.v(), d["bias_s"][:, :])
            P.dma("sp", esink.v(), d["swa_sink"][l])
            P.I("act", "activation", esink.v(), esink.v(), AF.Exp)
            P.dma("sp", msk.v(), d["mask_s"][:, :, :, :])
            P.I("pool", "memset", Vt.v(), 1.0)
            import os
            STOP = int(os.environ.get("KSWA_STOP", "99"))
            if STOP <= 1:
                S.barrier(); return
            for tg in range(4):
                tsl = slice(tg * 512, (tg + 1) * 512)
                P.dma("sp", cosT.v(), d["rope_s"][0, :, tsl])
                P.dma("sp", sinT.v(), d["rope_s"][1, :, tsl])
                for h in range(4):
                    pq = self.fm_group(wb, [(h * 64, 64), (512 + h * 64, 64)], tg)
                    self.rope_evac(qT.v(s_[:, h, tsl], h), pq[0], pq[1], cosT.v(), sinT.v(), ta.v(), tb_.v())
                for j in range(2):
                    pk = self.fm_group(wb, [(256 + j * 64, 64), (768 + j * 64, 64)], tg)
                    self.rope_evac(kT.v(s_[:, j, 512 + tg * 512:512 + (tg + 1) * 512]), pk[0], pk[1], cosT.v(), sinT.v(), ta.v(), tb_.v())
            if STOP <= 2:
                S.barrier(); return
            for tb in range(NB):
                pt = ps[self.psi % 6]; self.psi += 1
                for k in range(8):
                    P.mm(pt.v(s_[:, 0:256]), self.uT.v(s_[:, k, tb * 128:(tb + 1) * 128], tb), wb.v(s_[:, k, 256:512]), start=(k == 0), stop=(k == 7))
                ko = kvo[tb % 2]
                P.I("act", "activation", ko.v(), pt.v(s_[:, 0:256]), AF.Copy)
                P.dma("sp", self.dout["swakv_o"][l, tb * 128:(tb + 1) * 128, :], ko.v())
                P.I("dve", "tensor_copy", Vt.v(s_[:, 4 + tb, :, 0:64], 4 + tb), ko.v(s_[:, 128:256]).m(lambda a: a.rearrange("p (j c) -> p j c", j=2)))
            if STOP <= 3:
                S.barrier(); return
            for j in range(2):
                P.dma("sp", cch.v(s_[:, :, j, :]), d["cache_swak"][l, j].rearrange("(c p) e -> p c e", p=128))
            for c in range(4):
                for j in range(2):
                    p2 = ps[self.psi % 6]; self.psi += 1
                    P.I("pe", "transpose", p2.v(s_[0:64, 0:128]), cch.v(s_[:, c, j, :]), ident.v())
                    P.I("act", "activation", kT.v(s_[:, j, c * 128:(c + 1) * 128]), p2.v(s_[0:64, 0:128]), AF.Copy)
            cchv = P.sb(es, "cchV", [128, 4, 2, 64], F32)
            for j in range(2):
                P.dma("sp", cchv.v(s_[:, :, j, :]), d["cache_swav"][l, j].rearrange("(c p) e -> p c e", p=128))
            for c in range(4):
                P.I("dve", "tensor_copy", Vt.v(s_[:, c, :, 0:64], c), cchv.v(s_[:, c, :, :]))
            n = 0
            import os
            for tb in range(NB if "noatt" not in os.environ.get("KSKIP", "") else 0):
                qsl = slice(tb * 128, (tb + 1) * 128)
                for j in range(2):
                    po = [ps[6], ps[7]]
                    kcs = [(4 + tb + dd, dd) for dd in (-1, 0, 1) if 0 <= tb + dd < NB] + [(c, 2) for c in range(4)]
                    def emitS(i):
                        kc, kind = kcs[i]
                        ksl = slice(kc * 128, (kc + 1) * 128)
                        pt = ps[self.psi % 6]; self.psi += 1
                        for g in range(2):
                            P.mm(pt.v(s_[:, g * 128:(g + 1) * 128]), kT.v(s_[:, j, ksl]), qT.v(s_[:, 2 * j + g, qsl], 2 * j + g), start=True, stop=True)
                        return pt
                    pts = {0: emitS(0)}
                    for i, (kc, kind) in enumerate(kcs):
                        if i + 1 < len(kcs):
                            pts[i + 1] = emitS(i + 1)
                        pt = pts.pop(i)
                        pT = PT[n % 2]; n += 1
                        if kind == 2:
                            P.I("act", "activation", pT.v(), pt.v(s_[:, 0:256]), AF.Exp, bias=biasS.v(s_[:, 0:1]), scale=SWA_SCALE)
                        else:
                            P.I("act", "activation", pT.v(), pt.v(s_[:, 0:256]), AF.Exp, scale=SWA_SCALE)
                            if kind != 0:
                                mi = 0 if kind == -1 else 1
                                for g in range(2):
                                    P.I("dve", "tensor_tensor", pT.v(s_[:, g * 128:(g + 1) * 128]), pT.v(s_[:, g * 128:(g + 1) * 128]), msk.v(s_[:, tb, mi, :]), op=ALU.mult)
                        for g in range(2):
                            P.mm(po[g].v(s_[:, 0:65]), pT.v(s_[:, g * 128:(g + 1) * 128]), Vt.v(s_[:, kc, j, :], kc), start=(i == 0), stop=(i == len(kcs) - 1))
                    for g in range(2):
                        hh = 2 * j + g
                        P.I("dve", "tensor_tensor", rden.v(), po[g].v(s_[:, 64:65]), esink.v(s_[:, hh:hh + 1]), op=ALU.add)
                        P.I("dve", "reciprocal", rden.v(), rden.v())
                        P.I("dve", "tensor_scalar_mul", oa.v(s_[:, hh * 64:(hh + 1) * 64]), po[g].v(s_[:, 0:64]), rden.v())
                for c in range(2):
                    p2 = ps[self.psi % 6]; self.psi += 1
                    P.I("pe", "transpose", p2.v(s_[:, 0:128]), oa.v(s_[:, c * 128:(c + 1) * 128]), ident.v())
                    P.I("act", "activation", self.brT[3].v(s_[:, c, tb * 128:(tb + 1) * 128], tb), p2.v(s_[:, 0:128]), AF.Copy)
            S.barrier()

    def fnet(self, l):
        P = self; S = self.S
        d = self.din
        with ExitStack() as es:
            wb = P.sb(es, "wbF", [128, 8, 256], BF16)
            fT = P.sb(es, "fT", [128, 2, T], BF16, nsub=4)
            cd = P.sb(es, "cdft", [128, 2, 128], BF16)
            A = P.sb(es, "fA", [128, NB, 256], BF16, nsub=NB)
            B = P.sb(es, "fB", [128, NB, 256], BF16, nsub=NB)
            tc_ = [P.sb(es, "dc%d" % i, [128, 512], BF16) for i in range(4)]
            ts_ = [P.sb(es, "ds%d" % i, [128, 512], BF16) for i in range(4)]
            self.load_w(wb, l, 416, 256)
            P.dma("sp", cd.v(), d["cdft"].rearrange("a p c -> p a c"))
            for c in range(2):
                def ev(tg, pv, c=c):
                    P.I("act", "activation", fT.v(s_[:, c, tg * 512:(tg + 1) * 512], tg), pv, AF.Copy)
                self.proj_fm(wb, c * 128, 128, ev)
            for tb in range(NB):
                pa = self.ps[self.psi % 8]; self.psi += 1
                for c in range(2):
                    P.mm(pa.v(s_[:, c * 128:(c + 1) * 128]), fT.v(s_[:, c, tb * 128:(tb + 1) * 128], tb // 4), cd.v(s_[:, 0, :]), start=True, stop=True)
                    P.mm(pa.v(s_[:, 256 + c * 128:256 + (c + 1) * 128]), fT.v(s_[:, c, tb * 128:(tb + 1) * 128], tb // 4), cd.v(s_[:, 1, :]), start=True, stop=True)
                P.I("act", "activation", A.v(s_[:, tb, :], tb), pa.v(s_[:, 0:256]), AF.Copy)
                P.I("act", "activation", B.v(s_[:, tb, :], tb), pa.v(s_[:, 256:512]), AF.Copy)
            n = 0
            for tg in range(4):
                p0 = self.ps[self.psi % 8]; p1 = self.ps[(self.psi + 1) % 8]; self.psi += 2
                for tb in range(NB):
                    ct = tc_[n % 4]; st = ts_[n % 4]; n += 1
                    P.dma("sp", ct.v(), d["dft_c"][tb * 128:(tb + 1) * 128, tg * 512:(tg + 1) * 512])
                    P.dma("act", st.v(), d["dft_s"][tb * 128:(tb + 1) * 128, tg * 512:(tg + 1) * 512])
                    for c, pp in ((0, p0), (1, p1)):
                        P.mm(pp.v(), A.v(s_[:, tb, c * 128:(c + 1) * 128], tb), ct.v(), start=(tb == 0), stop=False)
                        P.mm(pp.v(), B.v(s_[:, tb, c * 128:(c + 1) * 128], tb), st.v(), start=False, stop=(tb == NB - 1))
                for c, pp in ((0, p0), (1, p1)):
                    P.I("act" if c == 0 else "dve", "activation" if c == 0 else "tensor_copy", self.brT[1].v(s_[:, c, tg * 512:(tg + 1) * 512], range(tg * 4, tg * 4 + 4)),
                        pp.v(), *((AF.Copy,) if c == 0 else ()))
            S.barrier()

    def merge(self, l, dbg):
        P = self; S = self.S
        d = self.din
        x, uT, ps, mcol, ident, ones = self.x, self.uT, self.ps, self.mcol, self.ident, self.ones
        with ExitStack() as es:
            wbr = P.sb(es, "wbr", [128, 8, D], BF16)
            wo = P.sb(es, "wo", [128, 8, D], BF16)
            wg = [P.sb(es, "wg%d" % i, [128, 8, 512], BF16) for i in range(1)]
            G = P.sb(es, "Gacc", [128, 512], F32)
            GT = P.sb(es, "GT", [128, 8, 512], BF16, nsub=8)
            sg = [P.sb(es, "sg%d" % i, [128, 512], F32) for i in range(2)]
            gbc = P.sb(es, "g1bc", [128, D], F32)
            lng = P.sb(es, "ln1g", [128, D], F32)
            lnb = P.sb(es, "ln1b", [128, D], F32)
            dg = P.sb(es, "dgm", [128, 128], F32)
            xt = [P.sb(es, "xt%d" % i, [128, D], F32) for i in range(1)]
            tmps = [self.ln_tmp(es, "g%d" % i) for i in range(2)]
            P.dma("pool", wbr.v(), d["w_branch"][l].rearrange("b (k p) d -> p (b k) d", p=128))
            P.dma("pool", wo.v(), d["w_out"][l].rearrange("(k p) d -> p k d", p=128))
            P.dma("sp", lng.v(), d["ln"][l, 0, :].partition_broadcast(128))
            P.dma("sp", lnb.v(), d["ln"][l, 1, :].partition_broadcast(128))
            for k in range(8):
                P.I("dve", "tensor_scalar_mul", dg.v(), ident.v(), mcol.v(s_[:, l, 16 + k:17 + k], l))
                pt = ps[self.psi % 8]; self.psi += 1
                P.mm(pt.v(s_[:, 0:128]), ones.v(), dg.v(), start=True, stop=True)
                P.I("act", "activation", gbc.v(s_[:, k * 128:(k + 1) * 128]), pt.v(s_[:, 0:128]), AF.Copy)
            n = 0
            for tg in range(4):
                tsub = range(tg * 4, tg * 4 + 4)
                tsl = slice(tg * 512, (tg + 1) * 512)
                for dc in range(8):
                    w = wg[0]; n += 1
                    P.dma("pool", w.v(), d["w_gate"][l, dc])
                    for b in range(4):
                        pg = ps[self.psi % 8]; pp = ps[(self.psi + 1) % 8]; self.psi += 2
                        for k in range(8):
                            P.mm(pg.v(), w.v(s_[:, k, b * 128:(b + 1) * 128]), uT.v(s_[:, k, tsl], tsub), start=(k == 0), stop=(k == 7))
                        for kc in range(2):
                            P.mm(pp.v(), wbr.v(s_[:, b * 2 + kc, dc * 128:(dc + 1) * 128]), self.brT[b].v(s_[:, kc, tsl], tsub),
                                 start=(kc == 0), stop=(kc == 1))
                        sgt = sg[b % 2]
                        P.I("act", "activation", sgt.v(), pg.v(), AF.Sigmoid)
                        if b == 0:
                            P.I("dve", "tensor_tensor", G.v(), sgt.v(), pp.v(), op=ALU.mult)
                        else:
                            P.I("dve", "tensor_tensor", sgt.v(), sgt.v(), pp.v(), op=ALU.mult)
                            if b < 3:
                                P.I("pool", "tensor_tensor", G.v(), G.v(), sgt.v(), op=ALU.add)
                            else:
                                P.I("pool", "tensor_tensor", GT.v(s_[:, dc, :], dc), G.v(), sgt.v(), op=ALU.add)
                for j in range(4):
                    tb = tg * 4 + j
                    xtb = xt[0]
                    for hf in range(2):
                        pm = ps[self.psi % 8]; self.psi += 1
                        for k in range(8):
                            P.mm(pm.v(), GT.v(s_[:, k, j * 128:(j + 1) * 128], k), wo.v(s_[:, k, hf * 512:(hf + 1) * 512]), start=(k == 0), stop=(k == 7))
                        hs = slice(hf * 512, (hf + 1) * 512)
                        P.I("dve", "tensor_tensor", xtb.v(s_[:, hs]), pm.v(), gbc.v(s_[:, hs]), op=ALU.mult)
                        P.I("dve", "scalar_tensor_tensor", xtb.v(s_[:, hs]), x.v(s_[:, tb, hs], tb), ALPHA, xtb.v(s_[:, hs]), op0=ALU.mult, op1=ALU.add)
                    self.ln_block(tmps[tb % 2], xtb.v(), x.v(s_[:, tb, :], tb))
                    P.I("pool", "tensor_tensor", x.v(s_[:, tb, :], tb), x.v(s_[:, tb, :], tb), lng.v(), op=ALU.mult)
                    P.I("pool", "tensor_tensor", x.v(s_[:, tb, :], tb), x.v(s_[:, tb, :], tb), lnb.v(), op=ALU.add)
            S.barrier()

    def post_ffn(self, l, yacc_is_x=True):
        P = self; S = self.S
        d = self.din
        x = self.x
        with ExitStack() as es:
            lng = P.sb(es, "ln2g", [128, D], F32)
            lnb = P.sb(es, "ln2b", [128, D], F32)
            xt = [P.sb(es, "xq%d" % i, [128, D], F32) for i in range(2)]
            tmps = [self.ln_tmp(es, "q%d" % i) for i in range(2)]
            P.dma("sp", lng.v(), d["ln"][l, 2, :].partition_broadcast(128))
            P.dma("sp", lnb.v(), d["ln"][l, 3, :].partition_broadcast(128))
            for tb in range(NB):
                xtb = xt[tb % 2]
                P.I("act", "activation", xtb.v(), x.v(s_[:, tb, :], tb), AF.Copy)
                self.ln_block(tmps[tb % 2], xtb.v(), x.v(s_[:, tb, :], tb))
                P.I("pool", "tensor_tensor", x.v(s_[:, tb, :], tb), x.v(s_[:, tb, :], tb), lng.v(), op=ALU.mult)
                P.I("pool", "tensor_tensor", x.v(s_[:, tb, :], tb), x.v(s_[:, tb, :], tb), lnb.v(), op=ALU.add)
            S.barrier()

    def layer(self, l, dbg):
        P = self; S = self.S
        self.psi = 0
        for g4 in range(16):
            P.dma("pool", self.ubf.v(s_[l, g4 * 4:(g4 + 1) * 4], l * 16 + g4), self.din["peer_uT"][l, g4 * 4:(g4 + 1) * 4].rearrange("g p k e -> g p (k e)"))
            P.dma("pool", self.vbf.v(s_[l, g4 * 4:(g4 + 1) * 4], l * 16 + g4), self.din["peer_v"][l, g4 * 4:(g4 + 1) * 4].rearrange("g p j d -> g p (j d)"))
        with ExitStack() as esl:
            self.uT = P.sb(esl, "uT", [128, 8, T], BF16, nsub=NB)
            self.mod_to_uT(l, 0)
            self.brT = [P.sb(esl, "brT%d" % b, [128, 2, T], BF16, nsub=NB) for b in range(4)]
            import os
            skip = os.environ.get("KSKIP", "")
            for b, nm in ((0, "mla"), (3, "swa"), (1, "fnet"), (2, "gla")):
                if nm in skip:
                    for tb4 in range(4):
                        P.I("pool", "memset", self.brT[b].v(s_[:, :, tb4 * 512:(tb4 + 1) * 512], range(tb4 * 4, tb4 * 4 + 4)), 0.0)
            if "mla" not in skip:
                self.mla(l, dbg)
            if "swa" not in skip:
                self.swa(l, dbg)
            else:
                self.swa_kv_only(l)
            if "fnet" not in skip:
                self.fnet(l)
            if "gla" not in skip:
                self.gla(l, dbg)
            self.merge(l, dbg)
            S.barrier()
        import os
        if "peer" in os.environ.get("KSKIP", ""):
            for tb in range(NB):
                P.I("act", "activation", self.x.v(s_[:, tb, :], tb), self.x.v(s_[:, tb, :], tb), AF.Copy, scale=ALPHA)
        else:
            self.peer(l, dbg)
        self.post_ffn(l)

    def peer(self, l, dbg):
        P = self; S = self.S
        d = self.din
        x, ps, mcol, ident, ones, iota, bm = self.x, self.ps, self.mcol, self.ident, self.ones, self.iota, self.bm
        NCH = 128
        TBS = 256
        with ExitStack() as es:
            u2Ts = [P.sb(es, "u2T%d" % i, [128, 8, TBS], BF16, nsub=2) for i in range(2)]
            lt = self.ln_tmp(es, "P")
            wq = P.sb(es, "wq", [128, 8, 256], BF16)
            kT = P.sb(es, "keysT", [128, 16, 128], BF16)
            gbc = P.sb(es, "g2bc", [128, D], BF16)
            qpT = P.sb(es, "qpT", [128, 16, 128], BF16, nsub=16)
            sc = P.sb(es, "psc", [128, 16, 128], F32, nsub=16)
            vtop = P.sb(es, "vtop", [128, 16, 16], F32, nsub=16)
            itop = P.sb(es, "itop", [128, 16, 16], U32, nsub=16)
            idx1f = P.sb(es, "idx1f", [128, 128], F32)
            dg = idx1f
            idx2f = P.sb(es, "idx2f", [128, 128], F32)
            idxTs = P.sb(es, "idxT", [128, 2, 2, 128], F32, nsub=2)
            cand = P.sb(es, "cand", [128, 8, 256], F32, nsub=8)
            t8a = P.sb(es, "t8a", [128, 8, 8], F32, nsub=8)
            t8b = P.sb(es, "t8b", [128, 8, 8], F32, nsub=8)
            nmx = P.sb(es, "nmx", [128, 8], F32)
            zz = P.sb(es, "pz", [128, 8], F32)
            wCTs = P.sb(es, "wCT", [128, 2, 128, 16], BF16, nsub=2)
            O1s = [P.sb(es, "O1_%d" % i, [128, 4, 128], BF16) for i in range(2)]
            O2s = [P.sb(es, "O2_%d" % i, [128, 4, 128], BF16) for i in range(2)]
            Cbds = [P.sb(es, "Cbd_%d" % i, [128, 4, 128], BF16) for i in range(2)]
            tmpS = [P.sb(es, "ptmp%d" % i, [128, 4, 128], BF16) for i in range(2)]
            WtT = P.sb(es, "WtT", [128, TBS, 128], BF16, nsub=TBS // 4)
            Ut = [P.sb(es, "Ut%d" % i, [128, 8, 256], BF16) for i in range(2)]
            Vt = [P.sb(es, "Vt%d" % i, [128, 2, D], BF16) for i in range(2)]
            actS = [P.sb(es, "pact%d" % i, [128, TBS], BF16) for i in range(2)]
            GS = [P.sb(es, "pG%d" % i, [128, TBS], BF16) for i in range(2)]
            P.dma("pool", kT.v(), d["peer_keysT"][l].rearrange("h q c k -> c (h q) k"))
            for k in range(8):
                P.I("dve", "tensor_scalar_mul", dg.v(), ident.v(), mcol.v(s_[:, l, 40 + k:41 + k], l))
                pt = ps[self.psi % 4]; self.psi += 1
                P.mm(pt.v(s_[:, 0:128]), ones.v(), dg.v(), start=True, stop=True)
                P.I("act", "activation", gbc.v(s_[:, k * 128:(k + 1) * 128]), pt.v(s_[:, 0:128]), AF.Copy)
            py = [ps[4], ps[5], ps[6], ps[7]]
            esc = lambda a: a.rearrange("p (h a) b -> p h (a b)", a=2)
            def sel_a(sb_):
                u2T = u2Ts[sb_ % 2]
                for sub in range(2):
                    tb = sb_ * 2 + sub
                    usl = slice(sub * 128, (sub + 1) * 128)
                    xnv = cand.v(s_[:, 0:4, :], [0, 1, 2, 3]).m(lambda a: a.rearrange("p a b -> p (a b)"))
                    self.ln_block(lt, x.v(s_[:, tb, :], tb), xnv)
                    yield
                    for half in range(2):
                        pt = ps[2 + self.psi % 2]; self.psi += 1
                        for j in range(4):
                            k = half * 4 + j
                            P.I("pe", "transpose", pt.v(s_[:, j * 128:(j + 1) * 128]), xnv.m(lambda a, k=k: a[:, k * 128:(k + 1) * 128]), ident.v())
                        for j in range(4):
                            k = half * 4 + j
                            P.I("dve", "tensor_scalar", u2T.v(s_[:, k, usl], sub), pt.v(s_[:, j * 128:(j + 1) * 128]),
                                mcol.v(s_[:, l, 32 + k:33 + k], l), mcol.v(s_[:, l, 24 + k:25 + k], l), op0=ALU.mult, op1=ALU.add)
                    for c4 in range(4):
                        pt = ps[2 + self.psi % 2]; self.psi += 1
                        for j in range(4):
                            c = c4 * 4 + j
                            if j % 2 == 0:
                                P.dma("pool", wq.v(), d["w_peer_q"][l, c // 2])
                            for k in range(8):
                                P.mm(pt.v(s_[:, j * 128:(j + 1) * 128]), wq.v(s_[:, k, (j % 2) * 128:(j % 2) * 128 + 128]), u2T.v(s_[:, k, usl], sub), start=(k == 0), stop=(k == 7))
                        P.I("act", "activation", qpT.v(s_[:, c4 * 4:(c4 + 1) * 4, :], range(c4 * 4, c4 * 4 + 4)), pt.v().m(lambda a: a.rearrange("p (j t) -> p j t", j=4)), AF.Copy)
                        yield
                    for c4 in range(4):
                        pt = ps[2 + self.psi % 2]; self.psi += 1
                        for j in range(4):
                            c = c4 * 4 + j
                            P.mm(pt.v(s_[:, j * 128:(j + 1) * 128]), qpT.v(s_[:, c, :], c), kT.v(s_[:, c, :]), start=True, stop=True)
                        P.I("act", "activation", sc.v(s_[:, c4 * 4:(c4 + 1) * 4, :], range(c4 * 4, c4 * 4 + 4)), pt.v().m(lambda a: a.rearrange("p (j t) -> p j t", j=4)), AF.Copy)
                        yield
                    wkc = lambda c: cand.v(s_[:, c // 2, (c % 2) * 128:(c % 2) * 128 + 128], c // 2)
                    for c in range(16):
                        P.I("dve", "max", vtop.v(s_[:, c, 0:8], c), sc.v(s_[:, c, :], c))
                    yield
                    for c in range(16):
                        P.I("dve", "max_index", itop.v(s_[:, c, 0:8], c), vtop.v(s_[:, c, 0:8], c), sc.v(s_[:, c, :], c))
                    yield
                    for c in range(16):
                        P.I("dve", "match_replace", wkc(c), vtop.v(s_[:, c, 0:8], c), sc.v(s_[:, c, :], c), -1e30)
                    yield
                    for c in range(16):
                        P.I("dve", "max", vtop.v(s_[:, c, 8:16], c), wkc(c))
                    yield
                    for c in range(16):
                        P.I("dve", "max_index", itop.v(s_[:, c, 8:16], c), vtop.v(s_[:, c, 8:16], c), wkc(c))
                    yield
                    v4 = lambda a: a.rearrange("p (h q) r -> p h q r", q=2)
                    P.I("dve", "tensor_copy", idx1f.v().m(lambda a: a.rearrange("p (h r) -> p h r", h=8)), itop.v().m(lambda a: v4(a)[:, :, 0, :]))
                    P.I("dve", "tensor_copy", idx2f.v().m(lambda a: a.rearrange("p (h r) -> p h r", h=8)), itop.v().m(lambda a: v4(a)[:, :, 1, :]))
                    P.I("dve", "tensor_tensor", cand.v().m(lambda a: a.rearrange("p h (a b) -> p h a b", a=16)),
                        vtop.v().m(lambda a: v4(a)[:, :, 0, :].unsqueeze(3).to_broadcast([128, 8, 16, 16])),
                        vtop.v().m(lambda a: v4(a)[:, :, 1, :].unsqueeze(2).to_broadcast([128, 8, 16, 16])), op=ALU.add)
                    yield
                    wkh = lambda h: sc.v(s_[:, 2 * h:2 * h + 2, :], [2 * h, 2 * h + 1]).m(lambda a: a.rearrange("p a b -> p (a b)"))
                    for h in range(8):
                        P.I("dve", "max", t8a.v(s_[:, h, :], h), cand.v(s_[:, h, :], h))
                    for h in range(8):
                        P.I("dve", "match_replace", wkh(h), t8a.v(s_[:, h, :], h), cand.v(s_[:, h, :], h), -1e30)
                    for h in range(8):
                        P.I("dve", "max", t8b.v(s_[:, h, :], h), wkh(h))
                    yield
                    P.I("dve", "tensor_scalar_mul", nmx.v(), t8a.v(s_[:, :, 0]), -1.0)
                    for h in range(8):
                        ev = sc.v(s_[:, 2 * h:2 * h + 2, :], [2 * h, 2 * h + 1]).m(lambda a: a.rearrange("p a b -> p (a b)"))
                        P.I("act", "activation", ev, cand.v(s_[:, h, :], h), AF.Exp, bias=nmx.v(s_[:, h:h + 1]), scale=1.0)
                        P.I("dve", "scalar_tensor_tensor", ev, cand.v(s_[:, h, :], h), t8b.v(s_[:, h, 7:8], h), ev, op0=ALU.is_ge, op1=ALU.mult)
                    yield
                    P.I("dve", "tensor_reduce", zz.v(), sc.v().m(esc), axis=AX.X, op=ALU.add)
                    P.I("dve", "reciprocal", zz.v(), zz.v())
                    P.I("dve", "tensor_tensor", sc.v().m(esc), sc.v().m(esc), zz.v().m(lambda a: a.unsqueeze(2).to_broadcast([128, 8, 256])), op=ALU.mult)
                    yield
                    pt = ps[2 + self.psi % 2]; self.psi += 1
                    P.I("pe", "transpose", pt.v(s_[:, 0:128]), idx1f.v(), ident.v())
                    P.I("pe", "transpose", pt.v(s_[:, 128:256]), idx2f.v(), ident.v())
                    P.I("act", "activation", idxTs.v(s_[:, sub], sub), pt.v(s_[:, 0:256]).m(lambda a: a.rearrange("p (a t) -> p a t", a=2)), AF.Copy)
                    for r4 in range(4):
                        yield
                        pt = ps[2 + self.psi % 2]; self.psi += 1
                        for j in range(4):
                            r2 = r4 * 4 + j
                            P.I("pe", "transpose", pt.v(s_[:, j * 128:(j + 1) * 128]),
                                sc.v().m(lambda a, r2=r2: a.rearrange("p c (a b) -> p (c a) b", b=16)[:, :, r2]), ident.v())
                        P.I("act", "activation", wCTs.v(s_[:, sub, :, r4 * 4:(r4 + 1) * 4], sub).m(lambda a: a.rearrange("p t j -> p j t")),
                            pt.v().m(lambda a: a.rearrange("p (j t) -> p j t", j=4)), AF.Copy)

                yield
            def expand(sb_):
                items = [(sub, sbk) for sub in range(2) for sbk in range(32)]
                def genO(i):
                    sub, sbk = items[i]
                    t0 = sbk * 4
                    o1, o2, cb = O1s[i % 2], O2s[i % 2], Cbds[i % 2]
                    P.I("dve", "tensor_tensor", o1.v(), iota.v().m(lambda a: a.unsqueeze(1).to_broadcast([128, 4, 128])),
                        idxTs.v(s_[:, sub, 0, t0:t0 + 4], sub).m(lambda a: a.unsqueeze(2).to_broadcast([128, 4, 128])), op=ALU.is_equal)
                    P.I("dve", "tensor_tensor", o2.v(), iota.v().m(lambda a: a.unsqueeze(1).to_broadcast([128, 4, 128])),
                        idxTs.v(s_[:, sub, 1, t0:t0 + 4], sub).m(lambda a: a.unsqueeze(2).to_broadcast([128, 4, 128])), op=ALU.is_equal)
                    P.I("pool", "tensor_tensor", cb.v().m(lambda a: a.rearrange("p t (h r) -> p t h r", h=8)),
                        wCTs.v(s_[:, sub, t0:t0 + 4, :], sub).m(lambda a: a.unsqueeze(2).to_broadcast([128, 4, 8, 16])),
                        bm.v().m(lambda a: a.unsqueeze(1).unsqueeze(3).to_broadcast([128, 4, 8, 16])), op=ALU.mult)
                def mm1(i):
                    o1, cb = O1s[i % 2], Cbds[i % 2]
                    pt = ps[self.psi % 4]; self.psi += 1
                    for j in range(4):
                        P.mm(pt.v(s_[:, j * 128:(j + 1) * 128]), cb.v(s_[:, j, :]), o1.v(s_[:, j, :]), start=True, stop=True)
                    P.I("act", "activation", tmpS[i % 2].v(), pt.v().m(lambda a: a.rearrange("p (j i) -> p j i", j=4)), AF.Copy)
                def mm2(i):
                    sub, sbk = items[i]
                    o2 = O2s[i % 2]
                    tS = tmpS[i % 2]
                    pt2 = ps[self.psi % 4]; self.psi += 1
                    for j in range(4):
                        P.mm(pt2.v(s_[:, j * 128:(j + 1) * 128]), o2.v(s_[:, j, :]), tS.v(s_[:, j, :]), start=True, stop=True)
                    ta = sub * 128 + sbk * 4
                    P.I("act", "activation", WtT.v(s_[:, ta:ta + 4, :], ta // 4), pt2.v().m(lambda a: a.rearrange("p (j i) -> p j i", j=4)), AF.Copy)
                genO(0); mm1(0)
                for i in range(len(items)):
                    if i + 1 < len(items):
                        genO(i + 1); mm1(i + 1)
                    mm2(i)

            def expert(sb_, gen):
                u2T = u2Ts[sb_ % 2]
                def emitU(c):
                    c2, j = c // 2, c % 2
                    if j == 0:
                        ut = Ut[c2 % 2]; vt = Vt[c2 % 2]
                        P.dma("sp", ut.v().m(lambda a: a.rearrange("p k e -> p (k e)")), self.ubf.v(s_[l, c2], l * 16 + c2 // 4))
                        P.dma("sp", vt.v().m(lambda a: a.rearrange("p j d -> p (j d)")), self.vbf.v(s_[l, c2], l * 16 + c2 // 4))
                    ut = Ut[c2 % 2]
                    pa = ps[c % 2]
                    for k in range(8):
                        P.mm(pa.v(s_[:, 0:TBS]), ut.v(s_[:, k, j * 128:(j + 1) * 128]), u2T.v(s_[:, k, :]), start=(k == 0), stop=(k == 7))
                def emitMV(c):
                    c2, j = c // 2, c % 2
                    vt = Vt[c2 % 2]
                    pa = ps[c % 2]
                    aS = actS[c % 2]; gS = GS[c % 2]
                    P.I("act", "activation", aS.v(), pa.v(s_[:, 0:TBS]), AF.Gelu)
                    P.I("dve", "tensor_tensor", gS.v(), aS.v(), WtT.v(s_[:, :, c]), op=ALU.mult)
                    for sub in range(2):
                        for hf in range(2):
                            P.mm(py[sub * 2 + hf].v(), gS.v(s_[:, sub * 128:(sub + 1) * 128]), vt.v(s_[:, j, hf * 512:(hf + 1) * 512]), start=(c == 0), stop=(c == NCH - 1))
                emitU(0)
                for c in range(NCH):
                    if c + 1 < NCH:
                        emitU(c + 1)
                    emitMV(c)
                    if gen is not None and c % 2 == 1:
                        next(gen, None)
                if gen is not None:
                    for _ in gen:
                        pass

            def finalize(sb_):
                for sub in range(2):
                    tb = sb_ * 2 + sub
                    for hf in range(2):
                        hs = slice(hf * 512, (hf + 1) * 512)
                        yv = cand.v(s_[:, 0:2, :], [0, 1]).m(lambda a: a.rearrange("p a b -> p (a b)"))
                        P.I("dve", "tensor_tensor", yv, py[sub * 2 + hf].v(), gbc.v(s_[:, hs]), op=ALU.mult)
                        P.I("dve", "scalar_tensor_tensor", x.v(s_[:, tb, hs], tb), x.v(s_[:, tb, hs], tb), ALPHA, yv, op0=ALU.mult, op1=ALU.add)

            NSB = T // TBS
            for _ in sel_a(0):
                pass
            expand(0)
            for sb_ in range(NSB):
                gen = sel_a(sb_ + 1) if sb_ + 1 < NSB else None
                expert(sb_, gen)
                finalize(sb_)
                if sb_ + 1 < NSB:
                    expand(sb_ + 1)
            S.barrier()

def _bf(a):
    return np.ascontiguousarray(a).astype(ml_dtypes.bfloat16)


def host_consts(kind):
    c = {}
    c["ident"] = np.eye(128, dtype=np.float32)
    c["bm"] = np.ascontiguousarray((np.arange(128)[:, None] // 16 == np.arange(8)[None, :]).astype(np.float32))
    seqlen = T if kind == "sample" else 256
    n = np.arange(seqlen)
    ang = 2.0 * np.pi * np.outer(n, n) / seqlen
    sc = 1.0 / np.sqrt(seqlen * 64.0)
    cb = np.cos(ang) * sc
    sbm = -np.sin(ang) * sc
    Cf = np.zeros((T, T), np.float64)
    Sf = np.zeros((T, T), np.float64)
    for i in range(T // seqlen):
        sl = slice(i * seqlen, (i + 1) * seqlen)
        Cf[sl, sl] = cb
        Sf[sl, sl] = sbm
    c["dft_c"] = _bf(Cf.astype(np.float32))
    c["dft_s"] = _bf(Sf.astype(np.float32))
    m = np.arange(64)
    a2 = 2.0 * np.pi * np.outer(m, m) / 64.0
    cc = np.zeros((2, 128, 128), np.float64)
    for g in range(2):
        cc[0, g * 64:(g + 1) * 64, g * 64:(g + 1) * 64] = np.cos(a2)
        cc[1, g * 64:(g + 1) * 64, g * 64:(g + 1) * 64] = np.sin(a2)
    c["cdft"] = _bf(cc.astype(np.float32))
    t = np.arange(T)
    rows = (t // 64).astype(np.float64); cols = (t % 64).astype(np.float64)
    for nm, R in (("rope_m", 32), ("rope_s", 64)):
        half = R // 2; q = R // 4
        tab = np.zeros((2, R, T), np.float64)
        for dd in range(R):
            pos = rows if dd < half else cols
            fi = dd % q
            freq = 10000.0 ** (-(2.0 * fi) / half)
            ang = pos * freq
            if kind == "sample":
                tab[0, dd] = np.cos(ang)
                tab[1, dd] = np.sin(ang) * (-1.0 if (dd // q) % 2 == 0 else 1.0)
            else:
                tab[0, dd] = 1.0
        c[nm] = np.ascontiguousarray(tab.astype(np.float32))
    bm_ = np.zeros((160,), np.float32)
    if kind == "prompt":
        for qu in range(8):
            for kc in range(20):
                ok = kc >= 4 and (kc - 4) // 2 == qu
                bm_[qu * 20 + kc] = 0.0 if ok else NEG
    c["bias_m"] = np.ascontiguousarray(np.broadcast_to(bm_[None, :], (128, 160)))
    bs_ = np.zeros((128, 4), np.float32)
    if kind == "prompt":
        bs_[:, 0] = NEG
    c["bias_s"] = bs_
    mk = np.zeros((128, NB, 2, 128), np.float32)
    kk = np.arange(128)[:, None]; qq = np.arange(128)[None, :]
    for tb in range(NB):
        if kind == "sample":
            mk[:, tb, 0, :] = (kk >= qq)
            mk[:, tb, 1, :] = (kk <= qq)
        else:
            mk[:, tb, 0, :] = 1.0 if tb % 2 == 1 else 0.0
            mk[:, tb, 1, :] = 1.0 if tb % 2 == 0 else 0.0
    c["mask_s"] = _bf(mk)
    tt = np.arange(128)[:, None]; tp = np.arange(128)[None, :]
    same = (tt // 64) == (tp // 64)
    cc_ = -1.0 / 16.0
    gm = np.zeros((5, 128, 128), np.float32)
    gm[0] = cc_ * (same & (tt <= tp))
    gm[1] = cc_ * (same & (tt > tp))
    gm[2] = cc_ * (same & (tt >= tp))
    gm[3] = cc_ * (same & (tt < tp))
    gm[4, :, 0] = cc_ * (np.arange(128) < 64)
    gm[4, :, 1] = cc_ * (np.arange(128) >= 64)
    c["gla_mats"] = gm
    c["gla_mask"] = _bf(np.stack([(same & (tt <= tp)), (same & (tt >= tp))]).astype(np.float32))
    kp = np.ones((128, 2, 32), np.float32)
    if kind == "prompt":
        for n_ in range(32):
            if n_ % 4 == 0:
                kp[:, 0, n_] = 0.0
            if n_ % 4 == 3:
                kp[:, 1, n_] = 0.0
    c["gla_keep"] = kp
    return c


def perm_swap(R):
    q = R // 4
    return np.array([d + q if (d // q) % 2 == 0 else d - q for d in range(R)])


def host_weights(inp):
    w = {}
    w["w_ada"] = np.ascontiguousarray(inp["w_ada"], dtype=np.float32)
    w["b_adaT"] = np.ascontiguousarray(inp["b_ada"].reshape(L, 48, 128).transpose(0, 2, 1), dtype=np.float32)
    w_in = np.asarray(inp["w_in"], dtype=np.float32)
    p32 = perm_swap(32); p64 = perm_swap(64)
    kr = w_in[:, :, 384:416][:, :, p32]
    sq = w_in[:, :, 1472:1728].reshape(L, D, 4, 64)[:, :, :, p64].reshape(L, D, 256)
    sk = w_in[:, :, 1728:1856].reshape(L, D, 2, 64)[:, :, :, p64].reshape(L, D, 128)
    w["w_in"] = np.ascontiguousarray(np.concatenate([w_in, kr, sq, sk], axis=2))
    w["w_gate"] = np.ascontiguousarray(w_in[:, :, 1984:6080].reshape(L, 8, 128, 4, 8, 128).transpose(0, 4, 2, 1, 3, 5).reshape(L, 8, 128, 8, 512))
    w["w_branch"] = np.ascontiguousarray(inp["w_branch"], dtype=np.float32)
    w["w_out"] = np.ascontiguousarray(inp["w_out"], dtype=np.float32)
    w_uq = np.asarray(inp["w_uq"], dtype=np.float32)
    uq_sw = w_uq.reshape(L, 256, 4, 96)[:, :, :, 64:96][:, :, :, p32].reshape(L, 256, 128)
    w["w_uq"] = np.ascontiguousarray(np.concatenate([w_uq, uq_sw], axis=2))
    w["w_ukv"] = np.ascontiguousarray(inp["w_ukv"], dtype=np.float32)
    w["mla_q_norm"] = np.ascontiguousarray(np.asarray(inp["mla_q_norm"], dtype=np.float32).reshape(L, 2, 128).transpose(0, 2, 1))
    w["mla_kv_norm"] = np.ascontiguousarray(inp["mla_kv_norm"], dtype=np.float32)
    w["w_gla_a"] = np.ascontiguousarray(np.stack([inp["w_gla_a_fwd"], inp["w_gla_a_bwd"]], axis=1), dtype=np.float32)
    w["b_gla_a"] = np.ascontiguousarray(np.stack([inp["b_gla_a_fwd"], inp["b_gla_a_bwd"]], axis=1), dtype=np.float32)
    w["gla_norm"] = np.ascontiguousarray(inp["gla_norm"], dtype=np.float32)
    w["swa_sink"] = np.ascontiguousarray(np.broadcast_to(np.asarray(inp["swa_sink"], dtype=np.float32)[:, None, :], (L, 128, 4)))
    w["w_peer_q"] = np.ascontiguousarray(np.asarray(inp["w_peer_q"], dtype=np.float32).reshape(L, 8, 128, 8, 256).transpose(0, 3, 2, 1, 4))
    w["peer_keysT"] = np.ascontiguousarray(np.asarray(inp["peer_keys"], dtype=np.float32).transpose(0, 1, 2, 4, 3))
    w["peer_uT"] = np.ascontiguousarray(np.asarray(inp["peer_u"], dtype=np.float32).reshape(L, 64, 256, 8, 128).transpose(0, 1, 4, 3, 2))
    w["peer_v"] = np.ascontiguousarray(np.asarray(inp["peer_v"], dtype=np.float32).reshape(L, 64, 2, 128, D).transpose(0, 1, 3, 2, 4))
    w["ln"] = np.ascontiguousarray(np.stack([inp["ln1_g"], inp["ln1_b"], inp["ln2_g"], inp["ln2_b"]], axis=1), dtype=np.float32)
    return w


def core_inputs(inp, core, W, CS, CP):
    m = dict(W)
    if core < 2:
        m.update(CS)
        m["x"] = np.ascontiguousarray(inp["x_sample"][core], dtype=np.float32)
        cond = np.asarray(inp["c"][core], dtype=np.float32)
        m["cache_ckv"] = np.ascontiguousarray(inp["cache_mla_ckv"][core], dtype=np.float32)
        m["cache_krope"] = np.ascontiguousarray(inp["cache_mla_krope"][core], dtype=np.float32)
        m["cache_swak"] = np.ascontiguousarray(inp["cache_swa_k"][core], dtype=np.float32)
        m["cache_swav"] = np.ascontiguousarray(inp["cache_swa_v"][core], dtype=np.float32)
        m["gla_init"] = np.ascontiguousarray(np.asarray(inp["state_gla"][core], dtype=np.float32).reshape(L, 2, 128, 64))
    else:
        m.update(CP)
        j = core - 2 if core < 6 else 0
        m["x"] = np.ascontiguousarray(np.asarray(inp["x_prompt"][8 * j:8 * j + 8], dtype=np.float32).reshape(T, D))
        cond = np.asarray(inp["c_ctx"], dtype=np.float32)
        m["cache_ckv"] = np.zeros((L, 512, 128), np.float32)
        m["cache_krope"] = np.zeros((L, 512, 32), np.float32)
        m["cache_swak"] = np.zeros((L, 2, 512, 64), np.float32)
        m["cache_swav"] = np.zeros((L, 2, 512, 64), np.float32)
        m["gla_init"] = np.zeros((L, 2, 128, 64), np.float32)
    m["condT"] = np.ascontiguousarray(cond.reshape(8, 128).T)
    return m


_CACHE = {}


def kernel(**inputs):
    cores = inputs.pop("_cores", list(range(8)))
    debug = inputs.pop("_debug", None)
    stop_after = inputs.pop("_stop_after", None)
    prog = Prog(debug=debug, stop_after=stop_after)
    nc = prog.build()
    W = host_weights(inputs)
    CS = host_consts("sample")
    CP = host_consts("prompt")
    in_maps = [core_inputs(inputs, c, W, CS, CP) for c in cores]
    import os as _os
    if _os.environ.get("KTRACE"):
        res = run_bass_kernel_spmd(nc, in_maps, core_ids=list(range(len(cores))), trace=True)
        print("EXEC_TIME_NS", res.exec_time_ns)
        globals()["_LAST_RES"] = res
    else:
        res = run_bass_kernel_spmd(nc, in_maps, core_ids=list(range(len(cores))))
    R = res.results
    if debug is not None:
        return R
    y_sample = np.stack([R[0]["y"], R[1]["y"]], axis=0)
    y_prompt = np.concatenate([R[2 + j]["y"].reshape(8, 256, D) for j in range(4)], axis=0)
    ckv = np.concatenate([R[2 + j]["ckv_o"].reshape(L, 8, 256, 128).transpose(1, 0, 2, 3) for j in range(4)], axis=0)
    kr = np.concatenate([R[2 + j]["krope_o"].reshape(L, 8, 256, 32).transpose(1, 0, 2, 3) for j in range(4)], axis=0)
    kvs = [R[2 + j]["swakv_o"].reshape(L, 8, 256, 2, 2, 64) for j in range(4)]
    sk = np.concatenate([a[:, :, :, 0].transpose(1, 0, 3, 2, 4) for a in kvs], axis=0)
    sv = np.concatenate([a[:, :, :, 1].transpose(1, 0, 3, 2, 4) for a in kvs], axis=0)
    gl = np.concatenate([R[2 + j]["gla_o"].reshape(L, 8, 2, 4, 32, 64).transpose(1, 0, 2, 3, 4, 5) for j in range(4)], axis=0)
    f = lambda a: np.ascontiguousarray(a, dtype=np.float32)
    return (f(y_prompt), f(y_sample), f(ckv), f(kr), f(sk), f(sv), f(gl))
```
